# Optimizing a Trainium2 kernel written in Bass

```python
import jax
import jax.numpy as jnp
from jax import lax
import numpy as np

D_MODEL = 1024
BATCH = 8
SEQ = 4096
DEPTH = 2

HEAD_DIM = 64
ROPE_DIM = HEAD_DIM // 4
ROPE_THETA = 500000.0
NORM_EPS = 1e-6
ATTN_SCALE = HEAD_DIM ** -0.5
NEG_INF = -1e30
FORCE_SCORE = 1e4
Q_CHUNK = 128

A_HEADS = 6
A_WIDTH = A_HEADS * HEAD_DIM
A_PATTERNS = ((128, 1), (512, 4), (2048, 16))
A_BLOCK = 128

B_HEADS = 4
B_WIDTH = B_HEADS * HEAD_DIM
CMP_BLOCK = 32
CMP_STRIDE = 16
CMP_HIDDEN = 128
SEL_BLOCK = 64
SEL_TOP = 16
WIN_SIZE = 512
WIN_BLOCK = 128

C_HEADS = 4
C_WIDTH = C_HEADS * HEAD_DIM
MOBA_BLOCK = 256
MOBA_TOP = 3
MOBA_Q_CHUNK = 64

IN_SIZES = (A_WIDTH, A_WIDTH, A_WIDTH, A_WIDTH,
            B_WIDTH, HEAD_DIM, HEAD_DIM, HEAD_DIM, HEAD_DIM, HEAD_DIM, HEAD_DIM, 3 * B_HEADS, B_WIDTH,
            C_WIDTH, C_WIDTH, C_WIDTH, C_WIDTH,
            D_MODEL, D_MODEL, D_MODEL)
IN_WIDTH = sum(IN_SIZES)

kernel_name = 'hybrid_dilated_nsa_moba_block'


def rms_norm(x, g):
    xf = x.astype(jnp.float32)
    y = xf * lax.rsqrt(jnp.mean(xf * xf, axis=-1, keepdims=True) + NORM_EPS)
    return (y * g.astype(jnp.float32)).astype(x.dtype)


def partial_rope(x, positions):
    half = ROPE_DIM // 2
    freqs = ROPE_THETA ** (-jnp.arange(half, dtype=jnp.float32) / half)
    ang = positions.astype(jnp.float32)[..., None] * freqs
    cos = jnp.cos(ang)[:, :, None, :]
    sin = jnp.sin(ang)[:, :, None, :]
    xr = x[..., :ROPE_DIM].astype(jnp.float32)
    x1, x2 = xr[..., :half], xr[..., half:]
    rot = jnp.concatenate([x1 * cos - x2 * sin, x2 * cos + x1 * sin], axis=-1).astype(x.dtype)
    return jnp.concatenate([rot, x[..., ROPE_DIM:]], axis=-1)


def chunk_major(x, c):
    b, s = x.shape[:2]
    return x.reshape(b, s // c, c, *x.shape[2:]).swapaxes(0, 1)


def chunk_restore(y):
    n, b, c = y.shape[:3]
    return y.swapaxes(0, 1).reshape(b, n * c, *y.shape[3:])


def banded_attention(q, k, v, max_dist, blk):
    n, L, hk, g, dh = q.shape
    nb = -(-L // blk)
    pad = nb * blk - L
    n_prev = -(-max_dist // blk)
    qb = jnp.pad(q, ((0, 0), (0, pad), (0, 0), (0, 0), (0, 0))).reshape(n, nb, blk, hk, g, dh)

    def windows(t):
        tp = jnp.pad(t, ((0, 0), (n_prev * blk, pad), (0, 0), (0, 0))).reshape(n, nb + n_prev, blk, hk, dh)
        return jnp.concatenate([tp[:, o:o + nb] for o in range(n_prev + 1)], axis=2)

    kw, vw = windows(k), windows(v)
    s = jnp.einsum('nbqhgd,nbkhd->nbhgqk', qb, kw, preferred_element_type=jnp.float32) * ATTN_SCALE
    qi = jnp.arange(blk)[:, None]
    kj = jnp.arange((n_prev + 1) * blk)[None, :]
    dist = qi - kj + n_prev * blk
    kpos = (jnp.arange(nb)[:, None, None] - n_prev) * blk + kj[None]
    mask = (dist >= 0) & (dist <= max_dist) & (kpos >= 0)
    s = jnp.where(mask[None, :, None, None], s, NEG_INF)
    m = jnp.max(s, axis=-1, keepdims=True)
    e = jnp.exp(s - m)
    den = jnp.sum(e, axis=-1, keepdims=True)
    o = jnp.einsum('nbhgqk,nbkhd->nbqhgd', (e / den).astype(v.dtype), vw)
    lse = (m + jnp.log(den))[..., 0]
    o = o.reshape(n, nb * blk, hk, g, dh)[:, :L]
    lse = lse.transpose(0, 1, 4, 2, 3).reshape(n, nb * blk, hk, g)[:, :L]
    return o, lse


def dilated_mixture_attention(q, k, v):
    b, s, h, dh = q.shape
    outs, lses = [], []
    for window, dil in A_PATTERNS:
        L = s // dil

        def to_res(t):
            return t.reshape(b, L, dil, h, dh).transpose(0, 2, 1, 3, 4).reshape(b * dil, L, h, dh)

        o, lse = banded_attention(to_res(q)[:, :, :, None], to_res(k), to_res(v), window // dil, A_BLOCK)
        outs.append(o[:, :, :, 0].reshape(b, dil, L, h, dh).transpose(0, 2, 1, 3, 4).reshape(b, s, h, dh))
        lses.append(lse[..., 0].reshape(b, dil, L, h).transpose(0, 2, 1, 3).reshape(b, s, h))
    w = jax.nn.softmax(jnp.stack(lses), axis=0)
    return jnp.einsum('pbsh,pbshd->bshd', w.astype(q.dtype), jnp.stack(outs))


def nsa_attention(q, k_cmp, v_cmp, k_sel, v_sel, k_win, v_win, gates,
                  cmp_pos, ck_w1, ck_w2, cv_w1, cv_w2):
    b, s, g, dh = q.shape
    t = jnp.arange(s)
    r = CMP_BLOCK // CMP_STRIDE
    nc = s // CMP_STRIDE - r + 1

    def compress(x, w1, w2):
        xc = x.reshape(b, s // CMP_STRIDE, CMP_STRIDE, dh)
        blocks = jnp.concatenate([xc[:, o:o + nc] for o in range(r)], axis=2) + cmp_pos
        return jax.nn.silu(blocks.reshape(b, nc, CMP_BLOCK * dh) @ w1) @ w2

    kc = compress(k_cmp, ck_w1, ck_w2)
    vc = compress(v_cmp, cv_w1, cv_w2)
    s_c = jnp.einsum('bsgd,bcd->bsgc', q, kc, preferred_element_type=jnp.float32) * ATTN_SCALE
    c_start = jnp.arange(nc) * CMP_STRIDE
    c_mask = (c_start + CMP_BLOCK - 1)[None, :] <= t[:, None]
    cm = c_mask[None, :, None, :]
    p_c = jax.nn.softmax(jnp.where(cm, s_c, NEG_INF), axis=-1) * cm
    o_cmp = jnp.einsum('bsgc,bcd->bsgd', p_c.astype(vc.dtype), vc)

    ns = s // SEL_BLOCK
    j_start = jnp.arange(ns) * SEL_BLOCK
    cover = ((c_start[:, None] < j_start[None] + SEL_BLOCK) &
             (c_start[:, None] + CMP_BLOCK > j_start[None])).astype(jnp.float32)
    p_slc = jnp.einsum('bsgc,cj->bsj', p_c, cover)
    cur = t // SEL_BLOCK
    jj = jnp.arange(ns)
    valid = j_start[None, :] <= t[:, None]
    forced = (jj[None] == 0) | (jj[None] == cur[:, None]) | (jj[None] == cur[:, None] - 1)
    score = jnp.where(forced, FORCE_SCORE, jnp.where(valid, p_slc, NEG_INF))
    n_sel = min(SEL_TOP, ns)
    _, idx = lax.top_k(score, n_sel)
    sel_valid = jnp.take_along_axis(jnp.broadcast_to(valid, (b, s, ns)), idx, axis=-1)

    kb = k_sel.reshape(b, ns, SEL_BLOCK, dh)
    vb = v_sel.reshape(b, ns, SEL_BLOCK, dh)
    b_ix = jnp.arange(b)[:, None, None]
    kk = jnp.arange(SEL_BLOCK)

    def sel_chunk(args):
        qc, ic, mc, c = args
        kg = kb[b_ix, ic]
        vg = vb[b_ix, ic]
        tq = c * Q_CHUNK + jnp.arange(Q_CHUNK)
        kpos = ic[..., None] * SEL_BLOCK + kk
        msk = (kpos <= tq[None, :, None, None]) & mc[..., None]
        sc = jnp.einsum('bqgd,bqnkd->bqgnk', qc, kg, preferred_element_type=jnp.float32) * ATTN_SCALE
        sc = jnp.where(msk[:, :, None], sc, NEG_INF).reshape(b, Q_CHUNK, g, n_sel * SEL_BLOCK)
        p = jax.nn.softmax(sc, axis=-1).reshape(b, Q_CHUNK, g, n_sel, SEL_BLOCK)
        return jnp.einsum('bqgnk,bqnkd->bqgd', p.astype(vg.dtype), vg)

    nch = s // Q_CHUNK
    o_sel = chunk_restore(lax.map(sel_chunk, (chunk_major(q, Q_CHUNK), chunk_major(idx, Q_CHUNK),
                                              chunk_major(sel_valid, Q_CHUNK), jnp.arange(nch))))

    o_win, _ = banded_attention(q[:, :, None], k_win[:, :, None], v_win[:, :, None], WIN_SIZE - 1, WIN_BLOCK)
    o_win = o_win[:, :, 0]

    gs = jax.nn.sigmoid(gates.astype(jnp.float32)).astype(q.dtype)[..., None]
    return gs[:, :, 0] * o_cmp + gs[:, :, 1] * o_sel + gs[:, :, 2] * o_win


def moba_attention(q, k, v):
    b, s, h, dh = q.shape
    nblk = -(-s // MOBA_BLOCK)
    sp = nblk * MOBA_BLOCK
    padw = ((0, 0), (0, sp - s), (0, 0), (0, 0))
    q, k, v = jnp.pad(q, padw), jnp.pad(k, padw), jnp.pad(v, padw)
    t = jnp.arange(sp)
    kblk = k.reshape(b, nblk, MOBA_BLOCK, h, dh)
    vblk = v.reshape(b, nblk, MOBA_BLOCK, h, dh)
    n_top = min(MOBA_TOP, nblk - 1)
    kk = jnp.arange(MOBA_BLOCK)
    nch = sp // MOBA_Q_CHUNK
    xs = (chunk_major(q, MOBA_Q_CHUNK), jnp.arange(nch))
    if n_top > 0:
        kmean = jnp.mean(kblk.astype(jnp.float32), axis=2)
        s_blk = jnp.einsum('bshd,bnhd->bshn', q.astype(jnp.float32), kmean)
        past = (jnp.arange(nblk)[None, :] < (t // MOBA_BLOCK)[:, None])[None, :, None, :]
        _, idx = lax.top_k(jnp.where(past, s_blk, NEG_INF), n_top)
        sel_valid = jnp.take_along_axis(jnp.broadcast_to(past, s_blk.shape), idx, axis=-1)
        xs = xs + (chunk_major(idx, MOBA_Q_CHUNK), chunk_major(sel_valid, MOBA_Q_CHUNK))
    kt = kblk.transpose(0, 3, 1, 2, 4)
    vt = vblk.transpose(0, 3, 1, 2, 4)
    b_ix = jnp.arange(b)[:, None, None, None]
    h_ix = jnp.arange(h)[None, None, :, None]

    def chunk(args):
        qc, c = args[0], args[1]
        tq = c * MOBA_Q_CHUNK + jnp.arange(MOBA_Q_CHUNK)
        bo = (c * MOBA_Q_CHUNK) // MOBA_BLOCK
        k_own = lax.dynamic_index_in_dim(kblk, bo, axis=1, keepdims=False)
        v_own = lax.dynamic_index_in_dim(vblk, bo, axis=1, keepdims=False)
        s_own = jnp.einsum('bqhd,bkhd->bqhk', qc, k_own, preferred_element_type=jnp.float32) * ATTN_SCALE
        own_mask = (bo * MOBA_BLOCK + kk)[None, :] <= tq[:, None]
        s_own = jnp.where(own_mask[None, :, None, :], s_own, NEG_INF)
        if n_top == 0:
            p_own = jax.nn.softmax(s_own, axis=-1)
            return jnp.einsum('bqhk,bkhd->bqhd', p_own.astype(v_own.dtype), v_own)
        ic, mc = args[2], args[3]
        kg = kt[b_ix, h_ix, ic]
        vg = vt[b_ix, h_ix, ic]
        s_sel = jnp.einsum('bqhd,bqhnkd->bqhnk', qc, kg, preferred_element_type=jnp.float32) * ATTN_SCALE
        s_sel = jnp.where(mc[..., None], s_sel, NEG_INF).reshape(b, MOBA_Q_CHUNK, h, n_top * MOBA_BLOCK)
        p = jax.nn.softmax(jnp.concatenate([s_sel, s_own], axis=-1), axis=-1).astype(v.dtype)
        p_sel = p[..., :n_top * MOBA_BLOCK].reshape(b, MOBA_Q_CHUNK, h, n_top, MOBA_BLOCK)
        p_own = p[..., n_top * MOBA_BLOCK:]
        return (jnp.einsum('bqhnk,bqhnkd->bqhd', p_sel, vg) +
                jnp.einsum('bqhk,bkhd->bqhd', p_own, v_own))

    return chunk_restore(lax.map(chunk, xs))[:, :s]


def hybrid_layer(x, positions, norm_g, w_in, q_norm_a, k_norm_a, q_norm_b, k_norm_b, q_norm_c, k_norm_c,
                 cmp_pos, cmp_k_w1, cmp_k_w2, cmp_v_w1, cmp_v_w2, w_br_a, w_br_b, w_br_c, w_out):
    b, s, _ = x.shape
    hn = rms_norm(x, norm_g)
    proj = hn @ w_in
    points = np.cumsum(IN_SIZES)[:-1].tolist()
    (qa, ka, va, za, qb, kcb, vcb, ksb, vsb, kwb, vwb, gb, zb,
     qc, kc, vc, zc, g_a, g_b, g_c) = jnp.split(proj, points, axis=-1)

    def heads(t, n):
        return t.reshape(b, s, n, HEAD_DIM)

    def qk(t, n, g):
        return partial_rope(rms_norm(heads(t, n), g), positions)

    def single_key(t):
        return partial_rope(rms_norm(t, k_norm_b)[:, :, None], positions)[:, :, 0]

    o_a = dilated_mixture_attention(qk(qa, A_HEADS, q_norm_a), qk(ka, A_HEADS, k_norm_a), heads(va, A_HEADS))
    u_a = (o_a.reshape(b, s, A_WIDTH) * jax.nn.silu(za)) @ w_br_a

    o_b = nsa_attention(qk(qb, B_HEADS, q_norm_b), single_key(kcb), vcb, single_key(ksb), vsb,
                        single_key(kwb), vwb, gb.reshape(b, s, 3, B_HEADS),
                        cmp_pos, cmp_k_w1, cmp_k_w2, cmp_v_w1, cmp_v_w2)
    u_b = (o_b.reshape(b, s, B_WIDTH) * jax.nn.silu(zb)) @ w_br_b

    o_c = moba_attention(qk(qc, C_HEADS, q_norm_c), qk(kc, C_HEADS, k_norm_c), heads(vc, C_HEADS))
    u_c = (o_c.reshape(b, s, C_WIDTH) * jax.nn.silu(zc)) @ w_br_c

    y = jax.nn.sigmoid(g_a) * u_a + jax.nn.sigmoid(g_b) * u_b + jax.nn.sigmoid(g_c) * u_c
    return x + y @ w_out


def setup_inputs(seed: int = 0) -> dict:
    key = jax.random.key(seed)
    ks = jax.random.split(key, 20)

    def nrm(k, shape, fan_in):
        return jax.random.normal(k, shape, jnp.float32) * fan_in ** -0.5

    def gain(k, shape):
        return 1.0 + 0.05 * jax.random.normal(k, shape, jnp.float32)

    return {
        'x': jax.random.normal(ks[0], (BATCH, SEQ, D_MODEL), jnp.float32),
        'positions': jnp.broadcast_to(jnp.arange(SEQ, dtype=jnp.int32)[None, :], (BATCH, SEQ)),
        'norm_g': gain(ks[1], (DEPTH, D_MODEL)),
        'w_in': nrm(ks[2], (DEPTH, D_MODEL, IN_WIDTH), D_MODEL),
        'q_norm_a': gain(ks[3], (DEPTH, HEAD_DIM)),
        'k_norm_a': gain(ks[4], (DEPTH, HEAD_DIM)),
        'q_norm_b': gain(ks[5], (DEPTH, HEAD_DIM)),
        'k_norm_b': gain(ks[6], (DEPTH, HEAD_DIM)),
        'q_norm_c': gain(ks[7], (DEPTH, HEAD_DIM)),
        'k_norm_c': gain(ks[8], (DEPTH, HEAD_DIM)),
        'cmp_pos': 0.1 * jax.random.normal(ks[9], (DEPTH, CMP_BLOCK, HEAD_DIM), jnp.float32),
        'cmp_k_w1': nrm(ks[10], (DEPTH, CMP_BLOCK * HEAD_DIM, CMP_HIDDEN), CMP_BLOCK * HEAD_DIM),
        'cmp_k_w2': nrm(ks[11], (DEPTH, CMP_HIDDEN, HEAD_DIM), CMP_HIDDEN),
        'cmp_v_w1': nrm(ks[12], (DEPTH, CMP_BLOCK * HEAD_DIM, CMP_HIDDEN), CMP_BLOCK * HEAD_DIM),
        'cmp_v_w2': nrm(ks[13], (DEPTH, CMP_HIDDEN, HEAD_DIM), CMP_HIDDEN),
        'w_br_a': nrm(ks[14], (DEPTH, A_WIDTH, D_MODEL), A_WIDTH),
        'w_br_b': nrm(ks[15], (DEPTH, B_WIDTH, D_MODEL), B_WIDTH),
        'w_br_c': nrm(ks[16], (DEPTH, C_WIDTH, D_MODEL), C_WIDTH),
        'w_out': nrm(ks[17], (DEPTH, D_MODEL, D_MODEL), D_MODEL),
    }


def reference(x, positions, norm_g, w_in, q_norm_a, k_norm_a, q_norm_b, k_norm_b, q_norm_c, k_norm_c,
              cmp_pos, cmp_k_w1, cmp_k_w2, cmp_v_w1, cmp_v_w2, w_br_a, w_br_b, w_br_c, w_out):
    for l in range(DEPTH):
        x = hybrid_layer(x, positions, norm_g[l], w_in[l], q_norm_a[l], k_norm_a[l], q_norm_b[l], k_norm_b[l],
                         q_norm_c[l], k_norm_c[l], cmp_pos[l], cmp_k_w1[l], cmp_k_w2[l], cmp_v_w1[l],
                         cmp_v_w2[l], w_br_a[l], w_br_b[l], w_br_c[l], w_out[l])
    return x
```

```python
import contextlib
import math
import numpy as np
import concourse.bass as bass
import concourse.mybir as mybir
from concourse.bass_utils import run_bass_kernel_spmd

F32 = mybir.dt.float32
BF16 = mybir.dt.bfloat16
I32 = mybir.dt.int32
AF = mybir.ActivationFunctionType
ALU = mybir.AluOpType
AX = mybir.AxisListType

SAME_ENG_SYNC = {'pe': False, 'act': True, 'dve': True, 'pool': True, 'sp': True}
N_DMA_SEMS = 8

S_LEN = 4096
D = 1024
NT = 32
INW = 6540
EPS = 1e-6
NEGB = -30000.0


class _Op:
    __slots__ = ('eng', 'fn', 'deps', 'dma', 'signal', 'sem', 'val', 'idx', 'prev')


class Sched:
    def __init__(self, nc):
        self.nc = nc
        self.ops = []
        self.last_w = {}
        self.readers = {}
        self.fence_keys = []

    def add(self, eng, fn, reads=(), writes=(), dma=False):
        op = _Op()
        op.eng = eng
        op.fn = fn
        op.dma = dma
        op.signal = False
        op.sem = None
        op.val = 0
        op.idx = len(self.ops)
        deps = set()
        if self.fence_keys:
            reads = list(reads) + self.fence_keys
        for k in reads:
            w = self.last_w.get(k)
            if w is not None:
                deps.add(w)
        for k in writes:
            w = self.last_w.get(k)
            if w is not None:
                deps.add(w)
            for r in self.readers.get(k, ()):
                deps.add(r)
        op.deps = deps
        for k in reads:
            self.readers.setdefault(k, []).append(op.idx)
        for k in writes:
            self.last_w[k] = op.idx
            self.readers[k] = []
        self.ops.append(op)
        return op.idx

    def emit(self, final_waits=()):
        nc = self.nc
        ops = self.ops
        engs = ['pe', 'act', 'dve', 'pool', 'sp']
        for op in ops:
            need = set()
            best = {}
            for d in op.deps:
                Dp = ops[d]
                if Dp.eng == op.eng and not Dp.dma and not op.dma and not SAME_ENG_SYNC[op.eng]:
                    continue
                if Dp.dma:
                    need.add(d)
                    Dp.signal = True
                elif best.get(Dp.eng, -1) < d:
                    best[Dp.eng] = d
            for d in best.values():
                need.add(d)
                ops[d].signal = True
            op.deps = need
        for d in final_waits:
            ops[d].signal = True
        with contextlib.ExitStack() as st:
            esem = {e: st.enter_context(nc.semaphore('s_' + e)) for e in engs}
            dsem = {e: [st.enter_context(nc.semaphore('d_%s%d' % (e, i))) for i in range(N_DMA_SEMS)]
                    for e in ('sp', 'pool', 'act')}
            ecount = {e: 0 for e in engs}
            dcount = {e: 0 for e in engs}
            for op in ops:
                if op.dma:
                    op.signal = True
                if not op.signal:
                    continue
                if op.dma:
                    i = dcount[op.eng]
                    dcount[op.eng] += 1
                    op.sem = dsem[op.eng][i % N_DMA_SEMS]
                    op.val = 16 * (i // N_DMA_SEMS + 1)
                    op.prev = (op.sem, op.val - 16) if i >= N_DMA_SEMS else None
                else:
                    ecount[op.eng] += 1
                    op.sem = esem[op.eng]
                    op.val = ecount[op.eng]
            per = {e: [] for e in engs}
            for op in ops:
                per[op.eng].append(op)
            block = st.enter_context(nc.Block())

            def run(e, name, extra=()):
                waited = {}
                for op in per[name]:
                    ws = {}
                    for d in op.deps:
                        Dp = ops[d]
                        key = id(Dp.sem)
                        if waited.get(key, 0) >= Dp.val:
                            continue
                        if key not in ws or ws[key][1] < Dp.val:
                            ws[key] = (Dp.sem, Dp.val)
                    if op.dma and op.prev is not None:
                        key = id(op.prev[0])
                        if waited.get(key, 0) < op.prev[1] and (key not in ws or ws[key][1] < op.prev[1]):
                            ws[key] = op.prev
                    for key, (sem, val) in ws.items():
                        e.wait_ge(sem, val)
                        waited[key] = val
                    ins = op.fn(e)
                    if op.signal:
                        ins.then_inc(op.sem, 16 if op.dma else 1)
                for d in extra:
                    Dp = ops[d]
                    e.wait_ge(Dp.sem, Dp.val)

            @block.tensor
            def _(e):
                run(e, 'pe')

            @block.scalar
            def _(e):
                run(e, 'act')

            @block.vector
            def _(e):
                run(e, 'dve')

            @block.gpsimd
            def _(e):
                run(e, 'pool')

            @block.sync
            def _(e):
                run(e, 'sp', extra=final_waits)
        self.stats = {e: len(per[e]) for e in engs}
        self.stats['signals'] = dict(ecount)
        self.stats['dmasig'] = dict(dcount)


def bc(ap, axis, shape):
    return ap.unsqueeze(axis).to_broadcast(shape)


class Builder:
    def __init__(self, n_layers=2, dbg=None, stop_after=None):
        self.n_layers = n_layers
        self.dbg = dbg or ()
        self.stop_after = stop_after
        nc = bass.Bass("TRN2", target_bir_lowering=False)
        self.nc = nc
        self.S = Sched(nc)
        self.st = contextlib.ExitStack()
        self.uid = 0

    def sb(self, name, shape, dt):
        return self.st.enter_context(self.nc.sbuf_tensor(name, shape, dt))

    def ps(self, name, shape, dt=F32):
        return self.st.enter_context(self.nc.psum_tensor(name, shape, dt))

    def mm(self, out, lhsT, rhs, start, stop, r, w, skip=False):
        if skip:
            self.S.add('pe', lambda e: e.matmul(out, lhsT=lhsT, rhs=rhs, start=start, stop=stop, skip_group_check=True), reads=r, writes=w)
        else:
            self.S.add('pe', lambda e: e.matmul(out, lhsT=lhsT, rhs=rhs, start=start, stop=stop), reads=r, writes=w)

    def tr(self, out, in_, ident, r, w):
        self.S.add('pe', lambda e: e.transpose(out=out, in_=in_, identity=ident), reads=r, writes=w)

    def act(self, out, in_, func, r, w, bias=None, scale=None, accum_out=None):
        kw = {}
        if bias is not None:
            kw['bias'] = bias
        if scale is not None:
            kw['scale'] = scale
        if accum_out is not None:
            kw['accum_out'] = accum_out
        self.S.add('act', lambda e: e.activation(out=out, in_=in_, func=func, **kw), reads=r, writes=w)

    def rsqrt(self, out, in_, scale, r, w):
        self.act(out, in_, AF.Ln, r, w, bias=self.epsc[:out.shape[0], 0:1], scale=scale)
        self.act(out, out, AF.Exp, w, w, scale=-0.5)

    def tt(self, eng, out, in0, in1, op, r, w):
        self.S.add(eng, lambda e: e.tensor_tensor(out=out, in0=in0, in1=in1, op=op), reads=r, writes=w)

    def tsc(self, eng, out, in0, s1, s2, op0, op1, r, w):
        if op1 is None:
            self.S.add(eng, lambda e: e.tensor_scalar(out=out, in0=in0, scalar1=s1, scalar2=None, op0=op0), reads=r, writes=w)
        else:
            self.S.add(eng, lambda e: e.tensor_scalar(out=out, in0=in0, scalar1=s1, scalar2=s2, op0=op0, op1=op1), reads=r, writes=w)

    def cp(self, eng, out, in_, r, w):
        if eng == 'act':
            self.S.add('act', lambda e: e.copy(out=out, in_=in_), reads=r, writes=w)
        else:
            self.S.add(eng, lambda e: e.tensor_copy(out=out, in_=in_), reads=r, writes=w)

    def memset(self, eng, ap, val, w):
        self.S.add(eng, lambda e: e.memset(ap, val), writes=w)

    def asel(self, out, in_, pattern, op, fill, base, cm, r, w):
        self.S.add('pool', lambda e: e.affine_select(out=out, in_=in_, pattern=pattern, compare_op=op, fill=fill,
                                                     base=base, channel_multiplier=cm), reads=r, writes=w)

    def dma(self, out, in_, r, w, q='sp'):
        return self.S.add(q, lambda e: e.dma_start(out=out, in_=in_), reads=r, writes=w, dma=True)

    def build_E(self, nm, blk, npart, off):
        E = self.BIG[0:npart, off:off + S_LEN]
        self.memset('pool', E, 1.0, [nm])
        self.asel(E, E, [[1, S_LEN]], ALU.is_ge, 0.0, 0, -blk, [nm], [nm])
        self.asel(E, E, [[-1, S_LEN]], ALU.is_ge, 0.0, blk - 1, blk, [nm], [nm])
        return E

    def make_qz(self, i, nh):
        sl = self.qz_n % 2
        self.qz_n += 1
        Qz = self.Qz[sl]
        npair = nh // 2
        for par in range(2):
            self.cp('pool', Qz[par * 64:(par + 1) * 64, par:nh:2, :], self.QKT[par * 64:(par + 1) * 64, 0:npair, i * 128:(i + 1) * 128],
                    [('QKT', i), 'Qz0'], [('Qz', sl, par)])
        return Qz, [('Qz', sl, 0), ('Qz', sl, 1)]

    def fence(self):
        self.S.fence_keys = []
        self.fence_n = getattr(self, 'fence_n', 0) + 1
        n = self.fence_n
        fs = self.fsc
        self.mm(self.pb[7][0:1, 0:1], self.identb[0:1, 0:1], self.identb[0:1, 0:1], True, True, ['identb'], [('pb', 7), ('fence', 'pe', n)])
        self.cp('act', fs[0:1, 0:1], fs[0:1, 4:5], ['fsc'], [('fence', 'act', n), 'fscw_act'])
        self.cp('dve', fs[0:1, 1:2], fs[0:1, 5:6], ['fsc'], [('fence', 'dve', n), 'fscw_dve'])
        self.cp('pool', fs[0:1, 2:3], fs[0:1, 6:7], ['fsc'], [('fence', 'pool', n), 'fscw_pool'])
        self.S.fence_keys = [('fence', e, n) for e in ('pe', 'act', 'dve', 'pool')]

    def build(self):
        nc = self.nc
        dr = {}

        def din(name, shape, dt=F32):
            dr[name] = nc.dram_tensor(name, shape, dt, kind="ExternalInput").ap()

        din('x', [S_LEN, D])
        din('pos', [128, NT], I32)
        din('norm_g', [2, 128, 8])
        din('w_in', [2, D, INW])
        for nm in ('q_norm_a', 'k_norm_a', 'q_norm_b', 'k_norm_b', 'q_norm_c', 'k_norm_c'):
            din(nm, [2, 64])
        din('cmp_pos', [2, 64, 32])
        din('cmp_k_w1', [2, 64, 32, 128])
        din('cmp_k_w2', [2, 128, 64])
        din('cmp_v_w1', [2, 64, 32, 128])
        din('cmp_v_w2', [2, 128, 64])
        din('w_br_a', [2, 384, D])
        din('w_br_b', [2, 256, D])
        din('w_br_c', [2, 256, D])
        din('w_out', [2, D, D])
        dr['y'] = nc.dram_tensor('y', [S_LEN, D], F32, kind="ExternalOutput").ap()
        dr['x1'] = nc.dram_tensor('x1s', [S_LEN, D], F32).ap()
        dr['hnT'] = nc.dram_tensor('hnTs', [NT, 128, 1024], BF16).ap()
        dr['oT'] = nc.dram_tensor('oTs', [NT, 128, 7 * 128], BF16).ap()
        for nm, shape, dt in self.dbg:
            dr[nm] = nc.dram_tensor(nm, shape, dt, kind="ExternalOutput").ap()
        self.dr = dr
        self.final_ops = []
        with self.st:
            self.setup()
            for l in range(self.n_layers):
                xin = dr['x'] if l == 0 else dr['x1']
                xout = dr['y'] if l == self.n_layers - 1 else dr['x1']
                self.layer(l, xin, xout)
            self.S.emit(final_waits=self.final_ops)
        return nc

    def setup(self):
        nc = self.nc
        dr = self.dr
        self.alloc_common()
        self.identf = self.sb('identf', [128, 128], F32)
        self.identb = self.sb('identb', [128, 128], BF16)
        self.onesf = self.sb('onesf', [128, 128], F32)
        self.memset('pool', self.identf[:], 1.0, ['identf'])
        self.asel(self.identf[:], self.identf[:], [[-1, 128]], ALU.is_equal, 0.0, 0, 1, ['identf'], ['identf'])
        self.cp('dve', self.identb[:], self.identf[:], ['identf'], ['identb'])
        self.memset('pool', self.onesf[:], 1.0, ['onesf'])
        self.epsc = self.sb('epsc', [128, 1], F32)
        self.memset('dve', self.epsc[:], EPS, ['epsc'])
        for q_ in self.Qz:
            self.memset('pool', q_[:], 0.0, ['Qz0'])
        self.fsc = self.sb('fsc', [128, 8], F32)
        self.memset('dve', self.fsc[:], 0.0, ['fsc'])
        posi = self.sb('posi', [128, NT], I32)
        posf = self.sb('posf', [128, NT], F32)
        fr = self.sb('fr', [128, 8], F32)
        wpF = self.BIG[:, 46592:46592 + 9216].bitcast(F32)
        wpI = self.BIG[:, 46592:46592 + 9216].bitcast(I32)
        ang = wpF[:, 0:256].rearrange("p (t f) -> p t f", f=8)
        tmpa = wpF[:, 256:512].rearrange("p (t f) -> p t f", f=8)
        self.cos = self.sb('cos', [128, NT, 8], F32)
        self.sin = self.sb('sin', [128, NT, 8], F32)
        self.dma(posi[:], dr['pos'][:, :], [], ['posi'])
        self.cp('dve', posf[:], posi[:], ['posi'], ['posf'])
        for i in range(8):
            f = float(np.float32(500000.0) ** np.float32(-i / 8.0))
            self.memset('dve', fr[:, i:i + 1], f, ['fr'])
        self.tt('dve', ang[:], bc(posf[:], 2, [128, NT, 8]), bc(fr[:], 1, [128, NT, 8]), ALU.mult, ['posf', 'fr'], ['ang'])
        PI = math.pi
        HI = 6.28125
        LO = 2 * PI - 6.28125
        ni = wpI[:, 512:768].rearrange("p (t f) -> p t f", f=8)
        nf = wpF[:, 768:1024].rearrange("p (t f) -> p t f", f=8)
        rr = wpF[:, 1024:1280].rearrange("p (t f) -> p t f", f=8)
        self.tsc('dve', tmpa[:], ang[:], 1.0 / (2 * PI), None, ALU.mult, None, ['ang'], ['tmpa'])
        self.cp('dve', ni[:], tmpa[:], ['tmpa'], ['rr_ni'])
        self.cp('dve', nf[:], ni[:], ['rr_ni'], ['rr_nf'])
        self.S.add('dve', lambda e: e.scalar_tensor_tensor(out=rr[:], in0=nf[:], scalar=-HI, in1=ang[:], op0=ALU.mult, op1=ALU.add),
                   reads=['rr_nf', 'ang'], writes=['rr_r'])
        self.S.add('dve', lambda e: e.scalar_tensor_tensor(out=rr[:], in0=nf[:], scalar=-LO, in1=rr[:], op0=ALU.mult, op1=ALU.add),
                   reads=['rr_nf', 'rr_r'], writes=['rr_r'])

        def wrap(buf, key):
            self.tsc('dve', tmpa[:], buf[:], PI, -2 * PI, ALU.is_gt, ALU.mult, [key], ['tmpa'])
            self.tt('dve', buf[:], buf[:], tmpa[:], ALU.add, [key, 'tmpa'], [key])
            self.tsc('dve', tmpa[:], buf[:], -PI, 2 * PI, ALU.is_lt, ALU.mult, [key], ['tmpa'])
            self.tt('dve', buf[:], buf[:], tmpa[:], ALU.add, [key, 'tmpa'], [key])
            self.tsc('dve', buf[:], buf[:], -3.141592, 3.141592, ALU.max, ALU.min, [key], [key])
        wrap(rr, 'rr_r')
        self.act(self.sin[:], rr[:], AF.Sin, ['rr_r'], ['sin'])
        self.tsc('dve', rr[:], rr[:], PI / 2, None, ALU.add, None, ['rr_r', 'sin'], ['rr_r'])
        wrap(rr, 'rr_r')
        self.act(self.cos[:], rr[:], AF.Sin, ['rr_r'], ['cos'])
        scrF = self.BIG[:, 0:24576].bitcast(F32)
        scrI = self.BIG[:, 0:24576].bitcast(I32)

        def carve(src, k):
            return src[:, k * 2176:(k + 1) * 2176].rearrange("p (o q) -> p o q", q=128)
        dA = carve(scrF, 0)
        dAi = carve(scrI, 1)
        t1 = carve(scrF, 2)
        t2 = carve(scrF, 3)
        t3 = carve(scrF, 4)
        t4 = carve(scrI, 4)
        self.MA = self.sb('MA', [128, 17, 128], BF16)
        self.S.add('pool', lambda e: e.iota(dAi[:], pattern=[[128, 17], [1, 128]], base=0, channel_multiplier=-1), writes=['dAi'])
        self.cp('dve', dA[:], dAi[:], ['dAi'], ['dA'])
        self.tsc('dve', t1[:], dA[:], 128.0, None, ALU.is_le, None, ['dA'], ['mt1'])
        self.tsc('dve', t4[:], dAi[:], 3, None, ALU.bitwise_and, None, ['dAi'], ['mt3'])
        self.cp('dve', t2[:], t4[:], ['mt3'], ['mt2'])
        self.tsc('dve', t2[:], t2[:], 0.0, None, ALU.is_equal, None, ['mt2'], ['mt2'])
        self.tsc('dve', t3[:], dA[:], 512.0, None, ALU.is_le, None, ['dA'], ['mt3'])
        self.tt('dve', t2[:], t2[:], t3[:], ALU.mult, ['mt2', 'mt3'], ['mt2'])
        self.tt('dve', t1[:], t1[:], t2[:], ALU.add, ['mt1', 'mt2'], ['mt1'])
        self.tsc('dve', t4[:], dAi[:], 15, None, ALU.bitwise_and, None, ['dAi', 'mt2'], ['mt3'])
        self.cp('dve', t2[:], t4[:], ['mt3', 'mt1'], ['mt2'])
        self.tsc('dve', t2[:], t2[:], 0.0, None, ALU.is_equal, None, ['mt2'], ['mt2'])
        self.tsc('dve', t3[:], dA[:], 2048.0, None, ALU.is_le, None, ['dA', 'mt2'], ['mt3'])
        self.tt('dve', t2[:], t2[:], t3[:], ALU.mult, ['mt2', 'mt3'], ['mt2'])
        self.tt('dve', t1[:], t1[:], t2[:], ALU.add, ['mt1', 'mt2'], ['mt1'])
        self.tsc('dve', t2[:], dA[:], 0.0, None, ALU.is_ge, None, ['dA', 'mt1'], ['mt2'])
        self.tt('dve', self.MA[:], t1[:], t2[:], ALU.mult, ['mt1', 'mt2'], ['MA'])
        self.AM = self.sb('AM', [128, NT, 64], F32)
        vsF = self.BIG[:, 32768:32768 + NT * 432].bitcast(F32)
        vsI = self.BIG[:, 32768:32768 + NT * 432].bitcast(I32)
        am1 = vsF[:, 0:2048].rearrange("p (t j) -> p t j", j=64)
        am1i = vsI[:, 2048:4096].rearrange("p (t j) -> p t j", j=64)
        am2 = vsF[:, 4096:6144].rearrange("p (t j) -> p t j", j=64)
        for a in range(2):
            self.S.add('pool', (lambda a: lambda e: e.iota(am1i[a * 64:(a + 1) * 64], pattern=[[-2, NT], [1, 64]], base=-a,
                                                           channel_multiplier=0))(a),
                       writes=['am1i_%d' % a])
        self.cp('dve', am1[:], am1i[:], ['am1i_0', 'am1i_1'], ['am1_0', 'am1_1'])
        self.tsc('dve', am2[:], am1[:], 0.0, -1e30, ALU.is_gt, ALU.mult, ['am1_0', 'am1_1'], ['am2'])
        self.tsc('dve', am1[:], am1[:], -1.0, 1e4, ALU.is_ge, ALU.mult, ['am1_0', 'am1_1', 'am2'], ['am1', 'am1_0', 'am1_1'])
        self.tt('dve', self.AM[:], am1[:], am2[:], ALU.add, ['am1', 'am2'], ['AM'])
        self.tsc('dve', self.AM[:, :, 0:1], self.AM[:, :, 0:1], 1e4, None, ALU.add, None, ['AM'], ['AM'])
        self.cover = self.sb('cover', [128, 2, 64], F32)
        self.memset('pool', self.cover[:], 1.0, ['cover'])
        self.asel(self.cover[:], self.cover[:], [[-128, 2], [4, 64]], ALU.is_ge, 0.0, 3, -1, ['cover'], ['cover'])
        self.asel(self.cover[:], self.cover[:], [[128, 2], [-4, 64]], ALU.is_ge, 0.0, 1, 1, ['cover'], ['cover'])
        self.pb = [self.ps('pb%d' % i, [128, 512], F32) for i in range(8)]
        self.xt = [self.sb('xt%d' % i, [128, D], F32) for i in range(2)]
        self.hg = [self.sb('hg%d' % i, [128, 4, 8, 128], BF16) for i in range(2)]
        self.wstage = [self.sb('wst%d' % i, [128, 8, 128], F32) for i in range(2)]
        self.wst_n = 0
        self.hg_n = 0
        self.xt_n = 0
        if 'dbg_cs' in [d[0] for d in self.dbg]:
            o = self.dma(self.dr['dbg_cs'][:, 0:256], self.cos[:].rearrange("p t f -> p (t f)"), ['cos'], [])
            self.final_ops.append(o)
            o = self.dma(self.dr['dbg_cs'][:, 256:512], self.sin[:].rearrange("p t f -> p (t f)"), ['sin'], [])
            self.final_ops.append(o)
            mtmp = self.sb('mtmp', [128, 17 * 128], F32)
            self.cp('dve', mtmp[:], self.MA[:].rearrange("p o q -> p (o q)"), ['MA'], ['mtmp'])
            o = self.dma(self.dr['dbg_ma'][:, :], mtmp[:], ['mtmp'], [])
            self.final_ops.append(o)
            o = self.dma(self.dr['dbg_am'][:, :], self.AM[:].rearrange("p t f -> p (t f)"), ['AM'], [])
            self.final_ops.append(o)
            o = self.dma(self.dr['dbg_cov'][:, :], self.cover[:].rearrange("p t f -> p (t f)"), ['cover'], [])
            self.final_ops.append(o)

    def load_w(self, W, wkey, src, c0, n, o, nk=8, k0=0, eng_cycle=('pool', 'dve')):
        done = 0
        while done < n:
            m = min(128, n - done)
            sl = self.wst_n % 2
            self.wst_n += 1
            stg = self.wstage[sl]
            self.dma(stg[:, 0:nk, 0:m], src[:, c0 + done:c0 + done + m].rearrange("(k p) c -> p k c", p=128),
                     [], [('wst', sl)])
            eng = eng_cycle[self.wst_n % len(eng_cycle)]
            self.cp(eng, W[:, k0:k0 + nk, o + done:o + done + m], stg[:, 0:nk, 0:m], [('wst', sl)], [(wkey, self.wst_n)])
            self.wkeys.setdefault(wkey, []).append((wkey, self.wst_n))
            done += m

    def layer(self, l, xin, xout):
        if l > 0:
            self.fence()
        self.stage1(l, xin)
        if self.stop_after == 'stage1':
            return
        only = getattr(self, 'only', None)
        if only is None or 'A' in only:
            self.fence()
            self.pass_A(l)
        if self.stop_after in ('A', 'Aproj'):
            return
        if only is None or 'C' in only:
            self.fence()
            self.pass_C(l)
        if self.stop_after == 'C':
            return
        if only is None or 'B' in only:
            self.fence()
            self.pass_B(l)
        if self.stop_after == 'B':
            return
        self.fence()
        self.final(l, xin, xout)

    def stage1(self, l, xin):
        dr = self.dr
        if l == 0:
            self.gT = self.sb('gT', [128, 8], F32)
            self.s1_all = self.sb('s1all', [128, 4096], BF16)
            self.s1_sq = self.s1_all[:, 0:1024]
            self.s1_ss = self.sb('s1ss', [128, 2], F32)
            self.s1_xs = self.s1_all[:, 1024:2048]
            self.s1_hT = [self.s1_all[:, 2048 + i * 1024:3072 + i * 1024].rearrange("p (k c) -> p k c", c=128) for i in range(2)]
        self.dma(self.gT[:], dr['norm_g'][l], [], ['gT'])
        for t in range(NT):
            sl = t % 2
            xt = self.xt[sl]
            self.dma(xt[:], xin[t * 128:(t + 1) * 128, :], [('x1', t)], [('xt', sl)])
            ss = self.s1_ss[:, sl:sl + 1]
            xs = (self.s1_sq, self.s1_xs)[sl]
            self.act(xs, xt[:], AF.Square, [('xt', sl)], [('s1xs', sl), ('s1ss', sl)], accum_out=ss)
            self.rsqrt(ss, ss, 1.0 / D, [('s1ss', sl)], [('s1ss', sl)])
            self.act(xs, xt[:], AF.Copy, [('xt', sl), ('s1ss', sl)], [('s1xs', sl)], scale=ss)
            pT = self.pb[sl][:].bitcast(BF16)
            for k in range(8):
                self.tr(pT[:, k * 128:(k + 1) * 128], xs[:, k * 128:(k + 1) * 128], self.identb[:],
                        [('s1xs', sl), 'identb'], [('pb', sl)])
            hT = self.s1_hT[sl]
            self.tt('dve', hT, pT[:, 0:1024].rearrange("p (k t) -> p k t", k=8), bc(self.gT[:], 2, [128, 8, 128]), ALU.mult,
                    [('pb', sl), 'gT'], [('s1hT', sl)])
            self.dma(dr['hnT'][t], hT.rearrange("p k t -> p (k t)"), [('s1hT', sl)], [('hnT', t)])
        if 'dbg_hnT' in [d[0] for d in self.dbg]:
            for t in range(NT):
                o = self.dma(dr['dbg_hnT'][t], dr['hnT'][t], [('hnT', t)], [])
                self.final_ops.append(o)

    def load_hg(self, g):
        sl = self.hg_n % 2
        self.hg_n += 1
        self.dma(self.hg[sl][:].rearrange("p t k c -> p t (k c)"), self.dr['hnT'][g * 4:(g + 1) * 4].rearrange("t p c -> p t c"),
                 [('hnT', g * 4 + i) for i in range(4)], [('hg', sl)])
        return sl

    def proj_tile(self, hsl, tt, W, wreads, chunks, banks):
        for (c0, n), b in zip(chunks, banks):
            for k in range(8):
                self.mm(self.pb[b][:, 0:n], self.hg[hsl][:, tt, k, :], W[:, k, c0:c0 + n], k == 0, k == 7,
                        [('hg', hsl)] + wreads, [('pb', b)])

    def qk_post(self, pr, nh, Gt, t, xb, prk, xbk):
        W_ = nh * 64
        sq = self.qk_sq[:, 0:W_]
        ss = self.qk_ss[:, 0:nh]
        self.tt('dve', sq, pr, pr, ALU.mult, [prk], ['qk_sq'])
        self.S.add('dve', lambda e: e.tensor_reduce(out=ss, in_=sq.rearrange("p (h d) -> p h d", d=64), axis=AX.X, op=ALU.add),
                   reads=['qk_sq'], writes=['qk_ss'])
        self.rsqrt(ss, ss, 1.0 / 64, ['qk_ss'], ['qk_ss'])
        na = (2 * nh + 2) // 3
        pr3 = pr.rearrange("p (h d) -> p h d", d=64)
        xn3 = self.qk_xn[:, 0:W_].rearrange("p (h d) -> p h d", d=64)
        xb3 = xb.rearrange("p (h d) -> p h d", d=64)
        G3 = Gt.rearrange("p (h d) -> p h d", d=64)
        for eng, h0, h1 in (('dve', 0, na), ('pool', na, nh)):
            n_ = h1 - h0
            if n_ <= 0:
                continue
            kx = 'qk_xn_' + eng
            xn_ = xn3[:, h0:h1, :]
            self.tt(eng, xn_, pr3[:, h0:h1, :], bc(ss[:, h0:h1], 2, [128, n_, 64]), ALU.mult, [prk, 'qk_ss'], [kx])
            self.tt(eng, xn_, xn_, G3[:, h0:h1, :], ALU.mult, [kx, 'Gt'], [kx])
            cosb = bc(self.cos[:, t, :], 1, [128, n_, 8])
            sinb = bc(self.sin[:, t, :], 1, [128, n_, 8])
            r = [self.qk_r[i][:, h0:h1, :] for i in range(4)]
            rk = ['qk_r%d_%s' % (i, eng) for i in range(4)]
            self.tt(eng, r[0], xn_[:, :, 0:8], cosb, ALU.mult, [kx, 'cos'], [rk[0]])
            self.tt(eng, r[1], xn_[:, :, 8:16], sinb, ALU.mult, [kx, 'sin'], [rk[1]])
            self.tt(eng, r[2], xn_[:, :, 8:16], cosb, ALU.mult, [kx, 'cos'], [rk[2]])
            self.tt(eng, r[3], xn_[:, :, 0:8], sinb, ALU.mult, [kx, 'sin'], [rk[3]])
            xk = xbk + '_' + eng
            self.tt(eng, xb3[:, h0:h1, 0:8], r[0], r[1], ALU.subtract, [rk[0], rk[1]], [xk])
            self.tt(eng, xb3[:, h0:h1, 8:16], r[2], r[3], ALU.add, [rk[2], rk[3]], [xk])
            self.cp(eng, xb3[:, h0:h1, 16:64], xn_[:, :, 16:64], [kx], [xk])

    def alloc_common(self):
        if hasattr(self, 'qk_sq'):
            return
        self.qk_sq = self.sb('qk_sq', [128, 768], F32)
        self.qk_ss = self.sb('qk_ss', [128, 12], F32)
        self.qk_xn = self.sb('qk_xn', [128, 768], F32)
        self.qk_r = [self.sb('qk_r%d' % i, [128, 12, 8], F32) for i in range(4)]
        self.pr = self.sb('pr', [128, 768], F32)
        self.xb = self.sb('xb', [128, 768], BF16)
        self.Gt = self.sb('Gt', [128, 768], F32)
        self.g64 = self.sb('g64', [128, 2, 64], F32)
        self.PT = [self.sb('PT%d' % i, [128, 768], BF16) for i in range(3)]
        self.pt_n = 0
        self.ob = self.sb('ob', [128, 384], BF16)
        self.rec = self.sb('rec', [128, 12], F32)
        self.oTt = [self.sb('oTt%d' % i, [128, 384], BF16) for i in range(2)]
        self.selT = [self.sb('selT%d' % i, [64, 512], BF16) for i in range(2)]
        self.Qz = [self.sb('Qz%d' % i, [128, 6, 128], BF16) for i in range(2)]
        self.qz_n = 0
        self.ot_n = 0
        self.BIG = self.sb('BIG', [128, 57344], BF16)
        self.QKT = self.BIG[:, 0:32768].rearrange("p (a c) -> p a c", c=S_LEN)
        self.VS = self.BIG[:, 32768:32768 + NT * 432].rearrange("p (t c) -> p t c", c=432)
        self.Wp = self.BIG[:, 46592:46592 + 9216].rearrange("p (k c) -> p k c", c=1152)

    def load_gains(self, l, qn, kn, nq, nk):
        dr = self.dr
        self.dma(self.g64[:, 0, :], dr[qn][l].partition_broadcast(128), [], ['g64q'])
        self.dma(self.g64[:, 1, :], dr[kn][l].partition_broadcast(128), [], ['g64k'])
        G3 = self.Gt[:, 0:(nq + nk) * 64].rearrange("p (h d) -> p h d", d=64)
        self.cp('dve', G3[:, 0:nq, :], bc(self.g64[:, 0, :], 1, [128, nq, 64]), ['g64q'], ['Gt'])
        self.cp('dve', G3[:, nq:nq + nk, :], bc(self.g64[:, 1, :], 1, [128, nk, 64]), ['g64k'], ['Gt'])

    def out_tile(self, i, acc_key, ob_ap, ncol, c0):
        npair = ncol // 128
        pT = self.pb[7][:].bitcast(BF16)
        for p in range(npair):
            self.tr(pT[:, p * 128:(p + 1) * 128], ob_ap[:, p * 128:(p + 1) * 128], self.identb[:], [acc_key, 'identb'], [('pb', 7)])
        sl = self.ot_n % 2
        self.ot_n += 1
        self.cp('act', self.oTt[sl][:, 0:ncol], pT[:, 0:ncol], [('pb', 7)], [('oTt', sl)])
        self.dma(self.dr['oT'][i][:, c0:c0 + ncol], self.oTt[sl][:, 0:ncol], [('oTt', sl)], [('oT', i, c0)])

    def pass_A(self, l):
        dr = self.dr
        self.alloc_common()
        self.wkeys = {}
        W = self.Wp
        self.load_w(W, 'Wp', dr['w_in'][l], 0, 1152, 0)
        wreads = list(self.wkeys['Wp'])
        self.load_gains(l, 'q_norm_a', 'k_norm_a', 6, 6)
        VS4 = self.VS.rearrange("p t (h e) -> p t h e", e=72)
        self.S.add('pool', lambda e: e.memset(VS4[:, :, :, 64:65], 1.0), reads=['MA', 'AM'], writes=['VS_ones'])
        QKT = self.QKT
        for t in range(NT):
            if t % 4 == 0:
                hsl = self.load_hg(t // 4)
            self.proj_tile(hsl, t % 4, W, wreads, [(0, 384), (384, 384), (768, 384)], [0, 1, 2])
            self.cp('act', self.pr[:, 0:384], self.pb[0][:, 0:384], [('pb', 0)], ['pr'])
            self.cp('act', self.pr[:, 384:768], self.pb[1][:, 0:384], [('pb', 1)], ['pr'])
            self.cp('act', VS4[:, t, :, 0:64], self.pb[2][:, 0:384].rearrange("p (h d) -> p h d", d=64), [('pb', 2), 'VS_ones', 'MA', 'AM'], [('VS', t)])
            self.qk_post(self.pr[:, 0:768], 12, self.Gt[:, 0:768], t, self.xb[:, 0:768], 'pr', 'xb')
            pT = self.pb[3][:].bitcast(BF16)
            for p in range(6):
                self.tr(pT[:, p * 128:(p + 1) * 128], self.xb[:, p * 128:(p + 1) * 128], self.identb[:], ['xb_dve', 'xb_pool', 'identb'], [('pb', 3)])
            self.cp('act', QKT[:, 0:6, t * 128:(t + 1) * 128], pT[:, 0:768].rearrange("p (a c) -> p a c", c=128), [('pb', 3), 'MA', 'AM'], [('QKT', t)])
        if self.stop_after == 'Aproj':
            return
        sb_n = 0
        for i in range(NT):
            j0 = max(0, i - 16)
            Qz, qzk = self.make_qz(i, 6)
            for j in range(j0, i + 1):
                o = i - j
                bS = [2 + 2 * (sb_n % 2), 3 + 2 * (sb_n % 2)]
                sb_n += 1
                for p in range(3):
                    bb = bS[0] if p < 2 else bS[1]
                    c0 = (p % 2) * 256
                    self.mm(self.pb[bb][:, c0:c0 + 256], QKT[:, 3 + p, j * 128:(j + 1) * 128],
                            Qz[:, 2 * p:2 * p + 2, :], True, True, [('QKT', j)] + qzk, [('pb', bb)])
                ps_ = self.pt_n % 3
                self.pt_n += 1
                PT = self.PT[ps_]
                self.act(PT[:, 0:512], self.pb[bS[0]][:, 0:512], AF.Exp, [('pb', bS[0])], [('PT', ps_)], scale=0.125)
                self.act(PT[:, 512:768], self.pb[bS[1]][:, 0:256], AF.Exp, [('pb', bS[1])], [('PT', ps_)], scale=0.125)
                eng = 'dve'
                self.tt(eng, PT[:, 0:768].rearrange("p (h q) -> p h q", q=128), PT[:, 0:768].rearrange("p (h q) -> p h q", q=128),
                        bc(self.MA[:, o, :], 1, [128, 6, 128]), ALU.mult, [('PT', ps_), 'MA'], [('PT', ps_)])
                for h in range(6):
                    self.mm(self.pb[6][:, h * 72:h * 72 + 65], PT[:, h * 128:(h + 1) * 128], VS4[:, j, h, 0:65],
                            (j == j0 and h == 0), j == i, [('PT', ps_), ('VS', j), 'VS_ones'], [('pb', 6)], skip=True)
            acc = self.pb[6][:, 0:432].rearrange("p (h e) -> p h e", e=72)
            self.S.add('dve', lambda e, acc=acc: e.reciprocal(out=self.rec[:, 0:6], in_=acc[:, :, 64]), reads=[('pb', 6)], writes=['rec'])
            self.tt('dve', self.ob[:, 0:384].rearrange("p (h d) -> p h d", d=64), acc[:, :, 0:64], bc(self.rec[:, 0:6], 2, [128, 6, 64]),
                    ALU.mult, [('pb', 6), 'rec'], ['ob'])
            self.out_tile(i, 'ob', self.ob, 384, 0)
        self.dbg_oT()

    def dbg_oT(self):
        if 'dbg_oT' in [d[0] for d in self.dbg] and self.stop_after is not None:
            rng = [r for k, r in (('A', (0, 384)), ('B', (384, 640)), ('C', (640, 896))) if getattr(self, 'only', None) is None or k in self.only]
            for t in range(NT):
                for (a, b) in rng:
                    o = self.dma(self.dr['dbg_oT'][t][:, a:b], self.dr['oT'][t][:, a:b], [('oT', t, 0), ('oT', t, 384), ('oT', t, 640)], [])
                    self.final_ops.append(o)

    def pass_B(self, l):
        dr = self.dr
        if not hasattr(self, 'b_GS'):
            self.b_GS = self.sb('b_GS', [128, NT, 12], F32)
            self.b_posT = self.sb('b_posT', [64, 32], BF16)
            self.b_posTf = self.sb('b_posTf', [64, 32], F32)
            self.b_W2 = self.sb('b_W2', [128, 2, 64], BF16)
            self.b_W2f = self.sb('b_W2f', [128, 2, 64], F32)
            self.b_W2vf = self.b_W2f
            self.b_h1 = self.sb('b_h1', [128, 2, 256], BF16)
            self.b_hb = self.sb('b_hb', [128, 2], F32)
            self.b_kcT = self.sb('b_kcT', [64, 256], BF16)
            self.b_vc = self.sb('b_vc', [128, 2, 64], F32)
            self.b_Pc = self.sb('b_Pc', [128, 2, 512], F32)
            self.b_rden = self.qk_sq[:, 0:512]
            self.b_sc = self.sb('b_sc', [128, 64], F32)
            self.b_sc2 = self.sb('b_sc2', [128, 64], F32)
            self.b_m1 = self.sb('b_m1', [128, 8], F32)
            self.b_m2 = self.sb('b_m2', [128, 8], F32)
            self.b_selb = self.sb('b_selb', [128, 64], F32)
            self.b_selbT2 = [t_[0:64, :].rearrange('p (h q) -> p h q', q=128) for t_ in self.selT]
            self.b_f = self.sb('b_f', [128, 12], F32)
            self.b_obf = self.qk_xn[:, 0:256].rearrange('p (h d) -> p h d', d=64)
            self.b_tmp = self.qk_xn[:, 256:512].rearrange('p (h d) -> p h d', d=64)
        GS = self.b_GS
        self.wkeys = {}
        W = self.BIG[:, 37376:37376 + 8 * 652].rearrange("p (k c) -> p k c", c=652)
        self.load_w(W, 'Wp', dr['w_in'][l], 1536, 652, 0)
        wreads = list(self.wkeys['Wp'])
        self.load_gains(l, 'q_norm_b', 'k_norm_b', 4, 6)
        Esel = self.build_E('Esel', 64, 64, 50784 + 0)
        VSB = self.BIG[:, 32768:32768 + NT * 144].rearrange("p (t h e) -> p t h e", h=2, e=72)
        self.memset('pool', VSB[:, :, :, 64:65], 1.0, ['VS_ones'])
        QTB = self.BIG[0:64, 0:32768].rearrange("p (a c) -> p a c", c=S_LEN)
        W1 = [self.BIG[0:64, 42592 + i * 4096:42592 + (i + 1) * 4096].rearrange("p (q h) -> p q h", h=128) for i in range(2)]
        for wi, nm in enumerate(('cmp_k_w1', 'cmp_v_w1')):
            for qtr in range(4):
                sl = self.wst_n % 2
                self.wst_n += 1
                stg = self.wstage[sl][0:64]
                self.dma(stg, dr[nm][l][:, qtr * 8:(qtr + 1) * 8, :], [], [('wst', sl)])
                self.cp('pool', W1[wi][:, qtr * 8:(qtr + 1) * 8, :], stg, [('wst', sl)], [('W1', wi, qtr)])
        w1keys = [[('W1', wi, q) for q in range(4)] for wi in range(2)]
        self.dma(self.b_posTf[:], dr['cmp_pos'][l], [], ['b_posTf'])
        self.cp('dve', self.b_posT[:], self.b_posTf[:], ['b_posTf'], ['b_posT'])
        self.dma(self.b_W2f[:, 0, :], dr['cmp_k_w2'][l], [], ['b_W2f0'])
        self.dma(self.b_W2f[:, 1, :], dr['cmp_v_w2'][l], [], ['b_W2f1'])
        self.cp('dve', self.b_W2[:], self.b_W2f[:], ['b_W2f0', 'b_W2f1'], ['b_W2'])
        srcs = [0, 64, 128, 192, 256, 384, 512, 320]
        for t in range(NT):
            if t % 4 == 0:
                hsl = self.load_hg(t // 4)
            self.proj_tile(hsl, t % 4, W, wreads, [(0, 512), (512, 140)], [0, 1])
            self.cp('act', self.pr[:, 0:512], self.pb[0][:, 0:512], [('pb', 0)], ['pr'])
            self.cp('act', self.pr[:, 512:652], self.pb[1][:, 0:140], [('pb', 1)], ['pr'])
            self.qk_post(self.pr[:, 0:640], 10, self.Gt[:, 0:640], t, self.xb[:, 0:640], 'pr', 'xb')
            self.cp('dve', self.xb[:, 320:384], self.pr[:, 320:384], ['pr', 'xb_dve', 'xb_pool'], ['xb_dve', 'xb_pool'])
            self.cp('dve', VSB[:, t, 0, 0:64], self.pr[:, 448:512], ['pr', 'VS_ones'], [('VS', t)])
            self.cp('dve', VSB[:, t, 1, 0:64], self.pr[:, 576:640], ['pr', 'VS_ones'], [('VS', t)])
            self.cp('dve', GS[:, t, :], self.pr[:, 640:652], ['pr'], [('GS', t)])
            pT = self.pb[3][:].bitcast(BF16)
            for si, c0 in enumerate(srcs):
                self.tr(pT[0:64, si * 128:(si + 1) * 128], self.xb[:, c0:c0 + 64], self.identb[:], ['xb_dve', 'xb_pool', 'identb'], [('pb', 3)])
            self.cp('act', QTB[:, 0:8, t * 128:(t + 1) * 128], pT[0:64, 0:1024].rearrange("p (a c) -> p a c", c=128), [('pb', 3)], [('QKT', t)])
        allq = [('QKT', t) for t in range(NT)]
        self.act(GS[:].rearrange("p t g -> p (t g)"), GS[:].rearrange("p t g -> p (t g)"), AF.Sigmoid, [('GS', t) for t in range(NT)], ['GSs'])
        self.memset('dve', self.b_h1[:, :, 255:256], 0.0, ['b_h1z'])
        for wi, slot in ((0, 4), (1, 7)):
            for p in range(32):
                self.mm(self.pb[5][:, 0:255], W1[wi][:, p, :], QTB[:, slot, p:p + 16 * 254 + 1:16], p == 0, p == 31,
                        allq + w1keys[wi], [('pb', 5)])
            for p in range(32):
                self.mm(self.pb[4][:, 0:1], W1[wi][:, p, :], self.b_posT[:, p:p + 1], p == 0, p == 31, w1keys[wi] + ['b_posT'], [('pb', 4)])
            self.cp('dve', self.b_hb[:, wi:wi + 1], self.pb[4][:, 0:1], [('pb', 4)], [('b_hb', wi)])
            self.act(self.b_h1[:, wi, 0:255], self.pb[5][:, 0:255], AF.Silu, [('pb', 5), ('b_hb', wi), 'b_h1z'], [('b_h1', wi)],
                     bias=self.b_hb[:, wi:wi + 1])
        self.mm(self.pb[5][0:64, 0:256], self.b_W2[:, 0, :], self.b_h1[:, 0, :], True, True, ['b_W2', ('b_h1', 0), 'b_h1z'], [('pb', 5)])
        self.cp('dve', self.b_kcT[:, :], self.pb[5][0:64, 0:256], [('pb', 5)], ['b_kcT'])
        for ct in range(2):
            self.mm(self.pb[4][:, ct * 64:(ct + 1) * 64], self.b_h1[:, 1, ct * 128:(ct + 1) * 128], self.b_W2[:, 1, :], True, True,
                    ['b_W2', ('b_h1', 1), 'b_h1z'], [('pb', 4)])
        self.cp('dve', self.b_vc[:].rearrange("p c d -> p (c d)"), self.pb[4][:, 0:128], [('pb', 4)], ['b_vc'])
        Pc = self.b_Pc
        st = {'sb': 0}

        def qap(i):
            return QTB[:, 0:4, i * 128:(i + 1) * 128]

        def nbank():
            b = 2 + (st['sb'] % 2)
            st['sb'] += 1
            return b

        def sel_steps(i):
            qr = [('QKT', i)]
            nct = 2 if i >= 16 else 1
            ob = i % 2
            selbT = self.b_selbT2[i % 2]
            sk = ('b_selbT', i % 2)

            def s_a():
                for ct in range(nct):
                    b = nbank()
                    self.mm(self.pb[b][:, 0:512], self.b_kcT[:, ct * 128:(ct + 1) * 128], qap(i), True, True, qr + ['b_kcT'], [('pb', b)])
                    self.act(Pc[:, ct, :], self.pb[b][:, 0:512], AF.Exp, [('pb', b)], [('Pc', ct)], scale=0.125)
                    if ct == 1 or i < 17:
                        self.asel(Pc[:, ct, :].rearrange("p (h q) -> p h q", q=128), Pc[:, ct, :].rearrange("p (h q) -> p h q", q=128),
                                  [[0, 4], [1, 128]], ALU.is_ge, 0.0, 128 * i - 2048 * ct - 31, -16, [('Pc', ct)], [('Pc', ct)])

            def s_b():
                for ct in range(nct):
                    self.mm(self.pb[4][:, 0:512], self.onesf[:, :], Pc[:, ct, :], ct == 0, ct == nct - 1, [('Pc', ct), 'onesf'], [('pb', 4)])

            def s_c():
                self.tsc('dve', self.b_rden, self.pb[4][:, 0:512], 1e-30, None, ALU.add, None, [('pb', 4)], ['b_rden'])
                self.S.add('dve', lambda e: e.reciprocal(out=self.b_rden, in_=self.b_rden), reads=['b_rden'], writes=['b_rden'])
                for ct in range(nct):
                    self.tt('dve', Pc[:, ct, :], Pc[:, ct, :], self.b_rden, ALU.mult, [('Pc', ct), 'b_rden'], [('Pc', ct)])

            def s_d():
                first = True
                for ct in range(nct):
                    for h in range(4):
                        self.mm(self.pb[ob][:, 0:64], Pc[:, ct, h * 128:(h + 1) * 128], self.cover[:, ct, :], first, False,
                                [('Pc', ct), 'cover'], [('pb', ob)], skip=True)
                        first = False
                for ct in range(nct):
                    for h in range(4):
                        self.mm(self.pb[ob][:, 64 + h * 64:128 + h * 64], Pc[:, ct, h * 128:(h + 1) * 128], self.b_vc[:, ct, :], False,
                                (ct == nct - 1 and h == 3), [('Pc', ct), 'b_vc'], [('pb', ob)], skip=True)

            def s_e():
                self.tt('dve', self.b_sc[:], self.pb[ob][:, 0:64], self.AM[:, i, :], ALU.add, [('pb', ob), 'AM'], ['b_sc'])
                self.S.add('dve', lambda e: e.max(out=self.b_m1[:], in_=self.b_sc[:]), reads=['b_sc'], writes=['b_m1'])
                self.S.add('dve', lambda e: e.match_replace(out=self.b_sc2[:], in_to_replace=self.b_m1[:], in_values=self.b_sc[:],
                                                            imm_value=-3e38), reads=['b_sc', 'b_m1'], writes=['b_sc2'])
                self.S.add('dve', lambda e: e.max(out=self.b_m2[:], in_=self.b_sc2[:]), reads=['b_sc2'], writes=['b_m2'])
                self.tsc('dve', self.b_selb[:], self.b_sc[:], self.b_m2[:, 7:8], None, ALU.is_ge, None, ['b_sc', 'b_m2'], ['b_selb'])
                self.tsc('dve', self.b_selb[:], self.b_selb[:], 1.0, -NEGB, ALU.subtract, ALU.mult, ['b_selb'], ['b_selb'])

            def s_f():
                self.tr(self.pb[4][0:64, 0:128], self.b_selb[:, :], self.identf[:], ['b_selb', 'identf'], [('pb', 4)])
                self.cp('act', selbT[:], bc(self.pb[4][0:64, 0:128], 1, [64, 4, 128]), [('pb', 4)], [sk])
            return [s_a, s_b, s_c, s_d, s_e, s_f]

        def attn_steps(i):
            qr = [('QKT', i)]
            ob = i % 2
            selbT = self.b_selbT2[i % 2]
            sk = ('b_selbT', i % 2)
            steps = []

            def sel_pair(j):
                def f():
                    b = nbank()
                    bank = self.pb[b]
                    self.mm(bank[:, 0:512], Esel[:, j * 128:(j + 1) * 128], selbT[:].rearrange("p h q -> p (h q)"), True, False,
                            ['Esel', sk], [('pb', b)])
                    self.mm(bank[:, 0:512], QTB[:, 5, j * 128:(j + 1) * 128], qap(i), False, True, qr + [('QKT', j)], [('pb', b)])
                    ps_ = self.pt_n % 3
                    self.pt_n += 1
                    PT = self.PT[ps_]
                    self.act(PT[:, 0:512], bank[:, 0:512], AF.Exp, [('pb', b)], [('PT', ps_)], scale=0.125)
                    if j == i:
                        self.asel(PT[:, 0:512].rearrange("p (h q) -> p h q", q=128), PT[:, 0:512].rearrange("p (h q) -> p h q", q=128),
                                  [[0, 4], [1, 128]], ALU.is_ge, 0.0, 0, -1, [('PT', ps_)], [('PT', ps_)])
                    for h in range(4):
                        self.mm(self.pb[6][:, h * 72:h * 72 + 65], PT[:, h * 128:(h + 1) * 128], VSB[:, j, 0, 0:65],
                                (j == 0 and h == 0), j == i, [('PT', ps_), ('VS', j), 'VS_ones'], [('pb', 6)], skip=True)
                return f
            j0 = max(0, i - 4)

            def win_pair(j):
                def f():
                    b = nbank()
                    bank = self.pb[b]
                    self.mm(bank[:, 0:512], QTB[:, 6, j * 128:(j + 1) * 128], qap(i), True, True, qr + [('QKT', j)], [('pb', b)])
                    ps_ = self.pt_n % 3
                    self.pt_n += 1
                    PT = self.PT[ps_]
                    self.act(PT[:, 0:512], bank[:, 0:512], AF.Exp, [('pb', b)], [('PT', ps_)], scale=0.125)
                    PT3 = PT[:, 0:512].rearrange("p (h q) -> p h q", q=128)
                    if j == i:
                        self.asel(PT3, PT3, [[0, 4], [1, 128]], ALU.is_ge, 0.0, 0, -1, [('PT', ps_)], [('PT', ps_)])
                    if j == i - 4:
                        self.asel(PT3, PT3, [[0, 4], [-1, 128]], ALU.is_ge, 0.0, -1, 1, [('PT', ps_)], [('PT', ps_)])
                    for h in range(4):
                        self.mm(self.pb[5][:, h * 72:h * 72 + 65], PT[:, h * 128:(h + 1) * 128], VSB[:, j, 1, 0:65],
                                (j == j0 and h == 0), j == i, [('PT', ps_), ('VS', j), 'VS_ones'], [('pb', 5)], skip=True)
                return f
            for j in range(i + 1):
                steps.append(sel_pair(j))
            for j in range(j0, i + 1):
                steps.append(win_pair(j))

            def combine():
                accs = self.pb[6][:, 0:288].rearrange("p (h e) -> p h e", e=72)
                accw = self.pb[5][:, 0:288].rearrange("p (h e) -> p h e", e=72)
                ocmp = self.pb[ob][:, 64:320].rearrange("p (h d) -> p h d", d=64)
                f = self.b_f
                self.S.add('dve', lambda e: e.reciprocal(out=f[:, 4:8], in_=accs[:, :, 64]), reads=[('pb', 6)], writes=['b_f'])
                self.S.add('dve', lambda e: e.reciprocal(out=f[:, 8:12], in_=accw[:, :, 64]), reads=[('pb', 5)], writes=['b_f'])
                self.tt('dve', f[:, 4:12], f[:, 4:12], GS[:, i, 4:12], ALU.mult, ['b_f', 'GSs'], ['b_f'])
                self.tt('dve', self.b_obf, ocmp, bc(GS[:, i, 0:4], 2, [128, 4, 64]), ALU.mult, [('pb', ob), 'GSs'], ['b_obf'])
                self.tt('dve', self.b_tmp, accs[:, :, 0:64], bc(f[:, 4:8], 2, [128, 4, 64]), ALU.mult, [('pb', 6), 'b_f'], ['b_tmp'])
                self.tt('dve', self.b_obf, self.b_obf, self.b_tmp, ALU.add, ['b_obf', 'b_tmp'], ['b_obf'])
                self.tt('dve', self.b_tmp, accw[:, :, 0:64], bc(f[:, 8:12], 2, [128, 4, 64]), ALU.mult, [('pb', 5), 'b_f'], ['b_tmp'])
                self.tt('dve', self.ob[:, 0:256].rearrange("p (h d) -> p h d", d=64), self.b_obf, self.b_tmp, ALU.add,
                        ['b_obf', 'b_tmp'], ['ob'])
                self.out_tile(i, 'ob', self.ob, 256, 384)
            steps.append(combine)
            return steps

        for f_ in sel_steps(0):
            f_()
        for i in range(NT):
            pend = sel_steps(i + 1) if i + 1 < NT else []
            asteps = attn_steps(i)
            gap = max(1, (len(asteps) - 1) // (len(pend) + 1)) if pend else 1
            for n_, a in enumerate(asteps):
                a()
                if pend and (n_ % gap == gap - 1) and n_ < len(asteps) - 1:
                    pend.pop(0)()
            while pend:
                pend.pop(0)()
        self.dbg_oT()

    def pass_C(self, l):
        dr = self.dr
        if not hasattr(self, 'c_ksum'):
            self.c_ksum = self.sb('c_ksum', [128, 2, 16], F32)
            self.c_khi = self.sb('c_khi', [128, 2, 16], BF16)
            self.c_klo = self.sb('c_klo', [128, 2, 16], BF16)
            self.c_tmp = self.sb('c_tmp', [128, 2, 16], F32)
            self.c_sc = self.sb('c_sc', [128, 4, 16], F32)
            self.c_mx = self.sb('c_mx', [128, 4, 8], F32)
            self.c_selb = self.sb('c_selb', [128, 4, 16], F32)
            self.c_selbT2 = [t_[0:16, :] for t_ in self.selT]
        self.wkeys = {}
        W = self.Wp
        self.load_w(W, 'Wp', dr['w_in'][l], 2444, 768, 0)
        wreads = list(self.wkeys['Wp'])
        self.load_gains(l, 'q_norm_c', 'k_norm_c', 4, 4)
        self.Ec = self.build_E('Ec', 256, 16, 16384)
        VS4 = self.VS.rearrange("p t (h e) -> p t h e", e=72)
        self.memset('pool', VS4[:, :, 0:4, 64:65], 1.0, ['VS_ones'])
        QKT = self.QKT
        for t in range(NT):
            if t % 4 == 0:
                hsl = self.load_hg(t // 4)
            self.proj_tile(hsl, t % 4, W, wreads, [(0, 512), (512, 256)], [0, 1])
            self.cp('act', self.pr[:, 0:512], self.pb[0][:, 0:512], [('pb', 0)], ['pr'])
            self.cp('act', VS4[:, t, 0:4, 0:64], self.pb[1][:, 0:256].rearrange("p (h d) -> p h d", d=64), [('pb', 1), 'VS_ones'], [('VS', t)])
            self.qk_post(self.pr[:, 0:512], 8, self.Gt[:, 0:512], t, self.xb[:, 0:512], 'pr', 'xb')
            pT = self.pb[3][:].bitcast(BF16)
            for p in range(4):
                self.tr(pT[:, p * 128:(p + 1) * 128], self.xb[:, p * 128:(p + 1) * 128], self.identb[:], ['xb_dve', 'xb_pool', 'identb'], [('pb', 3)])
            self.cp('act', QKT[:, 0:4, t * 128:(t + 1) * 128], pT[:, 0:512].rearrange("p (a c) -> p a c", c=128), [('pb', 3)], [('QKT', t)])
        allq = [('QKT', t) for t in range(NT)]
        ksum, khi, klo, ktmp = self.c_ksum, self.c_khi, self.c_klo, self.c_tmp
        self.S.add('dve', lambda e: e.tensor_reduce(out=ksum[:], in_=QKT[:, 2:4, :].rearrange("p a (n k) -> p a n k", k=256),
                                                    axis=AX.X, op=ALU.add), reads=allq, writes=['c_ksum'])
        self.cp('dve', khi[:], ksum[:], ['c_ksum'], ['c_khi'])
        self.cp('dve', ktmp[:], khi[:], ['c_khi'], ['c_tmp'])
        self.tt('dve', ktmp[:], ksum[:], ktmp[:], ALU.subtract, ['c_ksum', 'c_tmp'], ['c_tmp'])
        self.cp('dve', klo[:], ktmp[:], ['c_tmp'], ['c_klo'])
        st = {'sb': 0}

        def sel_steps(i):
            nb = i // 2
            if nb == 0:
                return []
            sl = i % 2
            Qz, qzk = self.c_qz[i]
            sc = self.c_sc
            selbT = self.c_selbT2[sl]
            sk = ('c_selbT', sl)

            def s_a():
                for h in range(4):
                    self.mm(self.pb[5][:, h * 16:(h + 1) * 16], Qz[:, h, :], khi[:, h // 2, :], True, False, qzk + ['c_khi'], [('pb', 5)])
                    self.mm(self.pb[5][:, h * 16:(h + 1) * 16], Qz[:, h, :], klo[:, h // 2, :], False, True, qzk + ['c_klo'], [('pb', 5)])

            def s_b():
                self.cp('dve', sc[:].rearrange("p h n -> p (h n)"), self.pb[5][:, 0:64], [('pb', 5)], ['c_sc'])
                if nb < 16:
                    self.memset('dve', sc[:, :, nb:16], -1e30, ['c_sc'])
                for h in range(4):
                    self.S.add('dve', (lambda h: lambda e: e.max(out=self.c_mx[:, h, :], in_=sc[:, h, :]))(h), reads=['c_sc'], writes=['c_mx'])
                self.tt('dve', self.c_selb[:], sc[:], bc(self.c_mx[:, :, 2], 2, [128, 4, 16]), ALU.is_ge, ['c_sc', 'c_mx'], ['c_selb'])
                self.tsc('dve', self.c_selb[:], self.c_selb[:], 1.0, -NEGB, ALU.subtract, ALU.mult, ['c_selb'], ['c_selb'])
                if nb < 16:
                    self.memset('dve', self.c_selb[:, :, nb:16], NEGB, ['c_selb'])

            def s_c():
                for h in range(4):
                    self.tr(self.pb[4][0:16, h * 128:(h + 1) * 128], self.c_selb[:, h, :], self.identf[:], ['c_selb', 'identf'], [('pb', 4)])
                self.cp('act', selbT[:, :], self.pb[4][0:16, 0:512], [('pb', 4)], [sk])
            return [s_a, s_b, s_c]

        def attn_steps(i):
            nb = i // 2
            Qz, qzk = self.c_qz[i]
            selbT = self.c_selbT2[i % 2]
            sk = ('c_selbT', i % 2)
            steps = []

            def pair(j):
                def f():
                    past = j < 2 * nb
                    b = 2 + (st['sb'] % 2)
                    st['sb'] += 1
                    bank = self.pb[b]
                    if past:
                        self.mm(bank[:, 0:512], self.Ec[:, j * 128:(j + 1) * 128], selbT[0:16, 0:512], True, False,
                                ['Ec', sk], [('pb', b)], skip=True)
                    for p in range(2):
                        self.mm(bank[:, p * 256:(p + 1) * 256], QKT[:, 2 + p, j * 128:(j + 1) * 128], Qz[:, 2 * p:2 * p + 2, :],
                                (not past) and p == 0, p == 1, [('QKT', j)] + qzk, [('pb', b)], skip=True)
                    ps_ = self.pt_n % 3
                    self.pt_n += 1
                    PT = self.PT[ps_]
                    self.act(PT[:, 0:512], bank[:, 0:512], AF.Exp, [('pb', b)], [('PT', ps_)], scale=0.125)
                    if j == i:
                        self.asel(PT[:, 0:512].rearrange("p (h q) -> p h q", q=128), PT[:, 0:512].rearrange("p (h q) -> p h q", q=128),
                                  [[0, 4], [1, 128]], ALU.is_ge, 0.0, 0, -1, [('PT', ps_)], [('PT', ps_)])
                    for h in range(4):
                        self.mm(self.pb[6][:, h * 72:h * 72 + 65], PT[:, h * 128:(h + 1) * 128], VS4[:, j, h, 0:65],
                                (j == 0 and h == 0), j == i, [('PT', ps_), ('VS', j), 'VS_ones'], [('pb', 6)], skip=True)
                return f
            for j in range(i + 1):
                steps.append(pair(j))

            def fin():
                acc = self.pb[6][:, 0:288].rearrange("p (h e) -> p h e", e=72)
                self.S.add('dve', lambda e: e.reciprocal(out=self.rec[:, 0:4], in_=acc[:, :, 64]), reads=[('pb', 6)], writes=['rec'])
                self.tt('dve', self.ob[:, 0:256].rearrange("p (h d) -> p h d", d=64), acc[:, :, 0:64], bc(self.rec[:, 0:4], 2, [128, 4, 64]),
                        ALU.mult, [('pb', 6), 'rec'], ['ob'])
                self.out_tile(i, 'ob', self.ob, 256, 640)
            steps.append(fin)
            return steps

        self.c_qz = {}
        self.c_qz[0] = self.make_qz(0, 4)
        for i in range(NT):
            if i + 1 < NT:
                self.c_qz[i + 1] = self.make_qz(i + 1, 4)
            pend = sel_steps(i + 1) if i + 1 < NT else []
            asteps = attn_steps(i)
            gap = max(1, (len(asteps) - 1) // (len(pend) + 1)) if pend else 1
            for n_, a in enumerate(asteps):
                a()
                if pend and (n_ % gap == gap - 1) and n_ < len(asteps) - 1:
                    pend.pop(0)()
            while pend:
                pend.pop(0)()
        self.dbg_oT()

    def final(self, l, xin, xout):
        dr = self.dr
        if not hasattr(self, 'f_yacc'):
            self.f_yacc = self.b_Pc[:, 0, :]
            self.f_ytmp = self.b_Pc[:, 1, :]
            self.f_yT = self.s1_all[:, 0:4096].rearrange("p (k c) -> p k c", c=512)
        Wzg = self.BIG[:, 0:31744].rearrange("p (k c) -> p k c", c=3968)
        Wbr = self.BIG[:, 31744:38912].rearrange("p (k c) -> p k c", c=1024)
        Wout = self.BIG[:, 38912:47104].rearrange("p (k c) -> p k c", c=1024)
        oz = self.BIG[:, 47104:47104 + 3584].rearrange("p (t k c) -> p t k c", k=7, c=128)
        og = self.BIG[:, 50688:50688 + 3584].rearrange("p (t c) -> p t c", c=896)
        sz = [self.qk_sq[:, 0:512], self.qk_xn[:, 0:512]]
        sg = [self.pr[:, 0:512], self.Gt[:, 0:512]]
        self.wkeys = {}
        for (c0, n, o) in ((1152, 384, 0), (2188, 256, 384), (3212, 256, 640), (3468, 3072, 896)):
            self.load_w(Wzg, 'Wzg', dr['w_in'][l], c0, n, o)
        self.load_w(Wbr, 'Wbr', dr['w_br_a'][l], 0, 1024, 0, nk=3, k0=0)
        self.load_w(Wbr, 'Wbr', dr['w_br_b'][l], 0, 1024, 0, nk=2, k0=3)
        self.load_w(Wbr, 'Wbr', dr['w_br_c'][l], 0, 1024, 0, nk=2, k0=5)
        self.load_w(Wout, 'Wout', dr['w_out'][l], 0, 1024, 0)
        kz, kb, ko = list(self.wkeys['Wzg']), list(self.wkeys['Wbr']), list(self.wkeys['Wout'])
        last = (xout is dr['y'])
        n = 0
        for g in range(NT // 4):
            hsl = self.load_hg(g)
            hgt = self.hg[hsl]
            self.dma(og, dr['oT'][g * 4:(g + 1) * 4].rearrange("t p c -> p t c"),
                     [('oT', g * 4 + i, c) for i in range(4) for c in (0, 384, 640)], ['og'])
            for zc in range(7):
                b = zc % 2
                for k in range(8):
                    self.mm(self.pb[b][:, 0:512], Wzg[:, k, zc * 128:(zc + 1) * 128], hgt[:, :, k, :], k == 0, k == 7,
                            [('hg', hsl)] + kz, [('pb', b)])
                self.act(sz[b], self.pb[b][:, 0:512], AF.Silu, [('pb', b)], [('sz', b)])
                self.tt('dve', oz[:, :, zc, :], og[:, :, zc * 128:(zc + 1) * 128], sz[b].rearrange("p (t c) -> p t c", c=128), ALU.mult,
                        [('sz', b), 'og'], [('oz', zc)])
            for m in range(8):
                for br in range(3):
                    gb = 2 + n % 2
                    ub = 4 + n % 2
                    sgb = sg[n % 2]
                    sgk = ('sg', n % 2)
                    n += 1
                    c0 = 896 + br * 1024 + m * 128
                    for k in range(8):
                        self.mm(self.pb[gb][:, 0:512], Wzg[:, k, c0:c0 + 128], hgt[:, :, k, :], k == 0, k == 7,
                                [('hg', hsl)] + kz, [('pb', gb)])
                    self.act(sgb, self.pb[gb][:, 0:512], AF.Sigmoid, [('pb', gb)], [sgk])
                    kcs = ([0, 1, 2], [3, 4], [5, 6])[br]
                    for ii, kc in enumerate(kcs):
                        self.mm(self.pb[ub][:, 0:512], Wbr[:, kc, m * 128:(m + 1) * 128], oz[:, :, kc, :], ii == 0, ii == len(kcs) - 1,
                                [('oz', kc)] + kb, [('pb', ub)])
                    if br == 0:
                        self.tt('dve', self.f_yacc, self.pb[ub][:, 0:512], sgb, ALU.mult, [('pb', ub), sgk], ['f_yacc'])
                    else:
                        self.tt('dve', self.f_ytmp, self.pb[ub][:, 0:512], sgb, ALU.mult, [('pb', ub), sgk], ['f_ytmp'])
                        if br == 1:
                            self.tt('dve', self.f_yacc, self.f_yacc, self.f_ytmp, ALU.add, ['f_yacc', 'f_ytmp'], ['f_yacc'])
                        else:
                            self.tt('dve', self.f_yT[:, m, :], self.f_yacc, self.f_ytmp, ALU.add, ['f_yacc', 'f_ytmp'], [('yT', m)])
            for tt_ in range(4):
                t = g * 4 + tt_
                xsl = self.xt_n % 2
                self.xt_n += 1
                xt = self.xt[xsl]
                self.dma(xt[:], xin[t * 128:(t + 1) * 128, :], [('x1', t)], [('xt', xsl)])
                osl = xsl
                orow = xt
                for nch in range(2):
                    b = 6 + nch
                    for k in range(8):
                        self.mm(self.pb[b][:, 0:512], self.f_yT[:, k, tt_ * 128:(tt_ + 1) * 128], Wout[:, k, nch * 512:(nch + 1) * 512],
                                k == 0, k == 7, [('yT', k)] + ko, [('pb', b)])
                    self.tt('dve', orow[:, nch * 512:(nch + 1) * 512], self.pb[b][:, 0:512], xt[:, nch * 512:(nch + 1) * 512], ALU.add,
                            [('pb', b), ('xt', xsl)], [('xt', xsl)])
                o = self.dma(xout[t * 128:(t + 1) * 128, :], orow[:], [('xt', xsl)], [('y', t) if last else ('x1', t)])
                if last:
                    self.final_ops.append(o)


def host_layout(sh):
    sh = dict(sh)
    sh['norm_g'] = np.ascontiguousarray(sh['norm_g'].reshape(2, 8, 128).transpose(0, 2, 1))
    sh['cmp_pos'] = np.ascontiguousarray(sh['cmp_pos'].transpose(0, 2, 1))
    for nm in ('cmp_k_w1', 'cmp_v_w1'):
        sh[nm] = np.ascontiguousarray(sh[nm].reshape(2, 32, 64, 128).transpose(0, 2, 1, 3))
    return sh


_CACHE = {}


def kernel(**inputs):
    n = 8
    if 'nc' not in _CACHE:
        _CACHE['nc'] = Builder(2).build()
    nc = _CACHE['nc']
    x = np.ascontiguousarray(inputs['x'], dtype=np.float32)
    pos = np.ascontiguousarray(inputs['positions']).astype(np.int32)
    shared = {}
    for k in ('norm_g', 'w_in', 'q_norm_a', 'k_norm_a', 'q_norm_b', 'k_norm_b', 'q_norm_c', 'k_norm_c', 'cmp_pos',
              'cmp_k_w1', 'cmp_k_w2', 'cmp_v_w1', 'cmp_v_w2', 'w_br_a', 'w_br_b', 'w_br_c', 'w_out'):
        shared[k] = np.ascontiguousarray(inputs[k], dtype=np.float32)
    shared = host_layout(shared)
    in_maps = []
    for c in range(n):
        m = dict(shared)
        m['x'] = x[c]
        m['pos'] = np.ascontiguousarray(pos[c].reshape(NT, 128).T)
        in_maps.append(m)
    res = run_bass_kernel_spmd(nc, in_maps, core_ids=list(range(n)))
    return np.stack([np.asarray(r['y'], dtype=np.float32) for r in res.results], axis=0)
```

```python
import contextlib
import math
import numpy as np
import concourse.bass as bass
import concourse.mybir as mybir
from concourse.bass_utils import run_bass_kernel_spmd

F32 = mybir.dt.float32
BF16 = mybir.dt.bfloat16
I32 = mybir.dt.int32
AF = mybir.ActivationFunctionType
ALU = mybir.AluOpType
AX = mybir.AxisListType

SAME_ENG_SYNC = {'pe': False, 'act': True, 'dve': True, 'pool': True, 'sp': True}
N_DMA_SEMS = 8

S_LEN = 4096
D = 1024
NT = 32
INW = 6540
EPS = 1e-6
NEGB = -30000.0


class _Op:
    __slots__ = ('eng', 'fn', 'deps', 'dma', 'signal', 'sem', 'val', 'idx', 'prev')


class Sched:
    def __init__(self, nc):
        self.nc = nc
        self.ops = []
        self.last_w = {}
        self.readers = {}
        self.fence_keys = []

    def add(self, eng, fn, reads=(), writes=(), dma=False):
        op = _Op()
        op.eng = eng
        op.fn = fn
        op.dma = dma
        op.signal = False
        op.sem = None
        op.val = 0
        op.idx = len(self.ops)
        deps = set()
        if self.fence_keys:
            reads = list(reads) + self.fence_keys
        for k in reads:
            w = self.last_w.get(k)
            if w is not None:
                deps.add(w)
        for k in writes:
            w = self.last_w.get(k)
            if w is not None:
                deps.add(w)
            for r in self.readers.get(k, ()):
                deps.add(r)
        op.deps = deps
        for k in reads:
            self.readers.setdefault(k, []).append(op.idx)
        for k in writes:
            self.last_w[k] = op.idx
            self.readers[k] = []
        self.ops.append(op)
        return op.idx

    def emit(self, final_waits=()):
        nc = self.nc
        ops = self.ops
        engs = ['pe', 'act', 'dve', 'pool', 'sp']
        for op in ops:
            need = set()
            best = {}
            for d in op.deps:
                Dp = ops[d]
                if Dp.eng == op.eng and not Dp.dma and not op.dma and not SAME_ENG_SYNC[op.eng]:
                    continue
                if Dp.dma:
                    need.add(d)
                    Dp.signal = True
                elif best.get(Dp.eng, -1) < d:
                    best[Dp.eng] = d
            for d in best.values():
                need.add(d)
                ops[d].signal = True
            op.deps = need
        for d in final_waits:
            ops[d].signal = True
        with contextlib.ExitStack() as st:
            esem = {e: st.enter_context(nc.semaphore('s_' + e)) for e in engs}
            dsem = {e: [st.enter_context(nc.semaphore('d_%s%d' % (e, i))) for i in range(N_DMA_SEMS)]
                    for e in ('sp', 'pool', 'act')}
            ecount = {e: 0 for e in engs}
            dcount = {e: 0 for e in engs}
            for op in ops:
                if op.dma:
                    op.signal = True
                if not op.signal:
                    continue
                if op.dma:
                    i = dcount[op.eng]
                    dcount[op.eng] += 1
                    op.sem = dsem[op.eng][i % N_DMA_SEMS]
                    op.val = 16 * (i // N_DMA_SEMS + 1)
                    op.prev = (op.sem, op.val - 16) if i >= N_DMA_SEMS else None
                else:
                    ecount[op.eng] += 1
                    op.sem = esem[op.eng]
                    op.val = ecount[op.eng]
            per = {e: [] for e in engs}
            for op in ops:
                per[op.eng].append(op)
            block = st.enter_context(nc.Block())

            def run(e, name, extra=()):
                waited = {}
                for op in per[name]:
                    ws = {}
                    for d in op.deps:
                        Dp = ops[d]
                        key = id(Dp.sem)
                        if waited.get(key, 0) >= Dp.val:
                            continue
                        if key not in ws or ws[key][1] < Dp.val:
                            ws[key] = (Dp.sem, Dp.val)
                    if op.dma and op.prev is not None:
                        key = id(op.prev[0])
                        if waited.get(key, 0) < op.prev[1] and (key not in ws or ws[key][1] < op.prev[1]):
                            ws[key] = op.prev
                    for key, (sem, val) in ws.items():
                        e.wait_ge(sem, val)
                        waited[key] = val
                    ins = op.fn(e)
                    if op.signal:
                        ins.then_inc(op.sem, 16 if op.dma else 1)
                for d in extra:
                    Dp = ops[d]
                    e.wait_ge(Dp.sem, Dp.val)

            @block.tensor
            def _(e):
                run(e, 'pe')

            @block.scalar
            def _(e):
                run(e, 'act')

            @block.vector
            def _(e):
                run(e, 'dve')

            @block.gpsimd
            def _(e):
                run(e, 'pool')

            @block.sync
            def _(e):
                run(e, 'sp', extra=final_waits)
        self.stats = {e: len(per[e]) for e in engs}
        self.stats['signals'] = dict(ecount)
        self.stats['dmasig'] = dict(dcount)


def bc(ap, axis, shape):
    return ap.unsqueeze(axis).to_broadcast(shape)


class Builder:
    def __init__(self, n_layers=2, dbg=None, stop_after=None):
        self.n_layers = n_layers
        self.dbg = dbg or ()
        self.stop_after = stop_after
        nc = bass.Bass("TRN2", target_bir_lowering=False)
        self.nc = nc
        self.S = Sched(nc)
        self.st = contextlib.ExitStack()
        self.uid = 0

    def sb(self, name, shape, dt):
        return self.st.enter_context(self.nc.sbuf_tensor(name, shape, dt))

    def ps(self, name, shape, dt=F32):
        return self.st.enter_context(self.nc.psum_tensor(name, shape, dt))

    def mm(self, out, lhsT, rhs, start, stop, r, w, skip=False):
        if skip:
            self.S.add('pe', lambda e: e.matmul(out, lhsT=lhsT, rhs=rhs, start=start, stop=stop, skip_group_check=True), reads=r, writes=w)
        else:
            self.S.add('pe', lambda e: e.matmul(out, lhsT=lhsT, rhs=rhs, start=start, stop=stop), reads=r, writes=w)

    def tr(self, out, in_, ident, r, w):
        self.S.add('pe', lambda e: e.transpose(out=out, in_=in_, identity=ident), reads=r, writes=w)

    def act(self, out, in_, func, r, w, bias=None, scale=None, accum_out=None):
        kw = {}
        if bias is not None:
            kw['bias'] = bias
        if scale is not None:
            kw['scale'] = scale
        if accum_out is not None:
            kw['accum_out'] = accum_out
        self.S.add('act', lambda e: e.activation(out=out, in_=in_, func=func, **kw), reads=r, writes=w)

    def rsqrt(self, out, in_, scale, r, w):
        self.act(out, in_, AF.Ln, r, w, bias=self.epsc[:out.shape[0], 0:1], scale=scale)
        self.act(out, out, AF.Exp, w, w, scale=-0.5)

    def tt(self, eng, out, in0, in1, op, r, w):
        self.S.add(eng, lambda e: e.tensor_tensor(out=out, in0=in0, in1=in1, op=op), reads=r, writes=w)

    def tsc(self, eng, out, in0, s1, s2, op0, op1, r, w):
        if op1 is None:
            self.S.add(eng, lambda e: e.tensor_scalar(out=out, in0=in0, scalar1=s1, scalar2=None, op0=op0), reads=r, writes=w)
        else:
            self.S.add(eng, lambda e: e.tensor_scalar(out=out, in0=in0, scalar1=s1, scalar2=s2, op0=op0, op1=op1), reads=r, writes=w)

    def cp(self, eng, out, in_, r, w):
        if eng == 'act':
            self.S.add('act', lambda e: e.copy(out=out, in_=in_), reads=r, writes=w)
        else:
            self.S.add(eng, lambda e: e.tensor_copy(out=out, in_=in_), reads=r, writes=w)

    def memset(self, eng, ap, val, w):
        self.S.add(eng, lambda e: e.memset(ap, val), writes=w)

    def asel(self, out, in_, pattern, op, fill, base, cm, r, w):
        self.S.add('pool', lambda e: e.affine_select(out=out, in_=in_, pattern=pattern, compare_op=op, fill=fill,
                                                     base=base, channel_multiplier=cm), reads=r, writes=w)

    def dma(self, out, in_, r, w, q='sp'):
        return self.S.add(q, lambda e: e.dma_start(out=out, in_=in_), reads=r, writes=w, dma=True)

    def build_E(self, nm, blk, npart, off):
        E = self.BIG[0:npart, off:off + S_LEN]
        self.memset('pool', E, 1.0, [nm])
        self.asel(E, E, [[1, S_LEN]], ALU.is_ge, 0.0, 0, -blk, [nm], [nm])
        self.asel(E, E, [[-1, S_LEN]], ALU.is_ge, 0.0, blk - 1, blk, [nm], [nm])
        return E

    def make_qz(self, i, nh):
        sl = self.qz_n % 2
        self.qz_n += 1
        Qz = self.Qz[sl]
        npair = nh // 2
        for par in range(2):
            self.cp('pool', Qz[par * 64:(par + 1) * 64, par:nh:2, :], self.QKT[par * 64:(par + 1) * 64, 0:npair, i * 128:(i + 1) * 128],
                    [('QKT', i), 'Qz0'], [('Qz', sl, par)])
        return Qz, [('Qz', sl, 0), ('Qz', sl, 1)]

    def emit_step(self, s_fn, r_fn):
        if s_fn is not None:
            s_fn()
        p = getattr(self, '_pend_r', None)
        if p is not None:
            p()
        self._pend_r = r_fn

    def flush_steps(self):
        p = getattr(self, '_pend_r', None)
        if p is not None:
            p()
        self._pend_r = None

    def fence(self):
        self.S.fence_keys = []
        self.fence_n = getattr(self, 'fence_n', 0) + 1
        n = self.fence_n
        fs = self.fsc
        self.mm(self.pb[7][0:1, 0:1], self.identb[0:1, 0:1], self.identb[0:1, 0:1], True, True, ['identb'], [('pb', 7), ('fence', 'pe', n)])
        self.cp('act', fs[0:1, 0:1], fs[0:1, 4:5], ['fsc'], [('fence', 'act', n), 'fscw_act'])
        self.cp('dve', fs[0:1, 1:2], fs[0:1, 5:6], ['fsc'], [('fence', 'dve', n), 'fscw_dve'])
        self.cp('pool', fs[0:1, 2:3], fs[0:1, 6:7], ['fsc'], [('fence', 'pool', n), 'fscw_pool'])
        self.S.fence_keys = [('fence', e, n) for e in ('pe', 'act', 'dve', 'pool')]

    def build(self):
        nc = self.nc
        dr = {}

        def din(name, shape, dt=F32):
            dr[name] = nc.dram_tensor(name, shape, dt, kind="ExternalInput").ap()

        din('x', [S_LEN, D])
        din('pos', [128, NT], I32)
        din('norm_g', [2, 128, 8])
        din('w_in', [2, D, INW])
        for nm in ('q_norm_a', 'k_norm_a', 'q_norm_b', 'k_norm_b', 'q_norm_c', 'k_norm_c'):
            din(nm, [2, 64])
        din('cmp_pos', [2, 64, 32])
        din('cmp_k_w1', [2, 64, 32, 128])
        din('cmp_k_w2', [2, 128, 64])
        din('cmp_v_w1', [2, 64, 32, 128])
        din('cmp_v_w2', [2, 128, 64])
        din('w_br_a', [2, 384, D])
        din('w_br_b', [2, 256, D])
        din('w_br_c', [2, 256, D])
        din('w_out', [2, D, D])
        dr['y'] = nc.dram_tensor('y', [S_LEN, D], F32, kind="ExternalOutput").ap()
        dr['x1'] = nc.dram_tensor('x1s', [S_LEN, D], F32).ap()
        dr['hnT'] = nc.dram_tensor('hnTs', [NT, 128, 1024], BF16).ap()
        dr['oT'] = nc.dram_tensor('oTs', [NT, 128, 7 * 128], BF16).ap()
        for nm, shape, dt in self.dbg:
            dr[nm] = nc.dram_tensor(nm, shape, dt, kind="ExternalOutput").ap()
        self.dr = dr
        self.final_ops = []
        with self.st:
            self.setup()
            for l in range(self.n_layers):
                xin = dr['x'] if l == 0 else dr['x1']
                xout = dr['y'] if l == self.n_layers - 1 else dr['x1']
                self.layer(l, xin, xout)
            self.S.emit(final_waits=self.final_ops)
        return nc

    def setup(self):
        nc = self.nc
        dr = self.dr
        self.alloc_common()
        self.identf = self.sb('identf', [128, 128], F32)
        self.identb = self.sb('identb', [128, 128], BF16)
        self.onesf = self.sb('onesf', [128, 128], F32)
        self.memset('pool', self.identf[:], 1.0, ['identf'])
        self.asel(self.identf[:], self.identf[:], [[-1, 128]], ALU.is_equal, 0.0, 0, 1, ['identf'], ['identf'])
        self.cp('dve', self.identb[:], self.identf[:], ['identf'], ['identb'])
        self.memset('pool', self.onesf[:], 1.0, ['onesf'])
        self.epsc = self.sb('epsc', [128, 1], F32)
        self.memset('dve', self.epsc[:], EPS, ['epsc'])
        for q_ in self.Qz:
            self.memset('pool', q_[:], 0.0, ['Qz0'])
        self.fsc = self.sb('fsc', [128, 8], F32)
        self.memset('dve', self.fsc[:], 0.0, ['fsc'])
        posi = self.sb('posi', [128, NT], I32)
        posf = self.sb('posf', [128, NT], F32)
        fr = self.sb('fr', [128, 8], F32)
        wpF = self.BIG[:, 46592:46592 + 9216].bitcast(F32)
        wpI = self.BIG[:, 46592:46592 + 9216].bitcast(I32)
        ang = wpF[:, 0:256].rearrange("p (t f) -> p t f", f=8)
        tmpa = wpF[:, 256:512].rearrange("p (t f) -> p t f", f=8)
        self.cos = self.sb('cos', [128, NT, 8], F32)
        self.sin = self.sb('sin', [128, NT, 8], F32)
        self.dma(posi[:], dr['pos'][:, :], [], ['posi'])
        self.cp('dve', posf[:], posi[:], ['posi'], ['posf'])
        for i in range(8):
            f = float(np.float32(500000.0) ** np.float32(-i / 8.0))
            self.memset('dve', fr[:, i:i + 1], f, ['fr'])
        self.tt('dve', ang[:], bc(posf[:], 2, [128, NT, 8]), bc(fr[:], 1, [128, NT, 8]), ALU.mult, ['posf', 'fr'], ['ang'])
        PI = math.pi
        HI = 6.28125
        LO = 2 * PI - 6.28125
        ni = wpI[:, 512:768].rearrange("p (t f) -> p t f", f=8)
        nf = wpF[:, 768:1024].rearrange("p (t f) -> p t f", f=8)
        rr = wpF[:, 1024:1280].rearrange("p (t f) -> p t f", f=8)
        self.tsc('dve', tmpa[:], ang[:], 1.0 / (2 * PI), None, ALU.mult, None, ['ang'], ['tmpa'])
        self.cp('dve', ni[:], tmpa[:], ['tmpa'], ['rr_ni'])
        self.cp('dve', nf[:], ni[:], ['rr_ni'], ['rr_nf'])
        self.S.add('dve', lambda e: e.scalar_tensor_tensor(out=rr[:], in0=nf[:], scalar=-HI, in1=ang[:], op0=ALU.mult, op1=ALU.add),
                   reads=['rr_nf', 'ang'], writes=['rr_r'])
        self.S.add('dve', lambda e: e.scalar_tensor_tensor(out=rr[:], in0=nf[:], scalar=-LO, in1=rr[:], op0=ALU.mult, op1=ALU.add),
                   reads=['rr_nf', 'rr_r'], writes=['rr_r'])

        def wrap(buf, key):
            self.tsc('dve', tmpa[:], buf[:], PI, -2 * PI, ALU.is_gt, ALU.mult, [key], ['tmpa'])
            self.tt('dve', buf[:], buf[:], tmpa[:], ALU.add, [key, 'tmpa'], [key])
            self.tsc('dve', tmpa[:], buf[:], -PI, 2 * PI, ALU.is_lt, ALU.mult, [key], ['tmpa'])
            self.tt('dve', buf[:], buf[:], tmpa[:], ALU.add, [key, 'tmpa'], [key])
            self.tsc('dve', buf[:], buf[:], -3.141592, 3.141592, ALU.max, ALU.min, [key], [key])
        wrap(rr, 'rr_r')
        self.act(self.sin[:], rr[:], AF.Sin, ['rr_r'], ['sin'])
        self.tsc('dve', rr[:], rr[:], PI / 2, None, ALU.add, None, ['rr_r', 'sin'], ['rr_r'])
        wrap(rr, 'rr_r')
        self.act(self.cos[:], rr[:], AF.Sin, ['rr_r'], ['cos'])
        scrF = self.BIG[:, 0:24576].bitcast(F32)
        scrI = self.BIG[:, 0:24576].bitcast(I32)

        def carve(src, k):
            return src[:, k * 2176:(k + 1) * 2176].rearrange("p (o q) -> p o q", q=128)
        dA = carve(scrF, 0)
        dAi = carve(scrI, 1)
        t1 = carve(scrF, 2)
        t2 = carve(scrF, 3)
        t3 = carve(scrF, 4)
        t4 = carve(scrI, 4)
        self.MA = self.sb('MA', [128, 17, 128], BF16)
        self.S.add('pool', lambda e: e.iota(dAi[:], pattern=[[128, 17], [1, 128]], base=0, channel_multiplier=-1), writes=['dAi'])
        self.cp('dve', dA[:], dAi[:], ['dAi'], ['dA'])
        self.tsc('dve', t1[:], dA[:], 128.0, None, ALU.is_le, None, ['dA'], ['mt1'])
        self.tsc('dve', t4[:], dAi[:], 3, None, ALU.bitwise_and, None, ['dAi'], ['mt3'])
        self.cp('dve', t2[:], t4[:], ['mt3'], ['mt2'])
        self.tsc('dve', t2[:], t2[:], 0.0, None, ALU.is_equal, None, ['mt2'], ['mt2'])
        self.tsc('dve', t3[:], dA[:], 512.0, None, ALU.is_le, None, ['dA'], ['mt3'])
        self.tt('dve', t2[:], t2[:], t3[:], ALU.mult, ['mt2', 'mt3'], ['mt2'])
        self.tt('dve', t1[:], t1[:], t2[:], ALU.add, ['mt1', 'mt2'], ['mt1'])
        self.tsc('dve', t4[:], dAi[:], 15, None, ALU.bitwise_and, None, ['dAi', 'mt2'], ['mt3'])
        self.cp('dve', t2[:], t4[:], ['mt3', 'mt1'], ['mt2'])
        self.tsc('dve', t2[:], t2[:], 0.0, None, ALU.is_equal, None, ['mt2'], ['mt2'])
        self.tsc('dve', t3[:], dA[:], 2048.0, None, ALU.is_le, None, ['dA', 'mt2'], ['mt3'])
        self.tt('dve', t2[:], t2[:], t3[:], ALU.mult, ['mt2', 'mt3'], ['mt2'])
        self.tt('dve', t1[:], t1[:], t2[:], ALU.add, ['mt1', 'mt2'], ['mt1'])
        self.tsc('dve', t2[:], dA[:], 0.0, None, ALU.is_ge, None, ['dA', 'mt1'], ['mt2'])
        self.tt('dve', self.MA[:], t1[:], t2[:], ALU.mult, ['mt1', 'mt2'], ['MA'])
        self.AM = self.sb('AM', [128, NT, 64], F32)
        vsF = self.BIG[:, 32768:32768 + NT * 432].bitcast(F32)
        vsI = self.BIG[:, 32768:32768 + NT * 432].bitcast(I32)
        am1 = vsF[:, 0:2048].rearrange("p (t j) -> p t j", j=64)
        am1i = vsI[:, 2048:4096].rearrange("p (t j) -> p t j", j=64)
        am2 = vsF[:, 4096:6144].rearrange("p (t j) -> p t j", j=64)
        for a in range(2):
            self.S.add('pool', (lambda a: lambda e: e.iota(am1i[a * 64:(a + 1) * 64], pattern=[[-2, NT], [1, 64]], base=-a,
                                                           channel_multiplier=0))(a),
                       writes=['am1i_%d' % a])
        self.cp('dve', am1[:], am1i[:], ['am1i_0', 'am1i_1'], ['am1_0', 'am1_1'])
        self.tsc('dve', am2[:], am1[:], 0.0, -1e30, ALU.is_gt, ALU.mult, ['am1_0', 'am1_1'], ['am2'])
        self.tsc('dve', am1[:], am1[:], -1.0, 1e4, ALU.is_ge, ALU.mult, ['am1_0', 'am1_1', 'am2'], ['am1', 'am1_0', 'am1_1'])
        self.tt('dve', self.AM[:], am1[:], am2[:], ALU.add, ['am1', 'am2'], ['AM'])
        self.tsc('dve', self.AM[:, :, 0:1], self.AM[:, :, 0:1], 1e4, None, ALU.add, None, ['AM'], ['AM'])
        self.cover = self.sb('cover', [128, 2, 64], F32)
        self.memset('pool', self.cover[:], 1.0, ['cover'])
        self.asel(self.cover[:], self.cover[:], [[-128, 2], [4, 64]], ALU.is_ge, 0.0, 3, -1, ['cover'], ['cover'])
        self.asel(self.cover[:], self.cover[:], [[128, 2], [-4, 64]], ALU.is_ge, 0.0, 1, 1, ['cover'], ['cover'])
        self.pb = [self.ps('pb%d' % i, [128, 512], F32) for i in range(8)]
        self.xt = [self.sb('xt%d' % i, [128, D], F32) for i in range(2)]
        self.hg = [self.sb('hg%d' % i, [128, 4, 8, 128], BF16) for i in range(2)]
        self.wstage = [self.sb('wst%d' % i, [128, 8, 128], F32) for i in range(2)]
        self.wst_n = 0
        self.hg_n = 0
        self.xt_n = 0
        if 'dbg_cs' in [d[0] for d in self.dbg]:
            o = self.dma(self.dr['dbg_cs'][:, 0:256], self.cos[:].rearrange("p t f -> p (t f)"), ['cos'], [])
            self.final_ops.append(o)
            o = self.dma(self.dr['dbg_cs'][:, 256:512], self.sin[:].rearrange("p t f -> p (t f)"), ['sin'], [])
            self.final_ops.append(o)
            mtmp = self.sb('mtmp', [128, 17 * 128], F32)
            self.cp('dve', mtmp[:], self.MA[:].rearrange("p o q -> p (o q)"), ['MA'], ['mtmp'])
            o = self.dma(self.dr['dbg_ma'][:, :], mtmp[:], ['mtmp'], [])
            self.final_ops.append(o)
            o = self.dma(self.dr['dbg_am'][:, :], self.AM[:].rearrange("p t f -> p (t f)"), ['AM'], [])
            self.final_ops.append(o)
            o = self.dma(self.dr['dbg_cov'][:, :], self.cover[:].rearrange("p t f -> p (t f)"), ['cover'], [])
            self.final_ops.append(o)

    def load_w(self, W, wkey, src, c0, n, o, nk=8, k0=0, eng_cycle=('pool', 'dve')):
        done = 0
        while done < n:
            m = min(128, n - done)
            sl = self.wst_n % 2
            self.wst_n += 1
            stg = self.wstage[sl]
            self.dma(stg[:, 0:nk, 0:m], src[:, c0 + done:c0 + done + m].rearrange("(k p) c -> p k c", p=128),
                     [], [('wst', sl)])
            eng = eng_cycle[self.wst_n % len(eng_cycle)]
            self.cp(eng, W[:, k0:k0 + nk, o + done:o + done + m], stg[:, 0:nk, 0:m], [('wst', sl)], [(wkey, self.wst_n)])
            self.wkeys.setdefault(wkey, []).append((wkey, self.wst_n))
            done += m

    def layer(self, l, xin, xout):
        if l > 0:
            self.fence()
        self.stage1(l, xin)
        if self.stop_after == 'stage1':
            return
        only = getattr(self, 'only', None)
        if only is None or 'A' in only:
            self.fence()
            self.pass_A(l)
        if self.stop_after in ('A', 'Aproj'):
            return
        if only is None or 'C' in only:
            self.fence()
            self.pass_C(l)
        if self.stop_after == 'C':
            return
        if only is None or 'B' in only:
            self.fence()
            self.pass_B(l)
        if self.stop_after == 'B':
            return
        self.fence()
        self.final(l, xin, xout)

    def stage1(self, l, xin):
        dr = self.dr
        if l == 0:
            self.gT = self.sb('gT', [128, 8], F32)
            self.s1_all = self.sb('s1all', [128, 4096], BF16)
            self.s1_sq = self.s1_all[:, 0:1024]
            self.s1_ss = self.sb('s1ss', [128, 2], F32)
            self.s1_xs = self.s1_all[:, 1024:2048]
            self.s1_hT = [self.s1_all[:, 2048 + i * 1024:3072 + i * 1024].rearrange("p (k c) -> p k c", c=128) for i in range(2)]
        self.dma(self.gT[:], dr['norm_g'][l], [], ['gT'])
        for t in range(NT):
            sl = t % 2
            xt = self.xt[sl]
            self.dma(xt[:], xin[t * 128:(t + 1) * 128, :], [('x1', t)], [('xt', sl)])
            ss = self.s1_ss[:, sl:sl + 1]
            xs = (self.s1_sq, self.s1_xs)[sl]
            self.act(xs, xt[:], AF.Square, [('xt', sl)], [('s1xs', sl), ('s1ss', sl)], accum_out=ss)
            self.rsqrt(ss, ss, 1.0 / D, [('s1ss', sl)], [('s1ss', sl)])
            self.act(xs, xt[:], AF.Copy, [('xt', sl), ('s1ss', sl)], [('s1xs', sl)], scale=ss)
            pT = self.pb[sl][:].bitcast(BF16)
            for k in range(8):
                self.tr(pT[:, k * 128:(k + 1) * 128], xs[:, k * 128:(k + 1) * 128], self.identb[:],
                        [('s1xs', sl), 'identb'], [('pb', sl)])
            hT = self.s1_hT[sl]
            self.tt('dve', hT, pT[:, 0:1024].rearrange("p (k t) -> p k t", k=8), bc(self.gT[:], 2, [128, 8, 128]), ALU.mult,
                    [('pb', sl), 'gT'], [('s1hT', sl)])
            self.dma(dr['hnT'][t], hT.rearrange("p k t -> p (k t)"), [('s1hT', sl)], [('hnT', t)])
        if 'dbg_hnT' in [d[0] for d in self.dbg]:
            for t in range(NT):
                o = self.dma(dr['dbg_hnT'][t], dr['hnT'][t], [('hnT', t)], [])
                self.final_ops.append(o)

    def load_hg(self, g):
        sl = self.hg_n % 2
        self.hg_n += 1
        self.dma(self.hg[sl][:].rearrange("p t k c -> p t (k c)"), self.dr['hnT'][g * 4:(g + 1) * 4].rearrange("t p c -> p t c"),
                 [('hnT', g * 4 + i) for i in range(4)], [('hg', sl)])
        return sl

    def proj_tile(self, hsl, tt, W, wreads, chunks, banks):
        for (c0, n), b in zip(chunks, banks):
            for k in range(8):
                self.mm(self.pb[b][:, 0:n], self.hg[hsl][:, tt, k, :], W[:, k, c0:c0 + n], k == 0, k == 7,
                        [('hg', hsl)] + wreads, [('pb', b)])

    def qk_post(self, pr, nh, Gt, t, xb, prk, xbk):
        W_ = nh * 64
        sq = self.qk_sq[:, 0:W_]
        ss = self.qk_ss[:, 0:nh]
        self.tt('dve', sq, pr, pr, ALU.mult, [prk], ['qk_sq'])
        self.S.add('dve', lambda e: e.tensor_reduce(out=ss, in_=sq.rearrange("p (h d) -> p h d", d=64), axis=AX.X, op=ALU.add),
                   reads=['qk_sq'], writes=['qk_ss'])
        self.rsqrt(ss, ss, 1.0 / 64, ['qk_ss'], ['qk_ss'])
        na = (2 * nh + 2) // 3
        pr3 = pr.rearrange("p (h d) -> p h d", d=64)
        xn3 = self.qk_xn[:, 0:W_].rearrange("p (h d) -> p h d", d=64)
        xb3 = xb.rearrange("p (h d) -> p h d", d=64)
        G3 = Gt.rearrange("p (h d) -> p h d", d=64)
        for eng, h0, h1 in (('dve', 0, na), ('pool', na, nh)):
            n_ = h1 - h0
            if n_ <= 0:
                continue
            kx = 'qk_xn_' + eng
            xn_ = xn3[:, h0:h1, :]
            self.tt(eng, xn_, pr3[:, h0:h1, :], bc(ss[:, h0:h1], 2, [128, n_, 64]), ALU.mult, [prk, 'qk_ss'], [kx])
            self.tt(eng, xn_, xn_, G3[:, h0:h1, :], ALU.mult, [kx, 'Gt'], [kx])
            cosb = bc(self.cos[:, t, :], 1, [128, n_, 8])
            sinb = bc(self.sin[:, t, :], 1, [128, n_, 8])
            r = [self.qk_r[i][:, h0:h1, :] for i in range(4)]
            rk = ['qk_r%d_%s' % (i, eng) for i in range(4)]
            self.tt(eng, r[0], xn_[:, :, 0:8], cosb, ALU.mult, [kx, 'cos'], [rk[0]])
            self.tt(eng, r[1], xn_[:, :, 8:16], sinb, ALU.mult, [kx, 'sin'], [rk[1]])
            self.tt(eng, r[2], xn_[:, :, 8:16], cosb, ALU.mult, [kx, 'cos'], [rk[2]])
            self.tt(eng, r[3], xn_[:, :, 0:8], sinb, ALU.mult, [kx, 'sin'], [rk[3]])
            xk = xbk + '_' + eng
            self.tt(eng, xb3[:, h0:h1, 0:8], r[0], r[1], ALU.subtract, [rk[0], rk[1]], [xk])
            self.tt(eng, xb3[:, h0:h1, 8:16], r[2], r[3], ALU.add, [rk[2], rk[3]], [xk])
            self.cp(eng, xb3[:, h0:h1, 16:64], xn_[:, :, 16:64], [kx], [xk])

    def alloc_common(self):
        if hasattr(self, 'qk_sq'):
            return
        self.qk_sq = self.sb('qk_sq', [128, 768], F32)
        self.qk_ss = self.sb('qk_ss', [128, 12], F32)
        self.qk_xn = self.sb('qk_xn', [128, 768], F32)
        self.qk_r = [self.sb('qk_r%d' % i, [128, 12, 8], F32) for i in range(4)]
        self.pr = self.sb('pr', [128, 768], F32)
        self.xb = self.sb('xb', [128, 768], BF16)
        self.Gt = self.sb('Gt', [128, 768], F32)
        self.g64 = self.sb('g64', [128, 2, 64], F32)
        self.PT = [self.sb('PT%d' % i, [128, 768], BF16) for i in range(3)]
        self.pt_n = 0
        self.ob = self.sb('ob', [128, 384], BF16)
        self.rec = self.sb('rec', [128, 12], F32)
        self.oTt = [self.sb('oTt%d' % i, [128, 384], BF16) for i in range(2)]
        self.selT = [self.sb('selT%d' % i, [64, 512], BF16) for i in range(2)]
        self.Qz = [self.sb('Qz%d' % i, [128, 6, 128], BF16) for i in range(2)]
        self.qz_n = 0
        self.ot_n = 0
        self.BIG = self.sb('BIG', [128, 57344], BF16)
        self.QKT = self.BIG[:, 0:32768].rearrange("p (a c) -> p a c", c=S_LEN)
        self.VS = self.BIG[:, 32768:32768 + NT * 432].rearrange("p (t c) -> p t c", c=432)
        self.Wp = self.BIG[:, 46592:46592 + 9216].rearrange("p (k c) -> p k c", c=1152)

    def load_gains(self, l, qn, kn, nq, nk):
        dr = self.dr
        self.dma(self.g64[:, 0, :], dr[qn][l].partition_broadcast(128), [], ['g64q'])
        self.dma(self.g64[:, 1, :], dr[kn][l].partition_broadcast(128), [], ['g64k'])
        G3 = self.Gt[:, 0:(nq + nk) * 64].rearrange("p (h d) -> p h d", d=64)
        self.cp('dve', G3[:, 0:nq, :], bc(self.g64[:, 0, :], 1, [128, nq, 64]), ['g64q'], ['Gt'])
        self.cp('dve', G3[:, nq:nq + nk, :], bc(self.g64[:, 1, :], 1, [128, nk, 64]), ['g64k'], ['Gt'])

    def out_tile(self, i, acc_key, ob_ap, ncol, c0):
        npair = ncol // 128
        pT = self.pb[7][:].bitcast(BF16)
        for p in range(npair):
            self.tr(pT[:, p * 128:(p + 1) * 128], ob_ap[:, p * 128:(p + 1) * 128], self.identb[:], [acc_key, 'identb'], [('pb', 7)])
        sl = self.ot_n % 2
        self.ot_n += 1
        self.cp('act', self.oTt[sl][:, 0:ncol], pT[:, 0:ncol], [('pb', 7)], [('oTt', sl)])
        self.dma(self.dr['oT'][i][:, c0:c0 + ncol], self.oTt[sl][:, 0:ncol], [('oTt', sl)], [('oT', i, c0)])

    def pass_A(self, l):
        dr = self.dr
        self.alloc_common()
        self.wkeys = {}
        W = self.Wp
        self.load_w(W, 'Wp', dr['w_in'][l], 0, 1152, 0)
        wreads = list(self.wkeys['Wp'])
        self.load_gains(l, 'q_norm_a', 'k_norm_a', 6, 6)
        VS4 = self.VS.rearrange("p t (h e) -> p t h e", e=72)
        self.S.add('pool', lambda e: e.memset(VS4[:, :, :, 64:65], 1.0), reads=['MA', 'AM'], writes=['VS_ones'])
        QKT = self.QKT
        for t in range(NT):
            if t % 4 == 0:
                hsl = self.load_hg(t // 4)
            self.proj_tile(hsl, t % 4, W, wreads, [(0, 384), (384, 384), (768, 384)], [0, 1, 2])
            self.cp('act', self.pr[:, 0:384], self.pb[0][:, 0:384], [('pb', 0)], ['pr'])
            self.cp('act', self.pr[:, 384:768], self.pb[1][:, 0:384], [('pb', 1)], ['pr'])
            self.cp('act', VS4[:, t, :, 0:64], self.pb[2][:, 0:384].rearrange("p (h d) -> p h d", d=64), [('pb', 2), 'VS_ones', 'MA', 'AM'], [('VS', t)])
            self.qk_post(self.pr[:, 0:768], 12, self.Gt[:, 0:768], t, self.xb[:, 0:768], 'pr', 'xb')
            pT = self.pb[3][:].bitcast(BF16)
            for p in range(6):
                self.tr(pT[:, p * 128:(p + 1) * 128], self.xb[:, p * 128:(p + 1) * 128], self.identb[:], ['xb_dve', 'xb_pool', 'identb'], [('pb', 3)])
            self.cp('act', QKT[:, 0:6, t * 128:(t + 1) * 128], pT[:, 0:768].rearrange("p (a c) -> p a c", c=128), [('pb', 3), 'MA', 'AM'], [('QKT', t)])
        if self.stop_after == 'Aproj':
            return
        stA = {'sb': 0}

        def a_pair(i, j, j0, Qz, qzk):
            o = i - j
            d_ = {}

            def s_():
                bS = [2 + 2 * (stA['sb'] % 2), 3 + 2 * (stA['sb'] % 2)]
                stA['sb'] += 1
                d_['bS'] = bS
                for p in range(3):
                    bb = bS[0] if p < 2 else bS[1]
                    c0 = (p % 2) * 256
                    self.mm(self.pb[bb][:, c0:c0 + 256], QKT[:, 3 + p, j * 128:(j + 1) * 128],
                            Qz[:, 2 * p:2 * p + 2, :], True, True, [('QKT', j)] + qzk, [('pb', bb)])

            def r_():
                bS = d_['bS']
                ps_ = self.pt_n % 3
                self.pt_n += 1
                PT = self.PT[ps_]
                self.act(PT[:, 0:512], self.pb[bS[0]][:, 0:512], AF.Exp, [('pb', bS[0])], [('PT', ps_)], scale=0.125)
                self.act(PT[:, 512:768], self.pb[bS[1]][:, 0:256], AF.Exp, [('pb', bS[1])], [('PT', ps_)], scale=0.125)
                self.tt('dve', PT[:, 0:768].rearrange("p (h q) -> p h q", q=128), PT[:, 0:768].rearrange("p (h q) -> p h q", q=128),
                        bc(self.MA[:, o, :], 1, [128, 6, 128]), ALU.mult, [('PT', ps_), 'MA'], [('PT', ps_)])
                for h in range(6):
                    self.mm(self.pb[6][:, h * 72:h * 72 + 65], PT[:, h * 128:(h + 1) * 128], VS4[:, j, h, 0:65],
                            (j == j0 and h == 0), j == i, [('PT', ps_), ('VS', j), 'VS_ones'], [('pb', 6)], skip=True)
            return s_, r_

        def a_fin(i):
            def r_():
                acc = self.pb[6][:, 0:432].rearrange("p (h e) -> p h e", e=72)
                self.S.add('dve', lambda e: e.reciprocal(out=self.rec[:, 0:6], in_=acc[:, :, 64]), reads=[('pb', 6)], writes=['rec'])
                self.tt('dve', self.ob[:, 0:384].rearrange("p (h d) -> p h d", d=64), acc[:, :, 0:64], bc(self.rec[:, 0:6], 2, [128, 6, 64]),
                        ALU.mult, [('pb', 6), 'rec'], ['ob'])
                self.out_tile(i, 'ob', self.ob, 384, 0)
            return r_
        for i in range(NT):
            j0 = max(0, i - 16)
            Qz, qzk = self.make_qz(i, 6)
            for j in range(j0, i + 1):
                s_, r_ = a_pair(i, j, j0, Qz, qzk)
                self.emit_step(s_, r_)
            self.emit_step(None, a_fin(i))
        self.flush_steps()
        self.dbg_oT()

    def dbg_oT(self):
        if 'dbg_oT' in [d[0] for d in self.dbg] and self.stop_after is not None:
            rng = [r for k, r in (('A', (0, 384)), ('B', (384, 640)), ('C', (640, 896))) if getattr(self, 'only', None) is None or k in self.only]
            for t in range(NT):
                for (a, b) in rng:
                    o = self.dma(self.dr['dbg_oT'][t][:, a:b], self.dr['oT'][t][:, a:b], [('oT', t, 0), ('oT', t, 384), ('oT', t, 640)], [])
                    self.final_ops.append(o)

    def pass_B(self, l):
        dr = self.dr
        if not hasattr(self, 'b_GS'):
            self.b_GS = self.sb('b_GS', [128, NT, 12], F32)
            self.b_posT = self.sb('b_posT', [64, 32], BF16)
            self.b_posTf = self.sb('b_posTf', [64, 32], F32)
            self.b_W2 = self.sb('b_W2', [128, 2, 64], BF16)
            self.b_W2f = self.sb('b_W2f', [128, 2, 64], F32)
            self.b_W2vf = self.b_W2f
            self.b_h1 = self.sb('b_h1', [128, 2, 256], BF16)
            self.b_hb = self.sb('b_hb', [128, 2], F32)
            self.b_kcT = self.sb('b_kcT', [64, 256], BF16)
            self.b_vc = self.sb('b_vc', [128, 2, 64], F32)
            self.b_Pc = self.sb('b_Pc', [128, 2, 512], F32)
            self.b_rden = self.qk_sq[:, 0:512]
            self.b_sc = self.sb('b_sc', [128, 64], F32)
            self.b_sc2 = self.sb('b_sc2', [128, 64], F32)
            self.b_m1 = self.sb('b_m1', [128, 8], F32)
            self.b_m2 = self.sb('b_m2', [128, 8], F32)
            self.b_selb = self.sb('b_selb', [128, 64], F32)
            self.b_selbT2 = [t_[0:64, :].rearrange('p (h q) -> p h q', q=128) for t_ in self.selT]
            self.b_f = self.sb('b_f', [128, 12], F32)
            self.b_obf = self.qk_xn[:, 0:256].rearrange('p (h d) -> p h d', d=64)
            self.b_tmp = self.qk_xn[:, 256:512].rearrange('p (h d) -> p h d', d=64)
        GS = self.b_GS
        self.wkeys = {}
        W = self.BIG[:, 37376:37376 + 8 * 652].rearrange("p (k c) -> p k c", c=652)
        self.load_w(W, 'Wp', dr['w_in'][l], 1536, 652, 0)
        wreads = list(self.wkeys['Wp'])
        self.load_gains(l, 'q_norm_b', 'k_norm_b', 4, 6)
        Esel = self.build_E('Esel', 64, 64, 50784 + 0)
        VSB = self.BIG[:, 32768:32768 + NT * 144].rearrange("p (t h e) -> p t h e", h=2, e=72)
        self.memset('pool', VSB[:, :, :, 64:65], 1.0, ['VS_ones'])
        QTB = self.BIG[0:64, 0:32768].rearrange("p (a c) -> p a c", c=S_LEN)
        W1 = [self.BIG[0:64, 42592 + i * 4096:42592 + (i + 1) * 4096].rearrange("p (q h) -> p q h", h=128) for i in range(2)]
        for wi, nm in enumerate(('cmp_k_w1', 'cmp_v_w1')):
            for qtr in range(4):
                sl = self.wst_n % 2
                self.wst_n += 1
                stg = self.wstage[sl][0:64]
                self.dma(stg, dr[nm][l][:, qtr * 8:(qtr + 1) * 8, :], [], [('wst', sl)])
                self.cp('pool', W1[wi][:, qtr * 8:(qtr + 1) * 8, :], stg, [('wst', sl)], [('W1', wi, qtr)])
        w1keys = [[('W1', wi, q) for q in range(4)] for wi in range(2)]
        self.dma(self.b_posTf[:], dr['cmp_pos'][l], [], ['b_posTf'])
        self.cp('dve', self.b_posT[:], self.b_posTf[:], ['b_posTf'], ['b_posT'])
        self.dma(self.b_W2f[:, 0, :], dr['cmp_k_w2'][l], [], ['b_W2f0'])
        self.dma(self.b_W2f[:, 1, :], dr['cmp_v_w2'][l], [], ['b_W2f1'])
        self.cp('dve', self.b_W2[:], self.b_W2f[:], ['b_W2f0', 'b_W2f1'], ['b_W2'])
        srcs = [0, 64, 128, 192, 256, 384, 512, 320]
        for t in range(NT):
            if t % 4 == 0:
                hsl = self.load_hg(t // 4)
            self.proj_tile(hsl, t % 4, W, wreads, [(0, 512), (512, 140)], [0, 1])
            self.cp('act', self.pr[:, 0:512], self.pb[0][:, 0:512], [('pb', 0)], ['pr'])
            self.cp('act', self.pr[:, 512:652], self.pb[1][:, 0:140], [('pb', 1)], ['pr'])
            self.qk_post(self.pr[:, 0:640], 10, self.Gt[:, 0:640], t, self.xb[:, 0:640], 'pr', 'xb')
            self.cp('dve', self.xb[:, 320:384], self.pr[:, 320:384], ['pr', 'xb_dve', 'xb_pool'], ['xb_dve', 'xb_pool'])
            self.cp('dve', VSB[:, t, 0, 0:64], self.pr[:, 448:512], ['pr', 'VS_ones'], [('VS', t)])
            self.cp('dve', VSB[:, t, 1, 0:64], self.pr[:, 576:640], ['pr', 'VS_ones'], [('VS', t)])
            self.cp('dve', GS[:, t, :], self.pr[:, 640:652], ['pr'], [('GS', t)])
            pT = self.pb[3][:].bitcast(BF16)
            for si, c0 in enumerate(srcs):
                self.tr(pT[0:64, si * 128:(si + 1) * 128], self.xb[:, c0:c0 + 64], self.identb[:], ['xb_dve', 'xb_pool', 'identb'], [('pb', 3)])
            self.cp('act', QTB[:, 0:8, t * 128:(t + 1) * 128], pT[0:64, 0:1024].rearrange("p (a c) -> p a c", c=128), [('pb', 3)], [('QKT', t)])
        allq = [('QKT', t) for t in range(NT)]
        self.act(GS[:].rearrange("p t g -> p (t g)"), GS[:].rearrange("p t g -> p (t g)"), AF.Sigmoid, [('GS', t) for t in range(NT)], ['GSs'])
        self.memset('dve', self.b_h1[:, :, 255:256], 0.0, ['b_h1z'])
        for wi, slot in ((0, 4), (1, 7)):
            for p in range(32):
                self.mm(self.pb[5][:, 0:255], W1[wi][:, p, :], QTB[:, slot, p:p + 16 * 254 + 1:16], p == 0, p == 31,
                        allq + w1keys[wi], [('pb', 5)])
            for p in range(32):
                self.mm(self.pb[4][:, 0:1], W1[wi][:, p, :], self.b_posT[:, p:p + 1], p == 0, p == 31, w1keys[wi] + ['b_posT'], [('pb', 4)])
            self.cp('dve', self.b_hb[:, wi:wi + 1], self.pb[4][:, 0:1], [('pb', 4)], [('b_hb', wi)])
            self.act(self.b_h1[:, wi, 0:255], self.pb[5][:, 0:255], AF.Silu, [('pb', 5), ('b_hb', wi), 'b_h1z'], [('b_h1', wi)],
                     bias=self.b_hb[:, wi:wi + 1])
        self.mm(self.pb[5][0:64, 0:256], self.b_W2[:, 0, :], self.b_h1[:, 0, :], True, True, ['b_W2', ('b_h1', 0), 'b_h1z'], [('pb', 5)])
        self.cp('dve', self.b_kcT[:, :], self.pb[5][0:64, 0:256], [('pb', 5)], ['b_kcT'])
        for ct in range(2):
            self.mm(self.pb[4][:, ct * 64:(ct + 1) * 64], self.b_h1[:, 1, ct * 128:(ct + 1) * 128], self.b_W2[:, 1, :], True, True,
                    ['b_W2', ('b_h1', 1), 'b_h1z'], [('pb', 4)])
        self.cp('dve', self.b_vc[:].rearrange("p c d -> p (c d)"), self.pb[4][:, 0:128], [('pb', 4)], ['b_vc'])
        Pc = self.b_Pc
        st = {'sb': 0}

        def qap(i):
            return QTB[:, 0:4, i * 128:(i + 1) * 128]

        def nbank():
            b = 2 + (st['sb'] % 2)
            st['sb'] += 1
            return b

        def sel_steps(i):
            qr = [('QKT', i)]
            nct = 2 if i >= 16 else 1
            ob = i % 2
            selbT = self.b_selbT2[i % 2]
            sk = ('b_selbT', i % 2)

            def s_a():
                for ct in range(nct):
                    b = 7
                    self.mm(self.pb[b][:, 0:512], self.b_kcT[:, ct * 128:(ct + 1) * 128], qap(i), True, True, qr + ['b_kcT'], [('pb', b)])
                    self.act(Pc[:, ct, :], self.pb[b][:, 0:512], AF.Exp, [('pb', b)], [('Pc', ct)], scale=0.125)
                    if ct == 1 or i < 17:
                        self.asel(Pc[:, ct, :].rearrange("p (h q) -> p h q", q=128), Pc[:, ct, :].rearrange("p (h q) -> p h q", q=128),
                                  [[0, 4], [1, 128]], ALU.is_ge, 0.0, 128 * i - 2048 * ct - 31, -16, [('Pc', ct)], [('Pc', ct)])

            def s_b():
                for ct in range(nct):
                    self.mm(self.pb[4][:, 0:512], self.onesf[:, :], Pc[:, ct, :], ct == 0, ct == nct - 1, [('Pc', ct), 'onesf'], [('pb', 4)])

            def s_c():
                self.tsc('dve', self.b_rden, self.pb[4][:, 0:512], 1e-30, None, ALU.add, None, [('pb', 4)], ['b_rden'])
                self.S.add('dve', lambda e: e.reciprocal(out=self.b_rden, in_=self.b_rden), reads=['b_rden'], writes=['b_rden'])
                for ct in range(nct):
                    self.tt('dve', Pc[:, ct, :], Pc[:, ct, :], self.b_rden, ALU.mult, [('Pc', ct), 'b_rden'], [('Pc', ct)])

            def s_d():
                first = True
                for ct in range(nct):
                    for h in range(4):
                        self.mm(self.pb[ob][:, 0:64], Pc[:, ct, h * 128:(h + 1) * 128], self.cover[:, ct, :], first, False,
                                [('Pc', ct), 'cover'], [('pb', ob)], skip=True)
                        first = False
                for ct in range(nct):
                    for h in range(4):
                        self.mm(self.pb[ob][:, 64 + h * 64:128 + h * 64], Pc[:, ct, h * 128:(h + 1) * 128], self.b_vc[:, ct, :], False,
                                (ct == nct - 1 and h == 3), [('Pc', ct), 'b_vc'], [('pb', ob)], skip=True)

            def s_e():
                self.tt('dve', self.b_sc[:], self.pb[ob][:, 0:64], self.AM[:, i, :], ALU.add, [('pb', ob), 'AM'], ['b_sc'])
                self.S.add('dve', lambda e: e.max(out=self.b_m1[:], in_=self.b_sc[:]), reads=['b_sc'], writes=['b_m1'])
                self.S.add('dve', lambda e: e.match_replace(out=self.b_sc2[:], in_to_replace=self.b_m1[:], in_values=self.b_sc[:],
                                                            imm_value=-3e38), reads=['b_sc', 'b_m1'], writes=['b_sc2'])
                self.S.add('dve', lambda e: e.max(out=self.b_m2[:], in_=self.b_sc2[:]), reads=['b_sc2'], writes=['b_m2'])
                self.tsc('dve', self.b_selb[:], self.b_sc[:], self.b_m2[:, 7:8], None, ALU.is_ge, None, ['b_sc', 'b_m2'], ['b_selb'])
                self.tsc('dve', self.b_selb[:], self.b_selb[:], 1.0, -NEGB, ALU.subtract, ALU.mult, ['b_selb'], ['b_selb'])

            def s_f():
                self.tr(self.pb[4][0:64, 0:128], self.b_selb[:, :], self.identf[:], ['b_selb', 'identf'], [('pb', 4)])
                self.cp('act', selbT[:], bc(self.pb[4][0:64, 0:128], 1, [64, 4, 128]), [('pb', 4)], [sk])
            return [s_a, s_b, s_c, s_d, s_e, s_f]

        def attn_steps(i):
            qr = [('QKT', i)]
            ob = i % 2
            selbT = self.b_selbT2[i % 2]
            sk = ('b_selbT', i % 2)
            steps = []

            def sel_pair(j):
                d_ = {}

                def s_():
                    b = nbank()
                    d_['b'] = b
                    bank = self.pb[b]
                    self.mm(bank[:, 0:512], Esel[:, j * 128:(j + 1) * 128], selbT[:].rearrange("p h q -> p (h q)"), True, False,
                            ['Esel', sk], [('pb', b)])
                    self.mm(bank[:, 0:512], QTB[:, 5, j * 128:(j + 1) * 128], qap(i), False, True, qr + [('QKT', j)], [('pb', b)])

                def f():
                    b = d_['b']
                    bank = self.pb[b]
                    ps_ = self.pt_n % 3
                    self.pt_n += 1
                    PT = self.PT[ps_]
                    self.act(PT[:, 0:512], bank[:, 0:512], AF.Exp, [('pb', b)], [('PT', ps_)], scale=0.125)
                    if j == i:
                        self.asel(PT[:, 0:512].rearrange("p (h q) -> p h q", q=128), PT[:, 0:512].rearrange("p (h q) -> p h q", q=128),
                                  [[0, 4], [1, 128]], ALU.is_ge, 0.0, 0, -1, [('PT', ps_)], [('PT', ps_)])
                    for h in range(4):
                        self.mm(self.pb[6][:, h * 72:h * 72 + 65], PT[:, h * 128:(h + 1) * 128], VSB[:, j, 0, 0:65],
                                (j == 0 and h == 0), j == i, [('PT', ps_), ('VS', j), 'VS_ones'], [('pb', 6)], skip=True)
                return (s_, f)
            j0 = max(0, i - 4)

            def win_pair(j):
                d_ = {}

                def s_():
                    b = nbank()
                    d_['b'] = b
                    bank = self.pb[b]
                    self.mm(bank[:, 0:512], QTB[:, 6, j * 128:(j + 1) * 128], qap(i), True, True, qr + [('QKT', j)], [('pb', b)])

                def f():
                    b = d_['b']
                    bank = self.pb[b]
                    ps_ = self.pt_n % 3
                    self.pt_n += 1
                    PT = self.PT[ps_]
                    self.act(PT[:, 0:512], bank[:, 0:512], AF.Exp, [('pb', b)], [('PT', ps_)], scale=0.125)
                    PT3 = PT[:, 0:512].rearrange("p (h q) -> p h q", q=128)
                    if j == i:
                        self.asel(PT3, PT3, [[0, 4], [1, 128]], ALU.is_ge, 0.0, 0, -1, [('PT', ps_)], [('PT', ps_)])
                    if j == i - 4:
                        self.asel(PT3, PT3, [[0, 4], [-1, 128]], ALU.is_ge, 0.0, -1, 1, [('PT', ps_)], [('PT', ps_)])
                    for h in range(4):
                        self.mm(self.pb[5][:, h * 72:h * 72 + 65], PT[:, h * 128:(h + 1) * 128], VSB[:, j, 1, 0:65],
                                (j == j0 and h == 0), j == i, [('PT', ps_), ('VS', j), 'VS_ones'], [('pb', 5)], skip=True)
                return (s_, f)
            for j in range(i + 1):
                steps.append(sel_pair(j))
            for j in range(j0, i + 1):
                steps.append(win_pair(j))

            def combine():
                accs = self.pb[6][:, 0:288].rearrange("p (h e) -> p h e", e=72)
                accw = self.pb[5][:, 0:288].rearrange("p (h e) -> p h e", e=72)
                ocmp = self.pb[ob][:, 64:320].rearrange("p (h d) -> p h d", d=64)
                f = self.b_f
                self.S.add('dve', lambda e: e.reciprocal(out=f[:, 4:8], in_=accs[:, :, 64]), reads=[('pb', 6)], writes=['b_f'])
                self.S.add('dve', lambda e: e.reciprocal(out=f[:, 8:12], in_=accw[:, :, 64]), reads=[('pb', 5)], writes=['b_f'])
                self.tt('dve', f[:, 4:12], f[:, 4:12], GS[:, i, 4:12], ALU.mult, ['b_f', 'GSs'], ['b_f'])
                self.tt('dve', self.b_obf, ocmp, bc(GS[:, i, 0:4], 2, [128, 4, 64]), ALU.mult, [('pb', ob), 'GSs'], ['b_obf'])
                self.tt('dve', self.b_tmp, accs[:, :, 0:64], bc(f[:, 4:8], 2, [128, 4, 64]), ALU.mult, [('pb', 6), 'b_f'], ['b_tmp'])
                self.tt('dve', self.b_obf, self.b_obf, self.b_tmp, ALU.add, ['b_obf', 'b_tmp'], ['b_obf'])
                self.tt('dve', self.b_tmp, accw[:, :, 0:64], bc(f[:, 8:12], 2, [128, 4, 64]), ALU.mult, [('pb', 5), 'b_f'], ['b_tmp'])
                self.tt('dve', self.ob[:, 0:256].rearrange("p (h d) -> p h d", d=64), self.b_obf, self.b_tmp, ALU.add,
                        ['b_obf', 'b_tmp'], ['ob'])
                self.out_tile(i, 'ob', self.ob, 256, 384)
            steps.append((None, combine))
            return steps

        for f_ in sel_steps(0):
            f_()
        for i in range(NT):
            pend = sel_steps(i + 1) if i + 1 < NT else []
            asteps = attn_steps(i)
            gap = max(1, (len(asteps) - 1) // (len(pend) + 1)) if pend else 1
            for n_, a in enumerate(asteps):
                self.emit_step(a[0], a[1])
                if pend and (n_ % gap == gap - 1) and n_ < len(asteps) - 1:
                    pend.pop(0)()
            while pend:
                pend.pop(0)()
        self.flush_steps()
        self.dbg_oT()

    def pass_C(self, l):
        dr = self.dr
        if not hasattr(self, 'c_ksum'):
            self.c_ksum = self.sb('c_ksum', [128, 2, 16], F32)
            self.c_khi = self.sb('c_khi', [128, 2, 16], BF16)
            self.c_klo = self.sb('c_klo', [128, 2, 16], BF16)
            self.c_tmp = self.sb('c_tmp', [128, 2, 16], F32)
            self.c_sc = self.sb('c_sc', [128, 4, 16], F32)
            self.c_mx = self.sb('c_mx', [128, 4, 8], F32)
            self.c_selb = self.sb('c_selb', [128, 4, 16], F32)
            self.c_selbT2 = [t_[0:16, :] for t_ in self.selT]
        self.wkeys = {}
        W = self.Wp
        self.load_w(W, 'Wp', dr['w_in'][l], 2444, 768, 0)
        wreads = list(self.wkeys['Wp'])
        self.load_gains(l, 'q_norm_c', 'k_norm_c', 4, 4)
        self.Ec = self.build_E('Ec', 256, 16, 16384)
        VS4 = self.VS.rearrange("p t (h e) -> p t h e", e=72)
        self.memset('pool', VS4[:, :, 0:4, 64:65], 1.0, ['VS_ones'])
        QKT = self.QKT
        for t in range(NT):
            if t % 4 == 0:
                hsl = self.load_hg(t // 4)
            self.proj_tile(hsl, t % 4, W, wreads, [(0, 512), (512, 256)], [0, 1])
            self.cp('act', self.pr[:, 0:512], self.pb[0][:, 0:512], [('pb', 0)], ['pr'])
            self.cp('act', VS4[:, t, 0:4, 0:64], self.pb[1][:, 0:256].rearrange("p (h d) -> p h d", d=64), [('pb', 1), 'VS_ones'], [('VS', t)])
            self.qk_post(self.pr[:, 0:512], 8, self.Gt[:, 0:512], t, self.xb[:, 0:512], 'pr', 'xb')
            pT = self.pb[3][:].bitcast(BF16)
            for p in range(4):
                self.tr(pT[:, p * 128:(p + 1) * 128], self.xb[:, p * 128:(p + 1) * 128], self.identb[:], ['xb_dve', 'xb_pool', 'identb'], [('pb', 3)])
            self.cp('act', QKT[:, 0:4, t * 128:(t + 1) * 128], pT[:, 0:512].rearrange("p (a c) -> p a c", c=128), [('pb', 3)], [('QKT', t)])
        allq = [('QKT', t) for t in range(NT)]
        ksum, khi, klo, ktmp = self.c_ksum, self.c_khi, self.c_klo, self.c_tmp
        self.S.add('dve', lambda e: e.tensor_reduce(out=ksum[:], in_=QKT[:, 2:4, :].rearrange("p a (n k) -> p a n k", k=256),
                                                    axis=AX.X, op=ALU.add), reads=allq, writes=['c_ksum'])
        self.cp('dve', khi[:], ksum[:], ['c_ksum'], ['c_khi'])
        self.cp('dve', ktmp[:], khi[:], ['c_khi'], ['c_tmp'])
        self.tt('dve', ktmp[:], ksum[:], ktmp[:], ALU.subtract, ['c_ksum', 'c_tmp'], ['c_tmp'])
        self.cp('dve', klo[:], ktmp[:], ['c_tmp'], ['c_klo'])
        st = {'sb': 0}

        def sel_steps(i):
            nb = i // 2
            if nb == 0:
                return []
            sl = i % 2
            Qz, qzk = self.c_qz[i]
            sc = self.c_sc
            selbT = self.c_selbT2[sl]
            sk = ('c_selbT', sl)

            def s_a():
                for h in range(4):
                    self.mm(self.pb[5][:, h * 16:(h + 1) * 16], Qz[:, h, :], khi[:, h // 2, :], True, False, qzk + ['c_khi'], [('pb', 5)])
                    self.mm(self.pb[5][:, h * 16:(h + 1) * 16], Qz[:, h, :], klo[:, h // 2, :], False, True, qzk + ['c_klo'], [('pb', 5)])

            def s_b():
                self.cp('dve', sc[:].rearrange("p h n -> p (h n)"), self.pb[5][:, 0:64], [('pb', 5)], ['c_sc'])
                if nb < 16:
                    self.memset('dve', sc[:, :, nb:16], -1e30, ['c_sc'])
                for h in range(4):
                    self.S.add('dve', (lambda h: lambda e: e.max(out=self.c_mx[:, h, :], in_=sc[:, h, :]))(h), reads=['c_sc'], writes=['c_mx'])
                self.tt('dve', self.c_selb[:], sc[:], bc(self.c_mx[:, :, 2], 2, [128, 4, 16]), ALU.is_ge, ['c_sc', 'c_mx'], ['c_selb'])
                self.tsc('dve', self.c_selb[:], self.c_selb[:], 1.0, -NEGB, ALU.subtract, ALU.mult, ['c_selb'], ['c_selb'])
                if nb < 16:
                    self.memset('dve', self.c_selb[:, :, nb:16], NEGB, ['c_selb'])

            def s_c():
                for h in range(4):
                    self.tr(self.pb[4][0:16, h * 128:(h + 1) * 128], self.c_selb[:, h, :], self.identf[:], ['c_selb', 'identf'], [('pb', 4)])
                self.cp('act', selbT[:, :], self.pb[4][0:16, 0:512], [('pb', 4)], [sk])
            return [s_a, s_b, s_c]

        def attn_steps(i):
            nb = i // 2
            Qz, qzk = self.c_qz[i]
            selbT = self.c_selbT2[i % 2]
            sk = ('c_selbT', i % 2)
            steps = []

            def pair(j):
                d_ = {}
                past = j < 2 * nb

                def s_():
                    b = 2 + (st['sb'] % 2)
                    st['sb'] += 1
                    d_['b'] = b
                    bank = self.pb[b]
                    if past:
                        self.mm(bank[:, 0:512], self.Ec[:, j * 128:(j + 1) * 128], selbT[0:16, 0:512], True, False,
                                ['Ec', sk], [('pb', b)], skip=True)
                    for p in range(2):
                        self.mm(bank[:, p * 256:(p + 1) * 256], QKT[:, 2 + p, j * 128:(j + 1) * 128], Qz[:, 2 * p:2 * p + 2, :],
                                (not past) and p == 0, p == 1, [('QKT', j)] + qzk, [('pb', b)], skip=True)

                def r_():
                    b = d_['b']
                    bank = self.pb[b]
                    ps_ = self.pt_n % 3
                    self.pt_n += 1
                    PT = self.PT[ps_]
                    self.act(PT[:, 0:512], bank[:, 0:512], AF.Exp, [('pb', b)], [('PT', ps_)], scale=0.125)
                    if j == i:
                        self.asel(PT[:, 0:512].rearrange("p (h q) -> p h q", q=128), PT[:, 0:512].rearrange("p (h q) -> p h q", q=128),
                                  [[0, 4], [1, 128]], ALU.is_ge, 0.0, 0, -1, [('PT', ps_)], [('PT', ps_)])
                    for h in range(4):
                        self.mm(self.pb[6][:, h * 72:h * 72 + 65], PT[:, h * 128:(h + 1) * 128], VS4[:, j, h, 0:65],
                                (j == 0 and h == 0), j == i, [('PT', ps_), ('VS', j), 'VS_ones'], [('pb', 6)], skip=True)
                return (s_, r_)
            for j in range(i + 1):
                steps.append(pair(j))

            def fin():
                acc = self.pb[6][:, 0:288].rearrange("p (h e) -> p h e", e=72)
                self.S.add('dve', lambda e: e.reciprocal(out=self.rec[:, 0:4], in_=acc[:, :, 64]), reads=[('pb', 6)], writes=['rec'])
                self.tt('dve', self.ob[:, 0:256].rearrange("p (h d) -> p h d", d=64), acc[:, :, 0:64], bc(self.rec[:, 0:4], 2, [128, 4, 64]),
                        ALU.mult, [('pb', 6), 'rec'], ['ob'])
                self.out_tile(i, 'ob', self.ob, 256, 640)
            steps.append((None, fin))
            return steps

        self.c_qz = {}
        self.c_qz[0] = self.make_qz(0, 4)
        for i in range(NT):
            if i + 1 < NT:
                self.c_qz[i + 1] = self.make_qz(i + 1, 4)
            pend = sel_steps(i + 1) if i + 1 < NT else []
            asteps = attn_steps(i)
            gap = max(1, (len(asteps) - 1) // (len(pend) + 1)) if pend else 1
            for n_, a in enumerate(asteps):
                self.emit_step(a[0], a[1])
                if pend and (n_ % gap == gap - 1) and n_ < len(asteps) - 1:
                    pend.pop(0)()
            while pend:
                pend.pop(0)()
        self.flush_steps()
        self.dbg_oT()

    def final(self, l, xin, xout):
        dr = self.dr
        if not hasattr(self, 'f_yacc'):
            self.f_yacc = self.b_Pc[:, 0, :]
            self.f_ytmp = self.b_Pc[:, 1, :]
            self.f_yT = self.s1_all[:, 0:4096].rearrange("p (k c) -> p k c", c=512)
        Wzg = self.BIG[:, 0:31744].rearrange("p (k c) -> p k c", c=3968)
        Wbr = self.BIG[:, 31744:38912].rearrange("p (k c) -> p k c", c=1024)
        Wout = self.BIG[:, 38912:47104].rearrange("p (k c) -> p k c", c=1024)
        oz = self.BIG[:, 47104:47104 + 3584].rearrange("p (t k c) -> p t k c", k=7, c=128)
        og = self.BIG[:, 50688:50688 + 3584].rearrange("p (t c) -> p t c", c=896)
        sz = [self.qk_sq[:, 0:512], self.qk_xn[:, 0:512]]
        sg = [self.pr[:, 0:512], self.Gt[:, 0:512]]
        self.wkeys = {}
        for (c0, n, o) in ((1152, 384, 0), (2188, 256, 384), (3212, 256, 640), (3468, 3072, 896)):
            self.load_w(Wzg, 'Wzg', dr['w_in'][l], c0, n, o)
        self.load_w(Wbr, 'Wbr', dr['w_br_a'][l], 0, 1024, 0, nk=3, k0=0)
        self.load_w(Wbr, 'Wbr', dr['w_br_b'][l], 0, 1024, 0, nk=2, k0=3)
        self.load_w(Wbr, 'Wbr', dr['w_br_c'][l], 0, 1024, 0, nk=2, k0=5)
        self.load_w(Wout, 'Wout', dr['w_out'][l], 0, 1024, 0)
        kz, kb, ko = list(self.wkeys['Wzg']), list(self.wkeys['Wbr']), list(self.wkeys['Wout'])
        last = (xout is dr['y'])
        n = 0
        for g in range(NT // 4):
            hsl = self.load_hg(g)
            hgt = self.hg[hsl]
            self.dma(og, dr['oT'][g * 4:(g + 1) * 4].rearrange("t p c -> p t c"),
                     [('oT', g * 4 + i, c) for i in range(4) for c in (0, 384, 640)], ['og'])
            for zc in range(7):
                b = zc % 2
                for k in range(8):
                    self.mm(self.pb[b][:, 0:512], Wzg[:, k, zc * 128:(zc + 1) * 128], hgt[:, :, k, :], k == 0, k == 7,
                            [('hg', hsl)] + kz, [('pb', b)])
                self.act(sz[b], self.pb[b][:, 0:512], AF.Silu, [('pb', b)], [('sz', b)])
                self.tt('dve', oz[:, :, zc, :], og[:, :, zc * 128:(zc + 1) * 128], sz[b].rearrange("p (t c) -> p t c", c=128), ALU.mult,
                        [('sz', b), 'og'], [('oz', zc)])
            for m in range(8):
                for br in range(3):
                    gb = 2 + n % 2
                    ub = 4 + n % 2
                    sgb = sg[n % 2]
                    sgk = ('sg', n % 2)
                    n += 1
                    c0 = 896 + br * 1024 + m * 128
                    for k in range(8):
                        self.mm(self.pb[gb][:, 0:512], Wzg[:, k, c0:c0 + 128], hgt[:, :, k, :], k == 0, k == 7,
                                [('hg', hsl)] + kz, [('pb', gb)])
                    self.act(sgb, self.pb[gb][:, 0:512], AF.Sigmoid, [('pb', gb)], [sgk])
                    kcs = ([0, 1, 2], [3, 4], [5, 6])[br]
                    for ii, kc in enumerate(kcs):
                        self.mm(self.pb[ub][:, 0:512], Wbr[:, kc, m * 128:(m + 1) * 128], oz[:, :, kc, :], ii == 0, ii == len(kcs) - 1,
                                [('oz', kc)] + kb, [('pb', ub)])
                    if br == 0:
                        self.tt('dve', self.f_yacc, self.pb[ub][:, 0:512], sgb, ALU.mult, [('pb', ub), sgk], ['f_yacc'])
                    else:
                        self.tt('dve', self.f_ytmp, self.pb[ub][:, 0:512], sgb, ALU.mult, [('pb', ub), sgk], ['f_ytmp'])
                        if br == 1:
                            self.tt('dve', self.f_yacc, self.f_yacc, self.f_ytmp, ALU.add, ['f_yacc', 'f_ytmp'], ['f_yacc'])
                        else:
                            self.tt('dve', self.f_yT[:, m, :], self.f_yacc, self.f_ytmp, ALU.add, ['f_yacc', 'f_ytmp'], [('yT', m)])
            for tt_ in range(4):
                t = g * 4 + tt_
                xsl = self.xt_n % 2
                self.xt_n += 1
                xt = self.xt[xsl]
                self.dma(xt[:], xin[t * 128:(t + 1) * 128, :], [('x1', t)], [('xt', xsl)])
                osl = xsl
                orow = xt
                for nch in range(2):
                    b = 6 + nch
                    for k in range(8):
                        self.mm(self.pb[b][:, 0:512], self.f_yT[:, k, tt_ * 128:(tt_ + 1) * 128], Wout[:, k, nch * 512:(nch + 1) * 512],
                                k == 0, k == 7, [('yT', k)] + ko, [('pb', b)])
                    self.tt('dve', orow[:, nch * 512:(nch + 1) * 512], self.pb[b][:, 0:512], xt[:, nch * 512:(nch + 1) * 512], ALU.add,
                            [('pb', b), ('xt', xsl)], [('xt', xsl)])
                o = self.dma(xout[t * 128:(t + 1) * 128, :], orow[:], [('xt', xsl)], [('y', t) if last else ('x1', t)])
                if last:
                    self.final_ops.append(o)


def host_layout(sh):
    sh = dict(sh)
    sh['norm_g'] = np.ascontiguousarray(sh['norm_g'].reshape(2, 8, 128).transpose(0, 2, 1))
    sh['cmp_pos'] = np.ascontiguousarray(sh['cmp_pos'].transpose(0, 2, 1))
    for nm in ('cmp_k_w1', 'cmp_v_w1'):
        sh[nm] = np.ascontiguousarray(sh[nm].reshape(2, 32, 64, 128).transpose(0, 2, 1, 3))
    return sh


_CACHE = {}


def kernel(**inputs):
    n = 8
    if 'nc' not in _CACHE:
        _CACHE['nc'] = Builder(2).build()
    nc = _CACHE['nc']
    x = np.ascontiguousarray(inputs['x'], dtype=np.float32)
    pos = np.ascontiguousarray(inputs['positions']).astype(np.int32)
    shared = {}
    for k in ('norm_g', 'w_in', 'q_norm_a', 'k_norm_a', 'q_norm_b', 'k_norm_b', 'q_norm_c', 'k_norm_c', 'cmp_pos',
              'cmp_k_w1', 'cmp_k_w2', 'cmp_v_w1', 'cmp_v_w2', 'w_br_a', 'w_br_b', 'w_br_c', 'w_out'):
        shared[k] = np.ascontiguousarray(inputs[k], dtype=np.float32)
    shared = host_layout(shared)
    in_maps = []
    for c in range(n):
        m = dict(shared)
        m['x'] = x[c]
        m['pos'] = np.ascontiguousarray(pos[c].reshape(NT, 128).T)
        in_maps.append(m)
    res = run_bass_kernel_spmd(nc, in_maps, core_ids=list(range(n)))
    return np.stack([np.asarray(r['y'], dtype=np.float32) for r in res.results], axis=0)
```

```python
import contextlib
import math
import numpy as np
import concourse.bass as bass
import concourse.mybir as mybir
from concourse.bass_utils import run_bass_kernel_spmd

F32 = mybir.dt.float32
BF16 = mybir.dt.bfloat16
I32 = mybir.dt.int32
AF = mybir.ActivationFunctionType
ALU = mybir.AluOpType
AX = mybir.AxisListType

SAME_ENG_SYNC = {'pe': False, 'act': True, 'dve': True, 'pool': True, 'sp': True}
N_DMA_SEMS = 8

S_LEN = 4096
D = 1024
NT = 32
INW = 6540
EPS = 1e-6
NEGB = -30000.0


class _Op:
    __slots__ = ('eng', 'fn', 'deps', 'dma', 'signal', 'sem', 'val', 'idx', 'prev')


class Sched:
    def __init__(self, nc):
        self.nc = nc
        self.ops = []
        self.last_w = {}
        self.readers = {}
        self.fence_keys = []

    def add(self, eng, fn, reads=(), writes=(), dma=False):
        op = _Op()
        op.eng = eng
        op.fn = fn
        op.dma = dma
        op.signal = False
        op.sem = None
        op.val = 0
        op.idx = len(self.ops)
        deps = set()
        if self.fence_keys:
            reads = list(reads) + self.fence_keys
        for k in reads:
            w = self.last_w.get(k)
            if w is not None:
                deps.add(w)
        for k in writes:
            w = self.last_w.get(k)
            if w is not None:
                deps.add(w)
            for r in self.readers.get(k, ()):
                deps.add(r)
        op.deps = deps
        for k in reads:
            self.readers.setdefault(k, []).append(op.idx)
        for k in writes:
            self.last_w[k] = op.idx
            self.readers[k] = []
        self.ops.append(op)
        return op.idx

    def emit(self, final_waits=()):
        nc = self.nc
        ops = self.ops
        engs = ['pe', 'act', 'dve', 'pool', 'sp']
        for op in ops:
            need = set()
            best = {}
            for d in op.deps:
                Dp = ops[d]
                if Dp.eng == op.eng and not Dp.dma and not op.dma and not SAME_ENG_SYNC[op.eng]:
                    continue
                if Dp.dma:
                    need.add(d)
                    Dp.signal = True
                elif best.get(Dp.eng, -1) < d:
                    best[Dp.eng] = d
            for d in best.values():
                need.add(d)
                ops[d].signal = True
            op.deps = need
        for d in final_waits:
            ops[d].signal = True
        with contextlib.ExitStack() as st:
            esem = {e: st.enter_context(nc.semaphore('s_' + e)) for e in engs}
            dsem = {e: [st.enter_context(nc.semaphore('d_%s%d' % (e, i))) for i in range(N_DMA_SEMS)]
                    for e in ('sp', 'pool', 'act')}
            ecount = {e: 0 for e in engs}
            dcount = {e: 0 for e in engs}
            for op in ops:
                if op.dma:
                    op.signal = True
                if not op.signal:
                    continue
                if op.dma:
                    i = dcount[op.eng]
                    dcount[op.eng] += 1
                    op.sem = dsem[op.eng][i % N_DMA_SEMS]
                    op.val = 16 * (i // N_DMA_SEMS + 1)
                    op.prev = (op.sem, op.val - 16) if i >= N_DMA_SEMS else None
                else:
                    ecount[op.eng] += 1
                    op.sem = esem[op.eng]
                    op.val = ecount[op.eng]
            per = {e: [] for e in engs}
            for op in ops:
                per[op.eng].append(op)
            block = st.enter_context(nc.Block())

            def run(e, name, extra=()):
                waited = {}
                for op in per[name]:
                    ws = {}
                    for d in op.deps:
                        Dp = ops[d]
                        key = id(Dp.sem)
                        if waited.get(key, 0) >= Dp.val:
                            continue
                        if key not in ws or ws[key][1] < Dp.val:
                            ws[key] = (Dp.sem, Dp.val)
                    if op.dma and op.prev is not None:
                        key = id(op.prev[0])
                        if waited.get(key, 0) < op.prev[1] and (key not in ws or ws[key][1] < op.prev[1]):
                            ws[key] = op.prev
                    for key, (sem, val) in ws.items():
                        e.wait_ge(sem, val)
                        waited[key] = val
                    ins = op.fn(e)
                    if op.signal:
                        ins.then_inc(op.sem, 16 if op.dma else 1)
                for d in extra:
                    Dp = ops[d]
                    e.wait_ge(Dp.sem, Dp.val)

            @block.tensor
            def _(e):
                run(e, 'pe')

            @block.scalar
            def _(e):
                run(e, 'act')

            @block.vector
            def _(e):
                run(e, 'dve')

            @block.gpsimd
            def _(e):
                run(e, 'pool')

            @block.sync
            def _(e):
                run(e, 'sp', extra=final_waits)
        self.stats = {e: len(per[e]) for e in engs}
        self.stats['signals'] = dict(ecount)
        self.stats['dmasig'] = dict(dcount)


def bc(ap, axis, shape):
    return ap.unsqueeze(axis).to_broadcast(shape)


class Builder:
    def __init__(self, n_layers=2, dbg=None, stop_after=None):
        self.n_layers = n_layers
        self.dbg = dbg or ()
        self.stop_after = stop_after
        nc = bass.Bass("TRN2", target_bir_lowering=False)
        self.nc = nc
        self.S = Sched(nc)
        self.st = contextlib.ExitStack()
        self.uid = 0

    def sb(self, name, shape, dt):
        return self.st.enter_context(self.nc.sbuf_tensor(name, shape, dt))

    def ps(self, name, shape, dt=F32):
        return self.st.enter_context(self.nc.psum_tensor(name, shape, dt))

    def mm(self, out, lhsT, rhs, start, stop, r, w, skip=False):
        if skip:
            self.S.add('pe', lambda e: e.matmul(out, lhsT=lhsT, rhs=rhs, start=start, stop=stop, skip_group_check=True), reads=r, writes=w)
        else:
            self.S.add('pe', lambda e: e.matmul(out, lhsT=lhsT, rhs=rhs, start=start, stop=stop), reads=r, writes=w)

    def tr(self, out, in_, ident, r, w):
        self.S.add('pe', lambda e: e.transpose(out=out, in_=in_, identity=ident), reads=r, writes=w)

    def act(self, out, in_, func, r, w, bias=None, scale=None, accum_out=None):
        kw = {}
        if bias is not None:
            kw['bias'] = bias
        if scale is not None:
            kw['scale'] = scale
        if accum_out is not None:
            kw['accum_out'] = accum_out
        self.S.add('act', lambda e: e.activation(out=out, in_=in_, func=func, **kw), reads=r, writes=w)

    def rsqrt(self, out, in_, scale, r, w):
        self.act(out, in_, AF.Ln, r, w, bias=self.epsc[:out.shape[0], 0:1], scale=scale)
        self.act(out, out, AF.Exp, w, w, scale=-0.5)

    def tt(self, eng, out, in0, in1, op, r, w):
        self.S.add(eng, lambda e: e.tensor_tensor(out=out, in0=in0, in1=in1, op=op), reads=r, writes=w)

    def tsc(self, eng, out, in0, s1, s2, op0, op1, r, w):
        if op1 is None:
            self.S.add(eng, lambda e: e.tensor_scalar(out=out, in0=in0, scalar1=s1, scalar2=None, op0=op0), reads=r, writes=w)
        else:
            self.S.add(eng, lambda e: e.tensor_scalar(out=out, in0=in0, scalar1=s1, scalar2=s2, op0=op0, op1=op1), reads=r, writes=w)

    def cp(self, eng, out, in_, r, w):
        if eng == 'act':
            self.S.add('act', lambda e: e.copy(out=out, in_=in_), reads=r, writes=w)
        else:
            self.S.add(eng, lambda e: e.tensor_copy(out=out, in_=in_), reads=r, writes=w)

    def memset(self, eng, ap, val, w):
        self.S.add(eng, lambda e: e.memset(ap, val), writes=w)

    def asel(self, out, in_, pattern, op, fill, base, cm, r, w):
        self.S.add('pool', lambda e: e.affine_select(out=out, in_=in_, pattern=pattern, compare_op=op, fill=fill,
                                                     base=base, channel_multiplier=cm), reads=r, writes=w)

    def dma(self, out, in_, r, w, q='sp'):
        return self.S.add(q, lambda e: e.dma_start(out=out, in_=in_), reads=r, writes=w, dma=True)

    def build_E(self, nm, blk, npart, off):
        E = self.BIG[0:npart, off:off + S_LEN]
        self.memset('pool', E, 1.0, [nm])
        self.asel(E, E, [[1, S_LEN]], ALU.is_ge, 0.0, 0, -blk, [nm], [nm])
        self.asel(E, E, [[-1, S_LEN]], ALU.is_ge, 0.0, blk - 1, blk, [nm], [nm])
        return E

    def make_qz(self, i, nh):
        sl = self.qz_n % 2
        self.qz_n += 1
        Qz = self.Qz[sl]
        npair = nh // 2
        for par in range(2):
            self.cp('pool', Qz[par * 64:(par + 1) * 64, par:nh:2, :], self.QKT[par * 64:(par + 1) * 64, 0:npair, i * 128:(i + 1) * 128],
                    [('QKT', i), 'Qz0'], [('Qz', sl, par)])
        return Qz, [('Qz', sl, 0), ('Qz', sl, 1)]

    def emit_step(self, s_fn, r_fn, L=1):
        if s_fn is not None:
            s_fn()
        if not hasattr(self, '_pend'):
            self._pend = []
        self._pend.append(r_fn)
        while len(self._pend) > L:
            self._pend.pop(0)()

    def flush_steps(self):
        for p in getattr(self, '_pend', []):
            p()
        self._pend = []

    def fence(self):
        self.S.fence_keys = []
        self.fence_n = getattr(self, 'fence_n', 0) + 1
        n = self.fence_n
        fs = self.fsc
        self.mm(self.pb[7][0:1, 0:1], self.identb[0:1, 0:1], self.identb[0:1, 0:1], True, True, ['identb'], [('pb', 7), ('fence', 'pe', n)])
        self.cp('act', fs[0:1, 0:1], fs[0:1, 4:5], ['fsc'], [('fence', 'act', n), 'fscw_act'])
        self.cp('dve', fs[0:1, 1:2], fs[0:1, 5:6], ['fsc'], [('fence', 'dve', n), 'fscw_dve'])
        self.cp('pool', fs[0:1, 2:3], fs[0:1, 6:7], ['fsc'], [('fence', 'pool', n), 'fscw_pool'])
        self.S.fence_keys = [('fence', e, n) for e in ('pe', 'act', 'dve', 'pool')]

    def build(self):
        nc = self.nc
        dr = {}

        def din(name, shape, dt=F32):
            dr[name] = nc.dram_tensor(name, shape, dt, kind="ExternalInput").ap()

        din('x', [S_LEN, D])
        din('pos', [128, NT], I32)
        din('norm_g', [2, 128, 8])
        din('w_in', [2, D, INW])
        for nm in ('q_norm_a', 'k_norm_a', 'q_norm_b', 'k_norm_b', 'q_norm_c', 'k_norm_c'):
            din(nm, [2, 64])
        din('cmp_pos', [2, 64, 32])
        din('cmp_k_w1', [2, 64, 32, 128])
        din('cmp_k_w2', [2, 128, 64])
        din('cmp_v_w1', [2, 64, 32, 128])
        din('cmp_v_w2', [2, 128, 64])
        din('w_br_a', [2, 384, D])
        din('w_br_b', [2, 256, D])
        din('w_br_c', [2, 256, D])
        din('w_out', [2, D, D])
        dr['y'] = nc.dram_tensor('y', [S_LEN, D], F32, kind="ExternalOutput").ap()
        dr['x1'] = nc.dram_tensor('x1s', [S_LEN, D], F32).ap()
        dr['hnT'] = nc.dram_tensor('hnTs', [NT, 128, 1024], BF16).ap()
        dr['oT'] = nc.dram_tensor('oTs', [NT, 128, 7 * 128], BF16).ap()
        for nm, shape, dt in self.dbg:
            dr[nm] = nc.dram_tensor(nm, shape, dt, kind="ExternalOutput").ap()
        self.dr = dr
        self.final_ops = []
        with self.st:
            self.setup()
            for l in range(self.n_layers):
                xin = dr['x'] if l == 0 else dr['x1']
                xout = dr['y'] if l == self.n_layers - 1 else dr['x1']
                self.layer(l, xin, xout)
            self.S.emit(final_waits=self.final_ops)
        return nc

    def setup(self):
        nc = self.nc
        dr = self.dr
        self.alloc_common()
        self.identf = self.sb('identf', [128, 128], F32)
        self.identb = self.sb('identb', [128, 128], BF16)
        self.onesf = self.sb('onesf', [128, 128], F32)
        self.memset('pool', self.identf[:], 1.0, ['identf'])
        self.asel(self.identf[:], self.identf[:], [[-1, 128]], ALU.is_equal, 0.0, 0, 1, ['identf'], ['identf'])
        self.cp('dve', self.identb[:], self.identf[:], ['identf'], ['identb'])
        self.memset('pool', self.onesf[:], 1.0, ['onesf'])
        self.epsc = self.sb('epsc', [128, 1], F32)
        self.memset('dve', self.epsc[:], EPS, ['epsc'])
        for q_ in self.Qz:
            self.memset('pool', q_[:], 0.0, ['Qz0'])
        self.fsc = self.sb('fsc', [128, 8], F32)
        self.memset('dve', self.fsc[:], 0.0, ['fsc'])
        posi = self.sb('posi', [128, NT], I32)
        posf = self.sb('posf', [128, NT], F32)
        fr = self.sb('fr', [128, 8], F32)
        wpF = self.BIG[:, 46592:46592 + 9216].bitcast(F32)
        wpI = self.BIG[:, 46592:46592 + 9216].bitcast(I32)
        ang = wpF[:, 0:256].rearrange("p (t f) -> p t f", f=8)
        tmpa = wpF[:, 256:512].rearrange("p (t f) -> p t f", f=8)
        self.cos = self.sb('cos', [128, NT, 8], F32)
        self.sin = self.sb('sin', [128, NT, 8], F32)
        self.dma(posi[:], dr['pos'][:, :], [], ['posi'])
        self.cp('dve', posf[:], posi[:], ['posi'], ['posf'])
        for i in range(8):
            f = float(np.float32(500000.0) ** np.float32(-i / 8.0))
            self.memset('dve', fr[:, i:i + 1], f, ['fr'])
        self.tt('dve', ang[:], bc(posf[:], 2, [128, NT, 8]), bc(fr[:], 1, [128, NT, 8]), ALU.mult, ['posf', 'fr'], ['ang'])
        PI = math.pi
        HI = 6.28125
        LO = 2 * PI - 6.28125
        ni = wpI[:, 512:768].rearrange("p (t f) -> p t f", f=8)
        nf = wpF[:, 768:1024].rearrange("p (t f) -> p t f", f=8)
        rr = wpF[:, 1024:1280].rearrange("p (t f) -> p t f", f=8)
        self.tsc('dve', tmpa[:], ang[:], 1.0 / (2 * PI), None, ALU.mult, None, ['ang'], ['tmpa'])
        self.cp('dve', ni[:], tmpa[:], ['tmpa'], ['rr_ni'])
        self.cp('dve', nf[:], ni[:], ['rr_ni'], ['rr_nf'])
        self.S.add('dve', lambda e: e.scalar_tensor_tensor(out=rr[:], in0=nf[:], scalar=-HI, in1=ang[:], op0=ALU.mult, op1=ALU.add),
                   reads=['rr_nf', 'ang'], writes=['rr_r'])
        self.S.add('dve', lambda e: e.scalar_tensor_tensor(out=rr[:], in0=nf[:], scalar=-LO, in1=rr[:], op0=ALU.mult, op1=ALU.add),
                   reads=['rr_nf', 'rr_r'], writes=['rr_r'])

        def wrap(buf, key):
            self.tsc('dve', tmpa[:], buf[:], PI, -2 * PI, ALU.is_gt, ALU.mult, [key], ['tmpa'])
            self.tt('dve', buf[:], buf[:], tmpa[:], ALU.add, [key, 'tmpa'], [key])
            self.tsc('dve', tmpa[:], buf[:], -PI, 2 * PI, ALU.is_lt, ALU.mult, [key], ['tmpa'])
            self.tt('dve', buf[:], buf[:], tmpa[:], ALU.add, [key, 'tmpa'], [key])
            self.tsc('dve', buf[:], buf[:], -3.141592, 3.141592, ALU.max, ALU.min, [key], [key])
        wrap(rr, 'rr_r')
        self.act(self.sin[:], rr[:], AF.Sin, ['rr_r'], ['sin'])
        self.tsc('dve', rr[:], rr[:], PI / 2, None, ALU.add, None, ['rr_r', 'sin'], ['rr_r'])
        wrap(rr, 'rr_r')
        self.act(self.cos[:], rr[:], AF.Sin, ['rr_r'], ['cos'])
        scrF = self.BIG[:, 0:24576].bitcast(F32)
        scrI = self.BIG[:, 0:24576].bitcast(I32)

        def carve(src, k):
            return src[:, k * 2176:(k + 1) * 2176].rearrange("p (o q) -> p o q", q=128)
        dA = carve(scrF, 0)
        dAi = carve(scrI, 1)
        t1 = carve(scrF, 2)
        t2 = carve(scrF, 3)
        t3 = carve(scrF, 4)
        t4 = carve(scrI, 4)
        self.MA = self.sb('MA', [128, 17, 128], BF16)
        self.S.add('pool', lambda e: e.iota(dAi[:], pattern=[[128, 17], [1, 128]], base=0, channel_multiplier=-1), writes=['dAi'])
        self.cp('dve', dA[:], dAi[:], ['dAi'], ['dA'])
        self.tsc('dve', t1[:], dA[:], 128.0, None, ALU.is_le, None, ['dA'], ['mt1'])
        self.tsc('dve', t4[:], dAi[:], 3, None, ALU.bitwise_and, None, ['dAi'], ['mt3'])
        self.cp('dve', t2[:], t4[:], ['mt3'], ['mt2'])
        self.tsc('dve', t2[:], t2[:], 0.0, None, ALU.is_equal, None, ['mt2'], ['mt2'])
        self.tsc('dve', t3[:], dA[:], 512.0, None, ALU.is_le, None, ['dA'], ['mt3'])
        self.tt('dve', t2[:], t2[:], t3[:], ALU.mult, ['mt2', 'mt3'], ['mt2'])
        self.tt('dve', t1[:], t1[:], t2[:], ALU.add, ['mt1', 'mt2'], ['mt1'])
        self.tsc('dve', t4[:], dAi[:], 15, None, ALU.bitwise_and, None, ['dAi', 'mt2'], ['mt3'])
        self.cp('dve', t2[:], t4[:], ['mt3', 'mt1'], ['mt2'])
        self.tsc('dve', t2[:], t2[:], 0.0, None, ALU.is_equal, None, ['mt2'], ['mt2'])
        self.tsc('dve', t3[:], dA[:], 2048.0, None, ALU.is_le, None, ['dA', 'mt2'], ['mt3'])
        self.tt('dve', t2[:], t2[:], t3[:], ALU.mult, ['mt2', 'mt3'], ['mt2'])
        self.tt('dve', t1[:], t1[:], t2[:], ALU.add, ['mt1', 'mt2'], ['mt1'])
        self.tsc('dve', t2[:], dA[:], 0.0, None, ALU.is_ge, None, ['dA', 'mt1'], ['mt2'])
        self.tt('dve', self.MA[:], t1[:], t2[:], ALU.mult, ['mt1', 'mt2'], ['MA'])
        self.AM = self.sb('AM', [128, NT, 64], F32)
        vsF = self.BIG[:, 32768:32768 + NT * 432].bitcast(F32)
        vsI = self.BIG[:, 32768:32768 + NT * 432].bitcast(I32)
        am1 = vsF[:, 0:2048].rearrange("p (t j) -> p t j", j=64)
        am1i = vsI[:, 2048:4096].rearrange("p (t j) -> p t j", j=64)
        am2 = vsF[:, 4096:6144].rearrange("p (t j) -> p t j", j=64)
        for a in range(2):
            self.S.add('pool', (lambda a: lambda e: e.iota(am1i[a * 64:(a + 1) * 64], pattern=[[-2, NT], [1, 64]], base=-a,
                                                           channel_multiplier=0))(a),
                       writes=['am1i_%d' % a])
        self.cp('dve', am1[:], am1i[:], ['am1i_0', 'am1i_1'], ['am1_0', 'am1_1'])
        self.tsc('dve', am2[:], am1[:], 0.0, -1e30, ALU.is_gt, ALU.mult, ['am1_0', 'am1_1'], ['am2'])
        self.tsc('dve', am1[:], am1[:], -1.0, 1e4, ALU.is_ge, ALU.mult, ['am1_0', 'am1_1', 'am2'], ['am1', 'am1_0', 'am1_1'])
        self.tt('dve', self.AM[:], am1[:], am2[:], ALU.add, ['am1', 'am2'], ['AM'])
        self.tsc('dve', self.AM[:, :, 0:1], self.AM[:, :, 0:1], 1e4, None, ALU.add, None, ['AM'], ['AM'])
        self.cover = self.sb('cover', [128, 2, 64], F32)
        self.memset('pool', self.cover[:], 1.0, ['cover'])
        self.asel(self.cover[:], self.cover[:], [[-128, 2], [4, 64]], ALU.is_ge, 0.0, 3, -1, ['cover'], ['cover'])
        self.asel(self.cover[:], self.cover[:], [[128, 2], [-4, 64]], ALU.is_ge, 0.0, 1, 1, ['cover'], ['cover'])
        self.pb = [self.ps('pb%d' % i, [128, 512], F32) for i in range(8)]
        self.xt = [self.sb('xt%d' % i, [128, D], F32) for i in range(2)]
        self.hg = [self.sb('hg%d' % i, [128, 4, 8, 128], BF16) for i in range(2)]
        self.wstage = [self.sb('wst%d' % i, [128, 8, 128], F32) for i in range(2)]
        self.wst_n = 0
        self.hg_n = 0
        self.xt_n = 0
        if 'dbg_cs' in [d[0] for d in self.dbg]:
            o = self.dma(self.dr['dbg_cs'][:, 0:256], self.cos[:].rearrange("p t f -> p (t f)"), ['cos'], [])
            self.final_ops.append(o)
            o = self.dma(self.dr['dbg_cs'][:, 256:512], self.sin[:].rearrange("p t f -> p (t f)"), ['sin'], [])
            self.final_ops.append(o)
            mtmp = self.sb('mtmp', [128, 17 * 128], F32)
            self.cp('dve', mtmp[:], self.MA[:].rearrange("p o q -> p (o q)"), ['MA'], ['mtmp'])
            o = self.dma(self.dr['dbg_ma'][:, :], mtmp[:], ['mtmp'], [])
            self.final_ops.append(o)
            o = self.dma(self.dr['dbg_am'][:, :], self.AM[:].rearrange("p t f -> p (t f)"), ['AM'], [])
            self.final_ops.append(o)
            o = self.dma(self.dr['dbg_cov'][:, :], self.cover[:].rearrange("p t f -> p (t f)"), ['cover'], [])
            self.final_ops.append(o)

    def load_w(self, W, wkey, src, c0, n, o, nk=8, k0=0, eng_cycle=('pool', 'dve')):
        done = 0
        while done < n:
            m = min(128, n - done)
            sl = self.wst_n % 2
            self.wst_n += 1
            stg = self.wstage[sl]
            self.dma(stg[:, 0:nk, 0:m], src[:, c0 + done:c0 + done + m].rearrange("(k p) c -> p k c", p=128),
                     [], [('wst', sl)])
            eng = eng_cycle[self.wst_n % len(eng_cycle)]
            self.cp(eng, W[:, k0:k0 + nk, o + done:o + done + m], stg[:, 0:nk, 0:m], [('wst', sl)], [(wkey, self.wst_n)])
            self.wkeys.setdefault(wkey, []).append((wkey, self.wst_n))
            done += m

    def layer(self, l, xin, xout):
        if l > 0:
            self.fence()
        self.stage1(l, xin)
        if self.stop_after == 'stage1':
            return
        only = getattr(self, 'only', None)
        if only is None or 'A' in only:
            self.fence()
            self.pass_A(l)
        if self.stop_after in ('A', 'Aproj'):
            return
        if only is None or 'C' in only:
            self.fence()
            self.pass_C(l)
        if self.stop_after == 'C':
            return
        if only is None or 'B' in only:
            self.fence()
            self.pass_B(l)
        if self.stop_after == 'B':
            return
        self.fence()
        self.final(l, xin, xout)

    def stage1(self, l, xin):
        dr = self.dr
        if l == 0:
            self.gT = self.sb('gT', [128, 8], F32)
            self.s1_all = self.sb('s1all', [128, 4096], BF16)
            self.s1_sq = self.s1_all[:, 0:1024]
            self.s1_ss = self.sb('s1ss', [128, 2], F32)
            self.s1_xs = self.s1_all[:, 1024:2048]
            self.s1_hT = [self.s1_all[:, 2048 + i * 1024:3072 + i * 1024].rearrange("p (k c) -> p k c", c=128) for i in range(2)]
        self.dma(self.gT[:], dr['norm_g'][l], [], ['gT'])
        for t in range(NT):
            sl = t % 2
            xt = self.xt[sl]
            self.dma(xt[:], xin[t * 128:(t + 1) * 128, :], [('x1', t)], [('xt', sl)])
            ss = self.s1_ss[:, sl:sl + 1]
            xs = (self.s1_sq, self.s1_xs)[sl]
            self.act(xs, xt[:], AF.Square, [('xt', sl)], [('s1xs', sl), ('s1ss', sl)], accum_out=ss)
            self.rsqrt(ss, ss, 1.0 / D, [('s1ss', sl)], [('s1ss', sl)])
            self.act(xs, xt[:], AF.Copy, [('xt', sl), ('s1ss', sl)], [('s1xs', sl)], scale=ss)
            pT = self.pb[sl][:].bitcast(BF16)
            for k in range(8):
                self.tr(pT[:, k * 128:(k + 1) * 128], xs[:, k * 128:(k + 1) * 128], self.identb[:],
                        [('s1xs', sl), 'identb'], [('pb', sl)])
            hT = self.s1_hT[sl]
            self.tt('dve', hT, pT[:, 0:1024].rearrange("p (k t) -> p k t", k=8), bc(self.gT[:], 2, [128, 8, 128]), ALU.mult,
                    [('pb', sl), 'gT'], [('s1hT', sl)])
            self.dma(dr['hnT'][t], hT.rearrange("p k t -> p (k t)"), [('s1hT', sl)], [('hnT', t)])
        if 'dbg_hnT' in [d[0] for d in self.dbg]:
            for t in range(NT):
                o = self.dma(dr['dbg_hnT'][t], dr['hnT'][t], [('hnT', t)], [])
                self.final_ops.append(o)

    def load_hg(self, g):
        sl = self.hg_n % 2
        self.hg_n += 1
        self.dma(self.hg[sl][:].rearrange("p t k c -> p t (k c)"), self.dr['hnT'][g * 4:(g + 1) * 4].rearrange("t p c -> p t c"),
                 [('hnT', g * 4 + i) for i in range(4)], [('hg', sl)])
        return sl

    def proj_tile(self, hsl, tt, W, wreads, chunks, banks):
        for (c0, n), b in zip(chunks, banks):
            for k in range(8):
                self.mm(self.pb[b][:, 0:n], self.hg[hsl][:, tt, k, :], W[:, k, c0:c0 + n], k == 0, k == 7,
                        [('hg', hsl)] + wreads, [('pb', b)])

    def qk_post(self, pr, nh, Gt, t, xb, prk, xbk):
        W_ = nh * 64
        sq = self.qk_sq[:, 0:W_]
        ss = self.qk_ss[:, 0:nh]
        self.tt('dve', sq, pr, pr, ALU.mult, [prk], ['qk_sq'])
        self.S.add('dve', lambda e: e.tensor_reduce(out=ss, in_=sq.rearrange("p (h d) -> p h d", d=64), axis=AX.X, op=ALU.add),
                   reads=['qk_sq'], writes=['qk_ss'])
        self.rsqrt(ss, ss, 1.0 / 64, ['qk_ss'], ['qk_ss'])
        na = (2 * nh + 2) // 3
        pr3 = pr.rearrange("p (h d) -> p h d", d=64)
        xn3 = self.qk_xn[:, 0:W_].rearrange("p (h d) -> p h d", d=64)
        xb3 = xb.rearrange("p (h d) -> p h d", d=64)
        G3 = Gt.rearrange("p (h d) -> p h d", d=64)
        for eng, h0, h1 in (('dve', 0, na), ('pool', na, nh)):
            n_ = h1 - h0
            if n_ <= 0:
                continue
            kx = 'qk_xn_' + eng
            xn_ = xn3[:, h0:h1, :]
            self.tt(eng, xn_, pr3[:, h0:h1, :], bc(ss[:, h0:h1], 2, [128, n_, 64]), ALU.mult, [prk, 'qk_ss'], [kx])
            self.tt(eng, xn_, xn_, G3[:, h0:h1, :], ALU.mult, [kx, 'Gt'], [kx])
            cosb = bc(self.cos[:, t, :], 1, [128, n_, 8])
            sinb = bc(self.sin[:, t, :], 1, [128, n_, 8])
            r = [self.qk_r[i][:, h0:h1, :] for i in range(4)]
            rk = ['qk_r%d_%s' % (i, eng) for i in range(4)]
            self.tt(eng, r[0], xn_[:, :, 0:8], cosb, ALU.mult, [kx, 'cos'], [rk[0]])
            self.tt(eng, r[1], xn_[:, :, 8:16], sinb, ALU.mult, [kx, 'sin'], [rk[1]])
            self.tt(eng, r[2], xn_[:, :, 8:16], cosb, ALU.mult, [kx, 'cos'], [rk[2]])
            self.tt(eng, r[3], xn_[:, :, 0:8], sinb, ALU.mult, [kx, 'sin'], [rk[3]])
            xk = xbk + '_' + eng
            self.tt(eng, xb3[:, h0:h1, 0:8], r[0], r[1], ALU.subtract, [rk[0], rk[1]], [xk])
            self.tt(eng, xb3[:, h0:h1, 8:16], r[2], r[3], ALU.add, [rk[2], rk[3]], [xk])
            self.cp(eng, xb3[:, h0:h1, 16:64], xn_[:, :, 16:64], [kx], [xk])

    def alloc_common(self):
        if hasattr(self, 'qk_sq'):
            return
        self.qk_sq = self.sb('qk_sq', [128, 768], F32)
        self.qk_ss = self.sb('qk_ss', [128, 12], F32)
        self.qk_xn = self.sb('qk_xn', [128, 768], F32)
        self.qk_r = [self.sb('qk_r%d' % i, [128, 12, 8], F32) for i in range(4)]
        self.pr = self.sb('pr', [128, 768], F32)
        self.xb = self.sb('xb', [128, 768], BF16)
        self.Gt = self.sb('Gt', [128, 768], F32)
        self.g64 = self.sb('g64', [128, 2, 64], F32)
        self.PT = [self.sb('PT%d' % i, [128, 768], BF16) for i in range(3)]
        self.pt_n = 0
        self.ob = self.sb('ob', [128, 384], BF16)
        self.rec = self.sb('rec', [128, 12], F32)
        self.oTt = [self.sb('oTt%d' % i, [128, 384], BF16) for i in range(2)]
        self.selT = [self.sb('selT%d' % i, [64, 512], BF16) for i in range(2)]
        self.Qz = [self.sb('Qz%d' % i, [128, 6, 128], BF16) for i in range(2)]
        self.qz_n = 0
        self.ot_n = 0
        self.BIG = self.sb('BIG', [128, 57344], BF16)
        self.QKT = self.BIG[:, 0:32768].rearrange("p (a c) -> p a c", c=S_LEN)
        self.VS = self.BIG[:, 32768:32768 + NT * 432].rearrange("p (t c) -> p t c", c=432)
        self.Wp = self.BIG[:, 46592:46592 + 9216].rearrange("p (k c) -> p k c", c=1152)

    def load_gains(self, l, qn, kn, nq, nk):
        dr = self.dr
        self.dma(self.g64[:, 0, :], dr[qn][l].partition_broadcast(128), [], ['g64q'])
        self.dma(self.g64[:, 1, :], dr[kn][l].partition_broadcast(128), [], ['g64k'])
        G3 = self.Gt[:, 0:(nq + nk) * 64].rearrange("p (h d) -> p h d", d=64)
        self.cp('dve', G3[:, 0:nq, :], bc(self.g64[:, 0, :], 1, [128, nq, 64]), ['g64q'], ['Gt'])
        self.cp('dve', G3[:, nq:nq + nk, :], bc(self.g64[:, 1, :], 1, [128, nk, 64]), ['g64k'], ['Gt'])

    def out_tile(self, i, acc_key, ob_ap, ncol, c0):
        npair = ncol // 128
        pT = self.pb[7][:].bitcast(BF16)
        for p in range(npair):
            self.tr(pT[:, p * 128:(p + 1) * 128], ob_ap[:, p * 128:(p + 1) * 128], self.identb[:], [acc_key, 'identb'], [('pb', 7)])
        sl = self.ot_n % 2
        self.ot_n += 1
        self.cp('act', self.oTt[sl][:, 0:ncol], pT[:, 0:ncol], [('pb', 7)], [('oTt', sl)])
        self.dma(self.dr['oT'][i][:, c0:c0 + ncol], self.oTt[sl][:, 0:ncol], [('oTt', sl)], [('oT', i, c0)])

    def pass_A(self, l):
        dr = self.dr
        self.alloc_common()
        self.wkeys = {}
        W = self.Wp
        self.load_w(W, 'Wp', dr['w_in'][l], 0, 1152, 0)
        wreads = list(self.wkeys['Wp'])
        self.load_gains(l, 'q_norm_a', 'k_norm_a', 6, 6)
        VS4 = self.VS.rearrange("p t (h e) -> p t h e", e=72)
        self.S.add('pool', lambda e: e.memset(VS4[:, :, :, 64:65], 1.0), reads=['MA', 'AM'], writes=['VS_ones'])
        QKT = self.QKT
        hs_ = {}

        def do_proj(t):
            if t % 4 == 0:
                hs_['sl'] = self.load_hg(t // 4)
            self.proj_tile(hs_['sl'], t % 4, W, wreads, [(0, 384), (384, 384), (768, 384)], [0, 1, 2])
        do_proj(0)
        for t in range(NT):
            self.cp('act', self.pr[:, 0:384], self.pb[0][:, 0:384], [('pb', 0)], ['pr'])
            self.cp('act', self.pr[:, 384:768], self.pb[1][:, 0:384], [('pb', 1)], ['pr'])
            self.cp('act', VS4[:, t, :, 0:64], self.pb[2][:, 0:384].rearrange("p (h d) -> p h d", d=64), [('pb', 2), 'VS_ones', 'MA', 'AM'], [('VS', t)])
            self.qk_post(self.pr[:, 0:768], 12, self.Gt[:, 0:768], t, self.xb[:, 0:768], 'pr', 'xb')
            if t + 1 < NT:
                do_proj(t + 1)
            pT = self.pb[3][:].bitcast(BF16)
            for p in range(6):
                self.tr(pT[:, p * 128:(p + 1) * 128], self.xb[:, p * 128:(p + 1) * 128], self.identb[:], ['xb_dve', 'xb_pool', 'identb'], [('pb', 3)])
            self.cp('act', QKT[:, 0:6, t * 128:(t + 1) * 128], pT[:, 0:768].rearrange("p (a c) -> p a c", c=128), [('pb', 3), 'MA', 'AM'], [('QKT', t)])
        if self.stop_after == 'Aproj':
            return
        stA = {'sb': 0}

        def a_pair(i, j, j0, Qz, qzk):
            o = i - j
            d_ = {}

            def s_():
                bS = [(2, 3), (4, 5), (0, 1)][stA['sb'] % 3]
                stA['sb'] += 1
                d_['bS'] = bS
                for p in range(3):
                    bb = bS[0] if p < 2 else bS[1]
                    c0 = (p % 2) * 256
                    self.mm(self.pb[bb][:, c0:c0 + 256], QKT[:, 3 + p, j * 128:(j + 1) * 128],
                            Qz[:, 2 * p:2 * p + 2, :], True, True, [('QKT', j)] + qzk, [('pb', bb)])

            def r_():
                bS = d_['bS']
                ps_ = self.pt_n % 3
                self.pt_n += 1
                PT = self.PT[ps_]
                self.act(PT[:, 0:512], self.pb[bS[0]][:, 0:512], AF.Exp, [('pb', bS[0])], [('PT', ps_)], scale=0.125)
                self.act(PT[:, 512:768], self.pb[bS[1]][:, 0:256], AF.Exp, [('pb', bS[1])], [('PT', ps_)], scale=0.125)
                self.tt('dve', PT[:, 0:768].rearrange("p (h q) -> p h q", q=128), PT[:, 0:768].rearrange("p (h q) -> p h q", q=128),
                        bc(self.MA[:, o, :], 1, [128, 6, 128]), ALU.mult, [('PT', ps_), 'MA'], [('PT', ps_)])
                for h in range(6):
                    self.mm(self.pb[6][:, h * 72:h * 72 + 65], PT[:, h * 128:(h + 1) * 128], VS4[:, j, h, 0:65],
                            (j == j0 and h == 0), j == i, [('PT', ps_), ('VS', j), 'VS_ones'], [('pb', 6)], skip=True)
            return s_, r_

        def a_fin(i):
            def r_():
                acc = self.pb[6][:, 0:432].rearrange("p (h e) -> p h e", e=72)
                self.S.add('dve', lambda e: e.reciprocal(out=self.rec[:, 0:6], in_=acc[:, :, 64]), reads=[('pb', 6)], writes=['rec'])
                self.tt('dve', self.ob[:, 0:384].rearrange("p (h d) -> p h d", d=64), acc[:, :, 0:64], bc(self.rec[:, 0:6], 2, [128, 6, 64]),
                        ALU.mult, [('pb', 6), 'rec'], ['ob'])
                self.out_tile(i, 'ob', self.ob, 384, 0)
            return r_
        for i in range(NT):
            j0 = max(0, i - 16)
            Qz, qzk = self.make_qz(i, 6)
            for j in range(j0, i + 1):
                s_, r_ = a_pair(i, j, j0, Qz, qzk)
                self.emit_step(s_, r_, L=2)
            self.emit_step(None, a_fin(i), L=2)
        self.flush_steps()
        self.dbg_oT()

    def dbg_oT(self):
        if 'dbg_oT' in [d[0] for d in self.dbg] and self.stop_after is not None:
            rng = [r for k, r in (('A', (0, 384)), ('B', (384, 640)), ('C', (640, 896))) if getattr(self, 'only', None) is None or k in self.only]
            for t in range(NT):
                for (a, b) in rng:
                    o = self.dma(self.dr['dbg_oT'][t][:, a:b], self.dr['oT'][t][:, a:b], [('oT', t, 0), ('oT', t, 384), ('oT', t, 640)], [])
                    self.final_ops.append(o)

    def pass_B(self, l):
        dr = self.dr
        if not hasattr(self, 'b_GS'):
            self.b_GS = self.sb('b_GS', [128, NT, 12], F32)
            self.b_posT = self.sb('b_posT', [64, 32], BF16)
            self.b_posTf = self.sb('b_posTf', [64, 32], F32)
            self.b_W2 = self.sb('b_W2', [128, 2, 64], BF16)
            self.b_W2f = self.sb('b_W2f', [128, 2, 64], F32)
            self.b_W2vf = self.b_W2f
            self.b_h1 = self.sb('b_h1', [128, 2, 256], BF16)
            self.b_hb = self.sb('b_hb', [128, 2], F32)
            self.b_kcT = self.sb('b_kcT', [64, 256], BF16)
            self.b_vc = self.sb('b_vc', [128, 2, 64], F32)
            self.b_Pc = self.sb('b_Pc', [128, 2, 512], F32)
            self.b_rden = self.qk_sq[:, 0:512]
            self.b_sc = self.sb('b_sc', [128, 64], F32)
            self.b_sc2 = self.sb('b_sc2', [128, 64], F32)
            self.b_m1 = self.sb('b_m1', [128, 8], F32)
            self.b_m2 = self.sb('b_m2', [128, 8], F32)
            self.b_selb = self.sb('b_selb', [128, 64], F32)
            self.b_selbT2 = [t_[0:64, :].rearrange('p (h q) -> p h q', q=128) for t_ in self.selT]
            self.b_f = self.sb('b_f', [128, 12], F32)
            self.b_obf = self.qk_xn[:, 0:256].rearrange('p (h d) -> p h d', d=64)
            self.b_tmp = self.qk_xn[:, 256:512].rearrange('p (h d) -> p h d', d=64)
        GS = self.b_GS
        self.wkeys = {}
        W = self.BIG[:, 37376:37376 + 8 * 652].rearrange("p (k c) -> p k c", c=652)
        self.load_w(W, 'Wp', dr['w_in'][l], 1536, 652, 0)
        wreads = list(self.wkeys['Wp'])
        self.load_gains(l, 'q_norm_b', 'k_norm_b', 4, 6)
        Esel = self.build_E('Esel', 64, 64, 50784 + 0)
        VSB = self.BIG[:, 32768:32768 + NT * 144].rearrange("p (t h e) -> p t h e", h=2, e=72)
        self.memset('pool', VSB[:, :, :, 64:65], 1.0, ['VS_ones'])
        QTB = self.BIG[0:64, 0:32768].rearrange("p (a c) -> p a c", c=S_LEN)
        W1 = [self.BIG[0:64, 42592 + i * 4096:42592 + (i + 1) * 4096].rearrange("p (q h) -> p q h", h=128) for i in range(2)]
        for wi, nm in enumerate(('cmp_k_w1', 'cmp_v_w1')):
            for qtr in range(4):
                sl = self.wst_n % 2
                self.wst_n += 1
                stg = self.wstage[sl][0:64]
                self.dma(stg, dr[nm][l][:, qtr * 8:(qtr + 1) * 8, :], [], [('wst', sl)])
                self.cp('pool', W1[wi][:, qtr * 8:(qtr + 1) * 8, :], stg, [('wst', sl)], [('W1', wi, qtr)])
        w1keys = [[('W1', wi, q) for q in range(4)] for wi in range(2)]
        self.dma(self.b_posTf[:], dr['cmp_pos'][l], [], ['b_posTf'])
        self.cp('dve', self.b_posT[:], self.b_posTf[:], ['b_posTf'], ['b_posT'])
        self.dma(self.b_W2f[:, 0, :], dr['cmp_k_w2'][l], [], ['b_W2f0'])
        self.dma(self.b_W2f[:, 1, :], dr['cmp_v_w2'][l], [], ['b_W2f1'])
        self.cp('dve', self.b_W2[:], self.b_W2f[:], ['b_W2f0', 'b_W2f1'], ['b_W2'])
        srcs = [0, 64, 128, 192, 256, 384, 512, 320]
        hs_ = {}

        def do_proj(t):
            if t % 4 == 0:
                hs_['sl'] = self.load_hg(t // 4)
            self.proj_tile(hs_['sl'], t % 4, W, wreads, [(0, 512), (512, 140)], [0, 1])
        do_proj(0)
        for t in range(NT):
            self.cp('act', self.pr[:, 0:512], self.pb[0][:, 0:512], [('pb', 0)], ['pr'])
            self.cp('act', self.pr[:, 512:652], self.pb[1][:, 0:140], [('pb', 1)], ['pr'])
            self.qk_post(self.pr[:, 0:640], 10, self.Gt[:, 0:640], t, self.xb[:, 0:640], 'pr', 'xb')
            self.cp('dve', self.xb[:, 320:384], self.pr[:, 320:384], ['pr', 'xb_dve', 'xb_pool'], ['xb_dve', 'xb_pool'])
            self.cp('dve', VSB[:, t, 0, 0:64], self.pr[:, 448:512], ['pr', 'VS_ones'], [('VS', t)])
            self.cp('dve', VSB[:, t, 1, 0:64], self.pr[:, 576:640], ['pr', 'VS_ones'], [('VS', t)])
            self.cp('dve', GS[:, t, :], self.pr[:, 640:652], ['pr'], [('GS', t)])
            if t + 1 < NT:
                do_proj(t + 1)
            pT = self.pb[3][:].bitcast(BF16)
            for si, c0 in enumerate(srcs):
                self.tr(pT[0:64, si * 128:(si + 1) * 128], self.xb[:, c0:c0 + 64], self.identb[:], ['xb_dve', 'xb_pool', 'identb'], [('pb', 3)])
            self.cp('act', QTB[:, 0:8, t * 128:(t + 1) * 128], pT[0:64, 0:1024].rearrange("p (a c) -> p a c", c=128), [('pb', 3)], [('QKT', t)])
        allq = [('QKT', t) for t in range(NT)]
        self.act(GS[:].rearrange("p t g -> p (t g)"), GS[:].rearrange("p t g -> p (t g)"), AF.Sigmoid, [('GS', t) for t in range(NT)], ['GSs'])
        self.memset('dve', self.b_h1[:, :, 255:256], 0.0, ['b_h1z'])
        for wi, slot in ((0, 4), (1, 7)):
            for p in range(32):
                self.mm(self.pb[5][:, 0:255], W1[wi][:, p, :], QTB[:, slot, p:p + 16 * 254 + 1:16], p == 0, p == 31,
                        allq + w1keys[wi], [('pb', 5)])
            for p in range(32):
                self.mm(self.pb[4][:, 0:1], W1[wi][:, p, :], self.b_posT[:, p:p + 1], p == 0, p == 31, w1keys[wi] + ['b_posT'], [('pb', 4)])
            self.cp('dve', self.b_hb[:, wi:wi + 1], self.pb[4][:, 0:1], [('pb', 4)], [('b_hb', wi)])
            self.act(self.b_h1[:, wi, 0:255], self.pb[5][:, 0:255], AF.Silu, [('pb', 5), ('b_hb', wi), 'b_h1z'], [('b_h1', wi)],
                     bias=self.b_hb[:, wi:wi + 1])
        self.mm(self.pb[5][0:64, 0:256], self.b_W2[:, 0, :], self.b_h1[:, 0, :], True, True, ['b_W2', ('b_h1', 0), 'b_h1z'], [('pb', 5)])
        self.cp('dve', self.b_kcT[:, :], self.pb[5][0:64, 0:256], [('pb', 5)], ['b_kcT'])
        for ct in range(2):
            self.mm(self.pb[4][:, ct * 64:(ct + 1) * 64], self.b_h1[:, 1, ct * 128:(ct + 1) * 128], self.b_W2[:, 1, :], True, True,
                    ['b_W2', ('b_h1', 1), 'b_h1z'], [('pb', 4)])
        self.cp('dve', self.b_vc[:].rearrange("p c d -> p (c d)"), self.pb[4][:, 0:128], [('pb', 4)], ['b_vc'])
        Pc = self.b_Pc
        st = {'sb': 0}

        def qap(i):
            return QTB[:, 0:4, i * 128:(i + 1) * 128]

        def nbank():
            b = 2 + (st['sb'] % 2)
            st['sb'] += 1
            return b

        def sel_steps(i):
            qr = [('QKT', i)]
            nct = 2 if i >= 16 else 1
            ob = i % 2
            selbT = self.b_selbT2[i % 2]
            sk = ('b_selbT', i % 2)

            def s_a():
                for ct in range(nct):
                    b = 7
                    self.mm(self.pb[b][:, 0:512], self.b_kcT[:, ct * 128:(ct + 1) * 128], qap(i), True, True, qr + ['b_kcT'], [('pb', b)])
                    self.act(Pc[:, ct, :], self.pb[b][:, 0:512], AF.Exp, [('pb', b)], [('Pc', ct)], scale=0.125)
                    if ct == 1 or i < 17:
                        self.asel(Pc[:, ct, :].rearrange("p (h q) -> p h q", q=128), Pc[:, ct, :].rearrange("p (h q) -> p h q", q=128),
                                  [[0, 4], [1, 128]], ALU.is_ge, 0.0, 128 * i - 2048 * ct - 31, -16, [('Pc', ct)], [('Pc', ct)])

            def s_b():
                for ct in range(nct):
                    self.mm(self.pb[4][:, 0:512], self.onesf[:, :], Pc[:, ct, :], ct == 0, ct == nct - 1, [('Pc', ct), 'onesf'], [('pb', 4)])

            def s_c():
                self.tsc('dve', self.b_rden, self.pb[4][:, 0:512], 1e-30, None, ALU.add, None, [('pb', 4)], ['b_rden'])
                self.S.add('dve', lambda e: e.reciprocal(out=self.b_rden, in_=self.b_rden), reads=['b_rden'], writes=['b_rden'])
                for ct in range(nct):
                    self.tt('dve', Pc[:, ct, :], Pc[:, ct, :], self.b_rden, ALU.mult, [('Pc', ct), 'b_rden'], [('Pc', ct)])

            def s_d():
                first = True
                for ct in range(nct):
                    for h in range(4):
                        self.mm(self.pb[ob][:, 0:64], Pc[:, ct, h * 128:(h + 1) * 128], self.cover[:, ct, :], first, False,
                                [('Pc', ct), 'cover'], [('pb', ob)], skip=True)
                        first = False
                for ct in range(nct):
                    for h in range(4):
                        self.mm(self.pb[ob][:, 64 + h * 64:128 + h * 64], Pc[:, ct, h * 128:(h + 1) * 128], self.b_vc[:, ct, :], False,
                                (ct == nct - 1 and h == 3), [('Pc', ct), 'b_vc'], [('pb', ob)], skip=True)

            def s_e():
                self.tt('dve', self.b_sc[:], self.pb[ob][:, 0:64], self.AM[:, i, :], ALU.add, [('pb', ob), 'AM'], ['b_sc'])
                self.S.add('dve', lambda e: e.max(out=self.b_m1[:], in_=self.b_sc[:]), reads=['b_sc'], writes=['b_m1'])
                self.S.add('dve', lambda e: e.match_replace(out=self.b_sc2[:], in_to_replace=self.b_m1[:], in_values=self.b_sc[:],
                                                            imm_value=-3e38), reads=['b_sc', 'b_m1'], writes=['b_sc2'])
                self.S.add('dve', lambda e: e.max(out=self.b_m2[:], in_=self.b_sc2[:]), reads=['b_sc2'], writes=['b_m2'])
                self.tsc('dve', self.b_selb[:], self.b_sc[:], self.b_m2[:, 7:8], None, ALU.is_ge, None, ['b_sc', 'b_m2'], ['b_selb'])
                self.tsc('dve', self.b_selb[:], self.b_selb[:], 1.0, -NEGB, ALU.subtract, ALU.mult, ['b_selb'], ['b_selb'])

            def s_f():
                self.tr(self.pb[4][0:64, 0:128], self.b_selb[:, :], self.identf[:], ['b_selb', 'identf'], [('pb', 4)])
                self.cp('act', selbT[:], bc(self.pb[4][0:64, 0:128], 1, [64, 4, 128]), [('pb', 4)], [sk])
            return [s_a, s_b, s_c, s_d, s_e, s_f]

        def attn_steps(i):
            qr = [('QKT', i)]
            ob = i % 2
            selbT = self.b_selbT2[i % 2]
            sk = ('b_selbT', i % 2)
            steps = []

            def sel_pair(j):
                d_ = {}

                def s_():
                    b = nbank()
                    d_['b'] = b
                    bank = self.pb[b]
                    self.mm(bank[:, 0:512], Esel[:, j * 128:(j + 1) * 128], selbT[:].rearrange("p h q -> p (h q)"), True, False,
                            ['Esel', sk], [('pb', b)])
                    self.mm(bank[:, 0:512], QTB[:, 5, j * 128:(j + 1) * 128], qap(i), False, True, qr + [('QKT', j)], [('pb', b)])

                def f():
                    b = d_['b']
                    bank = self.pb[b]
                    ps_ = self.pt_n % 3
                    self.pt_n += 1
                    PT = self.PT[ps_]
                    self.act(PT[:, 0:512], bank[:, 0:512], AF.Exp, [('pb', b)], [('PT', ps_)], scale=0.125)
                    if j == i:
                        self.asel(PT[:, 0:512].rearrange("p (h q) -> p h q", q=128), PT[:, 0:512].rearrange("p (h q) -> p h q", q=128),
                                  [[0, 4], [1, 128]], ALU.is_ge, 0.0, 0, -1, [('PT', ps_)], [('PT', ps_)])
                    for h in range(4):
                        self.mm(self.pb[6][:, h * 72:h * 72 + 65], PT[:, h * 128:(h + 1) * 128], VSB[:, j, 0, 0:65],
                                (j == 0 and h == 0), j == i, [('PT', ps_), ('VS', j), 'VS_ones'], [('pb', 6)], skip=True)
                return (s_, f)
            j0 = max(0, i - 4)

            def win_pair(j):
                d_ = {}

                def s_():
                    b = nbank()
                    d_['b'] = b
                    bank = self.pb[b]
                    self.mm(bank[:, 0:512], QTB[:, 6, j * 128:(j + 1) * 128], qap(i), True, True, qr + [('QKT', j)], [('pb', b)])

                def f():
                    b = d_['b']
                    bank = self.pb[b]
                    ps_ = self.pt_n % 3
                    self.pt_n += 1
                    PT = self.PT[ps_]
                    self.act(PT[:, 0:512], bank[:, 0:512], AF.Exp, [('pb', b)], [('PT', ps_)], scale=0.125)
                    PT3 = PT[:, 0:512].rearrange("p (h q) -> p h q", q=128)
                    if j == i:
                        self.asel(PT3, PT3, [[0, 4], [1, 128]], ALU.is_ge, 0.0, 0, -1, [('PT', ps_)], [('PT', ps_)])
                    if j == i - 4:
                        self.asel(PT3, PT3, [[0, 4], [-1, 128]], ALU.is_ge, 0.0, -1, 1, [('PT', ps_)], [('PT', ps_)])
                    for h in range(4):
                        self.mm(self.pb[5][:, h * 72:h * 72 + 65], PT[:, h * 128:(h + 1) * 128], VSB[:, j, 1, 0:65],
                                (j == j0 and h == 0), j == i, [('PT', ps_), ('VS', j), 'VS_ones'], [('pb', 5)], skip=True)
                return (s_, f)
            for j in range(i + 1):
                steps.append(sel_pair(j))
            for j in range(j0, i + 1):
                steps.append(win_pair(j))

            def combine():
                accs = self.pb[6][:, 0:288].rearrange("p (h e) -> p h e", e=72)
                accw = self.pb[5][:, 0:288].rearrange("p (h e) -> p h e", e=72)
                ocmp = self.pb[ob][:, 64:320].rearrange("p (h d) -> p h d", d=64)
                f = self.b_f
                self.S.add('dve', lambda e: e.reciprocal(out=f[:, 4:8], in_=accs[:, :, 64]), reads=[('pb', 6)], writes=['b_f'])
                self.S.add('dve', lambda e: e.reciprocal(out=f[:, 8:12], in_=accw[:, :, 64]), reads=[('pb', 5)], writes=['b_f'])
                self.tt('dve', f[:, 4:12], f[:, 4:12], GS[:, i, 4:12], ALU.mult, ['b_f', 'GSs'], ['b_f'])
                self.tt('dve', self.b_obf, ocmp, bc(GS[:, i, 0:4], 2, [128, 4, 64]), ALU.mult, [('pb', ob), 'GSs'], ['b_obf'])
                self.tt('dve', self.b_tmp, accs[:, :, 0:64], bc(f[:, 4:8], 2, [128, 4, 64]), ALU.mult, [('pb', 6), 'b_f'], ['b_tmp'])
                self.tt('dve', self.b_obf, self.b_obf, self.b_tmp, ALU.add, ['b_obf', 'b_tmp'], ['b_obf'])
                self.tt('dve', self.b_tmp, accw[:, :, 0:64], bc(f[:, 8:12], 2, [128, 4, 64]), ALU.mult, [('pb', 5), 'b_f'], ['b_tmp'])
                self.tt('dve', self.ob[:, 0:256].rearrange("p (h d) -> p h d", d=64), self.b_obf, self.b_tmp, ALU.add,
                        ['b_obf', 'b_tmp'], ['ob'])
                self.out_tile(i, 'ob', self.ob, 256, 384)
            steps.append((None, combine))
            return steps

        for f_ in sel_steps(0):
            f_()
        for i in range(NT):
            pend = sel_steps(i + 1) if i + 1 < NT else []
            asteps = attn_steps(i)
            gap = max(1, (len(asteps) - 1) // (len(pend) + 1)) if pend else 1
            for n_, a in enumerate(asteps):
                self.emit_step(a[0], a[1])
                if pend and (n_ % gap == gap - 1) and n_ < len(asteps) - 1:
                    pend.pop(0)()
            while pend:
                pend.pop(0)()
        self.flush_steps()
        self.dbg_oT()

    def pass_C(self, l):
        dr = self.dr
        if not hasattr(self, 'c_ksum'):
            self.c_ksum = self.sb('c_ksum', [128, 2, 16], F32)
            self.c_khi = self.sb('c_khi', [128, 2, 16], BF16)
            self.c_klo = self.sb('c_klo', [128, 2, 16], BF16)
            self.c_tmp = self.sb('c_tmp', [128, 2, 16], F32)
            self.c_sc = self.sb('c_sc', [128, 4, 16], F32)
            self.c_mx = self.sb('c_mx', [128, 4, 8], F32)
            self.c_selb = self.sb('c_selb', [128, 4, 16], F32)
            self.c_selbT2 = [t_[0:16, :] for t_ in self.selT]
        self.wkeys = {}
        W = self.Wp
        self.load_w(W, 'Wp', dr['w_in'][l], 2444, 768, 0)
        wreads = list(self.wkeys['Wp'])
        self.load_gains(l, 'q_norm_c', 'k_norm_c', 4, 4)
        self.Ec = self.build_E('Ec', 256, 16, 16384)
        VS4 = self.VS.rearrange("p t (h e) -> p t h e", e=72)
        self.memset('pool', VS4[:, :, 0:4, 64:65], 1.0, ['VS_ones'])
        QKT = self.QKT
        hs_ = {}

        def do_proj(t):
            if t % 4 == 0:
                hs_['sl'] = self.load_hg(t // 4)
            self.proj_tile(hs_['sl'], t % 4, W, wreads, [(0, 512), (512, 256)], [0, 1])
        do_proj(0)
        for t in range(NT):
            self.cp('act', self.pr[:, 0:512], self.pb[0][:, 0:512], [('pb', 0)], ['pr'])
            self.cp('act', VS4[:, t, 0:4, 0:64], self.pb[1][:, 0:256].rearrange("p (h d) -> p h d", d=64), [('pb', 1), 'VS_ones'], [('VS', t)])
            self.qk_post(self.pr[:, 0:512], 8, self.Gt[:, 0:512], t, self.xb[:, 0:512], 'pr', 'xb')
            if t + 1 < NT:
                do_proj(t + 1)
            pT = self.pb[3][:].bitcast(BF16)
            for p in range(4):
                self.tr(pT[:, p * 128:(p + 1) * 128], self.xb[:, p * 128:(p + 1) * 128], self.identb[:], ['xb_dve', 'xb_pool', 'identb'], [('pb', 3)])
            self.cp('act', QKT[:, 0:4, t * 128:(t + 1) * 128], pT[:, 0:512].rearrange("p (a c) -> p a c", c=128), [('pb', 3)], [('QKT', t)])
        allq = [('QKT', t) for t in range(NT)]
        ksum, khi, klo, ktmp = self.c_ksum, self.c_khi, self.c_klo, self.c_tmp
        self.S.add('dve', lambda e: e.tensor_reduce(out=ksum[:], in_=QKT[:, 2:4, :].rearrange("p a (n k) -> p a n k", k=256),
                                                    axis=AX.X, op=ALU.add), reads=allq, writes=['c_ksum'])
        self.cp('dve', khi[:], ksum[:], ['c_ksum'], ['c_khi'])
        self.cp('dve', ktmp[:], khi[:], ['c_khi'], ['c_tmp'])
        self.tt('dve', ktmp[:], ksum[:], ktmp[:], ALU.subtract, ['c_ksum', 'c_tmp'], ['c_tmp'])
        self.cp('dve', klo[:], ktmp[:], ['c_tmp'], ['c_klo'])
        st = {'sb': 0}

        def sel_steps(i):
            nb = i // 2
            if nb == 0:
                return []
            sl = i % 2
            Qz, qzk = self.c_qz[i]
            sc = self.c_sc
            selbT = self.c_selbT2[sl]
            sk = ('c_selbT', sl)

            def s_a():
                for h in range(4):
                    self.mm(self.pb[5][:, h * 16:(h + 1) * 16], Qz[:, h, :], khi[:, h // 2, :], True, False, qzk + ['c_khi'], [('pb', 5)])
                    self.mm(self.pb[5][:, h * 16:(h + 1) * 16], Qz[:, h, :], klo[:, h // 2, :], False, True, qzk + ['c_klo'], [('pb', 5)])

            def s_b():
                self.cp('dve', sc[:].rearrange("p h n -> p (h n)"), self.pb[5][:, 0:64], [('pb', 5)], ['c_sc'])
                if nb < 16:
                    self.memset('dve', sc[:, :, nb:16], -1e30, ['c_sc'])
                for h in range(4):
                    self.S.add('dve', (lambda h: lambda e: e.max(out=self.c_mx[:, h, :], in_=sc[:, h, :]))(h), reads=['c_sc'], writes=['c_mx'])
                self.tt('dve', self.c_selb[:], sc[:], bc(self.c_mx[:, :, 2], 2, [128, 4, 16]), ALU.is_ge, ['c_sc', 'c_mx'], ['c_selb'])
                self.tsc('dve', self.c_selb[:], self.c_selb[:], 1.0, -NEGB, ALU.subtract, ALU.mult, ['c_selb'], ['c_selb'])
                if nb < 16:
                    self.memset('dve', self.c_selb[:, :, nb:16], NEGB, ['c_selb'])

            def s_c():
                for h in range(4):
                    self.tr(self.pb[4][0:16, h * 128:(h + 1) * 128], self.c_selb[:, h, :], self.identf[:], ['c_selb', 'identf'], [('pb', 4)])
                self.cp('act', selbT[:, :], self.pb[4][0:16, 0:512], [('pb', 4)], [sk])
            return [s_a, s_b, s_c]

        def attn_steps(i):
            nb = i // 2
            Qz, qzk = self.c_qz[i]
            selbT = self.c_selbT2[i % 2]
            sk = ('c_selbT', i % 2)
            steps = []

            def pair(j):
                d_ = {}
                past = j < 2 * nb

                def s_():
                    b = (2, 3, 0, 1)[st['sb'] % 4]
                    st['sb'] += 1
                    d_['b'] = b
                    bank = self.pb[b]
                    if past:
                        self.mm(bank[:, 0:512], self.Ec[:, j * 128:(j + 1) * 128], selbT[0:16, 0:512], True, False,
                                ['Ec', sk], [('pb', b)], skip=True)
                    for p in range(2):
                        self.mm(bank[:, p * 256:(p + 1) * 256], QKT[:, 2 + p, j * 128:(j + 1) * 128], Qz[:, 2 * p:2 * p + 2, :],
                                (not past) and p == 0, p == 1, [('QKT', j)] + qzk, [('pb', b)], skip=True)

                def r_():
                    b = d_['b']
                    bank = self.pb[b]
                    ps_ = self.pt_n % 3
                    self.pt_n += 1
                    PT = self.PT[ps_]
                    self.act(PT[:, 0:512], bank[:, 0:512], AF.Exp, [('pb', b)], [('PT', ps_)], scale=0.125)
                    if j == i:
                        self.asel(PT[:, 0:512].rearrange("p (h q) -> p h q", q=128), PT[:, 0:512].rearrange("p (h q) -> p h q", q=128),
                                  [[0, 4], [1, 128]], ALU.is_ge, 0.0, 0, -1, [('PT', ps_)], [('PT', ps_)])
                    for h in range(4):
                        self.mm(self.pb[6][:, h * 72:h * 72 + 65], PT[:, h * 128:(h + 1) * 128], VS4[:, j, h, 0:65],
                                (j == 0 and h == 0), j == i, [('PT', ps_), ('VS', j), 'VS_ones'], [('pb', 6)], skip=True)
                return (s_, r_)
            for j in range(i + 1):
                steps.append(pair(j))

            def fin():
                acc = self.pb[6][:, 0:288].rearrange("p (h e) -> p h e", e=72)
                self.S.add('dve', lambda e: e.reciprocal(out=self.rec[:, 0:4], in_=acc[:, :, 64]), reads=[('pb', 6)], writes=['rec'])
                self.tt('dve', self.ob[:, 0:256].rearrange("p (h d) -> p h d", d=64), acc[:, :, 0:64], bc(self.rec[:, 0:4], 2, [128, 4, 64]),
                        ALU.mult, [('pb', 6), 'rec'], ['ob'])
                self.out_tile(i, 'ob', self.ob, 256, 640)
            steps.append((None, fin))
            return steps

        self.c_qz = {}
        self.c_qz[0] = self.make_qz(0, 4)
        for i in range(NT):
            if i + 1 < NT:
                self.c_qz[i + 1] = self.make_qz(i + 1, 4)
            pend = sel_steps(i + 1) if i + 1 < NT else []
            asteps = attn_steps(i)
            gap = max(1, (len(asteps) - 1) // (len(pend) + 1)) if pend else 1
            for n_, a in enumerate(asteps):
                self.emit_step(a[0], a[1], L=3)
                if pend and (n_ % gap == gap - 1) and n_ < len(asteps) - 1:
                    pend.pop(0)()
            while pend:
                pend.pop(0)()
        self.flush_steps()
        self.dbg_oT()

    def final(self, l, xin, xout):
        dr = self.dr
        if not hasattr(self, 'f_yacc'):
            self.f_yacc = self.b_Pc[:, 0, :]
            self.f_ytmp = self.b_Pc[:, 1, :]
            self.f_yT = self.s1_all[:, 0:4096].rearrange("p (k c) -> p k c", c=512)
        Wzg = self.BIG[:, 0:31744].rearrange("p (k c) -> p k c", c=3968)
        Wbr = self.BIG[:, 31744:38912].rearrange("p (k c) -> p k c", c=1024)
        Wout = self.BIG[:, 38912:47104].rearrange("p (k c) -> p k c", c=1024)
        oz = self.BIG[:, 47104:47104 + 3584].rearrange("p (t k c) -> p t k c", k=7, c=128)
        og = self.BIG[:, 50688:50688 + 3584].rearrange("p (t c) -> p t c", c=896)
        sz = [self.qk_sq[:, 0:512], self.qk_xn[:, 0:512]]
        sg = [self.pr[:, 0:512], self.Gt[:, 0:512]]
        self.wkeys = {}
        for (c0, n, o) in ((1152, 384, 0), (2188, 256, 384), (3212, 256, 640), (3468, 3072, 896)):
            self.load_w(Wzg, 'Wzg', dr['w_in'][l], c0, n, o)
        self.load_w(Wbr, 'Wbr', dr['w_br_a'][l], 0, 1024, 0, nk=3, k0=0)
        self.load_w(Wbr, 'Wbr', dr['w_br_b'][l], 0, 1024, 0, nk=2, k0=3)
        self.load_w(Wbr, 'Wbr', dr['w_br_c'][l], 0, 1024, 0, nk=2, k0=5)
        self.load_w(Wout, 'Wout', dr['w_out'][l], 0, 1024, 0)
        kz, kb, ko = list(self.wkeys['Wzg']), list(self.wkeys['Wbr']), list(self.wkeys['Wout'])
        last = (xout is dr['y'])
        n = 0
        for g in range(NT // 4):
            hsl = self.load_hg(g)
            hgt = self.hg[hsl]
            self.dma(og, dr['oT'][g * 4:(g + 1) * 4].rearrange("t p c -> p t c"),
                     [('oT', g * 4 + i, c) for i in range(4) for c in (0, 384, 640)], ['og'])
            for zc in range(7):
                b = zc % 2
                for k in range(8):
                    self.mm(self.pb[b][:, 0:512], Wzg[:, k, zc * 128:(zc + 1) * 128], hgt[:, :, k, :], k == 0, k == 7,
                            [('hg', hsl)] + kz, [('pb', b)])
                self.act(sz[b], self.pb[b][:, 0:512], AF.Silu, [('pb', b)], [('sz', b)])
                self.tt('dve', oz[:, :, zc, :], og[:, :, zc * 128:(zc + 1) * 128], sz[b].rearrange("p (t c) -> p t c", c=128), ALU.mult,
                        [('sz', b), 'og'], [('oz', zc)])
            for m in range(8):
                for br in range(3):
                    gb = 2 + n % 2
                    ub = 4 + n % 2
                    sgb = sg[n % 2]
                    sgk = ('sg', n % 2)
                    n += 1
                    c0 = 896 + br * 1024 + m * 128
                    for k in range(8):
                        self.mm(self.pb[gb][:, 0:512], Wzg[:, k, c0:c0 + 128], hgt[:, :, k, :], k == 0, k == 7,
                                [('hg', hsl)] + kz, [('pb', gb)])
                    self.act(sgb, self.pb[gb][:, 0:512], AF.Sigmoid, [('pb', gb)], [sgk])
                    kcs = ([0, 1, 2], [3, 4], [5, 6])[br]
                    for ii, kc in enumerate(kcs):
                        self.mm(self.pb[ub][:, 0:512], Wbr[:, kc, m * 128:(m + 1) * 128], oz[:, :, kc, :], ii == 0, ii == len(kcs) - 1,
                                [('oz', kc)] + kb, [('pb', ub)])
                    if br == 0:
                        self.tt('dve', self.f_yacc, self.pb[ub][:, 0:512], sgb, ALU.mult, [('pb', ub), sgk], ['f_yacc'])
                    else:
                        self.tt('dve', self.f_ytmp, self.pb[ub][:, 0:512], sgb, ALU.mult, [('pb', ub), sgk], ['f_ytmp'])
                        if br == 1:
                            self.tt('dve', self.f_yacc, self.f_yacc, self.f_ytmp, ALU.add, ['f_yacc', 'f_ytmp'], ['f_yacc'])
                        else:
                            self.tt('dve', self.f_yT[:, m, :], self.f_yacc, self.f_ytmp, ALU.add, ['f_yacc', 'f_ytmp'], [('yT', m)])
            for tt_ in range(4):
                t = g * 4 + tt_
                xsl = self.xt_n % 2
                self.xt_n += 1
                xt = self.xt[xsl]
                self.dma(xt[:], xin[t * 128:(t + 1) * 128, :], [('x1', t)], [('xt', xsl)])
                osl = xsl
                orow = xt
                for nch in range(2):
                    b = 6 + nch
                    for k in range(8):
                        self.mm(self.pb[b][:, 0:512], self.f_yT[:, k, tt_ * 128:(tt_ + 1) * 128], Wout[:, k, nch * 512:(nch + 1) * 512],
                                k == 0, k == 7, [('yT', k)] + ko, [('pb', b)])
                    self.tt('dve', orow[:, nch * 512:(nch + 1) * 512], self.pb[b][:, 0:512], xt[:, nch * 512:(nch + 1) * 512], ALU.add,
                            [('pb', b), ('xt', xsl)], [('xt', xsl)])
                o = self.dma(xout[t * 128:(t + 1) * 128, :], orow[:], [('xt', xsl)], [('y', t) if last else ('x1', t)])
                if last:
                    self.final_ops.append(o)


def host_layout(sh):
    sh = dict(sh)
    sh['norm_g'] = np.ascontiguousarray(sh['norm_g'].reshape(2, 8, 128).transpose(0, 2, 1))
    sh['cmp_pos'] = np.ascontiguousarray(sh['cmp_pos'].transpose(0, 2, 1))
    for nm in ('cmp_k_w1', 'cmp_v_w1'):
        sh[nm] = np.ascontiguousarray(sh[nm].reshape(2, 32, 64, 128).transpose(0, 2, 1, 3))
    return sh


_CACHE = {}


def kernel(**inputs):
    n = 8
    if 'nc' not in _CACHE:
        _CACHE['nc'] = Builder(2).build()
    nc = _CACHE['nc']
    x = np.ascontiguousarray(inputs['x'], dtype=np.float32)
    pos = np.ascontiguousarray(inputs['positions']).astype(np.int32)
    shared = {}
    for k in ('norm_g', 'w_in', 'q_norm_a', 'k_norm_a', 'q_norm_b', 'k_norm_b', 'q_norm_c', 'k_norm_c', 'cmp_pos',
              'cmp_k_w1', 'cmp_k_w2', 'cmp_v_w1', 'cmp_v_w2', 'w_br_a', 'w_br_b', 'w_br_c', 'w_out'):
        shared[k] = np.ascontiguousarray(inputs[k], dtype=np.float32)
    shared = host_layout(shared)
    in_maps = []
    for c in range(n):
        m = dict(shared)
        m['x'] = x[c]
        m['pos'] = np.ascontiguousarray(pos[c].reshape(NT, 128).T)
        in_maps.append(m)
    res = run_bass_kernel_spmd(nc, in_maps, core_ids=list(range(n)))
    return np.stack([np.asarray(r['y'], dtype=np.float32) for r in res.results], axis=0)
```

```python
import contextlib
import math
import numpy as np
import concourse.bass as bass
import concourse.mybir as mybir
from concourse.bass_utils import run_bass_kernel_spmd

F32 = mybir.dt.float32
BF16 = mybir.dt.bfloat16
I32 = mybir.dt.int32
AF = mybir.ActivationFunctionType
ALU = mybir.AluOpType
AX = mybir.AxisListType

SAME_ENG_SYNC = {'pe': False, 'act': True, 'dve': True, 'pool': True, 'sp': True}
N_DMA_SEMS = 8

S_LEN = 4096
D = 1024
NT = 32
INW = 6540
EPS = 1e-6
NEGB = -30000.0


class _Op:
    __slots__ = ('eng', 'fn', 'deps', 'dma', 'signal', 'sem', 'val', 'idx', 'prev')


class Sched:
    def __init__(self, nc):
        self.nc = nc
        self.ops = []
        self.last_w = {}
        self.readers = {}
        self.fence_keys = []

    def add(self, eng, fn, reads=(), writes=(), dma=False):
        op = _Op()
        op.eng = eng
        op.fn = fn
        op.dma = dma
        op.signal = False
        op.sem = None
        op.val = 0
        op.idx = len(self.ops)
        deps = set()
        if self.fence_keys:
            reads = list(reads) + self.fence_keys
        for k in reads:
            w = self.last_w.get(k)
            if w is not None:
                deps.add(w)
        for k in writes:
            w = self.last_w.get(k)
            if w is not None:
                deps.add(w)
            for r in self.readers.get(k, ()):
                deps.add(r)
        op.deps = deps
        for k in reads:
            self.readers.setdefault(k, []).append(op.idx)
        for k in writes:
            self.last_w[k] = op.idx
            self.readers[k] = []
        self.ops.append(op)
        return op.idx

    def emit(self, final_waits=()):
        nc = self.nc
        ops = self.ops
        engs = ['pe', 'act', 'dve', 'pool', 'sp']
        for op in ops:
            need = set()
            best = {}
            for d in op.deps:
                Dp = ops[d]
                if Dp.eng == op.eng and not Dp.dma and not op.dma and not SAME_ENG_SYNC[op.eng]:
                    continue
                if Dp.dma:
                    need.add(d)
                    Dp.signal = True
                elif best.get(Dp.eng, -1) < d:
                    best[Dp.eng] = d
            for d in best.values():
                need.add(d)
                ops[d].signal = True
            op.deps = need
        for d in final_waits:
            ops[d].signal = True
        with contextlib.ExitStack() as st:
            esem = {e: st.enter_context(nc.semaphore('s_' + e)) for e in engs}
            dsem = {e: [st.enter_context(nc.semaphore('d_%s%d' % (e, i))) for i in range(N_DMA_SEMS)]
                    for e in ('sp', 'pool', 'act')}
            ecount = {e: 0 for e in engs}
            dcount = {e: 0 for e in engs}
            for op in ops:
                if op.dma:
                    op.signal = True
                if not op.signal:
                    continue
                if op.dma:
                    i = dcount[op.eng]
                    dcount[op.eng] += 1
                    op.sem = dsem[op.eng][i % N_DMA_SEMS]
                    op.val = 16 * (i // N_DMA_SEMS + 1)
                    op.prev = (op.sem, op.val - 16) if i >= N_DMA_SEMS else None
                else:
                    ecount[op.eng] += 1
                    op.sem = esem[op.eng]
                    op.val = ecount[op.eng]
            per = {e: [] for e in engs}
            for op in ops:
                per[op.eng].append(op)
            block = st.enter_context(nc.Block())

            def run(e, name, extra=()):
                waited = {}
                for op in per[name]:
                    ws = {}
                    for d in op.deps:
                        Dp = ops[d]
                        key = id(Dp.sem)
                        if waited.get(key, 0) >= Dp.val:
                            continue
                        if key not in ws or ws[key][1] < Dp.val:
                            ws[key] = (Dp.sem, Dp.val)
                    if op.dma and op.prev is not None:
                        key = id(op.prev[0])
                        if waited.get(key, 0) < op.prev[1] and (key not in ws or ws[key][1] < op.prev[1]):
                            ws[key] = op.prev
                    for key, (sem, val) in ws.items():
                        e.wait_ge(sem, val)
                        waited[key] = val
                    ins = op.fn(e)
                    if op.signal:
                        ins.then_inc(op.sem, 16 if op.dma else 1)
                for d in extra:
                    Dp = ops[d]
                    e.wait_ge(Dp.sem, Dp.val)

            @block.tensor
            def _(e):
                run(e, 'pe')

            @block.scalar
            def _(e):
                run(e, 'act')

            @block.vector
            def _(e):
                run(e, 'dve')

            @block.gpsimd
            def _(e):
                run(e, 'pool')

            @block.sync
            def _(e):
                run(e, 'sp', extra=final_waits)
        self.stats = {e: len(per[e]) for e in engs}
        self.stats['signals'] = dict(ecount)
        self.stats['dmasig'] = dict(dcount)


def bc(ap, axis, shape):
    return ap.unsqueeze(axis).to_broadcast(shape)


class Builder:
    def __init__(self, n_layers=2, dbg=None, stop_after=None):
        self.n_layers = n_layers
        self.dbg = dbg or ()
        self.stop_after = stop_after
        nc = bass.Bass("TRN2", target_bir_lowering=False)
        self.nc = nc
        self.S = Sched(nc)
        self.st = contextlib.ExitStack()
        self.uid = 0

    def sb(self, name, shape, dt):
        return self.st.enter_context(self.nc.sbuf_tensor(name, shape, dt))

    def ps(self, name, shape, dt=F32):
        return self.st.enter_context(self.nc.psum_tensor(name, shape, dt))

    def mm(self, out, lhsT, rhs, start, stop, r, w, skip=False):
        if skip:
            self.S.add('pe', lambda e: e.matmul(out, lhsT=lhsT, rhs=rhs, start=start, stop=stop, skip_group_check=True), reads=r, writes=w)
        else:
            self.S.add('pe', lambda e: e.matmul(out, lhsT=lhsT, rhs=rhs, start=start, stop=stop), reads=r, writes=w)

    def tr(self, out, in_, ident, r, w):
        self.S.add('pe', lambda e: e.transpose(out=out, in_=in_, identity=ident), reads=r, writes=w)

    def act(self, out, in_, func, r, w, bias=None, scale=None, accum_out=None):
        kw = {}
        if bias is not None:
            kw['bias'] = bias
        if scale is not None:
            kw['scale'] = scale
        if accum_out is not None:
            kw['accum_out'] = accum_out
        self.S.add('act', lambda e: e.activation(out=out, in_=in_, func=func, **kw), reads=r, writes=w)

    def rsqrt(self, out, in_, scale, r, w):
        self.act(out, in_, AF.Ln, r, w, bias=self.epsc[:out.shape[0], 0:1], scale=scale)
        self.act(out, out, AF.Exp, w, w, scale=-0.5)

    def tt(self, eng, out, in0, in1, op, r, w):
        self.S.add(eng, lambda e: e.tensor_tensor(out=out, in0=in0, in1=in1, op=op), reads=r, writes=w)

    def tsc(self, eng, out, in0, s1, s2, op0, op1, r, w):
        if op1 is None:
            self.S.add(eng, lambda e: e.tensor_scalar(out=out, in0=in0, scalar1=s1, scalar2=None, op0=op0), reads=r, writes=w)
        else:
            self.S.add(eng, lambda e: e.tensor_scalar(out=out, in0=in0, scalar1=s1, scalar2=s2, op0=op0, op1=op1), reads=r, writes=w)

    def cp(self, eng, out, in_, r, w):
        if eng == 'act':
            self.S.add('act', lambda e: e.copy(out=out, in_=in_), reads=r, writes=w)
        else:
            self.S.add(eng, lambda e: e.tensor_copy(out=out, in_=in_), reads=r, writes=w)

    def memset(self, eng, ap, val, w):
        self.S.add(eng, lambda e: e.memset(ap, val), writes=w)

    def asel(self, out, in_, pattern, op, fill, base, cm, r, w):
        self.S.add('pool', lambda e: e.affine_select(out=out, in_=in_, pattern=pattern, compare_op=op, fill=fill,
                                                     base=base, channel_multiplier=cm), reads=r, writes=w)

    def dma(self, out, in_, r, w, q='sp'):
        return self.S.add(q, lambda e: e.dma_start(out=out, in_=in_), reads=r, writes=w, dma=True)

    def build_E(self, nm, blk, npart, off):
        E = self.BIG[0:npart, off:off + S_LEN]
        self.memset('pool', E, 1.0, [nm])
        self.asel(E, E, [[1, S_LEN]], ALU.is_ge, 0.0, 0, -blk, [nm], [nm])
        self.asel(E, E, [[-1, S_LEN]], ALU.is_ge, 0.0, blk - 1, blk, [nm], [nm])
        return E

    def make_qz(self, i, nh):
        sl = self.qz_n % 2
        self.qz_n += 1
        Qz = self.Qz[sl]
        npair = nh // 2
        for par in range(2):
            self.cp('pool', Qz[par * 64:(par + 1) * 64, par:nh:2, :], self.QKT[par * 64:(par + 1) * 64, 0:npair, i * 128:(i + 1) * 128],
                    [('QKT', i), 'Qz0'], [('Qz', sl, par)])
        return Qz, [('Qz', sl, 0), ('Qz', sl, 1)]

    def emit_step(self, s_fn, r_fn, L=1):
        if s_fn is not None:
            s_fn()
        if not hasattr(self, '_pend'):
            self._pend = []
        self._pend.append(r_fn)
        while len(self._pend) > L:
            self._pend.pop(0)()

    def flush_steps(self):
        for p in getattr(self, '_pend', []):
            p()
        self._pend = []

    def fence(self):
        self.S.fence_keys = []
        self.fence_n = getattr(self, 'fence_n', 0) + 1
        n = self.fence_n
        fs = self.fsc
        self.mm(self.pb[7][0:1, 0:1], self.identb[0:1, 0:1], self.identb[0:1, 0:1], True, True, ['identb'], [('pb', 7), ('fence', 'pe', n)])
        self.cp('act', fs[0:1, 0:1], fs[0:1, 4:5], ['fsc'], [('fence', 'act', n), 'fscw_act'])
        self.cp('dve', fs[0:1, 1:2], fs[0:1, 5:6], ['fsc'], [('fence', 'dve', n), 'fscw_dve'])
        self.cp('pool', fs[0:1, 2:3], fs[0:1, 6:7], ['fsc'], [('fence', 'pool', n), 'fscw_pool'])
        self.S.fence_keys = [('fence', e, n) for e in ('pe', 'act', 'dve', 'pool')]

    def build(self):
        nc = self.nc
        dr = {}

        def din(name, shape, dt=F32):
            dr[name] = nc.dram_tensor(name, shape, dt, kind="ExternalInput").ap()

        din('x', [S_LEN, D])
        din('pos', [128, NT], I32)
        din('norm_g', [2, 128, 8])
        din('w_in', [2, D, INW])
        for nm in ('q_norm_a', 'k_norm_a', 'q_norm_b', 'k_norm_b', 'q_norm_c', 'k_norm_c'):
            din(nm, [2, 64])
        din('cmp_pos', [2, 64, 32])
        din('cmp_k_w1', [2, 64, 32, 128])
        din('cmp_k_w2', [2, 128, 64])
        din('cmp_v_w1', [2, 64, 32, 128])
        din('cmp_v_w2', [2, 128, 64])
        din('w_br_a', [2, 384, D])
        din('w_br_b', [2, 256, D])
        din('w_br_c', [2, 256, D])
        din('w_out', [2, D, D])
        dr['y'] = nc.dram_tensor('y', [S_LEN, D], F32, kind="ExternalOutput").ap()
        dr['x1'] = nc.dram_tensor('x1s', [S_LEN, D], F32).ap()
        dr['hnT'] = nc.dram_tensor('hnTs', [NT, 128, 1024], BF16).ap()
        dr['oT'] = nc.dram_tensor('oTs', [NT, 128, 7 * 128], BF16).ap()
        for nm, shape, dt in self.dbg:
            dr[nm] = nc.dram_tensor(nm, shape, dt, kind="ExternalOutput").ap()
        self.dr = dr
        self.final_ops = []
        with self.st:
            self.setup()
            for l in range(self.n_layers):
                xin = dr['x'] if l == 0 else dr['x1']
                xout = dr['y'] if l == self.n_layers - 1 else dr['x1']
                self.layer(l, xin, xout)
            self.S.emit(final_waits=self.final_ops)
        return nc

    def setup(self):
        nc = self.nc
        dr = self.dr
        self.alloc_common()
        self.identf = self.sb('identf', [128, 128], F32)
        self.identb = self.sb('identb', [128, 128], BF16)
        self.onesf = self.sb('onesf', [128, 128], F32)
        self.memset('pool', self.identf[:], 1.0, ['identf'])
        self.asel(self.identf[:], self.identf[:], [[-1, 128]], ALU.is_equal, 0.0, 0, 1, ['identf'], ['identf'])
        self.cp('dve', self.identb[:], self.identf[:], ['identf'], ['identb'])
        self.memset('pool', self.onesf[:], 1.0, ['onesf'])
        self.epsc = self.sb('epsc', [128, 1], F32)
        self.memset('dve', self.epsc[:], EPS, ['epsc'])
        for q_ in self.Qz:
            self.memset('pool', q_[:], 0.0, ['Qz0'])
        self.fsc = self.sb('fsc', [128, 8], F32)
        self.memset('dve', self.fsc[:], 0.0, ['fsc'])
        posi = self.sb('posi', [128, NT], I32)
        posf = self.sb('posf', [128, NT], F32)
        fr = self.sb('fr', [128, 8], F32)
        wpF = self.BIG[:, 46592:46592 + 9216].bitcast(F32)
        wpI = self.BIG[:, 46592:46592 + 9216].bitcast(I32)
        ang = wpF[:, 0:256].rearrange("p (t f) -> p t f", f=8)
        tmpa = wpF[:, 256:512].rearrange("p (t f) -> p t f", f=8)
        self.cos = self.sb('cos', [128, NT, 8], F32)
        self.sin = self.sb('sin', [128, NT, 8], F32)
        self.dma(posi[:], dr['pos'][:, :], [], ['posi'])
        self.cp('dve', posf[:], posi[:], ['posi'], ['posf'])
        for i in range(8):
            f = float(np.float32(500000.0) ** np.float32(-i / 8.0))
            self.memset('dve', fr[:, i:i + 1], f, ['fr'])
        self.tt('dve', ang[:], bc(posf[:], 2, [128, NT, 8]), bc(fr[:], 1, [128, NT, 8]), ALU.mult, ['posf', 'fr'], ['ang'])
        PI = math.pi
        HI = 6.28125
        LO = 2 * PI - 6.28125
        ni = wpI[:, 512:768].rearrange("p (t f) -> p t f", f=8)
        nf = wpF[:, 768:1024].rearrange("p (t f) -> p t f", f=8)
        rr = wpF[:, 1024:1280].rearrange("p (t f) -> p t f", f=8)
        self.tsc('dve', tmpa[:], ang[:], 1.0 / (2 * PI), None, ALU.mult, None, ['ang'], ['tmpa'])
        self.cp('dve', ni[:], tmpa[:], ['tmpa'], ['rr_ni'])
        self.cp('dve', nf[:], ni[:], ['rr_ni'], ['rr_nf'])
        self.S.add('dve', lambda e: e.scalar_tensor_tensor(out=rr[:], in0=nf[:], scalar=-HI, in1=ang[:], op0=ALU.mult, op1=ALU.add),
                   reads=['rr_nf', 'ang'], writes=['rr_r'])
        self.S.add('dve', lambda e: e.scalar_tensor_tensor(out=rr[:], in0=nf[:], scalar=-LO, in1=rr[:], op0=ALU.mult, op1=ALU.add),
                   reads=['rr_nf', 'rr_r'], writes=['rr_r'])

        def wrap(buf, key):
            self.tsc('dve', tmpa[:], buf[:], PI, -2 * PI, ALU.is_gt, ALU.mult, [key], ['tmpa'])
            self.tt('dve', buf[:], buf[:], tmpa[:], ALU.add, [key, 'tmpa'], [key])
            self.tsc('dve', tmpa[:], buf[:], -PI, 2 * PI, ALU.is_lt, ALU.mult, [key], ['tmpa'])
            self.tt('dve', buf[:], buf[:], tmpa[:], ALU.add, [key, 'tmpa'], [key])
            self.tsc('dve', buf[:], buf[:], -3.141592, 3.141592, ALU.max, ALU.min, [key], [key])
        wrap(rr, 'rr_r')
        self.act(self.sin[:], rr[:], AF.Sin, ['rr_r'], ['sin'])
        self.tsc('dve', rr[:], rr[:], PI / 2, None, ALU.add, None, ['rr_r', 'sin'], ['rr_r'])
        wrap(rr, 'rr_r')
        self.act(self.cos[:], rr[:], AF.Sin, ['rr_r'], ['cos'])
        scrF = self.BIG[:, 0:24576].bitcast(F32)
        scrI = self.BIG[:, 0:24576].bitcast(I32)

        def carve(src, k):
            return src[:, k * 2176:(k + 1) * 2176].rearrange("p (o q) -> p o q", q=128)
        dA = carve(scrF, 0)
        dAi = carve(scrI, 1)
        t1 = carve(scrF, 2)
        t2 = carve(scrF, 3)
        t3 = carve(scrF, 4)
        t4 = carve(scrI, 4)
        self.MA = self.sb('MA', [128, 17, 128], BF16)
        self.S.add('pool', lambda e: e.iota(dAi[:], pattern=[[128, 17], [1, 128]], base=0, channel_multiplier=-1), writes=['dAi'])
        self.cp('dve', dA[:], dAi[:], ['dAi'], ['dA'])
        self.tsc('dve', t1[:], dA[:], 128.0, None, ALU.is_le, None, ['dA'], ['mt1'])
        self.tsc('dve', t4[:], dAi[:], 3, None, ALU.bitwise_and, None, ['dAi'], ['mt3'])
        self.cp('dve', t2[:], t4[:], ['mt3'], ['mt2'])
        self.tsc('dve', t2[:], t2[:], 0.0, None, ALU.is_equal, None, ['mt2'], ['mt2'])
        self.tsc('dve', t3[:], dA[:], 512.0, None, ALU.is_le, None, ['dA'], ['mt3'])
        self.tt('dve', t2[:], t2[:], t3[:], ALU.mult, ['mt2', 'mt3'], ['mt2'])
        self.tt('dve', t1[:], t1[:], t2[:], ALU.add, ['mt1', 'mt2'], ['mt1'])
        self.tsc('dve', t4[:], dAi[:], 15, None, ALU.bitwise_and, None, ['dAi', 'mt2'], ['mt3'])
        self.cp('dve', t2[:], t4[:], ['mt3', 'mt1'], ['mt2'])
        self.tsc('dve', t2[:], t2[:], 0.0, None, ALU.is_equal, None, ['mt2'], ['mt2'])
        self.tsc('dve', t3[:], dA[:], 2048.0, None, ALU.is_le, None, ['dA', 'mt2'], ['mt3'])
        self.tt('dve', t2[:], t2[:], t3[:], ALU.mult, ['mt2', 'mt3'], ['mt2'])
        self.tt('dve', t1[:], t1[:], t2[:], ALU.add, ['mt1', 'mt2'], ['mt1'])
        self.tsc('dve', t2[:], dA[:], 0.0, None, ALU.is_ge, None, ['dA', 'mt1'], ['mt2'])
        self.tt('dve', self.MA[:], t1[:], t2[:], ALU.mult, ['mt1', 'mt2'], ['MA'])
        self.AM = self.sb('AM', [128, NT, 64], F32)
        vsF = self.BIG[:, 32768:32768 + NT * 432].bitcast(F32)
        vsI = self.BIG[:, 32768:32768 + NT * 432].bitcast(I32)
        am1 = vsF[:, 0:2048].rearrange("p (t j) -> p t j", j=64)
        am1i = vsI[:, 2048:4096].rearrange("p (t j) -> p t j", j=64)
        am2 = vsF[:, 4096:6144].rearrange("p (t j) -> p t j", j=64)
        for a in range(2):
            self.S.add('pool', (lambda a: lambda e: e.iota(am1i[a * 64:(a + 1) * 64], pattern=[[-2, NT], [1, 64]], base=-a,
                                                           channel_multiplier=0))(a),
                       writes=['am1i_%d' % a])
        self.cp('dve', am1[:], am1i[:], ['am1i_0', 'am1i_1'], ['am1_0', 'am1_1'])
        self.tsc('dve', am2[:], am1[:], 0.0, -1e30, ALU.is_gt, ALU.mult, ['am1_0', 'am1_1'], ['am2'])
        self.tsc('dve', am1[:], am1[:], -1.0, 1e4, ALU.is_ge, ALU.mult, ['am1_0', 'am1_1', 'am2'], ['am1', 'am1_0', 'am1_1'])
        self.tt('dve', self.AM[:], am1[:], am2[:], ALU.add, ['am1', 'am2'], ['AM'])
        self.tsc('dve', self.AM[:, :, 0:1], self.AM[:, :, 0:1], 1e4, None, ALU.add, None, ['AM'], ['AM'])
        self.cover = self.sb('cover', [128, 2, 64], F32)
        self.memset('pool', self.cover[:], 1.0, ['cover'])
        self.asel(self.cover[:], self.cover[:], [[-128, 2], [4, 64]], ALU.is_ge, 0.0, 3, -1, ['cover'], ['cover'])
        self.asel(self.cover[:], self.cover[:], [[128, 2], [-4, 64]], ALU.is_ge, 0.0, 1, 1, ['cover'], ['cover'])
        self.pb = [self.ps('pb%d' % i, [128, 512], F32) for i in range(8)]
        self.xt = [self.sb('xt%d' % i, [128, D], F32) for i in range(2)]
        self.hg = [self.sb('hg%d' % i, [128, 4, 8, 128], BF16) for i in range(2)]
        self.wstage = [self.sb('wst%d' % i, [128, 8, 128], F32) for i in range(2)]
        self.wst_n = 0
        self.hg_n = 0
        self.xt_n = 0
        if 'dbg_cs' in [d[0] for d in self.dbg]:
            o = self.dma(self.dr['dbg_cs'][:, 0:256], self.cos[:].rearrange("p t f -> p (t f)"), ['cos'], [])
            self.final_ops.append(o)
            o = self.dma(self.dr['dbg_cs'][:, 256:512], self.sin[:].rearrange("p t f -> p (t f)"), ['sin'], [])
            self.final_ops.append(o)
            mtmp = self.sb('mtmp', [128, 17 * 128], F32)
            self.cp('dve', mtmp[:], self.MA[:].rearrange("p o q -> p (o q)"), ['MA'], ['mtmp'])
            o = self.dma(self.dr['dbg_ma'][:, :], mtmp[:], ['mtmp'], [])
            self.final_ops.append(o)
            o = self.dma(self.dr['dbg_am'][:, :], self.AM[:].rearrange("p t f -> p (t f)"), ['AM'], [])
            self.final_ops.append(o)
            o = self.dma(self.dr['dbg_cov'][:, :], self.cover[:].rearrange("p t f -> p (t f)"), ['cover'], [])
            self.final_ops.append(o)

    def load_w(self, W, wkey, src, c0, n, o, nk=8, k0=0, eng_cycle=('pool', 'dve')):
        done = 0
        while done < n:
            m = min(128, n - done)
            sl = self.wst_n % 4
            self.wst_n += 1
            if sl < 2:
                stg = self.wstage[sl]
                skey = ('wst', sl)
            else:
                stg = self.xt[sl - 2][:].rearrange("p (k c) -> p k c", c=128)
                skey = ('xt', sl - 2)
            self.dma(stg[:, 0:nk, 0:m], src[:, c0 + done:c0 + done + m].rearrange("(k p) c -> p k c", p=128),
                     [], [skey])
            eng = eng_cycle[self.wst_n % len(eng_cycle)]
            self.cp(eng, W[:, k0:k0 + nk, o + done:o + done + m], stg[:, 0:nk, 0:m], [skey], [(wkey, self.wst_n)])
            self.wkeys.setdefault(wkey, []).append((wkey, self.wst_n))
            done += m

    def layer(self, l, xin, xout):
        if l > 0:
            self.fence()
        self.stage1(l, xin)
        if self.stop_after == 'stage1':
            return
        only = getattr(self, 'only', None)
        if only is None or 'A' in only:
            self.fence()
            self.pass_A(l)
        if self.stop_after in ('A', 'Aproj'):
            return
        if only is None or 'C' in only:
            self.fence()
            self.pass_C(l)
        if self.stop_after == 'C':
            return
        if only is None or 'B' in only:
            self.fence()
            self.pass_B(l)
        if self.stop_after == 'B':
            return
        self.fence()
        self.final(l, xin, xout)

    def stage1(self, l, xin):
        dr = self.dr
        if l == 0:
            self.gT = self.sb('gT', [128, 8], F32)
            self.s1_all = self.sb('s1all', [128, 4096], BF16)
            self.s1_sq = self.s1_all[:, 0:1024]
            self.s1_ss = self.sb('s1ss', [128, 2], F32)
            self.s1_xs = self.s1_all[:, 1024:2048]
            self.s1_hT = [self.s1_all[:, 2048 + i * 1024:3072 + i * 1024].rearrange("p (k c) -> p k c", c=128) for i in range(2)]
        self.dma(self.gT[:], dr['norm_g'][l], [], ['gT'])
        ring = [(self.xt[0][:], ('xt', 0)), (self.xt[1][:], ('xt', 1)),
                (self.wstage[0][:].rearrange("p k c -> p (k c)"), ('wst', 0)), (self.wstage[1][:].rearrange("p k c -> p (k c)"), ('wst', 1))]
        for t in range(NT):
            sl = t % 2
            xt_ap, xkey = ring[t % 4]
            self.dma(xt_ap, xin[t * 128:(t + 1) * 128, :], [('x1', t)], [xkey])
            ss = self.s1_ss[:, sl:sl + 1]
            xs = (self.s1_sq, self.s1_xs)[sl]
            self.act(xs, xt_ap, AF.Square, [xkey], [('s1xs', sl), ('s1ss', sl)], accum_out=ss)
            self.rsqrt(ss, ss, 1.0 / D, [('s1ss', sl)], [('s1ss', sl)])
            self.act(xs, xt_ap, AF.Copy, [xkey, ('s1ss', sl)], [('s1xs', sl)], scale=ss)
            pT = self.pb[sl][:].bitcast(BF16)
            for k in range(8):
                self.tr(pT[:, k * 128:(k + 1) * 128], xs[:, k * 128:(k + 1) * 128], self.identb[:],
                        [('s1xs', sl), 'identb'], [('pb', sl)])
            hT = self.s1_hT[sl]
            self.tt('dve', hT, pT[:, 0:1024].rearrange("p (k t) -> p k t", k=8), bc(self.gT[:], 2, [128, 8, 128]), ALU.mult,
                    [('pb', sl), 'gT'], [('s1hT', sl)])
            self.dma(dr['hnT'][t], hT.rearrange("p k t -> p (k t)"), [('s1hT', sl)], [('hnT', t)])
        if 'dbg_hnT' in [d[0] for d in self.dbg]:
            for t in range(NT):
                o = self.dma(dr['dbg_hnT'][t], dr['hnT'][t], [('hnT', t)], [])
                self.final_ops.append(o)

    def load_hg(self, g):
        sl = self.hg_n % 2
        self.hg_n += 1
        self.dma(self.hg[sl][:].rearrange("p t k c -> p t (k c)"), self.dr['hnT'][g * 4:(g + 1) * 4].rearrange("t p c -> p t c"),
                 [('hnT', g * 4 + i) for i in range(4)], [('hg', sl)])
        return sl

    def proj_tile(self, hsl, tt, W, wreads, chunks, banks):
        for (c0, n), b in zip(chunks, banks):
            for k in range(8):
                self.mm(self.pb[b][:, 0:n], self.hg[hsl][:, tt, k, :], W[:, k, c0:c0 + n], k == 0, k == 7,
                        [('hg', hsl)] + wreads, [('pb', b)])

    def qk_post(self, pr, nh, Gt, t, xb, prk, xbk):
        W_ = nh * 64
        sq = self.qk_sq[:, 0:W_]
        ss = self.qk_ss[:, 0:nh]
        self.tt('dve', sq, pr, pr, ALU.mult, [prk], ['qk_sq'])
        self.S.add('dve', lambda e: e.tensor_reduce(out=ss, in_=sq.rearrange("p (h d) -> p h d", d=64), axis=AX.X, op=ALU.add),
                   reads=['qk_sq'], writes=['qk_ss'])
        self.rsqrt(ss, ss, 1.0 / 64, ['qk_ss'], ['qk_ss'])
        na = (2 * nh + 2) // 3
        pr3 = pr.rearrange("p (h d) -> p h d", d=64)
        xn3 = self.qk_xn[:, 0:W_].rearrange("p (h d) -> p h d", d=64)
        xb3 = xb.rearrange("p (h d) -> p h d", d=64)
        G3 = Gt.rearrange("p (h d) -> p h d", d=64)
        for eng, h0, h1 in (('dve', 0, na), ('pool', na, nh)):
            n_ = h1 - h0
            if n_ <= 0:
                continue
            kx = 'qk_xn_' + eng
            xn_ = xn3[:, h0:h1, :]
            self.tt(eng, xn_, pr3[:, h0:h1, :], bc(ss[:, h0:h1], 2, [128, n_, 64]), ALU.mult, [prk, 'qk_ss'], [kx])
            self.tt(eng, xn_, xn_, G3[:, h0:h1, :], ALU.mult, [kx, 'Gt'], [kx])
            cosb = bc(self.cos[:, t, :], 1, [128, n_, 8])
            sinb = bc(self.sin[:, t, :], 1, [128, n_, 8])
            r = [self.qk_r[i][:, h0:h1, :] for i in range(4)]
            rk = ['qk_r%d_%s' % (i, eng) for i in range(4)]
            self.tt(eng, r[0], xn_[:, :, 0:8], cosb, ALU.mult, [kx, 'cos'], [rk[0]])
            self.tt(eng, r[1], xn_[:, :, 8:16], sinb, ALU.mult, [kx, 'sin'], [rk[1]])
            self.tt(eng, r[2], xn_[:, :, 8:16], cosb, ALU.mult, [kx, 'cos'], [rk[2]])
            self.tt(eng, r[3], xn_[:, :, 0:8], sinb, ALU.mult, [kx, 'sin'], [rk[3]])
            xk = xbk + '_' + eng
            self.tt(eng, xb3[:, h0:h1, 0:8], r[0], r[1], ALU.subtract, [rk[0], rk[1]], [xk])
            self.tt(eng, xb3[:, h0:h1, 8:16], r[2], r[3], ALU.add, [rk[2], rk[3]], [xk])
            self.cp(eng, xb3[:, h0:h1, 16:64], xn_[:, :, 16:64], [kx], [xk])

    def alloc_common(self):
        if hasattr(self, 'qk_sq'):
            return
        self.qk_sq = self.sb('qk_sq', [128, 768], F32)
        self.qk_ss = self.sb('qk_ss', [128, 12], F32)
        self.qk_xn = self.sb('qk_xn', [128, 768], F32)
        self.qk_r = [self.sb('qk_r%d' % i, [128, 12, 8], F32) for i in range(4)]
        self.pr = self.sb('pr', [128, 768], F32)
        self.xb = self.sb('xb', [128, 768], BF16)
        self.Gt = self.sb('Gt', [128, 768], F32)
        self.g64 = self.sb('g64', [128, 2, 64], F32)
        self.PT = [self.sb('PT%d' % i, [128, 768], BF16) for i in range(3)]
        self.pt_n = 0
        self.ob = self.sb('ob', [128, 384], BF16)
        self.rec = self.sb('rec', [128, 12], F32)
        self.oTt = [self.sb('oTt%d' % i, [128, 384], BF16) for i in range(2)]
        self.selT = [self.sb('selT%d' % i, [64, 512], BF16) for i in range(2)]
        self.Qz = [self.sb('Qz%d' % i, [128, 6, 128], BF16) for i in range(2)]
        self.qz_n = 0
        self.ot_n = 0
        self.BIG = self.sb('BIG', [128, 57344], BF16)
        self.QKT = self.BIG[:, 0:32768].rearrange("p (a c) -> p a c", c=S_LEN)
        self.VS = self.BIG[:, 32768:32768 + NT * 432].rearrange("p (t c) -> p t c", c=432)
        self.Wp = self.BIG[:, 46592:46592 + 9216].rearrange("p (k c) -> p k c", c=1152)

    def load_gains(self, l, qn, kn, nq, nk):
        dr = self.dr
        self.dma(self.g64[:, 0, :], dr[qn][l].partition_broadcast(128), [], ['g64q'])
        self.dma(self.g64[:, 1, :], dr[kn][l].partition_broadcast(128), [], ['g64k'])
        G3 = self.Gt[:, 0:(nq + nk) * 64].rearrange("p (h d) -> p h d", d=64)
        self.cp('dve', G3[:, 0:nq, :], bc(self.g64[:, 0, :], 1, [128, nq, 64]), ['g64q'], ['Gt'])
        self.cp('dve', G3[:, nq:nq + nk, :], bc(self.g64[:, 1, :], 1, [128, nk, 64]), ['g64k'], ['Gt'])

    def out_tile(self, i, acc_key, ob_ap, ncol, c0):
        npair = ncol // 128
        pT = self.pb[7][:].bitcast(BF16)
        for p in range(npair):
            self.tr(pT[:, p * 128:(p + 1) * 128], ob_ap[:, p * 128:(p + 1) * 128], self.identb[:], [acc_key, 'identb'], [('pb', 7)])
        sl = self.ot_n % 2
        self.ot_n += 1
        self.cp('act', self.oTt[sl][:, 0:ncol], pT[:, 0:ncol], [('pb', 7)], [('oTt', sl)])
        self.dma(self.dr['oT'][i][:, c0:c0 + ncol], self.oTt[sl][:, 0:ncol], [('oTt', sl)], [('oT', i, c0)])

    def pass_A(self, l):
        dr = self.dr
        self.alloc_common()
        self.wkeys = {}
        W = self.Wp
        self.load_w(W, 'Wp', dr['w_in'][l], 0, 1152, 0)
        wreads = list(self.wkeys['Wp'])
        self.load_gains(l, 'q_norm_a', 'k_norm_a', 6, 6)
        VS4 = self.VS.rearrange("p t (h e) -> p t h e", e=72)
        self.S.add('pool', lambda e: e.memset(VS4[:, :, :, 64:65], 1.0), reads=['MA', 'AM'], writes=['VS_ones'])
        QKT = self.QKT
        hs_ = {}

        def do_proj(t):
            if t % 4 == 0:
                hs_['sl'] = self.load_hg(t // 4)
            self.proj_tile(hs_['sl'], t % 4, W, wreads, [(0, 384), (384, 384), (768, 384)], [0, 1, 2])
        do_proj(0)
        for t in range(NT):
            self.cp('act', self.pr[:, 0:384], self.pb[0][:, 0:384], [('pb', 0)], ['pr'])
            self.cp('act', self.pr[:, 384:768], self.pb[1][:, 0:384], [('pb', 1)], ['pr'])
            self.cp('act', VS4[:, t, :, 0:64], self.pb[2][:, 0:384].rearrange("p (h d) -> p h d", d=64), [('pb', 2), 'VS_ones', 'MA', 'AM'], [('VS', t)])
            self.qk_post(self.pr[:, 0:768], 12, self.Gt[:, 0:768], t, self.xb[:, 0:768], 'pr', 'xb')
            if t + 1 < NT:
                do_proj(t + 1)
            pT = self.pb[3][:].bitcast(BF16)
            for p in range(6):
                self.tr(pT[:, p * 128:(p + 1) * 128], self.xb[:, p * 128:(p + 1) * 128], self.identb[:], ['xb_dve', 'xb_pool', 'identb'], [('pb', 3)])
            self.cp('act', QKT[:, 0:6, t * 128:(t + 1) * 128], pT[:, 0:768].rearrange("p (a c) -> p a c", c=128), [('pb', 3), 'MA', 'AM'], [('QKT', t)])
        if self.stop_after == 'Aproj':
            return
        stA = {'sb': 0}

        def a_pair(i, j, j0, Qz, qzk):
            o = i - j
            d_ = {}

            def s_():
                bS = [(2, 3), (4, 5), (0, 1)][stA['sb'] % 3]
                stA['sb'] += 1
                d_['bS'] = bS
                for p in range(3):
                    bb = bS[0] if p < 2 else bS[1]
                    c0 = (p % 2) * 256
                    self.mm(self.pb[bb][:, c0:c0 + 256], QKT[:, 3 + p, j * 128:(j + 1) * 128],
                            Qz[:, 2 * p:2 * p + 2, :], True, True, [('QKT', j)] + qzk, [('pb', bb)])

            def r_():
                bS = d_['bS']
                ps_ = self.pt_n % 3
                self.pt_n += 1
                PT = self.PT[ps_]
                self.act(PT[:, 0:512], self.pb[bS[0]][:, 0:512], AF.Exp, [('pb', bS[0])], [('PT', ps_)], scale=0.125)
                self.act(PT[:, 512:768], self.pb[bS[1]][:, 0:256], AF.Exp, [('pb', bS[1])], [('PT', ps_)], scale=0.125)
                self.tt('dve', PT[:, 0:768].rearrange("p (h q) -> p h q", q=128), PT[:, 0:768].rearrange("p (h q) -> p h q", q=128),
                        bc(self.MA[:, o, :], 1, [128, 6, 128]), ALU.mult, [('PT', ps_), 'MA'], [('PT', ps_)])
                for h in range(6):
                    self.mm(self.pb[6][:, h * 72:h * 72 + 65], PT[:, h * 128:(h + 1) * 128], VS4[:, j, h, 0:65],
                            (j == j0 and h == 0), j == i, [('PT', ps_), ('VS', j), 'VS_ones'], [('pb', 6)], skip=True)
            return s_, r_

        def a_fin(i):
            def r_():
                acc = self.pb[6][:, 0:432].rearrange("p (h e) -> p h e", e=72)
                self.S.add('dve', lambda e: e.reciprocal(out=self.rec[:, 0:6], in_=acc[:, :, 64]), reads=[('pb', 6)], writes=['rec'])
                self.tt('dve', self.ob[:, 0:384].rearrange("p (h d) -> p h d", d=64), acc[:, :, 0:64], bc(self.rec[:, 0:6], 2, [128, 6, 64]),
                        ALU.mult, [('pb', 6), 'rec'], ['ob'])
                self.out_tile(i, 'ob', self.ob, 384, 0)
            return r_
        for i in range(NT):
            j0 = max(0, i - 16)
            Qz, qzk = self.make_qz(i, 6)
            for j in range(j0, i + 1):
                s_, r_ = a_pair(i, j, j0, Qz, qzk)
                self.emit_step(s_, r_, L=2)
            self.emit_step(None, a_fin(i), L=2)
        self.flush_steps()
        self.dbg_oT()

    def dbg_oT(self):
        if 'dbg_oT' in [d[0] for d in self.dbg] and self.stop_after is not None:
            rng = [r for k, r in (('A', (0, 384)), ('B', (384, 640)), ('C', (640, 896))) if getattr(self, 'only', None) is None or k in self.only]
            for t in range(NT):
                for (a, b) in rng:
                    o = self.dma(self.dr['dbg_oT'][t][:, a:b], self.dr['oT'][t][:, a:b], [('oT', t, 0), ('oT', t, 384), ('oT', t, 640)], [])
                    self.final_ops.append(o)

    def pass_B(self, l):
        dr = self.dr
        if not hasattr(self, 'b_GS'):
            self.b_GS = self.sb('b_GS', [128, NT, 12], F32)
            self.b_posT = self.sb('b_posT', [64, 32], BF16)
            self.b_posTf = self.sb('b_posTf', [64, 32], F32)
            self.b_W2 = self.sb('b_W2', [128, 2, 64], BF16)
            self.b_W2f = self.sb('b_W2f', [128, 2, 64], F32)
            self.b_W2vf = self.b_W2f
            self.b_h1 = self.sb('b_h1', [128, 2, 256], BF16)
            self.b_hb = self.sb('b_hb', [128, 2], F32)
            self.b_kcT = self.sb('b_kcT', [64, 256], BF16)
            self.b_vc = self.sb('b_vc', [128, 2, 64], F32)
            self.b_Pc = self.sb('b_Pc', [128, 2, 512], F32)
            self.b_rden = self.qk_sq[:, 0:512]
            self.b_sc = self.sb('b_sc', [128, 64], F32)
            self.b_sc2 = self.sb('b_sc2', [128, 64], F32)
            self.b_m1 = self.sb('b_m1', [128, 8], F32)
            self.b_m2 = self.sb('b_m2', [128, 8], F32)
            self.b_selb = self.sb('b_selb', [128, 64], F32)
            self.b_selbT2 = [t_[0:64, :].rearrange('p (h q) -> p h q', q=128) for t_ in self.selT]
            self.b_f = self.sb('b_f', [128, 12], F32)
            self.b_ocmp = self.pr[:, 0:512].rearrange('p (a c) -> p a c', c=256)
            self.b_obf = self.qk_xn[:, 0:256].rearrange('p (h d) -> p h d', d=64)
            self.b_tmp = self.qk_xn[:, 256:512].rearrange('p (h d) -> p h d', d=64)
        GS = self.b_GS
        self.wkeys = {}
        W = self.BIG[:, 37376:37376 + 8 * 652].rearrange("p (k c) -> p k c", c=652)
        self.load_w(W, 'Wp', dr['w_in'][l], 1536, 652, 0)
        wreads = list(self.wkeys['Wp'])
        self.load_gains(l, 'q_norm_b', 'k_norm_b', 4, 6)
        Esel = self.BIG[64:128, 5 * S_LEN:6 * S_LEN]
        self.memset('pool', Esel, 1.0, ['Esel'])
        self.asel(Esel, Esel, [[1, S_LEN]], ALU.is_ge, 0.0, 0, -64, ['Esel'], ['Esel'])
        self.asel(Esel, Esel, [[-1, S_LEN]], ALU.is_ge, 0.0, 63, 64, ['Esel'], ['Esel'])
        QS = self.BIG[:, 0:32768].rearrange("p (a c) -> p a c", c=S_LEN)
        VSB = self.BIG[:, 32768:32768 + NT * 144].rearrange("p (t h e) -> p t h e", h=2, e=72)
        self.memset('pool', VSB[:, :, :, 64:65], 1.0, ['VS_ones'])
        QTB = self.BIG[0:64, 0:32768].rearrange("p (a c) -> p a c", c=S_LEN)
        W1 = [self.BIG[0:64, 42592 + i * 4096:42592 + (i + 1) * 4096].rearrange("p (q h) -> p q h", h=128) for i in range(2)]
        for wi, nm in enumerate(('cmp_k_w1', 'cmp_v_w1')):
            for qtr in range(4):
                sl = self.wst_n % 2
                self.wst_n += 1
                stg = self.wstage[sl][0:64]
                self.dma(stg, dr[nm][l][:, qtr * 8:(qtr + 1) * 8, :], [], [('wst', sl)])
                self.cp('pool', W1[wi][:, qtr * 8:(qtr + 1) * 8, :], stg, [('wst', sl)], [('W1', wi, qtr)])
        w1keys = [[('W1', wi, q) for q in range(4)] for wi in range(2)]
        self.dma(self.b_posTf[:], dr['cmp_pos'][l], [], ['b_posTf'])
        self.cp('dve', self.b_posT[:], self.b_posTf[:], ['b_posTf'], ['b_posT'])
        self.dma(self.b_W2f[:, 0, :], dr['cmp_k_w2'][l], [], ['b_W2f0'])
        self.dma(self.b_W2f[:, 1, :], dr['cmp_v_w2'][l], [], ['b_W2f1'])
        self.cp('dve', self.b_W2[:], self.b_W2f[:], ['b_W2f0', 'b_W2f1'], ['b_W2'])
        srcs = [0, 64, 128, 192, 256, 384, 512, 320]
        hs_ = {}

        def do_proj(t):
            if t % 4 == 0:
                hs_['sl'] = self.load_hg(t // 4)
            self.proj_tile(hs_['sl'], t % 4, W, wreads, [(0, 512), (512, 140)], [0, 1])
        do_proj(0)
        for t in range(NT):
            self.cp('act', self.pr[:, 0:512], self.pb[0][:, 0:512], [('pb', 0)], ['pr'])
            self.cp('act', self.pr[:, 512:652], self.pb[1][:, 0:140], [('pb', 1)], ['pr'])
            self.qk_post(self.pr[:, 0:640], 10, self.Gt[:, 0:640], t, self.xb[:, 0:640], 'pr', 'xb')
            self.cp('dve', self.xb[:, 320:384], self.pr[:, 320:384], ['pr', 'xb_dve', 'xb_pool'], ['xb_dve', 'xb_pool'])
            self.cp('dve', VSB[:, t, 0, 0:64], self.pr[:, 448:512], ['pr', 'VS_ones'], [('VS', t)])
            self.cp('dve', VSB[:, t, 1, 0:64], self.pr[:, 576:640], ['pr', 'VS_ones'], [('VS', t)])
            self.cp('dve', GS[:, t, :], self.pr[:, 640:652], ['pr'], [('GS', t)])
            if t + 1 < NT:
                do_proj(t + 1)
            pT = self.pb[3][:].bitcast(BF16)
            for si, c0 in enumerate(srcs):
                self.tr(pT[0:64, si * 128:(si + 1) * 128], self.xb[:, c0:c0 + 64], self.identb[:], ['xb_dve', 'xb_pool', 'identb'], [('pb', 3)])
            self.cp('act', QTB[:, 0:8, t * 128:(t + 1) * 128], pT[0:64, 0:1024].rearrange("p (a c) -> p a c", c=128), [('pb', 3)], [('QKT', t)])
        allq = [('QKT', t) for t in range(NT)]
        self.act(GS[:].rearrange("p t g -> p (t g)"), GS[:].rearrange("p t g -> p (t g)"), AF.Sigmoid, [('GS', t) for t in range(NT)], ['GSs'])
        self.memset('dve', self.b_h1[:, :, 255:256], 0.0, ['b_h1z'])
        for wi, slot in ((0, 4), (1, 7)):
            for p in range(32):
                self.mm(self.pb[5][:, 0:255], W1[wi][:, p, :], QTB[:, slot, p:p + 16 * 254 + 1:16], p == 0, p == 31,
                        allq + w1keys[wi], [('pb', 5)])
            for p in range(32):
                self.mm(self.pb[4][:, 0:1], W1[wi][:, p, :], self.b_posT[:, p:p + 1], p == 0, p == 31, w1keys[wi] + ['b_posT'], [('pb', 4)])
            self.cp('dve', self.b_hb[:, wi:wi + 1], self.pb[4][:, 0:1], [('pb', 4)], [('b_hb', wi)])
            self.act(self.b_h1[:, wi, 0:255], self.pb[5][:, 0:255], AF.Silu, [('pb', 5), ('b_hb', wi), 'b_h1z'], [('b_h1', wi)],
                     bias=self.b_hb[:, wi:wi + 1])
        self.mm(self.pb[5][0:64, 0:256], self.b_W2[:, 0, :], self.b_h1[:, 0, :], True, True, ['b_W2', ('b_h1', 0), 'b_h1z'], [('pb', 5)])
        self.cp('dve', self.b_kcT[:, :], self.pb[5][0:64, 0:256], [('pb', 5)], ['b_kcT'])
        for ct in range(2):
            self.mm(self.pb[4][:, ct * 64:(ct + 1) * 64], self.b_h1[:, 1, ct * 128:(ct + 1) * 128], self.b_W2[:, 1, :], True, True,
                    ['b_W2', ('b_h1', 1), 'b_h1z'], [('pb', 4)])
        self.cp('dve', self.b_vc[:].rearrange("p c d -> p (c d)"), self.pb[4][:, 0:128], [('pb', 4)], ['b_vc'])
        Pc = self.b_Pc
        st = {'sb': 0}

        def qap(i):
            return QTB[:, 0:4, i * 128:(i + 1) * 128]

        def nbank():
            b = (2, 3, 1)[st['sb'] % 3]
            st['sb'] += 1
            return b

        def sel_steps(i):
            qr = [('QKT', i)]
            nct = 2 if i >= 16 else 1
            ob = 0
            selbT = self.b_selbT2[i % 2]
            sk = ('b_selbT', i)

            def s_a():
                for ct in range(nct):
                    b = 7
                    self.mm(self.pb[b][:, 0:512], self.b_kcT[:, ct * 128:(ct + 1) * 128], qap(i), True, True, qr + ['b_kcT'], [('pb', b)])
                    self.act(Pc[:, ct, :], self.pb[b][:, 0:512], AF.Exp, [('pb', b)], [('Pc', ct)], scale=0.125)
                    if ct == 1 or i < 17:
                        self.asel(Pc[:, ct, :].rearrange("p (h q) -> p h q", q=128), Pc[:, ct, :].rearrange("p (h q) -> p h q", q=128),
                                  [[0, 4], [1, 128]], ALU.is_ge, 0.0, 128 * i - 2048 * ct - 31, -16, [('Pc', ct)], [('Pc', ct)])

            def s_b():
                for ct in range(nct):
                    self.mm(self.pb[4][:, 0:512], self.onesf[:, :], Pc[:, ct, :], ct == 0, ct == nct - 1, [('Pc', ct), 'onesf'], [('pb', 4)])

            def s_c():
                self.tsc('dve', self.b_rden, self.pb[4][:, 0:512], 1e-30, None, ALU.add, None, [('pb', 4)], ['b_rden'])
                self.S.add('dve', lambda e: e.reciprocal(out=self.b_rden, in_=self.b_rden), reads=['b_rden'], writes=['b_rden'])
                for ct in range(nct):
                    self.tt('dve', Pc[:, ct, :], Pc[:, ct, :], self.b_rden, ALU.mult, [('Pc', ct), 'b_rden'], [('Pc', ct)])

            def s_d():
                first = True
                for ct in range(nct):
                    for h in range(4):
                        self.mm(self.pb[ob][:, 0:64], Pc[:, ct, h * 128:(h + 1) * 128], self.cover[:, ct, :], first, False,
                                [('Pc', ct), 'cover'], [('pb', ob)], skip=True)
                        first = False
                for ct in range(nct):
                    for h in range(4):
                        self.mm(self.pb[ob][:, 64 + h * 64:128 + h * 64], Pc[:, ct, h * 128:(h + 1) * 128], self.b_vc[:, ct, :], False,
                                (ct == nct - 1 and h == 3), [('Pc', ct), 'b_vc'], [('pb', ob)], skip=True)

            def s_e():
                self.cp('dve', self.b_ocmp[:, i % 2, :], self.pb[ob][:, 64:320], [('pb', ob)], [('b_ocmp', i % 2)])
                self.tt('dve', self.b_sc[:], self.pb[ob][:, 0:64], self.AM[:, i, :], ALU.add, [('pb', ob), 'AM'], ['b_sc'])
                self.S.add('dve', lambda e: e.max(out=self.b_m1[:], in_=self.b_sc[:]), reads=['b_sc'], writes=['b_m1'])
                self.S.add('dve', lambda e: e.match_replace(out=self.b_sc2[:], in_to_replace=self.b_m1[:], in_values=self.b_sc[:],
                                                            imm_value=-3e38), reads=['b_sc', 'b_m1'], writes=['b_sc2'])
                self.S.add('dve', lambda e: e.max(out=self.b_m2[:], in_=self.b_sc2[:]), reads=['b_sc2'], writes=['b_m2'])
                self.tsc('dve', self.b_selb[:], self.b_sc[:], self.b_m2[:, 7:8], None, ALU.is_ge, None, ['b_sc', 'b_m2'], ['b_selb'])
                self.tsc('dve', self.b_selb[:], self.b_selb[:], 1.0, -NEGB, ALU.subtract, ALU.mult, ['b_selb'], ['b_selb'])

            def s_f():
                self.tr(self.pb[4][0:64, 0:128], self.b_selb[:, :], self.identf[:], ['b_selb', 'identf'], [('pb', 4)])
                self.cp('act', QS[64:128, 0:4, i * 128:(i + 1) * 128], bc(self.pb[4][0:64, 0:128], 1, [64, 4, 128]), [('pb', 4)], [sk])
            return [s_a, s_b, s_c, s_d, s_e, s_f]

        def attn_steps(i):
            qr = [('QKT', i)]
            sk = ('b_selbT', i)
            steps = []

            def sel_pair(j):
                d_ = {}

                def s_():
                    b = nbank()
                    d_['b'] = b
                    bank = self.pb[b]
                    self.mm(bank[:, 0:512], QS[:, 5, j * 128:(j + 1) * 128], QS[:, 0:4, i * 128:(i + 1) * 128], True, True,
                            qr + [('QKT', j), 'Esel', sk], [('pb', b)])

                def f():
                    b = d_['b']
                    bank = self.pb[b]
                    ps_ = self.pt_n % 3
                    self.pt_n += 1
                    PT = self.PT[ps_]
                    self.act(PT[:, 0:512], bank[:, 0:512], AF.Exp, [('pb', b)], [('PT', ps_)], scale=0.125)
                    if j == i:
                        self.asel(PT[:, 0:512].rearrange("p (h q) -> p h q", q=128), PT[:, 0:512].rearrange("p (h q) -> p h q", q=128),
                                  [[0, 4], [1, 128]], ALU.is_ge, 0.0, 0, -1, [('PT', ps_)], [('PT', ps_)])
                    for h in range(4):
                        self.mm(self.pb[6][:, h * 72:h * 72 + 65], PT[:, h * 128:(h + 1) * 128], VSB[:, j, 0, 0:65],
                                (j == 0 and h == 0), j == i, [('PT', ps_), ('VS', j), 'VS_ones'], [('pb', 6)], skip=True)
                return (s_, f)
            j0 = max(0, i - 4)

            def win_pair(j):
                d_ = {}

                def s_():
                    b = nbank()
                    d_['b'] = b
                    bank = self.pb[b]
                    self.mm(bank[:, 0:512], QTB[:, 6, j * 128:(j + 1) * 128], qap(i), True, True, qr + [('QKT', j)], [('pb', b)])

                def f():
                    b = d_['b']
                    bank = self.pb[b]
                    ps_ = self.pt_n % 3
                    self.pt_n += 1
                    PT = self.PT[ps_]
                    self.act(PT[:, 0:512], bank[:, 0:512], AF.Exp, [('pb', b)], [('PT', ps_)], scale=0.125)
                    PT3 = PT[:, 0:512].rearrange("p (h q) -> p h q", q=128)
                    if j == i:
                        self.asel(PT3, PT3, [[0, 4], [1, 128]], ALU.is_ge, 0.0, 0, -1, [('PT', ps_)], [('PT', ps_)])
                    if j == i - 4:
                        self.asel(PT3, PT3, [[0, 4], [-1, 128]], ALU.is_ge, 0.0, -1, 1, [('PT', ps_)], [('PT', ps_)])
                    for h in range(4):
                        self.mm(self.pb[5][:, h * 72:h * 72 + 65], PT[:, h * 128:(h + 1) * 128], VSB[:, j, 1, 0:65],
                                (j == j0 and h == 0), j == i, [('PT', ps_), ('VS', j), 'VS_ones'], [('pb', 5)], skip=True)
                return (s_, f)
            for j in range(i + 1):
                steps.append(sel_pair(j))
            for j in range(j0, i + 1):
                steps.append(win_pair(j))

            def combine():
                accs = self.pb[6][:, 0:288].rearrange("p (h e) -> p h e", e=72)
                accw = self.pb[5][:, 0:288].rearrange("p (h e) -> p h e", e=72)
                ocmp = self.b_ocmp[:, i % 2, :].rearrange("p (h d) -> p h d", d=64)
                f = self.b_f
                self.S.add('dve', lambda e: e.reciprocal(out=f[:, 4:8], in_=accs[:, :, 64]), reads=[('pb', 6)], writes=['b_f'])
                self.S.add('dve', lambda e: e.reciprocal(out=f[:, 8:12], in_=accw[:, :, 64]), reads=[('pb', 5)], writes=['b_f'])
                self.tt('dve', f[:, 4:12], f[:, 4:12], GS[:, i, 4:12], ALU.mult, ['b_f', 'GSs'], ['b_f'])
                self.tt('dve', self.b_obf, ocmp, bc(GS[:, i, 0:4], 2, [128, 4, 64]), ALU.mult, [('b_ocmp', i % 2), 'GSs'], ['b_obf'])
                self.tt('dve', self.b_tmp, accs[:, :, 0:64], bc(f[:, 4:8], 2, [128, 4, 64]), ALU.mult, [('pb', 6), 'b_f'], ['b_tmp'])
                self.tt('dve', self.b_obf, self.b_obf, self.b_tmp, ALU.add, ['b_obf', 'b_tmp'], ['b_obf'])
                self.tt('dve', self.b_tmp, accw[:, :, 0:64], bc(f[:, 8:12], 2, [128, 4, 64]), ALU.mult, [('pb', 5), 'b_f'], ['b_tmp'])
                self.tt('dve', self.ob[:, 0:256].rearrange("p (h d) -> p h d", d=64), self.b_obf, self.b_tmp, ALU.add,
                        ['b_obf', 'b_tmp'], ['ob'])
                self.out_tile(i, 'ob', self.ob, 256, 384)
            steps.append((None, combine))
            return steps

        for f_ in sel_steps(0):
            f_()
        for i in range(NT):
            pend = sel_steps(i + 1) if i + 1 < NT else []
            asteps = attn_steps(i)
            gap = max(1, (len(asteps) - 1) // (len(pend) + 1)) if pend else 1
            for n_, a in enumerate(asteps):
                self.emit_step(a[0], a[1], L=2)
                if pend and (n_ % gap == gap - 1) and n_ < len(asteps) - 1:
                    pend.pop(0)()
            while pend:
                pend.pop(0)()
        self.flush_steps()
        self.dbg_oT()

    def pass_C(self, l):
        dr = self.dr
        if not hasattr(self, 'c_ksum'):
            self.c_ksum = self.sb('c_ksum', [128, 2, 16], F32)
            self.c_khi = self.sb('c_khi', [128, 2, 16], BF16)
            self.c_klo = self.sb('c_klo', [128, 2, 16], BF16)
            self.c_tmp = self.sb('c_tmp', [128, 2, 16], F32)
            self.c_sc = self.sb('c_sc', [128, 4, 16], F32)
            self.c_mx = self.sb('c_mx', [128, 4, 8], F32)
            self.c_selb = self.sb('c_selb', [128, 4, 16], F32)
            self.c_selbT2 = [t_[0:16, :] for t_ in self.selT]
        self.wkeys = {}
        W = self.Wp
        self.load_w(W, 'Wp', dr['w_in'][l], 2444, 768, 0)
        wreads = list(self.wkeys['Wp'])
        self.load_gains(l, 'q_norm_c', 'k_norm_c', 4, 4)
        self.Ec = self.build_E('Ec', 256, 16, 16384)
        VS4 = self.VS.rearrange("p t (h e) -> p t h e", e=72)
        self.memset('pool', VS4[:, :, 0:4, 64:65], 1.0, ['VS_ones'])
        QKT = self.QKT
        hs_ = {}

        def do_proj(t):
            if t % 4 == 0:
                hs_['sl'] = self.load_hg(t // 4)
            self.proj_tile(hs_['sl'], t % 4, W, wreads, [(0, 512), (512, 256)], [0, 1])
        do_proj(0)
        for t in range(NT):
            self.cp('act', self.pr[:, 0:512], self.pb[0][:, 0:512], [('pb', 0)], ['pr'])
            self.cp('act', VS4[:, t, 0:4, 0:64], self.pb[1][:, 0:256].rearrange("p (h d) -> p h d", d=64), [('pb', 1), 'VS_ones'], [('VS', t)])
            self.qk_post(self.pr[:, 0:512], 8, self.Gt[:, 0:512], t, self.xb[:, 0:512], 'pr', 'xb')
            if t + 1 < NT:
                do_proj(t + 1)
            pT = self.pb[3][:].bitcast(BF16)
            for p in range(4):
                self.tr(pT[:, p * 128:(p + 1) * 128], self.xb[:, p * 128:(p + 1) * 128], self.identb[:], ['xb_dve', 'xb_pool', 'identb'], [('pb', 3)])
            self.cp('act', QKT[:, 0:4, t * 128:(t + 1) * 128], pT[:, 0:512].rearrange("p (a c) -> p a c", c=128), [('pb', 3)], [('QKT', t)])
        allq = [('QKT', t) for t in range(NT)]
        ksum, khi, klo, ktmp = self.c_ksum, self.c_khi, self.c_klo, self.c_tmp
        self.S.add('dve', lambda e: e.tensor_reduce(out=ksum[:], in_=QKT[:, 2:4, :].rearrange("p a (n k) -> p a n k", k=256),
                                                    axis=AX.X, op=ALU.add), reads=allq, writes=['c_ksum'])
        self.cp('dve', khi[:], ksum[:], ['c_ksum'], ['c_khi'])
        self.cp('dve', ktmp[:], khi[:], ['c_khi'], ['c_tmp'])
        self.tt('dve', ktmp[:], ksum[:], ktmp[:], ALU.subtract, ['c_ksum', 'c_tmp'], ['c_tmp'])
        self.cp('dve', klo[:], ktmp[:], ['c_tmp'], ['c_klo'])
        st = {'sb': 0}

        def sel_steps(i):
            nb = i // 2
            if nb == 0:
                return []
            sl = i % 2
            Qz, qzk = self.c_qz[i]
            sc = self.c_sc
            selbT = self.c_selbT2[sl]
            sk = ('c_selbT', sl)

            def s_a():
                for h in range(4):
                    self.mm(self.pb[5][:, h * 16:(h + 1) * 16], Qz[:, h, :], khi[:, h // 2, :], True, False, qzk + ['c_khi'], [('pb', 5)])
                    self.mm(self.pb[5][:, h * 16:(h + 1) * 16], Qz[:, h, :], klo[:, h // 2, :], False, True, qzk + ['c_klo'], [('pb', 5)])

            def s_b():
                self.cp('dve', sc[:].rearrange("p h n -> p (h n)"), self.pb[5][:, 0:64], [('pb', 5)], ['c_sc'])
                if nb < 16:
                    self.memset('dve', sc[:, :, nb:16], -1e30, ['c_sc'])
                for h in range(4):
                    self.S.add('dve', (lambda h: lambda e: e.max(out=self.c_mx[:, h, :], in_=sc[:, h, :]))(h), reads=['c_sc'], writes=['c_mx'])
                self.tt('dve', self.c_selb[:], sc[:], bc(self.c_mx[:, :, 2], 2, [128, 4, 16]), ALU.is_ge, ['c_sc', 'c_mx'], ['c_selb'])
                self.tsc('dve', self.c_selb[:], self.c_selb[:], 1.0, -NEGB, ALU.subtract, ALU.mult, ['c_selb'], ['c_selb'])
                if nb < 16:
                    self.memset('dve', self.c_selb[:, :, nb:16], NEGB, ['c_selb'])

            def s_c():
                for h in range(4):
                    self.tr(self.pb[4][0:16, h * 128:(h + 1) * 128], self.c_selb[:, h, :], self.identf[:], ['c_selb', 'identf'], [('pb', 4)])
                self.cp('act', selbT[:, :], self.pb[4][0:16, 0:512], [('pb', 4)], [sk])
            return [s_a, s_b, s_c]

        def attn_steps(i):
            nb = i // 2
            Qz, qzk = self.c_qz[i]
            selbT = self.c_selbT2[i % 2]
            sk = ('c_selbT', i % 2)
            steps = []

            def pair(j):
                d_ = {}
                past = j < 2 * nb

                def s_():
                    b = (2, 3, 0, 1)[st['sb'] % 4]
                    st['sb'] += 1
                    d_['b'] = b
                    bank = self.pb[b]
                    if past:
                        self.mm(bank[:, 0:512], self.Ec[:, j * 128:(j + 1) * 128], selbT[0:16, 0:512], True, False,
                                ['Ec', sk], [('pb', b)], skip=True)
                    for p in range(2):
                        self.mm(bank[:, p * 256:(p + 1) * 256], QKT[:, 2 + p, j * 128:(j + 1) * 128], Qz[:, 2 * p:2 * p + 2, :],
                                (not past) and p == 0, p == 1, [('QKT', j)] + qzk, [('pb', b)], skip=True)

                def r_():
                    b = d_['b']
                    bank = self.pb[b]
                    ps_ = self.pt_n % 3
                    self.pt_n += 1
                    PT = self.PT[ps_]
                    self.act(PT[:, 0:512], bank[:, 0:512], AF.Exp, [('pb', b)], [('PT', ps_)], scale=0.125)
                    if j == i:
                        self.asel(PT[:, 0:512].rearrange("p (h q) -> p h q", q=128), PT[:, 0:512].rearrange("p (h q) -> p h q", q=128),
                                  [[0, 4], [1, 128]], ALU.is_ge, 0.0, 0, -1, [('PT', ps_)], [('PT', ps_)])
                    for h in range(4):
                        self.mm(self.pb[6][:, h * 72:h * 72 + 65], PT[:, h * 128:(h + 1) * 128], VS4[:, j, h, 0:65],
                                (j == 0 and h == 0), j == i, [('PT', ps_), ('VS', j), 'VS_ones'], [('pb', 6)], skip=True)
                return (s_, r_)
            for j in range(i + 1):
                steps.append(pair(j))

            def fin():
                acc = self.pb[6][:, 0:288].rearrange("p (h e) -> p h e", e=72)
                self.S.add('dve', lambda e: e.reciprocal(out=self.rec[:, 0:4], in_=acc[:, :, 64]), reads=[('pb', 6)], writes=['rec'])
                self.tt('dve', self.ob[:, 0:256].rearrange("p (h d) -> p h d", d=64), acc[:, :, 0:64], bc(self.rec[:, 0:4], 2, [128, 4, 64]),
                        ALU.mult, [('pb', 6), 'rec'], ['ob'])
                self.out_tile(i, 'ob', self.ob, 256, 640)
            steps.append((None, fin))
            return steps

        self.c_qz = {}
        self.c_qz[0] = self.make_qz(0, 4)
        for i in range(NT):
            if i + 1 < NT:
                self.c_qz[i + 1] = self.make_qz(i + 1, 4)
            pend = sel_steps(i + 1) if i + 1 < NT else []
            asteps = attn_steps(i)
            gap = max(1, (len(asteps) - 1) // (len(pend) + 1)) if pend else 1
            for n_, a in enumerate(asteps):
                self.emit_step(a[0], a[1], L=3)
                if pend and (n_ % gap == gap - 1) and n_ < len(asteps) - 1:
                    pend.pop(0)()
            while pend:
                pend.pop(0)()
        self.flush_steps()
        self.dbg_oT()

    def final(self, l, xin, xout):
        dr = self.dr
        if not hasattr(self, 'f_yacc'):
            self.f_yacc = self.b_Pc[:, 0, :]
            self.f_ytmp = self.b_Pc[:, 1, :]
            self.f_yT = self.s1_all[:, 0:4096].rearrange("p (k c) -> p k c", c=512)
        Wzg = self.BIG[:, 0:31744].rearrange("p (k c) -> p k c", c=3968)
        Wbr = self.BIG[:, 31744:38912].rearrange("p (k c) -> p k c", c=1024)
        Wout = self.BIG[:, 38912:47104].rearrange("p (k c) -> p k c", c=1024)
        oz = self.BIG[:, 47104:47104 + 3584].rearrange("p (t k c) -> p t k c", k=7, c=128)
        og = self.BIG[:, 50688:50688 + 3584].rearrange("p (t c) -> p t c", c=896)
        sz = [self.qk_sq[:, 0:512], self.qk_xn[:, 0:512]]
        sg = [self.pr[:, 0:512], self.Gt[:, 0:512]]
        self.wkeys = {}
        for (c0, n, o) in ((1152, 384, 0), (2188, 256, 384), (3212, 256, 640), (3468, 3072, 896)):
            self.load_w(Wzg, 'Wzg', dr['w_in'][l], c0, n, o)
        self.load_w(Wbr, 'Wbr', dr['w_br_a'][l], 0, 1024, 0, nk=3, k0=0)
        self.load_w(Wbr, 'Wbr', dr['w_br_b'][l], 0, 1024, 0, nk=2, k0=3)
        self.load_w(Wbr, 'Wbr', dr['w_br_c'][l], 0, 1024, 0, nk=2, k0=5)
        self.load_w(Wout, 'Wout', dr['w_out'][l], 0, 1024, 0)
        kz, kb, ko = list(self.wkeys['Wzg']), list(self.wkeys['Wbr']), list(self.wkeys['Wout'])
        last = (xout is dr['y'])
        n = 0
        for g in range(NT // 4):
            hsl = self.load_hg(g)
            hgt = self.hg[hsl]
            self.dma(og, dr['oT'][g * 4:(g + 1) * 4].rearrange("t p c -> p t c"),
                     [('oT', g * 4 + i, c) for i in range(4) for c in (0, 384, 640)], ['og'])
            for zc in range(7):
                b = zc % 2
                for k in range(8):
                    self.mm(self.pb[b][:, 0:512], Wzg[:, k, zc * 128:(zc + 1) * 128], hgt[:, :, k, :], k == 0, k == 7,
                            [('hg', hsl)] + kz, [('pb', b)])
                self.act(sz[b], self.pb[b][:, 0:512], AF.Silu, [('pb', b)], [('sz', b)])
                self.tt('dve', oz[:, :, zc, :], og[:, :, zc * 128:(zc + 1) * 128], sz[b].rearrange("p (t c) -> p t c", c=128), ALU.mult,
                        [('sz', b), 'og'], [('oz', zc)])
            for m in range(8):
                for br in range(3):
                    gb = 2 + n % 2
                    ub = 4 + n % 2
                    sgb = sg[n % 2]
                    sgk = ('sg', n % 2)
                    n += 1
                    c0 = 896 + br * 1024 + m * 128
                    for k in range(8):
                        self.mm(self.pb[gb][:, 0:512], Wzg[:, k, c0:c0 + 128], hgt[:, :, k, :], k == 0, k == 7,
                                [('hg', hsl)] + kz, [('pb', gb)])
                    self.act(sgb, self.pb[gb][:, 0:512], AF.Sigmoid, [('pb', gb)], [sgk])
                    kcs = ([0, 1, 2], [3, 4], [5, 6])[br]
                    for ii, kc in enumerate(kcs):
                        self.mm(self.pb[ub][:, 0:512], Wbr[:, kc, m * 128:(m + 1) * 128], oz[:, :, kc, :], ii == 0, ii == len(kcs) - 1,
                                [('oz', kc)] + kb, [('pb', ub)])
                    if br == 0:
                        self.tt('dve', self.f_yacc, self.pb[ub][:, 0:512], sgb, ALU.mult, [('pb', ub), sgk], ['f_yacc'])
                    else:
                        self.tt('dve', self.f_ytmp, self.pb[ub][:, 0:512], sgb, ALU.mult, [('pb', ub), sgk], ['f_ytmp'])
                        if br == 1:
                            self.tt('dve', self.f_yacc, self.f_yacc, self.f_ytmp, ALU.add, ['f_yacc', 'f_ytmp'], ['f_yacc'])
                        else:
                            self.tt('dve', self.f_yT[:, m, :], self.f_yacc, self.f_ytmp, ALU.add, ['f_yacc', 'f_ytmp'], [('yT', m)])
            for tt_ in range(4):
                t = g * 4 + tt_
                xsl = self.xt_n % 2
                self.xt_n += 1
                xt = self.xt[xsl]
                self.dma(xt[:], xin[t * 128:(t + 1) * 128, :], [('x1', t)], [('xt', xsl)])
                osl = xsl
                orow = xt
                for nch in range(2):
                    b = 6 + nch
                    for k in range(8):
                        self.mm(self.pb[b][:, 0:512], self.f_yT[:, k, tt_ * 128:(tt_ + 1) * 128], Wout[:, k, nch * 512:(nch + 1) * 512],
                                k == 0, k == 7, [('yT', k)] + ko, [('pb', b)])
                    self.tt('dve', orow[:, nch * 512:(nch + 1) * 512], self.pb[b][:, 0:512], xt[:, nch * 512:(nch + 1) * 512], ALU.add,
                            [('pb', b), ('xt', xsl)], [('xt', xsl)])
                o = self.dma(xout[t * 128:(t + 1) * 128, :], orow[:], [('xt', xsl)], [('y', t) if last else ('x1', t)])
                if last:
                    self.final_ops.append(o)


def host_layout(sh):
    sh = dict(sh)
    sh['norm_g'] = np.ascontiguousarray(sh['norm_g'].reshape(2, 8, 128).transpose(0, 2, 1))
    sh['cmp_pos'] = np.ascontiguousarray(sh['cmp_pos'].transpose(0, 2, 1))
    for nm in ('cmp_k_w1', 'cmp_v_w1'):
        sh[nm] = np.ascontiguousarray(sh[nm].reshape(2, 32, 64, 128).transpose(0, 2, 1, 3))
    return sh


_CACHE = {}


def kernel(**inputs):
    n = 8
    if 'nc' not in _CACHE:
        _CACHE['nc'] = Builder(2).build()
    nc = _CACHE['nc']
    x = np.ascontiguousarray(inputs['x'], dtype=np.float32)
    pos = np.ascontiguousarray(inputs['positions']).astype(np.int32)
    shared = {}
    for k in ('norm_g', 'w_in', 'q_norm_a', 'k_norm_a', 'q_norm_b', 'k_norm_b', 'q_norm_c', 'k_norm_c', 'cmp_pos',
              'cmp_k_w1', 'cmp_k_w2', 'cmp_v_w1', 'cmp_v_w2', 'w_br_a', 'w_br_b', 'w_br_c', 'w_out'):
        shared[k] = np.ascontiguousarray(inputs[k], dtype=np.float32)
    shared = host_layout(shared)
    in_maps = []
    for c in range(n):
        m = dict(shared)
        m['x'] = x[c]
        m['pos'] = np.ascontiguousarray(pos[c].reshape(NT, 128).T)
        in_maps.append(m)
    res = run_bass_kernel_spmd(nc, in_maps, core_ids=list(range(n)))
    return np.stack([np.asarray(r['y'], dtype=np.float32) for r in res.results], axis=0)
```

```python
import contextlib
import math
import numpy as np
import concourse.bass as bass
import concourse.mybir as mybir
from concourse.bass_utils import run_bass_kernel_spmd

F32 = mybir.dt.float32
BF16 = mybir.dt.bfloat16
I32 = mybir.dt.int32
AF = mybir.ActivationFunctionType
ALU = mybir.AluOpType
AX = mybir.AxisListType

SAME_ENG_SYNC = {'pe': False, 'act': True, 'dve': True, 'pool': True, 'sp': True}
N_DMA_SEMS = 8

S_LEN = 4096
D = 1024
NT = 32
INW = 6540
EPS = 1e-6
NEGB = -30000.0


class _Op:
    __slots__ = ('eng', 'fn', 'deps', 'dma', 'signal', 'sem', 'val', 'idx', 'prev')


class Sched:
    def __init__(self, nc):
        self.nc = nc
        self.ops = []
        self.last_w = {}
        self.readers = {}
        self.fence_keys = []

    def add(self, eng, fn, reads=(), writes=(), dma=False):
        op = _Op()
        op.eng = eng
        op.fn = fn
        op.dma = dma
        op.signal = False
        op.sem = None
        op.val = 0
        op.idx = len(self.ops)
        deps = set()
        if self.fence_keys:
            reads = list(reads) + self.fence_keys
        for k in reads:
            w = self.last_w.get(k)
            if w is not None:
                deps.add(w)
        for k in writes:
            w = self.last_w.get(k)
            if w is not None:
                deps.add(w)
            for r in self.readers.get(k, ()):
                deps.add(r)
        op.deps = deps
        for k in reads:
            self.readers.setdefault(k, []).append(op.idx)
        for k in writes:
            self.last_w[k] = op.idx
            self.readers[k] = []
        self.ops.append(op)
        return op.idx

    def emit(self, final_waits=()):
        nc = self.nc
        ops = self.ops
        engs = ['pe', 'act', 'dve', 'pool', 'sp']
        for op in ops:
            need = set()
            best = {}
            for d in op.deps:
                Dp = ops[d]
                if Dp.eng == op.eng and not Dp.dma and not op.dma and not SAME_ENG_SYNC[op.eng]:
                    continue
                if Dp.dma:
                    need.add(d)
                    Dp.signal = True
                elif best.get(Dp.eng, -1) < d:
                    best[Dp.eng] = d
            for d in best.values():
                need.add(d)
                ops[d].signal = True
            op.deps = need
        for d in final_waits:
            ops[d].signal = True
        with contextlib.ExitStack() as st:
            esem = {e: st.enter_context(nc.semaphore('s_' + e)) for e in engs}
            dsem = {e: [st.enter_context(nc.semaphore('d_%s%d' % (e, i))) for i in range(N_DMA_SEMS)]
                    for e in ('sp', 'pool', 'act')}
            ecount = {e: 0 for e in engs}
            dcount = {e: 0 for e in engs}
            for op in ops:
                if op.dma:
                    op.signal = True
                if not op.signal:
                    continue
                if op.dma:
                    i = dcount[op.eng]
                    dcount[op.eng] += 1
                    op.sem = dsem[op.eng][i % N_DMA_SEMS]
                    op.val = 16 * (i // N_DMA_SEMS + 1)
                    op.prev = (op.sem, op.val - 16) if i >= N_DMA_SEMS else None
                else:
                    ecount[op.eng] += 1
                    op.sem = esem[op.eng]
                    op.val = ecount[op.eng]
            per = {e: [] for e in engs}
            for op in ops:
                per[op.eng].append(op)
            block = st.enter_context(nc.Block())

            def run(e, name, extra=()):
                waited = {}
                for op in per[name]:
                    ws = {}
                    for d in op.deps:
                        Dp = ops[d]
                        key = id(Dp.sem)
                        if waited.get(key, 0) >= Dp.val:
                            continue
                        if key not in ws or ws[key][1] < Dp.val:
                            ws[key] = (Dp.sem, Dp.val)
                    if op.dma and op.prev is not None:
                        key = id(op.prev[0])
                        if waited.get(key, 0) < op.prev[1] and (key not in ws or ws[key][1] < op.prev[1]):
                            ws[key] = op.prev
                    for key, (sem, val) in ws.items():
                        e.wait_ge(sem, val)
                        waited[key] = val
                    ins = op.fn(e)
                    if op.signal:
                        ins.then_inc(op.sem, 16 if op.dma else 1)
                for d in extra:
                    Dp = ops[d]
                    e.wait_ge(Dp.sem, Dp.val)

            @block.tensor
            def _(e):
                run(e, 'pe')

            @block.scalar
            def _(e):
                run(e, 'act')

            @block.vector
            def _(e):
                run(e, 'dve')

            @block.gpsimd
            def _(e):
                run(e, 'pool')

            @block.sync
            def _(e):
                run(e, 'sp', extra=final_waits)
        self.stats = {e: len(per[e]) for e in engs}
        self.stats['signals'] = dict(ecount)
        self.stats['dmasig'] = dict(dcount)


def bc(ap, axis, shape):
    return ap.unsqueeze(axis).to_broadcast(shape)


class Builder:
    def __init__(self, n_layers=2, dbg=None, stop_after=None):
        self.n_layers = n_layers
        self.dbg = dbg or ()
        self.stop_after = stop_after
        nc = bass.Bass("TRN2", target_bir_lowering=False)
        self.nc = nc
        self.S = Sched(nc)
        self.st = contextlib.ExitStack()
        self.uid = 0

    def sb(self, name, shape, dt):
        return self.st.enter_context(self.nc.sbuf_tensor(name, shape, dt))

    def ps(self, name, shape, dt=F32):
        return self.st.enter_context(self.nc.psum_tensor(name, shape, dt))

    def mm(self, out, lhsT, rhs, start, stop, r, w, skip=False):
        if skip:
            self.S.add('pe', lambda e: e.matmul(out, lhsT=lhsT, rhs=rhs, start=start, stop=stop, skip_group_check=True), reads=r, writes=w)
        else:
            self.S.add('pe', lambda e: e.matmul(out, lhsT=lhsT, rhs=rhs, start=start, stop=stop), reads=r, writes=w)

    def tr(self, out, in_, ident, r, w):
        self.S.add('pe', lambda e: e.transpose(out=out, in_=in_, identity=ident), reads=r, writes=w)

    def act(self, out, in_, func, r, w, bias=None, scale=None, accum_out=None):
        kw = {}
        if bias is not None:
            kw['bias'] = bias
        if scale is not None:
            kw['scale'] = scale
        if accum_out is not None:
            kw['accum_out'] = accum_out
        self.S.add('act', lambda e: e.activation(out=out, in_=in_, func=func, **kw), reads=r, writes=w)

    def rsqrt(self, out, in_, scale, r, w):
        self.act(out, in_, AF.Ln, r, w, bias=self.epsc[:out.shape[0], 0:1], scale=scale)
        self.act(out, out, AF.Exp, w, w, scale=-0.5)

    def tt(self, eng, out, in0, in1, op, r, w):
        self.S.add(eng, lambda e: e.tensor_tensor(out=out, in0=in0, in1=in1, op=op), reads=r, writes=w)

    def tsc(self, eng, out, in0, s1, s2, op0, op1, r, w):
        if op1 is None:
            self.S.add(eng, lambda e: e.tensor_scalar(out=out, in0=in0, scalar1=s1, scalar2=None, op0=op0), reads=r, writes=w)
        else:
            self.S.add(eng, lambda e: e.tensor_scalar(out=out, in0=in0, scalar1=s1, scalar2=s2, op0=op0, op1=op1), reads=r, writes=w)

    def cp(self, eng, out, in_, r, w):
        if eng == 'act':
            self.S.add('act', lambda e: e.copy(out=out, in_=in_), reads=r, writes=w)
        else:
            self.S.add(eng, lambda e: e.tensor_copy(out=out, in_=in_), reads=r, writes=w)

    def memset(self, eng, ap, val, w):
        self.S.add(eng, lambda e: e.memset(ap, val), writes=w)

    def asel(self, out, in_, pattern, op, fill, base, cm, r, w):
        self.S.add('pool', lambda e: e.affine_select(out=out, in_=in_, pattern=pattern, compare_op=op, fill=fill,
                                                     base=base, channel_multiplier=cm), reads=r, writes=w)

    def dma(self, out, in_, r, w, q='sp'):
        return self.S.add(q, lambda e: e.dma_start(out=out, in_=in_), reads=r, writes=w, dma=True)

    def build_E(self, nm, blk, npart, off):
        E = self.BIG[0:npart, off:off + S_LEN]
        self.memset('pool', E, 1.0, [nm])
        self.asel(E, E, [[1, S_LEN]], ALU.is_ge, 0.0, 0, -blk, [nm], [nm])
        self.asel(E, E, [[-1, S_LEN]], ALU.is_ge, 0.0, blk - 1, blk, [nm], [nm])
        return E

    def make_qz(self, i, nh):
        sl = self.qz_n % 2
        self.qz_n += 1
        Qz = self.Qz[sl]
        npair = nh // 2
        for par in range(2):
            self.cp('pool', Qz[par * 64:(par + 1) * 64, par:nh:2, :], self.QKT[par * 64:(par + 1) * 64, 0:npair, i * 128:(i + 1) * 128],
                    [('QKT', i), 'Qz0'], [('Qz', sl, par)])
        return Qz, [('Qz', sl, 0), ('Qz', sl, 1)]

    def emit_step(self, s_fn, r_fn, L=1):
        if s_fn is not None:
            s_fn()
        if not hasattr(self, '_pend'):
            self._pend = []
        self._pend.append(r_fn)
        while len(self._pend) > L:
            self._pend.pop(0)()

    def flush_steps(self):
        for p in getattr(self, '_pend', []):
            p()
        self._pend = []

    def fence(self):
        self.S.fence_keys = []
        self.fence_n = getattr(self, 'fence_n', 0) + 1
        n = self.fence_n
        fs = self.fsc
        self.mm(self.pb[7][0:1, 0:1], self.identb[0:1, 0:1], self.identb[0:1, 0:1], True, True, ['identb'], [('pb', 7), ('fence', 'pe', n)])
        self.cp('act', fs[0:1, 0:1], fs[0:1, 4:5], ['fsc'], [('fence', 'act', n), 'fscw_act'])
        self.cp('dve', fs[0:1, 1:2], fs[0:1, 5:6], ['fsc'], [('fence', 'dve', n), 'fscw_dve'])
        self.cp('pool', fs[0:1, 2:3], fs[0:1, 6:7], ['fsc'], [('fence', 'pool', n), 'fscw_pool'])
        self.S.fence_keys = [('fence', e, n) for e in ('pe', 'act', 'dve', 'pool')]

    def build(self):
        nc = self.nc
        dr = {}

        def din(name, shape, dt=F32):
            dr[name] = nc.dram_tensor(name, shape, dt, kind="ExternalInput").ap()

        din('x', [S_LEN, D])
        din('pos', [128, NT], I32)
        din('norm_g', [2, 128, 8])
        din('w_in', [2, D, INW])
        for nm in ('q_norm_a', 'k_norm_a', 'q_norm_b', 'k_norm_b', 'q_norm_c', 'k_norm_c'):
            din(nm, [2, 64])
        din('cmp_pos', [2, 64, 32])
        din('cmp_k_w1', [2, 64, 32, 128])
        din('cmp_k_w2', [2, 128, 64])
        din('cmp_v_w1', [2, 64, 32, 128])
        din('cmp_v_w2', [2, 128, 64])
        din('w_br_a', [2, 384, D])
        din('w_br_b', [2, 256, D])
        din('w_br_c', [2, 256, D])
        din('w_out', [2, D, D])
        dr['y'] = nc.dram_tensor('y', [S_LEN, D], F32, kind="ExternalOutput").ap()
        dr['x1'] = nc.dram_tensor('x1s', [S_LEN, D], F32).ap()
        dr['hnT'] = nc.dram_tensor('hnTs', [NT, 128, 1024], BF16).ap()
        dr['oT'] = nc.dram_tensor('oTs', [NT, 128, 7 * 128], BF16).ap()
        for nm, shape, dt in self.dbg:
            dr[nm] = nc.dram_tensor(nm, shape, dt, kind="ExternalOutput").ap()
        self.dr = dr
        self.final_ops = []
        with self.st:
            self.setup()
            for l in range(self.n_layers):
                xin = dr['x'] if l == 0 else dr['x1']
                xout = dr['y'] if l == self.n_layers - 1 else dr['x1']
                self.layer(l, xin, xout)
            self.S.emit(final_waits=self.final_ops)
        return nc

    def setup(self):
        nc = self.nc
        dr = self.dr
        self.alloc_common()
        self.identf = self.sb('identf', [128, 128], F32)
        self.identb = self.sb('identb', [128, 128], BF16)
        self.onesf = self.sb('onesf', [128, 128], F32)
        self.memset('pool', self.identf[:], 1.0, ['identf'])
        self.asel(self.identf[:], self.identf[:], [[-1, 128]], ALU.is_equal, 0.0, 0, 1, ['identf'], ['identf'])
        self.cp('dve', self.identb[:], self.identf[:], ['identf'], ['identb'])
        self.memset('pool', self.onesf[:], 1.0, ['onesf'])
        self.epsc = self.sb('epsc', [128, 1], F32)
        self.memset('dve', self.epsc[:], EPS, ['epsc'])
        for q_ in self.Qz:
            self.memset('pool', q_[:], 0.0, ['Qz0'])
        self.fsc = self.sb('fsc', [128, 8], F32)
        self.memset('dve', self.fsc[:], 0.0, ['fsc'])
        posi = self.sb('posi', [128, NT], I32)
        posf = self.sb('posf', [128, NT], F32)
        fr = self.sb('fr', [128, 8], F32)
        wpF = self.BIG[:, 46592:46592 + 9216].bitcast(F32)
        wpI = self.BIG[:, 46592:46592 + 9216].bitcast(I32)
        ang = wpF[:, 0:256].rearrange("p (t f) -> p t f", f=8)
        tmpa = wpF[:, 256:512].rearrange("p (t f) -> p t f", f=8)
        self.cos = self.sb('cos', [128, NT, 8], F32)
        self.sin = self.sb('sin', [128, NT, 8], F32)
        self.dma(posi[:], dr['pos'][:, :], [], ['posi'])
        self.cp('dve', posf[:], posi[:], ['posi'], ['posf'])
        for i in range(8):
            f = float(np.float32(500000.0) ** np.float32(-i / 8.0))
            self.memset('dve', fr[:, i:i + 1], f, ['fr'])
        self.tt('dve', ang[:], bc(posf[:], 2, [128, NT, 8]), bc(fr[:], 1, [128, NT, 8]), ALU.mult, ['posf', 'fr'], ['ang'])
        PI = math.pi
        HI = 6.28125
        LO = 2 * PI - 6.28125
        ni = wpI[:, 512:768].rearrange("p (t f) -> p t f", f=8)
        nf = wpF[:, 768:1024].rearrange("p (t f) -> p t f", f=8)
        rr = wpF[:, 1024:1280].rearrange("p (t f) -> p t f", f=8)
        self.tsc('dve', tmpa[:], ang[:], 1.0 / (2 * PI), None, ALU.mult, None, ['ang'], ['tmpa'])
        self.cp('dve', ni[:], tmpa[:], ['tmpa'], ['rr_ni'])
        self.cp('dve', nf[:], ni[:], ['rr_ni'], ['rr_nf'])
        self.S.add('dve', lambda e: e.scalar_tensor_tensor(out=rr[:], in0=nf[:], scalar=-HI, in1=ang[:], op0=ALU.mult, op1=ALU.add),
                   reads=['rr_nf', 'ang'], writes=['rr_r'])
        self.S.add('dve', lambda e: e.scalar_tensor_tensor(out=rr[:], in0=nf[:], scalar=-LO, in1=rr[:], op0=ALU.mult, op1=ALU.add),
                   reads=['rr_nf', 'rr_r'], writes=['rr_r'])

        def wrap(buf, key):
            self.tsc('dve', tmpa[:], buf[:], PI, -2 * PI, ALU.is_gt, ALU.mult, [key], ['tmpa'])
            self.tt('dve', buf[:], buf[:], tmpa[:], ALU.add, [key, 'tmpa'], [key])
            self.tsc('dve', tmpa[:], buf[:], -PI, 2 * PI, ALU.is_lt, ALU.mult, [key], ['tmpa'])
            self.tt('dve', buf[:], buf[:], tmpa[:], ALU.add, [key, 'tmpa'], [key])
            self.tsc('dve', buf[:], buf[:], -3.141592, 3.141592, ALU.max, ALU.min, [key], [key])
        wrap(rr, 'rr_r')
        self.act(self.sin[:], rr[:], AF.Sin, ['rr_r'], ['sin'])
        self.tsc('dve', rr[:], rr[:], PI / 2, None, ALU.add, None, ['rr_r', 'sin'], ['rr_r'])
        wrap(rr, 'rr_r')
        self.act(self.cos[:], rr[:], AF.Sin, ['rr_r'], ['cos'])
        scrF = self.BIG[:, 0:24576].bitcast(F32)
        scrI = self.BIG[:, 0:24576].bitcast(I32)

        def carve(src, k):
            return src[:, k * 2176:(k + 1) * 2176].rearrange("p (o q) -> p o q", q=128)
        dA = carve(scrF, 0)
        dAi = carve(scrI, 1)
        t1 = carve(scrF, 2)
        t2 = carve(scrF, 3)
        t3 = carve(scrF, 4)
        t4 = carve(scrI, 4)
        self.MA = self.sb('MA', [128, 17, 128], BF16)
        self.S.add('pool', lambda e: e.iota(dAi[:], pattern=[[128, 17], [1, 128]], base=0, channel_multiplier=-1), writes=['dAi'])
        self.cp('dve', dA[:], dAi[:], ['dAi'], ['dA'])
        self.tsc('dve', t1[:], dA[:], 128.0, None, ALU.is_le, None, ['dA'], ['mt1'])
        self.tsc('dve', t4[:], dAi[:], 3, None, ALU.bitwise_and, None, ['dAi'], ['mt3'])
        self.cp('dve', t2[:], t4[:], ['mt3'], ['mt2'])
        self.tsc('dve', t2[:], t2[:], 0.0, None, ALU.is_equal, None, ['mt2'], ['mt2'])
        self.tsc('dve', t3[:], dA[:], 512.0, None, ALU.is_le, None, ['dA'], ['mt3'])
        self.tt('dve', t2[:], t2[:], t3[:], ALU.mult, ['mt2', 'mt3'], ['mt2'])
        self.tt('dve', t1[:], t1[:], t2[:], ALU.add, ['mt1', 'mt2'], ['mt1'])
        self.tsc('dve', t4[:], dAi[:], 15, None, ALU.bitwise_and, None, ['dAi', 'mt2'], ['mt3'])
        self.cp('dve', t2[:], t4[:], ['mt3', 'mt1'], ['mt2'])
        self.tsc('dve', t2[:], t2[:], 0.0, None, ALU.is_equal, None, ['mt2'], ['mt2'])
        self.tsc('dve', t3[:], dA[:], 2048.0, None, ALU.is_le, None, ['dA', 'mt2'], ['mt3'])
        self.tt('dve', t2[:], t2[:], t3[:], ALU.mult, ['mt2', 'mt3'], ['mt2'])
        self.tt('dve', t1[:], t1[:], t2[:], ALU.add, ['mt1', 'mt2'], ['mt1'])
        self.tsc('dve', t2[:], dA[:], 0.0, None, ALU.is_ge, None, ['dA', 'mt1'], ['mt2'])
        self.tt('dve', self.MA[:], t1[:], t2[:], ALU.mult, ['mt1', 'mt2'], ['MA'])
        self.AM = self.sb('AM', [128, NT, 64], F32)
        vsF = self.BIG[:, 32768:32768 + NT * 432].bitcast(F32)
        vsI = self.BIG[:, 32768:32768 + NT * 432].bitcast(I32)
        am1 = vsF[:, 0:2048].rearrange("p (t j) -> p t j", j=64)
        am1i = vsI[:, 2048:4096].rearrange("p (t j) -> p t j", j=64)
        am2 = vsF[:, 4096:6144].rearrange("p (t j) -> p t j", j=64)
        for a in range(2):
            self.S.add('pool', (lambda a: lambda e: e.iota(am1i[a * 64:(a + 1) * 64], pattern=[[-2, NT], [1, 64]], base=-a,
                                                           channel_multiplier=0))(a),
                       writes=['am1i_%d' % a])
        self.cp('dve', am1[:], am1i[:], ['am1i_0', 'am1i_1'], ['am1_0', 'am1_1'])
        self.tsc('dve', am2[:], am1[:], 0.0, -1e30, ALU.is_gt, ALU.mult, ['am1_0', 'am1_1'], ['am2'])
        self.tsc('dve', am1[:], am1[:], -1.0, 1e4, ALU.is_ge, ALU.mult, ['am1_0', 'am1_1', 'am2'], ['am1', 'am1_0', 'am1_1'])
        self.tt('dve', self.AM[:], am1[:], am2[:], ALU.add, ['am1', 'am2'], ['AM'])
        self.tsc('dve', self.AM[:, :, 0:1], self.AM[:, :, 0:1], 1e4, None, ALU.add, None, ['AM'], ['AM'])
        self.cover = self.sb('cover', [128, 2, 64], F32)
        self.memset('pool', self.cover[:], 1.0, ['cover'])
        self.asel(self.cover[:], self.cover[:], [[-128, 2], [4, 64]], ALU.is_ge, 0.0, 3, -1, ['cover'], ['cover'])
        self.asel(self.cover[:], self.cover[:], [[128, 2], [-4, 64]], ALU.is_ge, 0.0, 1, 1, ['cover'], ['cover'])
        self.pb = [self.ps('pb%d' % i, [128, 512], F32) for i in range(8)]
        self.xt = [self.sb('xt%d' % i, [128, D], F32) for i in range(2)]
        self.hg = [self.sb('hg%d' % i, [128, 4, 8, 128], BF16) for i in range(2)]
        self.wstage = [self.sb('wst%d' % i, [128, 8, 128], F32) for i in range(2)]
        self.wst_n = 0
        self.hg_n = 0
        self.xt_n = 0
        if 'dbg_cs' in [d[0] for d in self.dbg]:
            o = self.dma(self.dr['dbg_cs'][:, 0:256], self.cos[:].rearrange("p t f -> p (t f)"), ['cos'], [])
            self.final_ops.append(o)
            o = self.dma(self.dr['dbg_cs'][:, 256:512], self.sin[:].rearrange("p t f -> p (t f)"), ['sin'], [])
            self.final_ops.append(o)
            mtmp = self.sb('mtmp', [128, 17 * 128], F32)
            self.cp('dve', mtmp[:], self.MA[:].rearrange("p o q -> p (o q)"), ['MA'], ['mtmp'])
            o = self.dma(self.dr['dbg_ma'][:, :], mtmp[:], ['mtmp'], [])
            self.final_ops.append(o)
            o = self.dma(self.dr['dbg_am'][:, :], self.AM[:].rearrange("p t f -> p (t f)"), ['AM'], [])
            self.final_ops.append(o)
            o = self.dma(self.dr['dbg_cov'][:, :], self.cover[:].rearrange("p t f -> p (t f)"), ['cover'], [])
            self.final_ops.append(o)

    def load_w(self, W, wkey, src, c0, n, o, nk=8, k0=0, eng_cycle=('pool', 'dve')):
        done = 0
        while done < n:
            m = min(128, n - done)
            sl = self.wst_n % 4
            self.wst_n += 1
            if sl < 2:
                stg = self.wstage[sl]
                skey = ('wst', sl)
            else:
                stg = self.xt[sl - 2][:].rearrange("p (k c) -> p k c", c=128)
                skey = ('xt', sl - 2)
            self.dma(stg[:, 0:nk, 0:m], src[:, c0 + done:c0 + done + m].rearrange("(k p) c -> p k c", p=128),
                     [], [skey])
            eng = eng_cycle[self.wst_n % len(eng_cycle)]
            self.cp(eng, W[:, k0:k0 + nk, o + done:o + done + m], stg[:, 0:nk, 0:m], [skey], [(wkey, self.wst_n)])
            self.wkeys.setdefault(wkey, []).append((wkey, self.wst_n))
            done += m

    def layer(self, l, xin, xout):
        if l > 0:
            self.fence()
        self.stage1(l, xin)
        if self.stop_after == 'stage1':
            return
        only = getattr(self, 'only', None)
        if only is None or 'A' in only:
            self.fence()
            self.pass_A(l)
        if self.stop_after in ('A', 'Aproj'):
            return
        if only is None or 'C' in only:
            self.fence()
            self.pass_C(l)
        if self.stop_after == 'C':
            return
        if only is None or 'B' in only:
            self.fence()
            self.pass_B(l)
        if self.stop_after == 'B':
            return
        self.fence()
        self.final(l, xin, xout)

    def stage1(self, l, xin):
        dr = self.dr
        if l == 0:
            self.gT = self.sb('gT', [128, 8], F32)
            self.s1_all = self.sb('s1all', [128, 4096], BF16)
            self.s1_sq = self.s1_all[:, 0:1024]
            self.s1_ss = self.sb('s1ss', [128, 2], F32)
            self.s1_ss4 = self.sb('s1ss4', [128, 4], F32)
            self.s1_xs = self.s1_all[:, 1024:2048]
            self.s1_hT = [self.s1_all[:, 2048 + i * 1024:3072 + i * 1024].rearrange("p (k c) -> p k c", c=128) for i in range(2)]
        self.dma(self.gT[:], dr['norm_g'][l], [], ['gT'])
        ring = [(self.xt[0][:], ('xt', 0)), (self.xt[1][:], ('xt', 1)),
                (self.wstage[0][:].rearrange("p k c -> p (k c)"), ('wst', 0)), (self.wstage[1][:].rearrange("p k c -> p (k c)"), ('wst', 1))]
        ss4 = self.s1_ss4

        def st_a(t):
            xt_ap, xkey = ring[t % 4]
            self.dma(xt_ap, xin[t * 128:(t + 1) * 128, :], [('x1', t)], [xkey])
            junk = self.BIG[:, 24576 + (t % 2) * 1024:24576 + (t % 2 + 1) * 1024]
            self.act(junk, xt_ap, AF.Square, [xkey], [('s1junk', t % 2), ('s1ss', t % 4)], accum_out=ss4[:, t % 4:t % 4 + 1])

        def st_b(t):
            ss = ss4[:, t % 4:t % 4 + 1]
            self.act(ss, ss, AF.Ln, [('s1ss', t % 4)], [('s1ss', t % 4)], bias=self.epsc[:, 0:1], scale=1.0 / D)

        def st_c(t):
            ss = ss4[:, t % 4:t % 4 + 1]
            self.act(ss, ss, AF.Exp, [('s1ss', t % 4)], [('s1ss', t % 4)], scale=-0.5)

        def st_d(t):
            sl = t % 2
            xt_ap, xkey = ring[t % 4]
            ss = ss4[:, t % 4:t % 4 + 1]
            xs = (self.s1_sq, self.s1_xs)[sl]
            self.act(xs, xt_ap, AF.Copy, [xkey, ('s1ss', t % 4)], [('s1xs', sl)], scale=ss)
            pT = self.pb[sl][:].bitcast(BF16)
            for k in range(8):
                self.tr(pT[:, k * 128:(k + 1) * 128], xs[:, k * 128:(k + 1) * 128], self.identb[:],
                        [('s1xs', sl), 'identb'], [('pb', sl)])
            hT = self.s1_hT[sl]
            self.tt('dve', hT, pT[:, 0:1024].rearrange("p (k t) -> p k t", k=8), bc(self.gT[:], 2, [128, 8, 128]), ALU.mult,
                    [('pb', sl), 'gT'], [('s1hT', sl)])
            self.dma(dr['hnT'][t], hT.rearrange("p k t -> p (k t)"), [('s1hT', sl)], [('hnT', t)])
        for u in range(-3, NT):
            if 0 <= u + 3 < NT:
                st_a(u + 3)
            if 0 <= u + 2 < NT:
                st_b(u + 2)
            if 0 <= u + 1 < NT:
                st_c(u + 1)
            if 0 <= u < NT:
                st_d(u)
        if 'dbg_hnT' in [d[0] for d in self.dbg]:
            for t in range(NT):
                o = self.dma(dr['dbg_hnT'][t], dr['hnT'][t], [('hnT', t)], [])
                self.final_ops.append(o)

    def load_hg(self, g):
        sl = self.hg_n % 2
        self.hg_n += 1
        self.dma(self.hg[sl][:].rearrange("p t k c -> p t (k c)"), self.dr['hnT'][g * 4:(g + 1) * 4].rearrange("t p c -> p t c"),
                 [('hnT', g * 4 + i) for i in range(4)], [('hg', sl)])
        return sl

    def proj_tile(self, hsl, tt, W, wreads, chunks, banks):
        for (c0, n), b in zip(chunks, banks):
            for k in range(8):
                self.mm(self.pb[b][:, 0:n], self.hg[hsl][:, tt, k, :], W[:, k, c0:c0 + n], k == 0, k == 7,
                        [('hg', hsl)] + wreads, [('pb', b)])

    def qk_post(self, pr, nh, Gt, t, xb, prk, xbk):
        W_ = nh * 64
        sq = self.qk_sq[:, 0:W_]
        ss = self.qk_ss[:, 0:nh]
        self.tt('dve', sq, pr, pr, ALU.mult, [prk], ['qk_sq'])
        self.S.add('dve', lambda e: e.tensor_reduce(out=ss, in_=sq.rearrange("p (h d) -> p h d", d=64), axis=AX.X, op=ALU.add),
                   reads=['qk_sq'], writes=['qk_ss'])
        self.rsqrt(ss, ss, 1.0 / 64, ['qk_ss'], ['qk_ss'])
        na = (2 * nh + 2) // 3
        pr3 = pr.rearrange("p (h d) -> p h d", d=64)
        xn3 = self.qk_xn[:, 0:W_].rearrange("p (h d) -> p h d", d=64)
        xb3 = xb.rearrange("p (h d) -> p h d", d=64)
        G3 = Gt.rearrange("p (h d) -> p h d", d=64)
        for eng, h0, h1 in (('dve', 0, na), ('pool', na, nh)):
            n_ = h1 - h0
            if n_ <= 0:
                continue
            kx = 'qk_xn_' + eng
            xn_ = xn3[:, h0:h1, :]
            self.tt(eng, xn_, pr3[:, h0:h1, :], bc(ss[:, h0:h1], 2, [128, n_, 64]), ALU.mult, [prk, 'qk_ss'], [kx])
            self.tt(eng, xn_, xn_, G3[:, h0:h1, :], ALU.mult, [kx, 'Gt'], [kx])
            cosb = bc(self.cos[:, t, :], 1, [128, n_, 8])
            sinb = bc(self.sin[:, t, :], 1, [128, n_, 8])
            r = [self.qk_r[i][:, h0:h1, :] for i in range(4)]
            rk = ['qk_r%d_%s' % (i, eng) for i in range(4)]
            self.tt(eng, r[0], xn_[:, :, 0:8], cosb, ALU.mult, [kx, 'cos'], [rk[0]])
            self.tt(eng, r[1], xn_[:, :, 8:16], sinb, ALU.mult, [kx, 'sin'], [rk[1]])
            self.tt(eng, r[2], xn_[:, :, 8:16], cosb, ALU.mult, [kx, 'cos'], [rk[2]])
            self.tt(eng, r[3], xn_[:, :, 0:8], sinb, ALU.mult, [kx, 'sin'], [rk[3]])
            xk = xbk + '_' + eng
            self.tt(eng, xb3[:, h0:h1, 0:8], r[0], r[1], ALU.subtract, [rk[0], rk[1]], [xk])
            self.tt(eng, xb3[:, h0:h1, 8:16], r[2], r[3], ALU.add, [rk[2], rk[3]], [xk])
            self.cp(eng, xb3[:, h0:h1, 16:64], xn_[:, :, 16:64], [kx], [xk])

    def alloc_common(self):
        if hasattr(self, 'qk_sq'):
            return
        self.qk_sq = self.sb('qk_sq', [128, 768], F32)
        self.qk_ss = self.sb('qk_ss', [128, 12], F32)
        self.qk_xn = self.sb('qk_xn', [128, 768], F32)
        self.qk_r = [self.sb('qk_r%d' % i, [128, 12, 8], F32) for i in range(4)]
        self.pr = self.sb('pr', [128, 768], F32)
        self.xb = self.sb('xb', [128, 768], BF16)
        self.Gt = self.sb('Gt', [128, 768], F32)
        self.g64 = self.sb('g64', [128, 2, 64], F32)
        self.PT = [self.sb('PT%d' % i, [128, 768], BF16) for i in range(3)]
        self.pt_n = 0
        self.ob = self.sb('ob', [128, 384], BF16)
        self.rec = self.sb('rec', [128, 12], F32)
        self.oTt = [self.sb('oTt%d' % i, [128, 384], BF16) for i in range(2)]
        self.selT = [self.sb('selT%d' % i, [64, 512], BF16) for i in range(2)]
        self.Qz = [self.sb('Qz%d' % i, [128, 6, 128], BF16) for i in range(2)]
        self.qz_n = 0
        self.ot_n = 0
        self.BIG = self.sb('BIG', [128, 57344], BF16)
        self.QKT = self.BIG[:, 0:32768].rearrange("p (a c) -> p a c", c=S_LEN)
        self.VS = self.BIG[:, 32768:32768 + NT * 432].rearrange("p (t c) -> p t c", c=432)
        self.Wp = self.BIG[:, 46592:46592 + 9216].rearrange("p (k c) -> p k c", c=1152)

    def load_gains(self, l, qn, kn, nq, nk):
        dr = self.dr
        self.dma(self.g64[:, 0, :], dr[qn][l].partition_broadcast(128), [], ['g64q'])
        self.dma(self.g64[:, 1, :], dr[kn][l].partition_broadcast(128), [], ['g64k'])
        G3 = self.Gt[:, 0:(nq + nk) * 64].rearrange("p (h d) -> p h d", d=64)
        self.cp('dve', G3[:, 0:nq, :], bc(self.g64[:, 0, :], 1, [128, nq, 64]), ['g64q'], ['Gt'])
        self.cp('dve', G3[:, nq:nq + nk, :], bc(self.g64[:, 1, :], 1, [128, nk, 64]), ['g64k'], ['Gt'])

    def out_tile(self, i, acc_key, ob_ap, ncol, c0):
        npair = ncol // 128
        pT = self.pb[7][:].bitcast(BF16)
        for p in range(npair):
            self.tr(pT[:, p * 128:(p + 1) * 128], ob_ap[:, p * 128:(p + 1) * 128], self.identb[:], [acc_key, 'identb'], [('pb', 7)])
        sl = self.ot_n % 2
        self.ot_n += 1
        self.cp('act', self.oTt[sl][:, 0:ncol], pT[:, 0:ncol], [('pb', 7)], [('oTt', sl)])
        self.dma(self.dr['oT'][i][:, c0:c0 + ncol], self.oTt[sl][:, 0:ncol], [('oTt', sl)], [('oT', i, c0)])

    def pass_A(self, l):
        dr = self.dr
        self.alloc_common()
        self.wkeys = {}
        W = self.Wp
        self.load_w(W, 'Wp', dr['w_in'][l], 0, 1152, 0)
        wreads = list(self.wkeys['Wp'])
        self.load_gains(l, 'q_norm_a', 'k_norm_a', 6, 6)
        VS4 = self.VS.rearrange("p t (h e) -> p t h e", e=72)
        self.S.add('pool', lambda e: e.memset(VS4[:, :, :, 64:65], 1.0), reads=['MA', 'AM'], writes=['VS_ones'])
        QKT = self.QKT
        hs_ = {}

        def do_proj(t):
            if t % 4 == 0:
                hs_['sl'] = self.load_hg(t // 4)
            self.proj_tile(hs_['sl'], t % 4, W, wreads, [(0, 384), (384, 384), (768, 384)], [0, 1, 2])
        do_proj(0)
        for t in range(NT):
            self.cp('act', self.pr[:, 0:384], self.pb[0][:, 0:384], [('pb', 0)], ['pr'])
            self.cp('act', self.pr[:, 384:768], self.pb[1][:, 0:384], [('pb', 1)], ['pr'])
            self.cp('act', VS4[:, t, :, 0:64], self.pb[2][:, 0:384].rearrange("p (h d) -> p h d", d=64), [('pb', 2), 'VS_ones', 'MA', 'AM'], [('VS', t)])
            self.qk_post(self.pr[:, 0:768], 12, self.Gt[:, 0:768], t, self.xb[:, 0:768], 'pr', 'xb')
            if t + 1 < NT:
                do_proj(t + 1)
            pT = self.pb[3][:].bitcast(BF16)
            for p in range(6):
                self.tr(pT[:, p * 128:(p + 1) * 128], self.xb[:, p * 128:(p + 1) * 128], self.identb[:], ['xb_dve', 'xb_pool', 'identb'], [('pb', 3)])
            self.cp('act', QKT[:, 0:6, t * 128:(t + 1) * 128], pT[:, 0:768].rearrange("p (a c) -> p a c", c=128), [('pb', 3), 'MA', 'AM'], [('QKT', t)])
        if self.stop_after == 'Aproj':
            return
        stA = {'sb': 0}

        def a_pair(i, j, j0, Qz, qzk):
            o = i - j
            d_ = {}

            def s_():
                bS = [(2, 3), (4, 5), (0, 1)][stA['sb'] % 3]
                stA['sb'] += 1
                d_['bS'] = bS
                for p in range(3):
                    bb = bS[0] if p < 2 else bS[1]
                    c0 = (p % 2) * 256
                    self.mm(self.pb[bb][:, c0:c0 + 256], QKT[:, 3 + p, j * 128:(j + 1) * 128],
                            Qz[:, 2 * p:2 * p + 2, :], True, True, [('QKT', j)] + qzk, [('pb', bb)])

            def r_():
                bS = d_['bS']
                ps_ = self.pt_n % 3
                self.pt_n += 1
                PT = self.PT[ps_]
                self.act(PT[:, 0:512], self.pb[bS[0]][:, 0:512], AF.Exp, [('pb', bS[0])], [('PT', ps_)], scale=0.125)
                self.act(PT[:, 512:768], self.pb[bS[1]][:, 0:256], AF.Exp, [('pb', bS[1])], [('PT', ps_)], scale=0.125)
                self.tt('dve', PT[:, 0:768].rearrange("p (h q) -> p h q", q=128), PT[:, 0:768].rearrange("p (h q) -> p h q", q=128),
                        bc(self.MA[:, o, :], 1, [128, 6, 128]), ALU.mult, [('PT', ps_), 'MA'], [('PT', ps_)])
                for h in range(6):
                    self.mm(self.pb[6][:, h * 72:h * 72 + 65], PT[:, h * 128:(h + 1) * 128], VS4[:, j, h, 0:65],
                            (j == j0 and h == 0), j == i, [('PT', ps_), ('VS', j), 'VS_ones'], [('pb', 6)], skip=True)
            return s_, r_

        def a_fin(i):
            def r_():
                acc = self.pb[6][:, 0:432].rearrange("p (h e) -> p h e", e=72)
                self.S.add('dve', lambda e: e.reciprocal(out=self.rec[:, 0:6], in_=acc[:, :, 64]), reads=[('pb', 6)], writes=['rec'])
                self.tt('dve', self.ob[:, 0:384].rearrange("p (h d) -> p h d", d=64), acc[:, :, 0:64], bc(self.rec[:, 0:6], 2, [128, 6, 64]),
                        ALU.mult, [('pb', 6), 'rec'], ['ob'])
                self.out_tile(i, 'ob', self.ob, 384, 0)
            return r_
        for i in range(NT):
            j0 = max(0, i - 16)
            Qz, qzk = self.make_qz(i, 6)
            for j in range(j0, i + 1):
                s_, r_ = a_pair(i, j, j0, Qz, qzk)
                self.emit_step(s_, r_, L=2)
            self.emit_step(None, a_fin(i), L=2)
        self.flush_steps()
        self.dbg_oT()

    def dbg_oT(self):
        if 'dbg_oT' in [d[0] for d in self.dbg] and self.stop_after is not None:
            rng = [r for k, r in (('A', (0, 384)), ('B', (384, 640)), ('C', (640, 896))) if getattr(self, 'only', None) is None or k in self.only]
            for t in range(NT):
                for (a, b) in rng:
                    o = self.dma(self.dr['dbg_oT'][t][:, a:b], self.dr['oT'][t][:, a:b], [('oT', t, 0), ('oT', t, 384), ('oT', t, 640)], [])
                    self.final_ops.append(o)

    def pass_B(self, l):
        dr = self.dr
        if not hasattr(self, 'b_GS'):
            self.b_GS = self.sb('b_GS', [128, NT, 12], F32)
            self.b_posT = self.sb('b_posT', [64, 32], BF16)
            self.b_posTf = self.sb('b_posTf', [64, 32], F32)
            self.b_W2 = self.sb('b_W2', [128, 2, 64], BF16)
            self.b_W2f = self.sb('b_W2f', [128, 2, 64], F32)
            self.b_W2vf = self.b_W2f
            self.b_h1 = self.sb('b_h1', [128, 2, 256], BF16)
            self.b_hb = self.sb('b_hb', [128, 2], F32)
            self.b_kcT = self.sb('b_kcT', [64, 256], BF16)
            self.b_vc = self.sb('b_vc', [128, 2, 64], F32)
            self.b_Pc = self.sb('b_Pc', [128, 2, 512], F32)
            self.b_rden = self.qk_sq[:, 0:512]
            self.b_sc = self.sb('b_sc', [128, 64], F32)
            self.b_sc2 = self.sb('b_sc2', [128, 64], F32)
            self.b_m1 = self.sb('b_m1', [128, 8], F32)
            self.b_m2 = self.sb('b_m2', [128, 8], F32)
            self.b_selb = self.sb('b_selb', [128, 64], F32)
            self.b_selbT2 = [t_[0:64, :].rearrange('p (h q) -> p h q', q=128) for t_ in self.selT]
            self.b_f = self.sb('b_f', [128, 12], F32)
            self.b_ocmp = self.pr[:, 0:512].rearrange('p (a c) -> p a c', c=256)
            self.b_obf = self.qk_xn[:, 0:256].rearrange('p (h d) -> p h d', d=64)
            self.b_tmp = self.qk_xn[:, 256:512].rearrange('p (h d) -> p h d', d=64)
        GS = self.b_GS
        self.wkeys = {}
        W = self.BIG[:, 37376:37376 + 8 * 652].rearrange("p (k c) -> p k c", c=652)
        self.load_w(W, 'Wp', dr['w_in'][l], 1536, 652, 0)
        wreads = list(self.wkeys['Wp'])
        self.load_gains(l, 'q_norm_b', 'k_norm_b', 4, 6)
        Esel = self.BIG[64:128, 5 * S_LEN:6 * S_LEN]
        self.memset('pool', Esel, 1.0, ['Esel'])
        self.asel(Esel, Esel, [[1, S_LEN]], ALU.is_ge, 0.0, 0, -64, ['Esel'], ['Esel'])
        self.asel(Esel, Esel, [[-1, S_LEN]], ALU.is_ge, 0.0, 63, 64, ['Esel'], ['Esel'])
        QS = self.BIG[:, 0:32768].rearrange("p (a c) -> p a c", c=S_LEN)
        VSB = self.BIG[:, 32768:32768 + NT * 144].rearrange("p (t h e) -> p t h e", h=2, e=72)
        self.memset('pool', VSB[:, :, :, 64:65], 1.0, ['VS_ones'])
        QTB = self.BIG[0:64, 0:32768].rearrange("p (a c) -> p a c", c=S_LEN)
        W1 = [self.BIG[0:64, 42592 + i * 4096:42592 + (i + 1) * 4096].rearrange("p (q h) -> p q h", h=128) for i in range(2)]
        for wi, nm in enumerate(('cmp_k_w1', 'cmp_v_w1')):
            for qtr in range(4):
                sl = self.wst_n % 2
                self.wst_n += 1
                stg = self.wstage[sl][0:64]
                self.dma(stg, dr[nm][l][:, qtr * 8:(qtr + 1) * 8, :], [], [('wst', sl)])
                self.cp('pool', W1[wi][:, qtr * 8:(qtr + 1) * 8, :], stg, [('wst', sl)], [('W1', wi, qtr)])
        w1keys = [[('W1', wi, q) for q in range(4)] for wi in range(2)]
        self.dma(self.b_posTf[:], dr['cmp_pos'][l], [], ['b_posTf'])
        self.cp('dve', self.b_posT[:], self.b_posTf[:], ['b_posTf'], ['b_posT'])
        self.dma(self.b_W2f[:, 0, :], dr['cmp_k_w2'][l], [], ['b_W2f0'])
        self.dma(self.b_W2f[:, 1, :], dr['cmp_v_w2'][l], [], ['b_W2f1'])
        self.cp('dve', self.b_W2[:], self.b_W2f[:], ['b_W2f0', 'b_W2f1'], ['b_W2'])
        srcs = [0, 64, 128, 192, 256, 384, 512, 320]
        hs_ = {}

        def do_proj(t):
            if t % 4 == 0:
                hs_['sl'] = self.load_hg(t // 4)
            self.proj_tile(hs_['sl'], t % 4, W, wreads, [(0, 512), (512, 140)], [0, 1])
        do_proj(0)
        for t in range(NT):
            self.cp('act', self.pr[:, 0:512], self.pb[0][:, 0:512], [('pb', 0)], ['pr'])
            self.cp('act', self.pr[:, 512:652], self.pb[1][:, 0:140], [('pb', 1)], ['pr'])
            self.qk_post(self.pr[:, 0:640], 10, self.Gt[:, 0:640], t, self.xb[:, 0:640], 'pr', 'xb')
            self.cp('dve', self.xb[:, 320:384], self.pr[:, 320:384], ['pr', 'xb_dve', 'xb_pool'], ['xb_dve', 'xb_pool'])
            self.cp('dve', VSB[:, t, 0, 0:64], self.pr[:, 448:512], ['pr', 'VS_ones'], [('VS', t)])
            self.cp('dve', VSB[:, t, 1, 0:64], self.pr[:, 576:640], ['pr', 'VS_ones'], [('VS', t)])
            self.cp('dve', GS[:, t, :], self.pr[:, 640:652], ['pr'], [('GS', t)])
            if t + 1 < NT:
                do_proj(t + 1)
            pT = self.pb[3][:].bitcast(BF16)
            for si, c0 in enumerate(srcs):
                self.tr(pT[0:64, si * 128:(si + 1) * 128], self.xb[:, c0:c0 + 64], self.identb[:], ['xb_dve', 'xb_pool', 'identb'], [('pb', 3)])
            self.cp('act', QTB[:, 0:8, t * 128:(t + 1) * 128], pT[0:64, 0:1024].rearrange("p (a c) -> p a c", c=128), [('pb', 3)], [('QKT', t)])
        allq = [('QKT', t) for t in range(NT)]
        self.act(GS[:].rearrange("p t g -> p (t g)"), GS[:].rearrange("p t g -> p (t g)"), AF.Sigmoid, [('GS', t) for t in range(NT)], ['GSs'])
        self.memset('dve', self.b_h1[:, :, 255:256], 0.0, ['b_h1z'])
        for wi, slot in ((0, 4), (1, 7)):
            for p in range(32):
                self.mm(self.pb[5][:, 0:255], W1[wi][:, p, :], QTB[:, slot, p:p + 16 * 254 + 1:16], p == 0, p == 31,
                        allq + w1keys[wi], [('pb', 5)])
            for p in range(32):
                self.mm(self.pb[4][:, 0:1], W1[wi][:, p, :], self.b_posT[:, p:p + 1], p == 0, p == 31, w1keys[wi] + ['b_posT'], [('pb', 4)])
            self.cp('dve', self.b_hb[:, wi:wi + 1], self.pb[4][:, 0:1], [('pb', 4)], [('b_hb', wi)])
            self.act(self.b_h1[:, wi, 0:255], self.pb[5][:, 0:255], AF.Silu, [('pb', 5), ('b_hb', wi), 'b_h1z'], [('b_h1', wi)],
                     bias=self.b_hb[:, wi:wi + 1])
        self.mm(self.pb[5][0:64, 0:256], self.b_W2[:, 0, :], self.b_h1[:, 0, :], True, True, ['b_W2', ('b_h1', 0), 'b_h1z'], [('pb', 5)])
        self.cp('dve', self.b_kcT[:, :], self.pb[5][0:64, 0:256], [('pb', 5)], ['b_kcT'])
        for ct in range(2):
            self.mm(self.pb[4][:, ct * 64:(ct + 1) * 64], self.b_h1[:, 1, ct * 128:(ct + 1) * 128], self.b_W2[:, 1, :], True, True,
                    ['b_W2', ('b_h1', 1), 'b_h1z'], [('pb', 4)])
        self.cp('dve', self.b_vc[:].rearrange("p c d -> p (c d)"), self.pb[4][:, 0:128], [('pb', 4)], ['b_vc'])
        Pc = self.b_Pc
        st = {'sb': 0}

        def qap(i):
            return QTB[:, 0:4, i * 128:(i + 1) * 128]

        def nbank():
            b = (2, 3, 1)[st['sb'] % 3]
            st['sb'] += 1
            return b

        def sel_steps(i):
            qr = [('QKT', i)]
            nct = 2 if i >= 16 else 1
            ob = 0
            selbT = self.b_selbT2[i % 2]
            sk = ('b_selbT', i)

            def s_a():
                for ct in range(nct):
                    b = 7
                    self.mm(self.pb[b][:, 0:512], self.b_kcT[:, ct * 128:(ct + 1) * 128], qap(i), True, True, qr + ['b_kcT'], [('pb', b)])
                    self.act(Pc[:, ct, :], self.pb[b][:, 0:512], AF.Exp, [('pb', b)], [('Pc', ct)], scale=0.125)
                    if ct == 1 or i < 17:
                        self.asel(Pc[:, ct, :].rearrange("p (h q) -> p h q", q=128), Pc[:, ct, :].rearrange("p (h q) -> p h q", q=128),
                                  [[0, 4], [1, 128]], ALU.is_ge, 0.0, 128 * i - 2048 * ct - 31, -16, [('Pc', ct)], [('Pc', ct)])

            def s_b():
                for ct in range(nct):
                    self.mm(self.pb[4][:, 0:512], self.onesf[:, :], Pc[:, ct, :], ct == 0, ct == nct - 1, [('Pc', ct), 'onesf'], [('pb', 4)])

            def s_c():
                self.tsc('dve', self.b_rden, self.pb[4][:, 0:512], 1e-30, None, ALU.add, None, [('pb', 4)], ['b_rden'])
                self.S.add('dve', lambda e: e.reciprocal(out=self.b_rden, in_=self.b_rden), reads=['b_rden'], writes=['b_rden'])
                for ct in range(nct):
                    self.tt('dve', Pc[:, ct, :], Pc[:, ct, :], self.b_rden, ALU.mult, [('Pc', ct), 'b_rden'], [('Pc', ct)])

            def s_d():
                first = True
                for ct in range(nct):
                    for h in range(4):
                        self.mm(self.pb[ob][:, 0:64], Pc[:, ct, h * 128:(h + 1) * 128], self.cover[:, ct, :], first, False,
                                [('Pc', ct), 'cover'], [('pb', ob)], skip=True)
                        first = False
                for ct in range(nct):
                    for h in range(4):
                        self.mm(self.pb[ob][:, 64 + h * 64:128 + h * 64], Pc[:, ct, h * 128:(h + 1) * 128], self.b_vc[:, ct, :], False,
                                (ct == nct - 1 and h == 3), [('Pc', ct), 'b_vc'], [('pb', ob)], skip=True)

            def s_e():
                self.cp('dve', self.b_ocmp[:, i % 2, :], self.pb[ob][:, 64:320], [('pb', ob)], [('b_ocmp', i % 2)])
                self.tt('dve', self.b_sc[:], self.pb[ob][:, 0:64], self.AM[:, i, :], ALU.add, [('pb', ob), 'AM'], ['b_sc'])
                self.S.add('dve', lambda e: e.max(out=self.b_m1[:], in_=self.b_sc[:]), reads=['b_sc'], writes=['b_m1'])
                self.S.add('dve', lambda e: e.match_replace(out=self.b_sc2[:], in_to_replace=self.b_m1[:], in_values=self.b_sc[:],
                                                            imm_value=-3e38), reads=['b_sc', 'b_m1'], writes=['b_sc2'])
                self.S.add('dve', lambda e: e.max(out=self.b_m2[:], in_=self.b_sc2[:]), reads=['b_sc2'], writes=['b_m2'])
                self.tsc('dve', self.b_selb[:], self.b_sc[:], self.b_m2[:, 7:8], None, ALU.is_ge, None, ['b_sc', 'b_m2'], ['b_selb'])
                self.tsc('dve', self.b_selb[:], self.b_selb[:], 1.0, -NEGB, ALU.subtract, ALU.mult, ['b_selb'], ['b_selb'])

            def s_f():
                self.tr(self.pb[4][0:64, 0:128], self.b_selb[:, :], self.identf[:], ['b_selb', 'identf'], [('pb', 4)])
                self.cp('act', QS[64:128, 0:4, i * 128:(i + 1) * 128], bc(self.pb[4][0:64, 0:128], 1, [64, 4, 128]), [('pb', 4)], [sk])
            return [s_a, s_b, s_c, s_d, s_e, s_f]

        def attn_steps(i):
            qr = [('QKT', i)]
            sk = ('b_selbT', i)
            steps = []

            def sel_pair(j):
                d_ = {}

                def s_():
                    b = nbank()
                    d_['b'] = b
                    bank = self.pb[b]
                    self.mm(bank[:, 0:512], QS[:, 5, j * 128:(j + 1) * 128], QS[:, 0:4, i * 128:(i + 1) * 128], True, True,
                            qr + [('QKT', j), 'Esel', sk], [('pb', b)])

                def f():
                    b = d_['b']
                    bank = self.pb[b]
                    ps_ = self.pt_n % 3
                    self.pt_n += 1
                    PT = self.PT[ps_]
                    self.act(PT[:, 0:512], bank[:, 0:512], AF.Exp, [('pb', b)], [('PT', ps_)], scale=0.125)
                    if j == i:
                        self.asel(PT[:, 0:512].rearrange("p (h q) -> p h q", q=128), PT[:, 0:512].rearrange("p (h q) -> p h q", q=128),
                                  [[0, 4], [1, 128]], ALU.is_ge, 0.0, 0, -1, [('PT', ps_)], [('PT', ps_)])
                    for h in range(4):
                        self.mm(self.pb[6][:, h * 72:h * 72 + 65], PT[:, h * 128:(h + 1) * 128], VSB[:, j, 0, 0:65],
                                (j == 0 and h == 0), j == i, [('PT', ps_), ('VS', j), 'VS_ones'], [('pb', 6)], skip=True)
                return (s_, f)
            j0 = max(0, i - 4)

            def win_pair(j):
                d_ = {}

                def s_():
                    b = nbank()
                    d_['b'] = b
                    bank = self.pb[b]
                    self.mm(bank[:, 0:512], QTB[:, 6, j * 128:(j + 1) * 128], qap(i), True, True, qr + [('QKT', j)], [('pb', b)])

                def f():
                    b = d_['b']
                    bank = self.pb[b]
                    ps_ = self.pt_n % 3
                    self.pt_n += 1
                    PT = self.PT[ps_]
                    self.act(PT[:, 0:512], bank[:, 0:512], AF.Exp, [('pb', b)], [('PT', ps_)], scale=0.125)
                    PT3 = PT[:, 0:512].rearrange("p (h q) -> p h q", q=128)
                    if j == i:
                        self.asel(PT3, PT3, [[0, 4], [1, 128]], ALU.is_ge, 0.0, 0, -1, [('PT', ps_)], [('PT', ps_)])
                    if j == i - 4:
                        self.asel(PT3, PT3, [[0, 4], [-1, 128]], ALU.is_ge, 0.0, -1, 1, [('PT', ps_)], [('PT', ps_)])
                    for h in range(4):
                        self.mm(self.pb[5][:, h * 72:h * 72 + 65], PT[:, h * 128:(h + 1) * 128], VSB[:, j, 1, 0:65],
                                (j == j0 and h == 0), j == i, [('PT', ps_), ('VS', j), 'VS_ones'], [('pb', 5)], skip=True)
                return (s_, f)
            for j in range(i + 1):
                steps.append(sel_pair(j))
            for j in range(j0, i + 1):
                steps.append(win_pair(j))

            def combine():
                accs = self.pb[6][:, 0:288].rearrange("p (h e) -> p h e", e=72)
                accw = self.pb[5][:, 0:288].rearrange("p (h e) -> p h e", e=72)
                ocmp = self.b_ocmp[:, i % 2, :].rearrange("p (h d) -> p h d", d=64)
                f = self.b_f
                self.S.add('dve', lambda e: e.reciprocal(out=f[:, 4:8], in_=accs[:, :, 64]), reads=[('pb', 6)], writes=['b_f'])
                self.S.add('dve', lambda e: e.reciprocal(out=f[:, 8:12], in_=accw[:, :, 64]), reads=[('pb', 5)], writes=['b_f'])
                self.tt('dve', f[:, 4:12], f[:, 4:12], GS[:, i, 4:12], ALU.mult, ['b_f', 'GSs'], ['b_f'])
                self.tt('dve', self.b_obf, ocmp, bc(GS[:, i, 0:4], 2, [128, 4, 64]), ALU.mult, [('b_ocmp', i % 2), 'GSs'], ['b_obf'])
                self.tt('dve', self.b_tmp, accs[:, :, 0:64], bc(f[:, 4:8], 2, [128, 4, 64]), ALU.mult, [('pb', 6), 'b_f'], ['b_tmp'])
                self.tt('dve', self.b_obf, self.b_obf, self.b_tmp, ALU.add, ['b_obf', 'b_tmp'], ['b_obf'])
                self.tt('dve', self.b_tmp, accw[:, :, 0:64], bc(f[:, 8:12], 2, [128, 4, 64]), ALU.mult, [('pb', 5), 'b_f'], ['b_tmp'])
                self.tt('dve', self.ob[:, 0:256].rearrange("p (h d) -> p h d", d=64), self.b_obf, self.b_tmp, ALU.add,
                        ['b_obf', 'b_tmp'], ['ob'])
                self.out_tile(i, 'ob', self.ob, 256, 384)
            steps.append((None, combine))
            return steps

        for f_ in sel_steps(0):
            f_()
        for i in range(NT):
            pend = sel_steps(i + 1) if i + 1 < NT else []
            asteps = attn_steps(i)
            gap = max(1, (len(asteps) - 1) // (len(pend) + 1)) if pend else 1
            for n_, a in enumerate(asteps):
                self.emit_step(a[0], a[1], L=2)
                if pend and (n_ % gap == gap - 1) and n_ < len(asteps) - 1:
                    pend.pop(0)()
            while pend:
                pend.pop(0)()
        self.flush_steps()
        self.dbg_oT()

    def pass_C(self, l):
        dr = self.dr
        if not hasattr(self, 'c_ksum'):
            self.c_ksum = self.sb('c_ksum', [128, 2, 16], F32)
            self.c_khi = self.sb('c_khi', [128, 2, 16], BF16)
            self.c_klo = self.sb('c_klo', [128, 2, 16], BF16)
            self.c_tmp = self.sb('c_tmp', [128, 2, 16], F32)
            self.c_sc = self.sb('c_sc', [128, 4, 16], F32)
            self.c_mx = self.sb('c_mx', [128, 4, 8], F32)
            self.c_selb = self.sb('c_selb', [128, 4, 16], F32)
            self.c_selbT2 = [t_[0:16, :] for t_ in self.selT]
        self.wkeys = {}
        W = self.Wp
        self.load_w(W, 'Wp', dr['w_in'][l], 2444, 768, 0)
        wreads = list(self.wkeys['Wp'])
        self.load_gains(l, 'q_norm_c', 'k_norm_c', 4, 4)
        self.Ec = self.build_E('Ec', 256, 16, 16384)
        VS4 = self.VS.rearrange("p t (h e) -> p t h e", e=72)
        self.memset('pool', VS4[:, :, 0:4, 64:65], 1.0, ['VS_ones'])
        QKT = self.QKT
        hs_ = {}

        def do_proj(t):
            if t % 4 == 0:
                hs_['sl'] = self.load_hg(t // 4)
            self.proj_tile(hs_['sl'], t % 4, W, wreads, [(0, 512), (512, 256)], [0, 1])
        do_proj(0)
        for t in range(NT):
            self.cp('act', self.pr[:, 0:512], self.pb[0][:, 0:512], [('pb', 0)], ['pr'])
            self.cp('act', VS4[:, t, 0:4, 0:64], self.pb[1][:, 0:256].rearrange("p (h d) -> p h d", d=64), [('pb', 1), 'VS_ones'], [('VS', t)])
            self.qk_post(self.pr[:, 0:512], 8, self.Gt[:, 0:512], t, self.xb[:, 0:512], 'pr', 'xb')
            if t + 1 < NT:
                do_proj(t + 1)
            pT = self.pb[3][:].bitcast(BF16)
            for p in range(4):
                self.tr(pT[:, p * 128:(p + 1) * 128], self.xb[:, p * 128:(p + 1) * 128], self.identb[:], ['xb_dve', 'xb_pool', 'identb'], [('pb', 3)])
            self.cp('act', QKT[:, 0:4, t * 128:(t + 1) * 128], pT[:, 0:512].rearrange("p (a c) -> p a c", c=128), [('pb', 3)], [('QKT', t)])
        allq = [('QKT', t) for t in range(NT)]
        ksum, khi, klo, ktmp = self.c_ksum, self.c_khi, self.c_klo, self.c_tmp
        self.S.add('dve', lambda e: e.tensor_reduce(out=ksum[:], in_=QKT[:, 2:4, :].rearrange("p a (n k) -> p a n k", k=256),
                                                    axis=AX.X, op=ALU.add), reads=allq, writes=['c_ksum'])
        self.cp('dve', khi[:], ksum[:], ['c_ksum'], ['c_khi'])
        self.cp('dve', ktmp[:], khi[:], ['c_khi'], ['c_tmp'])
        self.tt('dve', ktmp[:], ksum[:], ktmp[:], ALU.subtract, ['c_ksum', 'c_tmp'], ['c_tmp'])
        self.cp('dve', klo[:], ktmp[:], ['c_tmp'], ['c_klo'])
        st = {'sb': 0}

        def sel_steps(i):
            nb = i // 2
            if nb == 0:
                return []
            sl = i % 2
            Qz, qzk = self.c_qz[i]
            sc = self.c_sc
            selbT = self.c_selbT2[sl]
            sk = ('c_selbT', sl)

            def s_a():
                for h in range(4):
                    self.mm(self.pb[5][:, h * 16:(h + 1) * 16], Qz[:, h, :], khi[:, h // 2, :], True, False, qzk + ['c_khi'], [('pb', 5)])
                    self.mm(self.pb[5][:, h * 16:(h + 1) * 16], Qz[:, h, :], klo[:, h // 2, :], False, True, qzk + ['c_klo'], [('pb', 5)])

            def s_b():
                self.cp('dve', sc[:].rearrange("p h n -> p (h n)"), self.pb[5][:, 0:64], [('pb', 5)], ['c_sc'])
                if nb < 16:
                    self.memset('dve', sc[:, :, nb:16], -1e30, ['c_sc'])
                for h in range(4):
                    self.S.add('dve', (lambda h: lambda e: e.max(out=self.c_mx[:, h, :], in_=sc[:, h, :]))(h), reads=['c_sc'], writes=['c_mx'])
                self.tt('dve', self.c_selb[:], sc[:], bc(self.c_mx[:, :, 2], 2, [128, 4, 16]), ALU.is_ge, ['c_sc', 'c_mx'], ['c_selb'])
                self.tsc('dve', self.c_selb[:], self.c_selb[:], 1.0, -NEGB, ALU.subtract, ALU.mult, ['c_selb'], ['c_selb'])
                if nb < 16:
                    self.memset('dve', self.c_selb[:, :, nb:16], NEGB, ['c_selb'])

            def s_c():
                for h in range(4):
                    self.tr(self.pb[4][0:16, h * 128:(h + 1) * 128], self.c_selb[:, h, :], self.identf[:], ['c_selb', 'identf'], [('pb', 4)])
                self.cp('act', selbT[:, :], self.pb[4][0:16, 0:512], [('pb', 4)], [sk])
            return [s_a, s_b, s_c]

        def attn_steps(i):
            nb = i // 2
            Qz, qzk = self.c_qz[i]
            selbT = self.c_selbT2[i % 2]
            sk = ('c_selbT', i % 2)
            steps = []

            def pair(j):
                d_ = {}
                past = j < 2 * nb

                def s_():
                    b = (2, 3, 0, 1)[st['sb'] % 4]
                    st['sb'] += 1
                    d_['b'] = b
                    bank = self.pb[b]
                    if past:
                        self.mm(bank[:, 0:512], self.Ec[:, j * 128:(j + 1) * 128], selbT[0:16, 0:512], True, False,
                                ['Ec', sk], [('pb', b)], skip=True)
                    for p in range(2):
                        self.mm(bank[:, p * 256:(p + 1) * 256], QKT[:, 2 + p, j * 128:(j + 1) * 128], Qz[:, 2 * p:2 * p + 2, :],
                                (not past) and p == 0, p == 1, [('QKT', j)] + qzk, [('pb', b)], skip=True)

                def r_():
                    b = d_['b']
                    bank = self.pb[b]
                    ps_ = self.pt_n % 3
                    self.pt_n += 1
                    PT = self.PT[ps_]
                    self.act(PT[:, 0:512], bank[:, 0:512], AF.Exp, [('pb', b)], [('PT', ps_)], scale=0.125)
                    if j == i:
                        self.asel(PT[:, 0:512].rearrange("p (h q) -> p h q", q=128), PT[:, 0:512].rearrange("p (h q) -> p h q", q=128),
                                  [[0, 4], [1, 128]], ALU.is_ge, 0.0, 0, -1, [('PT', ps_)], [('PT', ps_)])
                    for h in range(4):
                        self.mm(self.pb[6][:, h * 72:h * 72 + 65], PT[:, h * 128:(h + 1) * 128], VS4[:, j, h, 0:65],
                                (j == 0 and h == 0), j == i, [('PT', ps_), ('VS', j), 'VS_ones'], [('pb', 6)], skip=True)
                return (s_, r_)
            for j in range(i + 1):
                steps.append(pair(j))

            def fin():
                acc = self.pb[6][:, 0:288].rearrange("p (h e) -> p h e", e=72)
                self.S.add('dve', lambda e: e.reciprocal(out=self.rec[:, 0:4], in_=acc[:, :, 64]), reads=[('pb', 6)], writes=['rec'])
                self.tt('dve', self.ob[:, 0:256].rearrange("p (h d) -> p h d", d=64), acc[:, :, 0:64], bc(self.rec[:, 0:4], 2, [128, 4, 64]),
                        ALU.mult, [('pb', 6), 'rec'], ['ob'])
                self.out_tile(i, 'ob', self.ob, 256, 640)
            steps.append((None, fin))
            return steps

        self.c_qz = {}
        self.c_qz[0] = self.make_qz(0, 4)
        for i in range(NT):
            if i + 1 < NT:
                self.c_qz[i + 1] = self.make_qz(i + 1, 4)
            pend = sel_steps(i + 1) if i + 1 < NT else []
            asteps = attn_steps(i)
            gap = max(1, (len(asteps) - 1) // (len(pend) + 1)) if pend else 1
            for n_, a in enumerate(asteps):
                self.emit_step(a[0], a[1], L=3)
                if pend and (n_ % gap == gap - 1) and n_ < len(asteps) - 1:
                    pend.pop(0)()
            while pend:
                pend.pop(0)()
        self.flush_steps()
        self.dbg_oT()

    def final(self, l, xin, xout):
        dr = self.dr
        if not hasattr(self, 'f_yacc'):
            self.f_yacc = self.b_Pc[:, 0, :]
            self.f_ytmp = self.b_Pc[:, 1, :]
            self.f_yT = self.s1_all[:, 0:4096].rearrange("p (k c) -> p k c", c=512)
        Wzg = self.BIG[:, 0:31744].rearrange("p (k c) -> p k c", c=3968)
        Wbr = self.BIG[:, 31744:38912].rearrange("p (k c) -> p k c", c=1024)
        Wout = self.BIG[:, 38912:47104].rearrange("p (k c) -> p k c", c=1024)
        oz = self.BIG[:, 47104:47104 + 3584].rearrange("p (t k c) -> p t k c", k=7, c=128)
        og = self.BIG[:, 50688:50688 + 3584].rearrange("p (t c) -> p t c", c=896)
        sz = [self.qk_sq[:, 0:512], self.qk_xn[:, 0:512]]
        sg = [self.pr[:, 0:512], self.Gt[:, 0:512]]
        self.wkeys = {}
        for (c0, n, o) in ((1152, 384, 0), (2188, 256, 384), (3212, 256, 640), (3468, 3072, 896)):
            self.load_w(Wzg, 'Wzg', dr['w_in'][l], c0, n, o)
        self.load_w(Wbr, 'Wbr', dr['w_br_a'][l], 0, 1024, 0, nk=3, k0=0)
        self.load_w(Wbr, 'Wbr', dr['w_br_b'][l], 0, 1024, 0, nk=2, k0=3)
        self.load_w(Wbr, 'Wbr', dr['w_br_c'][l], 0, 1024, 0, nk=2, k0=5)
        self.load_w(Wout, 'Wout', dr['w_out'][l], 0, 1024, 0)
        kz, kb, ko = list(self.wkeys['Wzg']), list(self.wkeys['Wbr']), list(self.wkeys['Wout'])
        last = (xout is dr['y'])
        n = 0
        for g in range(NT // 4):
            hsl = self.load_hg(g)
            hgt = self.hg[hsl]
            self.dma(og, dr['oT'][g * 4:(g + 1) * 4].rearrange("t p c -> p t c"),
                     [('oT', g * 4 + i, c) for i in range(4) for c in (0, 384, 640)], ['og'])
            for zc in range(7):
                b = zc % 2
                for k in range(8):
                    self.mm(self.pb[b][:, 0:512], Wzg[:, k, zc * 128:(zc + 1) * 128], hgt[:, :, k, :], k == 0, k == 7,
                            [('hg', hsl)] + kz, [('pb', b)])
                self.act(sz[b], self.pb[b][:, 0:512], AF.Silu, [('pb', b)], [('sz', b)])
                self.tt('dve', oz[:, :, zc, :], og[:, :, zc * 128:(zc + 1) * 128], sz[b].rearrange("p (t c) -> p t c", c=128), ALU.mult,
                        [('sz', b), 'og'], [('oz', zc)])
            for m in range(8):
                for br in range(3):
                    gb = 2 + n % 2
                    ub = 4 + n % 2
                    sgb = sg[n % 2]
                    sgk = ('sg', n % 2)
                    n += 1
                    c0 = 896 + br * 1024 + m * 128
                    for k in range(8):
                        self.mm(self.pb[gb][:, 0:512], Wzg[:, k, c0:c0 + 128], hgt[:, :, k, :], k == 0, k == 7,
                                [('hg', hsl)] + kz, [('pb', gb)])
                    self.act(sgb, self.pb[gb][:, 0:512], AF.Sigmoid, [('pb', gb)], [sgk])
                    kcs = ([0, 1, 2], [3, 4], [5, 6])[br]
                    for ii, kc in enumerate(kcs):
                        self.mm(self.pb[ub][:, 0:512], Wbr[:, kc, m * 128:(m + 1) * 128], oz[:, :, kc, :], ii == 0, ii == len(kcs) - 1,
                                [('oz', kc)] + kb, [('pb', ub)])
                    if br == 0:
                        self.tt('dve', self.f_yacc, self.pb[ub][:, 0:512], sgb, ALU.mult, [('pb', ub), sgk], ['f_yacc'])
                    else:
                        self.tt('dve', self.f_ytmp, self.pb[ub][:, 0:512], sgb, ALU.mult, [('pb', ub), sgk], ['f_ytmp'])
                        if br == 1:
                            self.tt('dve', self.f_yacc, self.f_yacc, self.f_ytmp, ALU.add, ['f_yacc', 'f_ytmp'], ['f_yacc'])
                        else:
                            self.tt('dve', self.f_yT[:, m, :], self.f_yacc, self.f_ytmp, ALU.add, ['f_yacc', 'f_ytmp'], [('yT', m)])
            for tt_ in range(4):
                t = g * 4 + tt_
                xsl = self.xt_n % 2
                self.xt_n += 1
                xt = self.xt[xsl]
                self.dma(xt[:], xin[t * 128:(t + 1) * 128, :], [('x1', t)], [('xt', xsl)])
                osl = xsl
                orow = xt
                for nch in range(2):
                    b = 6 + nch
                    for k in range(8):
                        self.mm(self.pb[b][:, 0:512], self.f_yT[:, k, tt_ * 128:(tt_ + 1) * 128], Wout[:, k, nch * 512:(nch + 1) * 512],
                                k == 0, k == 7, [('yT', k)] + ko, [('pb', b)])
                    self.tt('dve', orow[:, nch * 512:(nch + 1) * 512], self.pb[b][:, 0:512], xt[:, nch * 512:(nch + 1) * 512], ALU.add,
                            [('pb', b), ('xt', xsl)], [('xt', xsl)])
                o = self.dma(xout[t * 128:(t + 1) * 128, :], orow[:], [('xt', xsl)], [('y', t) if last else ('x1', t)])
                if last:
                    self.final_ops.append(o)


def host_layout(sh):
    sh = dict(sh)
    sh['norm_g'] = np.ascontiguousarray(sh['norm_g'].reshape(2, 8, 128).transpose(0, 2, 1))
    sh['cmp_pos'] = np.ascontiguousarray(sh['cmp_pos'].transpose(0, 2, 1))
    for nm in ('cmp_k_w1', 'cmp_v_w1'):
        sh[nm] = np.ascontiguousarray(sh[nm].reshape(2, 32, 64, 128).transpose(0, 2, 1, 3))
    return sh


_CACHE = {}


def kernel(**inputs):
    n = 8
    if 'nc' not in _CACHE:
        _CACHE['nc'] = Builder(2).build()
    nc = _CACHE['nc']
    x = np.ascontiguousarray(inputs['x'], dtype=np.float32)
    pos = np.ascontiguousarray(inputs['positions']).astype(np.int32)
    shared = {}
    for k in ('norm_g', 'w_in', 'q_norm_a', 'k_norm_a', 'q_norm_b', 'k_norm_b', 'q_norm_c', 'k_norm_c', 'cmp_pos',
              'cmp_k_w1', 'cmp_k_w2', 'cmp_v_w1', 'cmp_v_w2', 'w_br_a', 'w_br_b', 'w_br_c', 'w_out'):
        shared[k] = np.ascontiguousarray(inputs[k], dtype=np.float32)
    shared = host_layout(shared)
    in_maps = []
    for c in range(n):
        m = dict(shared)
        m['x'] = x[c]
        m['pos'] = np.ascontiguousarray(pos[c].reshape(NT, 128).T)
        in_maps.append(m)
    res = run_bass_kernel_spmd(nc, in_maps, core_ids=list(range(n)))
    return np.stack([np.asarray(r['y'], dtype=np.float32) for r in res.results], axis=0)
```

```python
import contextlib
import math
import numpy as np
import concourse.bass as bass
import concourse.mybir as mybir
from concourse.bass_utils import run_bass_kernel_spmd

F32 = mybir.dt.float32
BF16 = mybir.dt.bfloat16
I32 = mybir.dt.int32
AF = mybir.ActivationFunctionType
ALU = mybir.AluOpType
AX = mybir.AxisListType

SAME_ENG_SYNC = {'pe': False, 'act': True, 'dve': True, 'pool': True, 'sp': True}
N_DMA_SEMS = 8

S_LEN = 4096
D = 1024
NT = 32
INW = 6540
EPS = 1e-6
NEGB = -30000.0


class _Op:
    __slots__ = ('eng', 'fn', 'deps', 'dma', 'signal', 'sem', 'val', 'idx', 'prev')


class Sched:
    def __init__(self, nc):
        self.nc = nc
        self.ops = []
        self.last_w = {}
        self.readers = {}
        self.fence_keys = []

    def add(self, eng, fn, reads=(), writes=(), dma=False):
        op = _Op()
        op.eng = eng
        op.fn = fn
        op.dma = dma
        op.signal = False
        op.sem = None
        op.val = 0
        op.idx = len(self.ops)
        deps = set()
        if self.fence_keys:
            reads = list(reads) + self.fence_keys
        for k in reads:
            w = self.last_w.get(k)
            if w is not None:
                deps.add(w)
        for k in writes:
            w = self.last_w.get(k)
            if w is not None:
                deps.add(w)
            for r in self.readers.get(k, ()):
                deps.add(r)
        op.deps = deps
        for k in reads:
            self.readers.setdefault(k, []).append(op.idx)
        for k in writes:
            self.last_w[k] = op.idx
            self.readers[k] = []
        self.ops.append(op)
        return op.idx

    def emit(self, final_waits=()):
        nc = self.nc
        ops = self.ops
        engs = ['pe', 'act', 'dve', 'pool', 'sp']
        for op in ops:
            need = set()
            best = {}
            for d in op.deps:
                Dp = ops[d]
                if Dp.eng == op.eng and not Dp.dma and not op.dma and not SAME_ENG_SYNC[op.eng]:
                    continue
                if Dp.dma:
                    need.add(d)
                    Dp.signal = True
                elif best.get(Dp.eng, -1) < d:
                    best[Dp.eng] = d
            for d in best.values():
                need.add(d)
                ops[d].signal = True
            op.deps = need
        for d in final_waits:
            ops[d].signal = True
        with contextlib.ExitStack() as st:
            esem = {e: st.enter_context(nc.semaphore('s_' + e)) for e in engs}
            dsem = {e: [st.enter_context(nc.semaphore('d_%s%d' % (e, i))) for i in range(N_DMA_SEMS)]
                    for e in ('sp', 'pool', 'act')}
            ecount = {e: 0 for e in engs}
            dcount = {e: 0 for e in engs}
            for op in ops:
                if op.dma:
                    op.signal = True
                if not op.signal:
                    continue
                if op.dma:
                    i = dcount[op.eng]
                    dcount[op.eng] += 1
                    op.sem = dsem[op.eng][i % N_DMA_SEMS]
                    op.val = 16 * (i // N_DMA_SEMS + 1)
                    op.prev = (op.sem, op.val - 16) if i >= N_DMA_SEMS else None
                else:
                    ecount[op.eng] += 1
                    op.sem = esem[op.eng]
                    op.val = ecount[op.eng]
            per = {e: [] for e in engs}
            for op in ops:
                per[op.eng].append(op)
            block = st.enter_context(nc.Block())

            def run(e, name, extra=()):
                waited = {}
                for op in per[name]:
                    ws = {}
                    for d in op.deps:
                        Dp = ops[d]
                        key = id(Dp.sem)
                        if waited.get(key, 0) >= Dp.val:
                            continue
                        if key not in ws or ws[key][1] < Dp.val:
                            ws[key] = (Dp.sem, Dp.val)
                    if op.dma and op.prev is not None:
                        key = id(op.prev[0])
                        if waited.get(key, 0) < op.prev[1] and (key not in ws or ws[key][1] < op.prev[1]):
                            ws[key] = op.prev
                    for key, (sem, val) in ws.items():
                        e.wait_ge(sem, val)
                        waited[key] = val
                    ins = op.fn(e)
                    if op.signal:
                        ins.then_inc(op.sem, 16 if op.dma else 1)
                for d in extra:
                    Dp = ops[d]
                    e.wait_ge(Dp.sem, Dp.val)

            @block.tensor
            def _(e):
                run(e, 'pe')

            @block.scalar
            def _(e):
                run(e, 'act')

            @block.vector
            def _(e):
                run(e, 'dve')

            @block.gpsimd
            def _(e):
                run(e, 'pool')

            @block.sync
            def _(e):
                run(e, 'sp', extra=final_waits)
        self.stats = {e: len(per[e]) for e in engs}
        self.stats['signals'] = dict(ecount)
        self.stats['dmasig'] = dict(dcount)


def bc(ap, axis, shape):
    return ap.unsqueeze(axis).to_broadcast(shape)


class Builder:
    def __init__(self, n_layers=2, dbg=None, stop_after=None):
        self.n_layers = n_layers
        self.dbg = dbg or ()
        self.stop_after = stop_after
        nc = bass.Bass("TRN2", target_bir_lowering=False)
        self.nc = nc
        self.S = Sched(nc)
        self.st = contextlib.ExitStack()
        self.uid = 0

    def sb(self, name, shape, dt):
        return self.st.enter_context(self.nc.sbuf_tensor(name, shape, dt))

    def ps(self, name, shape, dt=F32):
        return self.st.enter_context(self.nc.psum_tensor(name, shape, dt))

    def mm(self, out, lhsT, rhs, start, stop, r, w, skip=False):
        if skip:
            self.S.add('pe', lambda e: e.matmul(out, lhsT=lhsT, rhs=rhs, start=start, stop=stop, skip_group_check=True), reads=r, writes=w)
        else:
            self.S.add('pe', lambda e: e.matmul(out, lhsT=lhsT, rhs=rhs, start=start, stop=stop), reads=r, writes=w)

    def tr(self, out, in_, ident, r, w):
        self.S.add('pe', lambda e: e.transpose(out=out, in_=in_, identity=ident), reads=r, writes=w)

    def act(self, out, in_, func, r, w, bias=None, scale=None, accum_out=None):
        kw = {}
        if bias is not None:
            kw['bias'] = bias
        if scale is not None:
            kw['scale'] = scale
        if accum_out is not None:
            kw['accum_out'] = accum_out
        self.S.add('act', lambda e: e.activation(out=out, in_=in_, func=func, **kw), reads=r, writes=w)

    def rsqrt(self, out, in_, scale, r, w):
        self.act(out, in_, AF.Ln, r, w, bias=self.epsc[:out.shape[0], 0:1], scale=scale)
        self.act(out, out, AF.Exp, w, w, scale=-0.5)

    def tt(self, eng, out, in0, in1, op, r, w):
        self.S.add(eng, lambda e: e.tensor_tensor(out=out, in0=in0, in1=in1, op=op), reads=r, writes=w)

    def tsc(self, eng, out, in0, s1, s2, op0, op1, r, w):
        if op1 is None:
            self.S.add(eng, lambda e: e.tensor_scalar(out=out, in0=in0, scalar1=s1, scalar2=None, op0=op0), reads=r, writes=w)
        else:
            self.S.add(eng, lambda e: e.tensor_scalar(out=out, in0=in0, scalar1=s1, scalar2=s2, op0=op0, op1=op1), reads=r, writes=w)

    def cp(self, eng, out, in_, r, w):
        if eng == 'act':
            self.S.add('act', lambda e: e.copy(out=out, in_=in_), reads=r, writes=w)
        else:
            self.S.add(eng, lambda e: e.tensor_copy(out=out, in_=in_), reads=r, writes=w)

    def memset(self, eng, ap, val, w):
        self.S.add(eng, lambda e: e.memset(ap, val), writes=w)

    def asel(self, out, in_, pattern, op, fill, base, cm, r, w):
        self.S.add('pool', lambda e: e.affine_select(out=out, in_=in_, pattern=pattern, compare_op=op, fill=fill,
                                                     base=base, channel_multiplier=cm), reads=r, writes=w)

    def dma(self, out, in_, r, w, q='sp'):
        return self.S.add(q, lambda e: e.dma_start(out=out, in_=in_), reads=r, writes=w, dma=True)

    def build_E(self, nm, blk, npart, off):
        E = self.BIG[0:npart, off:off + S_LEN]
        self.memset('pool', E, 1.0, [nm])
        self.asel(E, E, [[1, S_LEN]], ALU.is_ge, 0.0, 0, -blk, [nm], [nm])
        self.asel(E, E, [[-1, S_LEN]], ALU.is_ge, 0.0, blk - 1, blk, [nm], [nm])
        return E

    def make_qz(self, i, nh):
        sl = self.qz_n % 2
        self.qz_n += 1
        Qz = self.Qz[sl]
        npair = nh // 2
        for par in range(2):
            self.cp('pool', Qz[par * 64:(par + 1) * 64, par:nh:2, :], self.QKT[par * 64:(par + 1) * 64, 0:npair, i * 128:(i + 1) * 128],
                    [('QKT', i), 'Qz0'], [('Qz', sl, par)])
        return Qz, [('Qz', sl, 0), ('Qz', sl, 1)]

    def emit_step(self, s_fn, r_fn, L=1):
        if s_fn is not None:
            s_fn()
        if not hasattr(self, '_pend'):
            self._pend = []
        self._pend.append(r_fn)
        while len(self._pend) > L:
            self._pend.pop(0)()

    def flush_steps(self):
        for p in getattr(self, '_pend', []):
            p()
        self._pend = []

    def fence(self):
        self.S.fence_keys = []
        self.fence_n = getattr(self, 'fence_n', 0) + 1
        n = self.fence_n
        fs = self.fsc
        self.mm(self.pb[7][0:1, 0:1], self.identb[0:1, 0:1], self.identb[0:1, 0:1], True, True, ['identb'], [('pb', 7), ('fence', 'pe', n)])
        self.cp('act', fs[0:1, 0:1], fs[0:1, 4:5], ['fsc'], [('fence', 'act', n), 'fscw_act'])
        self.cp('dve', fs[0:1, 1:2], fs[0:1, 5:6], ['fsc'], [('fence', 'dve', n), 'fscw_dve'])
        self.cp('pool', fs[0:1, 2:3], fs[0:1, 6:7], ['fsc'], [('fence', 'pool', n), 'fscw_pool'])
        self.S.fence_keys = [('fence', e, n) for e in ('pe', 'act', 'dve', 'pool')]

    def build(self):
        nc = self.nc
        dr = {}

        def din(name, shape, dt=F32):
            dr[name] = nc.dram_tensor(name, shape, dt, kind="ExternalInput").ap()

        din('x', [S_LEN, D])
        din('pos', [128, NT], I32)
        din('norm_g', [2, 128, 8])
        din('w_in', [2, D, INW])
        for nm in ('q_norm_a', 'k_norm_a', 'q_norm_b', 'k_norm_b', 'q_norm_c', 'k_norm_c'):
            din(nm, [2, 64])
        din('cmp_pos', [2, 64, 32])
        din('cmp_k_w1', [2, 64, 32, 128])
        din('cmp_k_w2', [2, 128, 64])
        din('cmp_v_w1', [2, 64, 32, 128])
        din('cmp_v_w2', [2, 128, 64])
        din('w_br_a', [2, 384, D])
        din('w_br_b', [2, 256, D])
        din('w_br_c', [2, 256, D])
        din('w_out', [2, D, D])
        dr['y'] = nc.dram_tensor('y', [S_LEN, D], F32, kind="ExternalOutput").ap()
        dr['x1'] = nc.dram_tensor('x1s', [S_LEN, D], F32).ap()
        dr['hnT'] = nc.dram_tensor('hnTs', [NT, 128, 1024], BF16).ap()
        dr['oT'] = nc.dram_tensor('oTs', [NT, 128, 7 * 128], BF16).ap()
        for nm, shape, dt in self.dbg:
            dr[nm] = nc.dram_tensor(nm, shape, dt, kind="ExternalOutput").ap()
        self.dr = dr
        self.final_ops = []
        with self.st:
            self.setup()
            for l in range(self.n_layers):
                xin = dr['x'] if l == 0 else dr['x1']
                xout = dr['y'] if l == self.n_layers - 1 else dr['x1']
                self.layer(l, xin, xout)
            self.S.emit(final_waits=self.final_ops)
        return nc

    def setup(self):
        nc = self.nc
        dr = self.dr
        self.alloc_common()
        self.identf = self.sb('identf', [128, 128], F32)
        self.identb = self.sb('identb', [128, 128], BF16)
        self.onesf = self.sb('onesf', [128, 128], F32)
        self.memset('pool', self.identf[:], 1.0, ['identf'])
        self.asel(self.identf[:], self.identf[:], [[-1, 128]], ALU.is_equal, 0.0, 0, 1, ['identf'], ['identf'])
        self.cp('dve', self.identb[:], self.identf[:], ['identf'], ['identb'])
        self.memset('pool', self.onesf[:], 1.0, ['onesf'])
        self.epsc = self.sb('epsc', [128, 1], F32)
        self.memset('dve', self.epsc[:], EPS, ['epsc'])
        for q_ in self.Qz:
            self.memset('pool', q_[:], 0.0, ['Qz0'])
        self.fsc = self.sb('fsc', [128, 8], F32)
        self.memset('dve', self.fsc[:], 0.0, ['fsc'])
        posi = self.sb('posi', [128, NT], I32)
        posf = self.sb('posf', [128, NT], F32)
        fr = self.sb('fr', [128, 8], F32)
        wpF = self.BIG[:, 46592:46592 + 9216].bitcast(F32)
        wpI = self.BIG[:, 46592:46592 + 9216].bitcast(I32)
        ang = wpF[:, 0:256].rearrange("p (t f) -> p t f", f=8)
        tmpa = wpF[:, 256:512].rearrange("p (t f) -> p t f", f=8)
        self.cos = self.sb('cos', [128, NT, 8], F32)
        self.sin = self.sb('sin', [128, NT, 8], F32)
        self.dma(posi[:], dr['pos'][:, :], [], ['posi'])
        self.cp('dve', posf[:], posi[:], ['posi'], ['posf'])
        for i in range(8):
            f = float(np.float32(500000.0) ** np.float32(-i / 8.0))
            self.memset('dve', fr[:, i:i + 1], f, ['fr'])
        self.tt('dve', ang[:], bc(posf[:], 2, [128, NT, 8]), bc(fr[:], 1, [128, NT, 8]), ALU.mult, ['posf', 'fr'], ['ang'])
        PI = math.pi
        HI = 6.28125
        LO = 2 * PI - 6.28125
        ni = wpI[:, 512:768].rearrange("p (t f) -> p t f", f=8)
        nf = wpF[:, 768:1024].rearrange("p (t f) -> p t f", f=8)
        rr = wpF[:, 1024:1280].rearrange("p (t f) -> p t f", f=8)
        self.tsc('dve', tmpa[:], ang[:], 1.0 / (2 * PI), None, ALU.mult, None, ['ang'], ['tmpa'])
        self.cp('dve', ni[:], tmpa[:], ['tmpa'], ['rr_ni'])
        self.cp('dve', nf[:], ni[:], ['rr_ni'], ['rr_nf'])
        self.S.add('dve', lambda e: e.scalar_tensor_tensor(out=rr[:], in0=nf[:], scalar=-HI, in1=ang[:], op0=ALU.mult, op1=ALU.add),
                   reads=['rr_nf', 'ang'], writes=['rr_r'])
        self.S.add('dve', lambda e: e.scalar_tensor_tensor(out=rr[:], in0=nf[:], scalar=-LO, in1=rr[:], op0=ALU.mult, op1=ALU.add),
                   reads=['rr_nf', 'rr_r'], writes=['rr_r'])

        def wrap(buf, key):
            self.tsc('dve', tmpa[:], buf[:], PI, -2 * PI, ALU.is_gt, ALU.mult, [key], ['tmpa'])
            self.tt('dve', buf[:], buf[:], tmpa[:], ALU.add, [key, 'tmpa'], [key])
            self.tsc('dve', tmpa[:], buf[:], -PI, 2 * PI, ALU.is_lt, ALU.mult, [key], ['tmpa'])
            self.tt('dve', buf[:], buf[:], tmpa[:], ALU.add, [key, 'tmpa'], [key])
            self.tsc('dve', buf[:], buf[:], -3.141592, 3.141592, ALU.max, ALU.min, [key], [key])
        wrap(rr, 'rr_r')
        self.act(self.sin[:], rr[:], AF.Sin, ['rr_r'], ['sin'])
        self.tsc('dve', rr[:], rr[:], PI / 2, None, ALU.add, None, ['rr_r', 'sin'], ['rr_r'])
        wrap(rr, 'rr_r')
        self.act(self.cos[:], rr[:], AF.Sin, ['rr_r'], ['cos'])
        scrF = self.BIG[:, 0:24576].bitcast(F32)
        scrI = self.BIG[:, 0:24576].bitcast(I32)

        def carve(src, k):
            return src[:, k * 2176:(k + 1) * 2176].rearrange("p (o q) -> p o q", q=128)
        dA = carve(scrF, 0)
        dAi = carve(scrI, 1)
        t1 = carve(scrF, 2)
        t2 = carve(scrF, 3)
        t3 = carve(scrF, 4)
        t4 = carve(scrI, 4)
        self.MA = self.sb('MA', [128, 17, 128], BF16)
        self.S.add('pool', lambda e: e.iota(dAi[:], pattern=[[128, 17], [1, 128]], base=0, channel_multiplier=-1), writes=['dAi'])
        self.cp('dve', dA[:], dAi[:], ['dAi'], ['dA'])
        self.tsc('dve', t1[:], dA[:], 128.0, None, ALU.is_le, None, ['dA'], ['mt1'])
        self.tsc('dve', t4[:], dAi[:], 3, None, ALU.bitwise_and, None, ['dAi'], ['mt3'])
        self.cp('dve', t2[:], t4[:], ['mt3'], ['mt2'])
        self.tsc('dve', t2[:], t2[:], 0.0, None, ALU.is_equal, None, ['mt2'], ['mt2'])
        self.tsc('dve', t3[:], dA[:], 512.0, None, ALU.is_le, None, ['dA'], ['mt3'])
        self.tt('dve', t2[:], t2[:], t3[:], ALU.mult, ['mt2', 'mt3'], ['mt2'])
        self.tt('dve', t1[:], t1[:], t2[:], ALU.add, ['mt1', 'mt2'], ['mt1'])
        self.tsc('dve', t4[:], dAi[:], 15, None, ALU.bitwise_and, None, ['dAi', 'mt2'], ['mt3'])
        self.cp('dve', t2[:], t4[:], ['mt3', 'mt1'], ['mt2'])
        self.tsc('dve', t2[:], t2[:], 0.0, None, ALU.is_equal, None, ['mt2'], ['mt2'])
        self.tsc('dve', t3[:], dA[:], 2048.0, None, ALU.is_le, None, ['dA', 'mt2'], ['mt3'])
        self.tt('dve', t2[:], t2[:], t3[:], ALU.mult, ['mt2', 'mt3'], ['mt2'])
        self.tt('dve', t1[:], t1[:], t2[:], ALU.add, ['mt1', 'mt2'], ['mt1'])
        self.tsc('dve', t2[:], dA[:], 0.0, None, ALU.is_ge, None, ['dA', 'mt1'], ['mt2'])
        self.tt('dve', self.MA[:], t1[:], t2[:], ALU.mult, ['mt1', 'mt2'], ['MA'])
        self.AM = self.sb('AM', [128, NT, 64], F32)
        vsF = self.BIG[:, 32768:32768 + NT * 432].bitcast(F32)
        vsI = self.BIG[:, 32768:32768 + NT * 432].bitcast(I32)
        am1 = vsF[:, 0:2048].rearrange("p (t j) -> p t j", j=64)
        am1i = vsI[:, 2048:4096].rearrange("p (t j) -> p t j", j=64)
        am2 = vsF[:, 4096:6144].rearrange("p (t j) -> p t j", j=64)
        for a in range(2):
            self.S.add('pool', (lambda a: lambda e: e.iota(am1i[a * 64:(a + 1) * 64], pattern=[[-2, NT], [1, 64]], base=-a,
                                                           channel_multiplier=0))(a),
                       writes=['am1i_%d' % a])
        self.cp('dve', am1[:], am1i[:], ['am1i_0', 'am1i_1'], ['am1_0', 'am1_1'])
        self.tsc('dve', am2[:], am1[:], 0.0, -1e30, ALU.is_gt, ALU.mult, ['am1_0', 'am1_1'], ['am2'])
        self.tsc('dve', am1[:], am1[:], -1.0, 1e4, ALU.is_ge, ALU.mult, ['am1_0', 'am1_1', 'am2'], ['am1', 'am1_0', 'am1_1'])
        self.tt('dve', self.AM[:], am1[:], am2[:], ALU.add, ['am1', 'am2'], ['AM'])
        self.tsc('dve', self.AM[:, :, 0:1], self.AM[:, :, 0:1], 1e4, None, ALU.add, None, ['AM'], ['AM'])
        self.cover = self.sb('cover', [128, 2, 64], F32)
        self.memset('pool', self.cover[:], 1.0, ['cover'])
        self.asel(self.cover[:], self.cover[:], [[-128, 2], [4, 64]], ALU.is_ge, 0.0, 3, -1, ['cover'], ['cover'])
        self.asel(self.cover[:], self.cover[:], [[128, 2], [-4, 64]], ALU.is_ge, 0.0, 1, 1, ['cover'], ['cover'])
        self.pb = [self.ps('pb%d' % i, [128, 512], F32) for i in range(8)]
        self.xt = [self.sb('xt%d' % i, [128, D], F32) for i in range(2)]
        self.hg = [self.sb('hg%d' % i, [128, 4, 8, 128], BF16) for i in range(2)]
        self.wstage = [self.sb('wst%d' % i, [128, 8, 128], F32) for i in range(2)]
        self.wst_n = 0
        self.hg_n = 0
        self.xt_n = 0
        if 'dbg_cs' in [d[0] for d in self.dbg]:
            o = self.dma(self.dr['dbg_cs'][:, 0:256], self.cos[:].rearrange("p t f -> p (t f)"), ['cos'], [])
            self.final_ops.append(o)
            o = self.dma(self.dr['dbg_cs'][:, 256:512], self.sin[:].rearrange("p t f -> p (t f)"), ['sin'], [])
            self.final_ops.append(o)
            mtmp = self.sb('mtmp', [128, 17 * 128], F32)
            self.cp('dve', mtmp[:], self.MA[:].rearrange("p o q -> p (o q)"), ['MA'], ['mtmp'])
            o = self.dma(self.dr['dbg_ma'][:, :], mtmp[:], ['mtmp'], [])
            self.final_ops.append(o)
            o = self.dma(self.dr['dbg_am'][:, :], self.AM[:].rearrange("p t f -> p (t f)"), ['AM'], [])
            self.final_ops.append(o)
            o = self.dma(self.dr['dbg_cov'][:, :], self.cover[:].rearrange("p t f -> p (t f)"), ['cover'], [])
            self.final_ops.append(o)

    def load_w(self, W, wkey, src, c0, n, o, nk=8, k0=0, eng_cycle=('pool', 'dve')):
        done = 0
        while done < n:
            m = min(128, n - done)
            sl = self.wst_n % 4
            self.wst_n += 1
            if sl < 2:
                stg = self.wstage[sl]
                skey = ('wst', sl)
            else:
                stg = self.xt[sl - 2][:].rearrange("p (k c) -> p k c", c=128)
                skey = ('xt', sl - 2)
            self.dma(stg[:, 0:nk, 0:m], src[:, c0 + done:c0 + done + m].rearrange("(k p) c -> p k c", p=128),
                     [], [skey])
            eng = eng_cycle[self.wst_n % len(eng_cycle)]
            self.cp(eng, W[:, k0:k0 + nk, o + done:o + done + m], stg[:, 0:nk, 0:m], [skey], [(wkey, self.wst_n)])
            self.wkeys.setdefault(wkey, []).append((wkey, self.wst_n))
            done += m

    def layer(self, l, xin, xout):
        if l > 0:
            self.fence()
        self.stage1(l, xin)
        if self.stop_after == 'stage1':
            return
        only = getattr(self, 'only', None)
        if only is None or 'A' in only:
            self.fence()
            self.pass_A(l)
        if self.stop_after in ('A', 'Aproj'):
            return
        if only is None or 'C' in only:
            self.fence()
            self.pass_C(l)
        if self.stop_after == 'C':
            return
        if only is None or 'B' in only:
            self.fence()
            self.pass_B(l)
        if self.stop_after == 'B':
            return
        self.fence()
        self.final(l, xin, xout)

    def stage1(self, l, xin):
        dr = self.dr
        if l == 0:
            self.gT = self.sb('gT', [128, 8], F32)
            self.s1_all = self.sb('s1all', [128, 4096], BF16)
            self.s1_sq = self.s1_all[:, 0:1024]
            self.s1_ss = self.sb('s1ss', [128, 2], F32)
            self.s1_ss4 = self.sb('s1ss4', [128, 4], F32)
            self.s1_xs = self.s1_all[:, 1024:2048]
            self.s1_hT = [self.s1_all[:, 2048 + i * 1024:3072 + i * 1024].rearrange("p (k c) -> p k c", c=128) for i in range(2)]
            self.prs = [self.pr, self.s1_all[:, 0:1536].bitcast(F32)]
        self.dma(self.gT[:], dr['norm_g'][l], [], ['gT'])
        ring = [(self.xt[0][:], ('xt', 0)), (self.xt[1][:], ('xt', 1)),
                (self.wstage[0][:].rearrange("p k c -> p (k c)"), ('wst', 0)), (self.wstage[1][:].rearrange("p k c -> p (k c)"), ('wst', 1))]
        ss4 = self.s1_ss4

        def st_a(t):
            xt_ap, xkey = ring[t % 4]
            self.dma(xt_ap, xin[t * 128:(t + 1) * 128, :], [('x1', t)], [xkey])
            junk = self.BIG[:, 24576 + (t % 2) * 1024:24576 + (t % 2 + 1) * 1024]
            self.act(junk, xt_ap, AF.Square, [xkey], [('s1junk', t % 2), ('s1ss', t % 4)], accum_out=ss4[:, t % 4:t % 4 + 1])

        def st_b(t):
            ss = ss4[:, t % 4:t % 4 + 1]
            self.act(ss, ss, AF.Ln, [('s1ss', t % 4)], [('s1ss', t % 4)], bias=self.epsc[:, 0:1], scale=1.0 / D)

        def st_c(t):
            ss = ss4[:, t % 4:t % 4 + 1]
            self.act(ss, ss, AF.Exp, [('s1ss', t % 4)], [('s1ss', t % 4)], scale=-0.5)

        def st_d(t):
            sl = t % 2
            xt_ap, xkey = ring[t % 4]
            ss = ss4[:, t % 4:t % 4 + 1]
            xs = (self.s1_sq, self.s1_xs)[sl]
            self.act(xs, xt_ap, AF.Copy, [xkey, ('s1ss', t % 4)], [('s1xs', sl)], scale=ss)
            pT = self.pb[sl][:].bitcast(BF16)
            for k in range(8):
                self.tr(pT[:, k * 128:(k + 1) * 128], xs[:, k * 128:(k + 1) * 128], self.identb[:],
                        [('s1xs', sl), 'identb'], [('pb', sl)])
            hT = self.s1_hT[sl]
            self.tt('dve', hT, pT[:, 0:1024].rearrange("p (k t) -> p k t", k=8), bc(self.gT[:], 2, [128, 8, 128]), ALU.mult,
                    [('pb', sl), 'gT'], [('s1hT', sl)])
            self.dma(dr['hnT'][t], hT.rearrange("p k t -> p (k t)"), [('s1hT', sl)], [('hnT', t)])
        for u in range(-3, NT):
            if 0 <= u + 3 < NT:
                st_a(u + 3)
            if 0 <= u + 2 < NT:
                st_b(u + 2)
            if 0 <= u + 1 < NT:
                st_c(u + 1)
            if 0 <= u < NT:
                st_d(u)
        if 'dbg_hnT' in [d[0] for d in self.dbg]:
            for t in range(NT):
                o = self.dma(dr['dbg_hnT'][t], dr['hnT'][t], [('hnT', t)], [])
                self.final_ops.append(o)

    def load_hg(self, g):
        sl = self.hg_n % 2
        self.hg_n += 1
        self.dma(self.hg[sl][:].rearrange("p t k c -> p t (k c)"), self.dr['hnT'][g * 4:(g + 1) * 4].rearrange("t p c -> p t c"),
                 [('hnT', g * 4 + i) for i in range(4)], [('hg', sl)])
        return sl

    def proj_tile(self, hsl, tt, W, wreads, chunks, banks):
        for (c0, n), b in zip(chunks, banks):
            for k in range(8):
                self.mm(self.pb[b][:, 0:n], self.hg[hsl][:, tt, k, :], W[:, k, c0:c0 + n], k == 0, k == 7,
                        [('hg', hsl)] + wreads, [('pb', b)])

    def qk_Q(self, pr, nh, prk, sl):
        W_ = nh * 64
        sq = self.qk_sq[:, 0:W_]
        ss = self.qk_ss2[:, sl, 0:nh]
        self.tt('dve', sq, pr, pr, ALU.mult, [prk], ['qk_sq'])
        self.S.add('dve', lambda e: e.tensor_reduce(out=ss, in_=sq.rearrange("p (h d) -> p h d", d=64), axis=AX.X, op=ALU.add),
                   reads=['qk_sq'], writes=[('qk_ss', sl)])

    def qk_R(self, nh, sl):
        ss = self.qk_ss2[:, sl, 0:nh]
        self.rsqrt(ss, ss, 1.0 / 64, [('qk_ss', sl)], [('qk_ss', sl)])

    def qk_N(self, pr, nh, Gt, t, xb, prk, xbk, sl):
        W_ = nh * 64
        ss = self.qk_ss2[:, sl, 0:nh]
        na = (2 * nh + 2) // 3
        pr3 = pr.rearrange("p (h d) -> p h d", d=64)
        xn3 = self.qk_xn[:, 0:W_].rearrange("p (h d) -> p h d", d=64)
        xb3 = xb.rearrange("p (h d) -> p h d", d=64)
        G3 = Gt.rearrange("p (h d) -> p h d", d=64)
        for eng, h0, h1 in (('dve', 0, na), ('pool', na, nh)):
            n_ = h1 - h0
            if n_ <= 0:
                continue
            kx = 'qk_xn_' + eng
            xn_ = xn3[:, h0:h1, :]
            self.tt(eng, xn_, pr3[:, h0:h1, :], bc(ss[:, h0:h1], 2, [128, n_, 64]), ALU.mult, [prk, ('qk_ss', sl)], [kx])
            self.tt(eng, xn_, xn_, G3[:, h0:h1, :], ALU.mult, [kx, 'Gt'], [kx])
            cosb = bc(self.cos[:, t, :], 1, [128, n_, 8])
            sinb = bc(self.sin[:, t, :], 1, [128, n_, 8])
            r = [self.qk_r[i][:, h0:h1, :] for i in range(4)]
            rk = ['qk_r%d_%s' % (i, eng) for i in range(4)]
            self.tt(eng, r[0], xn_[:, :, 0:8], cosb, ALU.mult, [kx, 'cos'], [rk[0]])
            self.tt(eng, r[1], xn_[:, :, 8:16], sinb, ALU.mult, [kx, 'sin'], [rk[1]])
            self.tt(eng, r[2], xn_[:, :, 8:16], cosb, ALU.mult, [kx, 'cos'], [rk[2]])
            self.tt(eng, r[3], xn_[:, :, 0:8], sinb, ALU.mult, [kx, 'sin'], [rk[3]])
            xk = xbk + '_' + eng
            self.tt(eng, xb3[:, h0:h1, 0:8], r[0], r[1], ALU.subtract, [rk[0], rk[1]], [xk])
            self.tt(eng, xb3[:, h0:h1, 8:16], r[2], r[3], ALU.add, [rk[2], rk[3]], [xk])
            self.cp(eng, xb3[:, h0:h1, 16:64], xn_[:, :, 16:64], [kx], [xk])

    def proj_pipeline(self, P, E, Q, R, N, T):
        P(0)
        E(0)
        Q(0)
        R(0)
        if NT > 1:
            P(1)
        for t in range(NT):
            if t + 1 < NT:
                E(t + 1)
                Q(t + 1)
                R(t + 1)
            N(t)
            if t + 2 < NT:
                P(t + 2)
            T(t)

    def alloc_common(self):
        if hasattr(self, 'qk_sq'):
            return
        self.qk_sq = self.sb('qk_sq', [128, 768], F32)
        self.qk_ss = self.sb('qk_ss', [128, 12], F32)
        self.qk_ss2 = self.sb('qk_ss2', [128, 2, 12], F32)
        self.qk_xn = self.sb('qk_xn', [128, 768], F32)
        self.qk_r = [self.sb('qk_r%d' % i, [128, 12, 8], F32) for i in range(4)]
        self.pr = self.sb('pr', [128, 768], F32)
        self.xb = self.sb('xb', [128, 768], BF16)
        self.Gt = self.sb('Gt', [128, 768], F32)
        self.g64 = self.sb('g64', [128, 2, 64], F32)
        self.PT = [self.sb('PT%d' % i, [128, 768], BF16) for i in range(3)]
        self.pt_n = 0
        self.ob = self.sb('ob', [128, 384], BF16)
        self.rec = self.sb('rec', [128, 12], F32)
        self.oTt = [self.sb('oTt%d' % i, [128, 384], BF16) for i in range(2)]
        self.selT = [self.sb('selT%d' % i, [64, 512], BF16) for i in range(2)]
        self.Qz = [self.sb('Qz%d' % i, [128, 6, 128], BF16) for i in range(2)]
        self.qz_n = 0
        self.ot_n = 0
        self.BIG = self.sb('BIG', [128, 57344], BF16)
        self.QKT = self.BIG[:, 0:32768].rearrange("p (a c) -> p a c", c=S_LEN)
        self.VS = self.BIG[:, 32768:32768 + NT * 432].rearrange("p (t c) -> p t c", c=432)
        self.Wp = self.BIG[:, 46592:46592 + 9216].rearrange("p (k c) -> p k c", c=1152)

    def load_gains(self, l, qn, kn, nq, nk):
        dr = self.dr
        self.dma(self.g64[:, 0, :], dr[qn][l].partition_broadcast(128), [], ['g64q'])
        self.dma(self.g64[:, 1, :], dr[kn][l].partition_broadcast(128), [], ['g64k'])
        G3 = self.Gt[:, 0:(nq + nk) * 64].rearrange("p (h d) -> p h d", d=64)
        self.cp('dve', G3[:, 0:nq, :], bc(self.g64[:, 0, :], 1, [128, nq, 64]), ['g64q'], ['Gt'])
        self.cp('dve', G3[:, nq:nq + nk, :], bc(self.g64[:, 1, :], 1, [128, nk, 64]), ['g64k'], ['Gt'])

    def out_tile(self, i, acc_key, ob_ap, ncol, c0):
        npair = ncol // 128
        pT = self.pb[7][:].bitcast(BF16)
        for p in range(npair):
            self.tr(pT[:, p * 128:(p + 1) * 128], ob_ap[:, p * 128:(p + 1) * 128], self.identb[:], [acc_key, 'identb'], [('pb', 7)])
        sl = self.ot_n % 2
        self.ot_n += 1
        self.cp('act', self.oTt[sl][:, 0:ncol], pT[:, 0:ncol], [('pb', 7)], [('oTt', sl)])
        self.dma(self.dr['oT'][i][:, c0:c0 + ncol], self.oTt[sl][:, 0:ncol], [('oTt', sl)], [('oT', i, c0)])

    def pass_A(self, l):
        dr = self.dr
        self.alloc_common()
        self.wkeys = {}
        W = self.Wp
        self.load_w(W, 'Wp', dr['w_in'][l], 0, 1152, 0)
        wreads = list(self.wkeys['Wp'])
        self.load_gains(l, 'q_norm_a', 'k_norm_a', 6, 6)
        VS4 = self.VS.rearrange("p t (h e) -> p t h e", e=72)
        self.S.add('pool', lambda e: e.memset(VS4[:, :, :, 64:65], 1.0), reads=['MA', 'AM'], writes=['VS_ones'])
        QKT = self.QKT
        hs_ = {}

        def do_proj(t):
            if t % 4 == 0:
                hs_['sl'] = self.load_hg(t // 4)
            self.proj_tile(hs_['sl'], t % 4, W, wreads, [(0, 384), (384, 384), (768, 384)], [0, 1, 2])

        def E_(t):
            pr = self.prs[t % 2]
            self.cp('act', pr[:, 0:384], self.pb[0][:, 0:384], [('pb', 0)], [('pr', t % 2)])
            self.cp('act', pr[:, 384:768], self.pb[1][:, 0:384], [('pb', 1)], [('pr', t % 2)])
            self.cp('act', VS4[:, t, :, 0:64], self.pb[2][:, 0:384].rearrange("p (h d) -> p h d", d=64), [('pb', 2), 'VS_ones', 'MA', 'AM'], [('VS', t)])

        def T_(t):
            pT = self.pb[3][:].bitcast(BF16)
            for p in range(6):
                self.tr(pT[:, p * 128:(p + 1) * 128], self.xb[:, p * 128:(p + 1) * 128], self.identb[:], ['xb_dve', 'xb_pool', 'identb'], [('pb', 3)])
            self.cp('act', QKT[:, 0:6, t * 128:(t + 1) * 128], pT[:, 0:768].rearrange("p (a c) -> p a c", c=128), [('pb', 3), 'MA', 'AM'], [('QKT', t)])
        self.proj_pipeline(do_proj, E_,
                           lambda t: self.qk_Q(self.prs[t % 2][:, 0:768], 12, ('pr', t % 2), t % 2),
                           lambda t: self.qk_R(12, t % 2),
                           lambda t: self.qk_N(self.prs[t % 2][:, 0:768], 12, self.Gt[:, 0:768], t, self.xb[:, 0:768], ('pr', t % 2), 'xb', t % 2),
                           T_)
        if self.stop_after == 'Aproj':
            return
        stA = {'sb': 0}

        def a_pair(i, j, j0, Qz, qzk):
            o = i - j
            d_ = {}

            def s_():
                bS = [(2, 3), (4, 5), (0, 1)][stA['sb'] % 3]
                stA['sb'] += 1
                d_['bS'] = bS
                for p in range(3):
                    bb = bS[0] if p < 2 else bS[1]
                    c0 = (p % 2) * 256
                    self.mm(self.pb[bb][:, c0:c0 + 256], QKT[:, 3 + p, j * 128:(j + 1) * 128],
                            Qz[:, 2 * p:2 * p + 2, :], True, True, [('QKT', j)] + qzk, [('pb', bb)])

            def r_():
                bS = d_['bS']
                ps_ = self.pt_n % 3
                self.pt_n += 1
                PT = self.PT[ps_]
                self.act(PT[:, 0:512], self.pb[bS[0]][:, 0:512], AF.Exp, [('pb', bS[0])], [('PT', ps_)], scale=0.125)
                self.act(PT[:, 512:768], self.pb[bS[1]][:, 0:256], AF.Exp, [('pb', bS[1])], [('PT', ps_)], scale=0.125)
                self.tt('dve', PT[:, 0:768].rearrange("p (h q) -> p h q", q=128), PT[:, 0:768].rearrange("p (h q) -> p h q", q=128),
                        bc(self.MA[:, o, :], 1, [128, 6, 128]), ALU.mult, [('PT', ps_), 'MA'], [('PT', ps_)])
                for h in range(6):
                    self.mm(self.pb[6][:, h * 72:h * 72 + 65], PT[:, h * 128:(h + 1) * 128], VS4[:, j, h, 0:65],
                            (j == j0 and h == 0), j == i, [('PT', ps_), ('VS', j), 'VS_ones'], [('pb', 6)], skip=True)
            return s_, r_

        def a_fin(i):
            def r_():
                acc = self.pb[6][:, 0:432].rearrange("p (h e) -> p h e", e=72)
                self.S.add('dve', lambda e: e.reciprocal(out=self.rec[:, 0:6], in_=acc[:, :, 64]), reads=[('pb', 6)], writes=['rec'])
                self.tt('dve', self.ob[:, 0:384].rearrange("p (h d) -> p h d", d=64), acc[:, :, 0:64], bc(self.rec[:, 0:6], 2, [128, 6, 64]),
                        ALU.mult, [('pb', 6), 'rec'], ['ob'])
                self.out_tile(i, 'ob', self.ob, 384, 0)
            return r_
        for i in range(NT):
            j0 = max(0, i - 16)
            Qz, qzk = self.make_qz(i, 6)
            for j in range(j0, i + 1):
                s_, r_ = a_pair(i, j, j0, Qz, qzk)
                self.emit_step(s_, r_, L=2)
            self.emit_step(None, a_fin(i), L=2)
        self.flush_steps()
        self.dbg_oT()

    def dbg_oT(self):
        if 'dbg_oT' in [d[0] for d in self.dbg] and self.stop_after is not None:
            rng = [r for k, r in (('A', (0, 384)), ('B', (384, 640)), ('C', (640, 896))) if getattr(self, 'only', None) is None or k in self.only]
            for t in range(NT):
                for (a, b) in rng:
                    o = self.dma(self.dr['dbg_oT'][t][:, a:b], self.dr['oT'][t][:, a:b], [('oT', t, 0), ('oT', t, 384), ('oT', t, 640)], [])
                    self.final_ops.append(o)

    def pass_B(self, l):
        dr = self.dr
        if not hasattr(self, 'b_GS'):
            self.b_GS = self.sb('b_GS', [128, NT, 12], F32)
            self.b_posT = self.sb('b_posT', [64, 32], BF16)
            self.b_posTf = self.sb('b_posTf', [64, 32], F32)
            self.b_W2 = self.sb('b_W2', [128, 2, 64], BF16)
            self.b_W2f = self.sb('b_W2f', [128, 2, 64], F32)
            self.b_W2vf = self.b_W2f
            self.b_h1 = self.sb('b_h1', [128, 2, 256], BF16)
            self.b_hb = self.sb('b_hb', [128, 2], F32)
            self.b_kcT = self.sb('b_kcT', [64, 256], BF16)
            self.b_vc = self.sb('b_vc', [128, 2, 72], F32)
            self.b_rdc = self.sb('b_rdc', [128, 2, 4], F32)
            self.b_Pc = self.sb('b_Pc', [128, 2, 512], F32)
            self.b_rden = self.qk_sq[:, 0:512]
            self.b_sc = self.sb('b_sc', [128, 64], F32)
            self.b_sc2 = self.sb('b_sc2', [128, 64], F32)
            self.b_m1 = self.sb('b_m1', [128, 8], F32)
            self.b_m2 = self.sb('b_m2', [128, 8], F32)
            self.b_selb = self.sb('b_selb', [128, 64], F32)
            self.b_selbT2 = [t_[0:64, :].rearrange('p (h q) -> p h q', q=128) for t_ in self.selT]
            self.b_f = self.sb('b_f', [128, 12], F32)
            self.b_ocmp = self.pr[:, 0:512].rearrange('p (a c) -> p a c', c=256)
            self.b_obf = self.qk_xn[:, 0:256].rearrange('p (h d) -> p h d', d=64)
            self.b_tmp = self.qk_xn[:, 256:512].rearrange('p (h d) -> p h d', d=64)
        GS = self.b_GS
        self.wkeys = {}
        W = self.BIG[:, 37376:37376 + 8 * 652].rearrange("p (k c) -> p k c", c=652)
        self.load_w(W, 'Wp', dr['w_in'][l], 1536, 652, 0)
        wreads = list(self.wkeys['Wp'])
        self.load_gains(l, 'q_norm_b', 'k_norm_b', 4, 6)
        Esel = self.BIG[64:128, 5 * S_LEN:6 * S_LEN]
        self.memset('pool', Esel, 1.0, ['Esel'])
        self.asel(Esel, Esel, [[1, S_LEN]], ALU.is_ge, 0.0, 0, -64, ['Esel'], ['Esel'])
        self.asel(Esel, Esel, [[-1, S_LEN]], ALU.is_ge, 0.0, 63, 64, ['Esel'], ['Esel'])
        QS = self.BIG[:, 0:32768].rearrange("p (a c) -> p a c", c=S_LEN)
        VSB = self.BIG[:, 32768:32768 + NT * 144].rearrange("p (t h e) -> p t h e", h=2, e=72)
        self.memset('pool', VSB[:, :, :, 64:65], 1.0, ['VS_ones'])
        QTB = self.BIG[0:64, 0:32768].rearrange("p (a c) -> p a c", c=S_LEN)
        W1 = [self.BIG[0:64, 42592 + i * 4096:42592 + (i + 1) * 4096].rearrange("p (q h) -> p q h", h=128) for i in range(2)]
        for wi, nm in enumerate(('cmp_k_w1', 'cmp_v_w1')):
            for qtr in range(4):
                sl = self.wst_n % 2
                self.wst_n += 1
                stg = self.wstage[sl][0:64]
                self.dma(stg, dr[nm][l][:, qtr * 8:(qtr + 1) * 8, :], [], [('wst', sl)])
                self.cp('pool', W1[wi][:, qtr * 8:(qtr + 1) * 8, :], stg, [('wst', sl)], [('W1', wi, qtr)])
        w1keys = [[('W1', wi, q) for q in range(4)] for wi in range(2)]
        self.dma(self.b_posTf[:], dr['cmp_pos'][l], [], ['b_posTf'])
        self.cp('dve', self.b_posT[:], self.b_posTf[:], ['b_posTf'], ['b_posT'])
        self.dma(self.b_W2f[:, 0, :], dr['cmp_k_w2'][l], [], ['b_W2f0'])
        self.dma(self.b_W2f[:, 1, :], dr['cmp_v_w2'][l], [], ['b_W2f1'])
        self.cp('dve', self.b_W2[:], self.b_W2f[:], ['b_W2f0', 'b_W2f1'], ['b_W2'])
        srcs = [0, 64, 128, 192, 256, 384, 512, 320]
        hs_ = {}

        def do_proj(t):
            if t % 4 == 0:
                hs_['sl'] = self.load_hg(t // 4)
            self.proj_tile(hs_['sl'], t % 4, W, wreads, [(0, 512), (512, 140)], [0, 1])

        def E_(t):
            pr = self.prs[t % 2]
            self.cp('act', pr[:, 0:512], self.pb[0][:, 0:512], [('pb', 0)], [('pr', t % 2)])
            self.cp('act', pr[:, 512:652], self.pb[1][:, 0:140], [('pb', 1)], [('pr', t % 2)])

        def N_(t):
            pr = self.prs[t % 2]
            pk = ('pr', t % 2)
            self.qk_N(pr[:, 0:640], 10, self.Gt[:, 0:640], t, self.xb[:, 0:640], pk, 'xb', t % 2)
            self.cp('dve', self.xb[:, 320:384], pr[:, 320:384], [pk, 'xb_dve', 'xb_pool'], ['xb_dve', 'xb_pool'])
            self.cp('dve', VSB[:, t, 0, 0:64], pr[:, 448:512], [pk, 'VS_ones'], [('VS', t)])
            self.cp('dve', VSB[:, t, 1, 0:64], pr[:, 576:640], [pk, 'VS_ones'], [('VS', t)])
            self.cp('dve', GS[:, t, :], pr[:, 640:652], [pk], [('GS', t)])

        def T_(t):
            pT = self.pb[3][:].bitcast(BF16)
            for si, c0 in enumerate(srcs):
                self.tr(pT[0:64, si * 128:(si + 1) * 128], self.xb[:, c0:c0 + 64], self.identb[:], ['xb_dve', 'xb_pool', 'identb'], [('pb', 3)])
            self.cp('act', QTB[:, 0:8, t * 128:(t + 1) * 128], pT[0:64, 0:1024].rearrange("p (a c) -> p a c", c=128), [('pb', 3)], [('QKT', t)])
        self.proj_pipeline(do_proj, E_,
                           lambda t: self.qk_Q(self.prs[t % 2][:, 0:640], 10, ('pr', t % 2), t % 2),
                           lambda t: self.qk_R(10, t % 2),
                           N_, T_)
        allq = [('QKT', t) for t in range(NT)]
        self.act(GS[:].rearrange("p t g -> p (t g)"), GS[:].rearrange("p t g -> p (t g)"), AF.Sigmoid, [('GS', t) for t in range(NT)], ['GSs'])
        self.memset('dve', self.b_h1[:, :, 255:256], 0.0, ['b_h1z'])
        for wi, slot in ((0, 4), (1, 7)):
            for p in range(32):
                self.mm(self.pb[5][:, 0:255], W1[wi][:, p, :], QTB[:, slot, p:p + 16 * 254 + 1:16], p == 0, p == 31,
                        allq + w1keys[wi], [('pb', 5)])
            for p in range(32):
                self.mm(self.pb[4][:, 0:1], W1[wi][:, p, :], self.b_posT[:, p:p + 1], p == 0, p == 31, w1keys[wi] + ['b_posT'], [('pb', 4)])
            self.cp('dve', self.b_hb[:, wi:wi + 1], self.pb[4][:, 0:1], [('pb', 4)], [('b_hb', wi)])
            self.act(self.b_h1[:, wi, 0:255], self.pb[5][:, 0:255], AF.Silu, [('pb', 5), ('b_hb', wi), 'b_h1z'], [('b_h1', wi)],
                     bias=self.b_hb[:, wi:wi + 1])
        self.mm(self.pb[5][0:64, 0:256], self.b_W2[:, 0, :], self.b_h1[:, 0, :], True, True, ['b_W2', ('b_h1', 0), 'b_h1z'], [('pb', 5)])
        self.cp('dve', self.b_kcT[:, :], self.pb[5][0:64, 0:256], [('pb', 5)], ['b_kcT'])
        for ct in range(2):
            self.mm(self.pb[4][:, ct * 64:(ct + 1) * 64], self.b_h1[:, 1, ct * 128:(ct + 1) * 128], self.b_W2[:, 1, :], True, True,
                    ['b_W2', ('b_h1', 1), 'b_h1z'], [('pb', 4)])
        self.cp('dve', self.b_vc[:, :, 0:64], self.pb[4][:, 0:128].rearrange("p (c d) -> p c d", d=64), [('pb', 4)], ['b_vc'])
        self.memset('dve', self.b_vc[:, :, 64:65], 1.0, ['b_vc'])
        Pc = self.b_Pc
        st = {'sb': 0}

        def qap(i):
            return QTB[:, 0:4, i * 128:(i + 1) * 128]

        def nbank():
            b = (2, 3, 1)[st['sb'] % 3]
            st['sb'] += 1
            return b

        def sel_steps(i):
            qr = [('QKT', i)]
            nct = 2 if i >= 16 else 1
            ob = 0
            selbT = self.b_selbT2[i % 2]
            sk = ('b_selbT', i)

            def s_a():
                for ct in range(nct):
                    b = 7
                    self.mm(self.pb[b][:, 0:512], self.b_kcT[:, ct * 128:(ct + 1) * 128], qap(i), True, True, qr + ['b_kcT'], [('pb', b)])
                    self.act(Pc[:, ct, :], self.pb[b][:, 0:512], AF.Exp, [('pb', b)], [('Pc', ct)], scale=0.125)
                    if ct == 1 or i < 17:
                        self.asel(Pc[:, ct, :].rearrange("p (h q) -> p h q", q=128), Pc[:, ct, :].rearrange("p (h q) -> p h q", q=128),
                                  [[0, 4], [1, 128]], ALU.is_ge, 0.0, 128 * i - 2048 * ct - 31, -16, [('Pc', ct)], [('Pc', ct)])

            def s_d():
                first = True
                for h in range(4):
                    for ct in range(nct):
                        self.mm(self.pb[4][:, h * 64:(h + 1) * 64], Pc[:, ct, h * 128:(h + 1) * 128], self.cover[:, ct, :], first,
                                (h == 3 and ct == nct - 1), [('Pc', ct), 'cover'], [('pb', 4)], skip=True)
                        first = False
                first = True
                for h in range(4):
                    for ct in range(nct):
                        self.mm(self.pb[ob][:, h * 65:h * 65 + 65], Pc[:, ct, h * 128:(h + 1) * 128], self.b_vc[:, ct, 0:65], first,
                                (h == 3 and ct == nct - 1), [('Pc', ct), 'b_vc'], [('pb', ob)], skip=True)
                        first = False

            def s_e():
                rd = self.b_rdc[:, i % 2, :]
                oc = self.pb[ob][:, 0:260].rearrange("p (h e) -> p h e", e=65)
                self.tsc('dve', rd, oc[:, :, 64], 1e-30, None, ALU.add, None, [('pb', ob)], [('b_rdc', i % 2)])
                self.S.add('dve', lambda e: e.reciprocal(out=rd, in_=rd), reads=[('b_rdc', i % 2)], writes=[('b_rdc', i % 2)])
                self.cp('dve', self.b_ocmp[:, i % 2, :].rearrange("p (h d) -> p h d", d=64), oc[:, :, 0:64], [('pb', ob)], [('b_ocmp', i % 2)])
                for h in range(4):
                    in1 = self.AM[:, i, :] if h == 0 else self.b_sc[:]
                    self.S.add('dve', (lambda h, in1: lambda e: e.scalar_tensor_tensor(
                        out=self.b_sc[:], in0=self.pb[4][:, h * 64:(h + 1) * 64], scalar=rd[:, h:h + 1], in1=in1,
                        op0=ALU.mult, op1=ALU.add))(h, in1), reads=[('pb', 4), ('b_rdc', i % 2), 'AM', 'b_sc'], writes=['b_sc'])
                self.S.add('dve', lambda e: e.max(out=self.b_m1[:], in_=self.b_sc[:]), reads=['b_sc'], writes=['b_m1'])
                self.S.add('dve', lambda e: e.match_replace(out=self.b_sc2[:], in_to_replace=self.b_m1[:], in_values=self.b_sc[:],
                                                            imm_value=-3e38), reads=['b_sc', 'b_m1'], writes=['b_sc2'])
                self.S.add('dve', lambda e: e.max(out=self.b_m2[:], in_=self.b_sc2[:]), reads=['b_sc2'], writes=['b_m2'])
                self.tsc('dve', self.b_selb[:], self.b_sc[:], self.b_m2[:, 7:8], None, ALU.is_ge, None, ['b_sc', 'b_m2'], ['b_selb'])
                self.tsc('dve', self.b_selb[:], self.b_selb[:], 1.0, -NEGB, ALU.subtract, ALU.mult, ['b_selb'], ['b_selb'])

            def s_f():
                self.tr(self.pb[4][0:64, 0:128], self.b_selb[:, :], self.identf[:], ['b_selb', 'identf'], [('pb', 4)])
                self.cp('act', QS[64:128, 0:4, i * 128:(i + 1) * 128], bc(self.pb[4][0:64, 0:128], 1, [64, 4, 128]), [('pb', 4)], [sk])
            return [s_a, s_d, s_e, s_f]

        def attn_steps(i):
            qr = [('QKT', i)]
            sk = ('b_selbT', i)
            steps = []

            def sel_pair(j):
                d_ = {}

                def s_():
                    b = nbank()
                    d_['b'] = b
                    bank = self.pb[b]
                    self.mm(bank[:, 0:512], QS[:, 5, j * 128:(j + 1) * 128], QS[:, 0:4, i * 128:(i + 1) * 128], True, True,
                            qr + [('QKT', j), 'Esel', sk], [('pb', b)])

                def f():
                    b = d_['b']
                    bank = self.pb[b]
                    ps_ = self.pt_n % 3
                    self.pt_n += 1
                    PT = self.PT[ps_]
                    self.act(PT[:, 0:512], bank[:, 0:512], AF.Exp, [('pb', b)], [('PT', ps_)], scale=0.125)
                    if j == i:
                        self.asel(PT[:, 0:512].rearrange("p (h q) -> p h q", q=128), PT[:, 0:512].rearrange("p (h q) -> p h q", q=128),
                                  [[0, 4], [1, 128]], ALU.is_ge, 0.0, 0, -1, [('PT', ps_)], [('PT', ps_)])
                    for h in range(4):
                        self.mm(self.pb[6][:, h * 72:h * 72 + 65], PT[:, h * 128:(h + 1) * 128], VSB[:, j, 0, 0:65],
                                (j == 0 and h == 0), j == i, [('PT', ps_), ('VS', j), 'VS_ones'], [('pb', 6)], skip=True)
                return (s_, f)
            j0 = max(0, i - 4)

            def win_pair(j):
                d_ = {}

                def s_():
                    b = nbank()
                    d_['b'] = b
                    bank = self.pb[b]
                    self.mm(bank[:, 0:512], QTB[:, 6, j * 128:(j + 1) * 128], qap(i), True, True, qr + [('QKT', j)], [('pb', b)])

                def f():
                    b = d_['b']
                    bank = self.pb[b]
                    ps_ = self.pt_n % 3
                    self.pt_n += 1
                    PT = self.PT[ps_]
                    self.act(PT[:, 0:512], bank[:, 0:512], AF.Exp, [('pb', b)], [('PT', ps_)], scale=0.125)
                    PT3 = PT[:, 0:512].rearrange("p (h q) -> p h q", q=128)
                    if j == i:
                        self.asel(PT3, PT3, [[0, 4], [1, 128]], ALU.is_ge, 0.0, 0, -1, [('PT', ps_)], [('PT', ps_)])
                    if j == i - 4:
                        self.asel(PT3, PT3, [[0, 4], [-1, 128]], ALU.is_ge, 0.0, -1, 1, [('PT', ps_)], [('PT', ps_)])
                    for h in range(4):
                        self.mm(self.pb[5][:, h * 72:h * 72 + 65], PT[:, h * 128:(h + 1) * 128], VSB[:, j, 1, 0:65],
                                (j == j0 and h == 0), j == i, [('PT', ps_), ('VS', j), 'VS_ones'], [('pb', 5)], skip=True)
                return (s_, f)
            for j in range(i + 1):
                steps.append(sel_pair(j))
            for j in range(j0, i + 1):
                steps.append(win_pair(j))

            def combine():
                accs = self.pb[6][:, 0:288].rearrange("p (h e) -> p h e", e=72)
                accw = self.pb[5][:, 0:288].rearrange("p (h e) -> p h e", e=72)
                ocmp = self.b_ocmp[:, i % 2, :].rearrange("p (h d) -> p h d", d=64)
                f = self.b_f
                self.S.add('dve', lambda e: e.reciprocal(out=f[:, 4:8], in_=accs[:, :, 64]), reads=[('pb', 6)], writes=['b_f'])
                self.S.add('dve', lambda e: e.reciprocal(out=f[:, 8:12], in_=accw[:, :, 64]), reads=[('pb', 5)], writes=['b_f'])
                self.tt('dve', f[:, 4:12], f[:, 4:12], GS[:, i, 4:12], ALU.mult, ['b_f', 'GSs'], ['b_f'])
                self.tt('dve', f[:, 0:4], GS[:, i, 0:4], self.b_rdc[:, i % 2, :], ALU.mult, ['GSs', ('b_rdc', i % 2)], ['b_f'])
                self.tt('dve', self.b_obf, ocmp, bc(f[:, 0:4], 2, [128, 4, 64]), ALU.mult, [('b_ocmp', i % 2), 'b_f'], ['b_obf'])
                self.tt('dve', self.b_tmp, accs[:, :, 0:64], bc(f[:, 4:8], 2, [128, 4, 64]), ALU.mult, [('pb', 6), 'b_f'], ['b_tmp'])
                self.tt('dve', self.b_obf, self.b_obf, self.b_tmp, ALU.add, ['b_obf', 'b_tmp'], ['b_obf'])
                self.tt('dve', self.b_tmp, accw[:, :, 0:64], bc(f[:, 8:12], 2, [128, 4, 64]), ALU.mult, [('pb', 5), 'b_f'], ['b_tmp'])
                self.tt('dve', self.ob[:, 0:256].rearrange("p (h d) -> p h d", d=64), self.b_obf, self.b_tmp, ALU.add,
                        ['b_obf', 'b_tmp'], ['ob'])
                self.out_tile(i, 'ob', self.ob, 256, 384)
            steps.append((None, combine))
            return steps

        for f_ in sel_steps(0):
            f_()
        for i in range(NT):
            pend = sel_steps(i + 1) if i + 1 < NT else []
            asteps = attn_steps(i)
            gap = max(1, (len(asteps) - 1) // (len(pend) + 1)) if pend else 1
            for n_, a in enumerate(asteps):
                self.emit_step(a[0], a[1], L=2)
                if pend and (n_ % gap == gap - 1) and n_ < len(asteps) - 1:
                    pend.pop(0)()
            while pend:
                pend.pop(0)()
        self.flush_steps()
        self.dbg_oT()

    def pass_C(self, l):
        dr = self.dr
        if not hasattr(self, 'c_ksum'):
            self.c_ksum = self.sb('c_ksum', [128, 2, 16], F32)
            self.c_khi = self.sb('c_khi', [128, 2, 16], BF16)
            self.c_klo = self.sb('c_klo', [128, 2, 16], BF16)
            self.c_tmp = self.sb('c_tmp', [128, 2, 16], F32)
            self.c_sc = self.sb('c_sc', [128, 4, 16], F32)
            self.c_mx = self.sb('c_mx', [128, 4, 8], F32)
            self.c_selb = self.sb('c_selb', [128, 4, 16], F32)
            self.c_selbT2 = [t_[0:16, :] for t_ in self.selT]
        self.wkeys = {}
        W = self.Wp
        self.load_w(W, 'Wp', dr['w_in'][l], 2444, 768, 0)
        wreads = list(self.wkeys['Wp'])
        self.load_gains(l, 'q_norm_c', 'k_norm_c', 4, 4)
        self.Ec = self.build_E('Ec', 256, 16, 16384)
        VS4 = self.VS.rearrange("p t (h e) -> p t h e", e=72)
        self.memset('pool', VS4[:, :, 0:4, 64:65], 1.0, ['VS_ones'])
        QKT = self.QKT
        hs_ = {}

        def do_proj(t):
            if t % 4 == 0:
                hs_['sl'] = self.load_hg(t // 4)
            self.proj_tile(hs_['sl'], t % 4, W, wreads, [(0, 512), (512, 256)], [0, 1])

        def E_(t):
            pr = self.prs[t % 2]
            self.cp('act', pr[:, 0:512], self.pb[0][:, 0:512], [('pb', 0)], [('pr', t % 2)])
            self.cp('act', VS4[:, t, 0:4, 0:64], self.pb[1][:, 0:256].rearrange("p (h d) -> p h d", d=64), [('pb', 1), 'VS_ones'], [('VS', t)])

        def T_(t):
            pT = self.pb[3][:].bitcast(BF16)
            for p in range(4):
                self.tr(pT[:, p * 128:(p + 1) * 128], self.xb[:, p * 128:(p + 1) * 128], self.identb[:], ['xb_dve', 'xb_pool', 'identb'], [('pb', 3)])
            self.cp('act', QKT[:, 0:4, t * 128:(t + 1) * 128], pT[:, 0:512].rearrange("p (a c) -> p a c", c=128), [('pb', 3)], [('QKT', t)])
        self.proj_pipeline(do_proj, E_,
                           lambda t: self.qk_Q(self.prs[t % 2][:, 0:512], 8, ('pr', t % 2), t % 2),
                           lambda t: self.qk_R(8, t % 2),
                           lambda t: self.qk_N(self.prs[t % 2][:, 0:512], 8, self.Gt[:, 0:512], t, self.xb[:, 0:512], ('pr', t % 2), 'xb', t % 2),
                           T_)
        allq = [('QKT', t) for t in range(NT)]
        ksum, khi, klo, ktmp = self.c_ksum, self.c_khi, self.c_klo, self.c_tmp
        self.S.add('dve', lambda e: e.tensor_reduce(out=ksum[:], in_=QKT[:, 2:4, :].rearrange("p a (n k) -> p a n k", k=256),
                                                    axis=AX.X, op=ALU.add), reads=allq, writes=['c_ksum'])
        self.cp('dve', khi[:], ksum[:], ['c_ksum'], ['c_khi'])
        self.cp('dve', ktmp[:], khi[:], ['c_khi'], ['c_tmp'])
        self.tt('dve', ktmp[:], ksum[:], ktmp[:], ALU.subtract, ['c_ksum', 'c_tmp'], ['c_tmp'])
        self.cp('dve', klo[:], ktmp[:], ['c_tmp'], ['c_klo'])
        st = {'sb': 0}

        def sel_steps(i):
            nb = i // 2
            if nb == 0:
                return []
            sl = i % 2
            Qz, qzk = self.c_qz[i]
            sc = self.c_sc
            selbT = self.c_selbT2[sl]
            sk = ('c_selbT', sl)

            def s_a():
                for h in range(4):
                    self.mm(self.pb[5][:, h * 16:(h + 1) * 16], Qz[:, h, :], khi[:, h // 2, :], True, False, qzk + ['c_khi'], [('pb', 5)])
                    self.mm(self.pb[5][:, h * 16:(h + 1) * 16], Qz[:, h, :], klo[:, h // 2, :], False, True, qzk + ['c_klo'], [('pb', 5)])

            def s_b():
                self.cp('dve', sc[:].rearrange("p h n -> p (h n)"), self.pb[5][:, 0:64], [('pb', 5)], ['c_sc'])
                if nb < 16:
                    self.memset('dve', sc[:, :, nb:16], -1e30, ['c_sc'])
                for h in range(4):
                    self.S.add('dve', (lambda h: lambda e: e.max(out=self.c_mx[:, h, :], in_=sc[:, h, :]))(h), reads=['c_sc'], writes=['c_mx'])
                self.tt('dve', self.c_selb[:], sc[:], bc(self.c_mx[:, :, 2], 2, [128, 4, 16]), ALU.is_ge, ['c_sc', 'c_mx'], ['c_selb'])
                self.tsc('dve', self.c_selb[:], self.c_selb[:], 1.0, -NEGB, ALU.subtract, ALU.mult, ['c_selb'], ['c_selb'])
                if nb < 16:
                    self.memset('dve', self.c_selb[:, :, nb:16], NEGB, ['c_selb'])

            def s_c():
                for h in range(4):
                    self.tr(self.pb[4][0:16, h * 128:(h + 1) * 128], self.c_selb[:, h, :], self.identf[:], ['c_selb', 'identf'], [('pb', 4)])
                self.cp('act', selbT[:, :], self.pb[4][0:16, 0:512], [('pb', 4)], [sk])
            return [s_a, s_b, s_c]

        def attn_steps(i):
            nb = i // 2
            Qz, qzk = self.c_qz[i]
            selbT = self.c_selbT2[i % 2]
            sk = ('c_selbT', i % 2)
            steps = []

            def pair(j):
                d_ = {}
                past = j < 2 * nb

                def s_():
                    b = (2, 3, 0, 1)[st['sb'] % 4]
                    st['sb'] += 1
                    d_['b'] = b
                    bank = self.pb[b]
                    if past:
                        self.mm(bank[:, 0:512], self.Ec[:, j * 128:(j + 1) * 128], selbT[0:16, 0:512], True, False,
                                ['Ec', sk], [('pb', b)], skip=True)
                    for p in range(2):
                        self.mm(bank[:, p * 256:(p + 1) * 256], QKT[:, 2 + p, j * 128:(j + 1) * 128], Qz[:, 2 * p:2 * p + 2, :],
                                (not past) and p == 0, p == 1, [('QKT', j)] + qzk, [('pb', b)], skip=True)

                def r_():
                    b = d_['b']
                    bank = self.pb[b]
                    ps_ = self.pt_n % 3
                    self.pt_n += 1
                    PT = self.PT[ps_]
                    self.act(PT[:, 0:512], bank[:, 0:512], AF.Exp, [('pb', b)], [('PT', ps_)], scale=0.125)
                    if j == i:
                        self.asel(PT[:, 0:512].rearrange("p (h q) -> p h q", q=128), PT[:, 0:512].rearrange("p (h q) -> p h q", q=128),
                                  [[0, 4], [1, 128]], ALU.is_ge, 0.0, 0, -1, [('PT', ps_)], [('PT', ps_)])
                    for h in range(4):
                        self.mm(self.pb[6][:, h * 72:h * 72 + 65], PT[:, h * 128:(h + 1) * 128], VS4[:, j, h, 0:65],
                                (j == 0 and h == 0), j == i, [('PT', ps_), ('VS', j), 'VS_ones'], [('pb', 6)], skip=True)
                return (s_, r_)
            for j in range(i + 1):
                steps.append(pair(j))

            def fin():
                acc = self.pb[6][:, 0:288].rearrange("p (h e) -> p h e", e=72)
                self.S.add('dve', lambda e: e.reciprocal(out=self.rec[:, 0:4], in_=acc[:, :, 64]), reads=[('pb', 6)], writes=['rec'])
                self.tt('dve', self.ob[:, 0:256].rearrange("p (h d) -> p h d", d=64), acc[:, :, 0:64], bc(self.rec[:, 0:4], 2, [128, 4, 64]),
                        ALU.mult, [('pb', 6), 'rec'], ['ob'])
                self.out_tile(i, 'ob', self.ob, 256, 640)
            steps.append((None, fin))
            return steps

        self.c_qz = {}
        self.c_qz[0] = self.make_qz(0, 4)
        for i in range(NT):
            if i + 1 < NT:
                self.c_qz[i + 1] = self.make_qz(i + 1, 4)
            pend = sel_steps(i + 1) if i + 1 < NT else []
            asteps = attn_steps(i)
            gap = max(1, (len(asteps) - 1) // (len(pend) + 1)) if pend else 1
            for n_, a in enumerate(asteps):
                self.emit_step(a[0], a[1], L=3)
                if pend and (n_ % gap == gap - 1) and n_ < len(asteps) - 1:
                    pend.pop(0)()
            while pend:
                pend.pop(0)()
        self.flush_steps()
        self.dbg_oT()

    def final(self, l, xin, xout):
        dr = self.dr
        if not hasattr(self, 'f_yacc'):
            self.f_yacc = self.b_Pc[:, 0, :]
            self.f_ytmp = self.b_Pc[:, 1, :]
            self.f_yT = self.s1_all[:, 0:4096].rearrange("p (k c) -> p k c", c=512)
        Wzg = self.BIG[:, 0:31744].rearrange("p (k c) -> p k c", c=3968)
        Wbr = self.BIG[:, 31744:38912].rearrange("p (k c) -> p k c", c=1024)
        Wout = self.BIG[:, 38912:47104].rearrange("p (k c) -> p k c", c=1024)
        oz = self.BIG[:, 47104:47104 + 3584].rearrange("p (t k c) -> p t k c", k=7, c=128)
        og = self.BIG[:, 50688:50688 + 3584].rearrange("p (t c) -> p t c", c=896)
        sz = [self.qk_sq[:, 0:512], self.qk_xn[:, 0:512]]
        sg = [self.pr[:, 0:512], self.Gt[:, 0:512]]
        self.wkeys = {}
        for (c0, n, o) in ((1152, 384, 0), (2188, 256, 384), (3212, 256, 640), (3468, 3072, 896)):
            self.load_w(Wzg, 'Wzg', dr['w_in'][l], c0, n, o)
        self.load_w(Wbr, 'Wbr', dr['w_br_a'][l], 0, 1024, 0, nk=3, k0=0)
        self.load_w(Wbr, 'Wbr', dr['w_br_b'][l], 0, 1024, 0, nk=2, k0=3)
        self.load_w(Wbr, 'Wbr', dr['w_br_c'][l], 0, 1024, 0, nk=2, k0=5)
        self.load_w(Wout, 'Wout', dr['w_out'][l], 0, 1024, 0)
        kz, kb, ko = list(self.wkeys['Wzg']), list(self.wkeys['Wbr']), list(self.wkeys['Wout'])
        last = (xout is dr['y'])
        n = 0
        for g in range(NT // 4):
            hsl = self.load_hg(g)
            hgt = self.hg[hsl]
            self.dma(og, dr['oT'][g * 4:(g + 1) * 4].rearrange("t p c -> p t c"),
                     [('oT', g * 4 + i, c) for i in range(4) for c in (0, 384, 640)], ['og'])
            for zc in range(7):
                b = zc % 2
                for k in range(8):
                    self.mm(self.pb[b][:, 0:512], Wzg[:, k, zc * 128:(zc + 1) * 128], hgt[:, :, k, :], k == 0, k == 7,
                            [('hg', hsl)] + kz, [('pb', b)])
                self.act(sz[b], self.pb[b][:, 0:512], AF.Silu, [('pb', b)], [('sz', b)])
                self.tt('dve', oz[:, :, zc, :], og[:, :, zc * 128:(zc + 1) * 128], sz[b].rearrange("p (t c) -> p t c", c=128), ALU.mult,
                        [('sz', b), 'og'], [('oz', zc)])
            for m in range(8):
                for br in range(3):
                    gb = 2 + n % 2
                    ub = 4 + n % 2
                    sgb = sg[n % 2]
                    sgk = ('sg', n % 2)
                    n += 1
                    c0 = 896 + br * 1024 + m * 128
                    for k in range(8):
                        self.mm(self.pb[gb][:, 0:512], Wzg[:, k, c0:c0 + 128], hgt[:, :, k, :], k == 0, k == 7,
                                [('hg', hsl)] + kz, [('pb', gb)])
                    self.act(sgb, self.pb[gb][:, 0:512], AF.Sigmoid, [('pb', gb)], [sgk])
                    kcs = ([0, 1, 2], [3, 4], [5, 6])[br]
                    for ii, kc in enumerate(kcs):
                        self.mm(self.pb[ub][:, 0:512], Wbr[:, kc, m * 128:(m + 1) * 128], oz[:, :, kc, :], ii == 0, ii == len(kcs) - 1,
                                [('oz', kc)] + kb, [('pb', ub)])
                    if br == 0:
                        self.tt('dve', self.f_yacc, self.pb[ub][:, 0:512], sgb, ALU.mult, [('pb', ub), sgk], ['f_yacc'])
                    else:
                        self.tt('dve', self.f_ytmp, self.pb[ub][:, 0:512], sgb, ALU.mult, [('pb', ub), sgk], ['f_ytmp'])
                        if br == 1:
                            self.tt('dve', self.f_yacc, self.f_yacc, self.f_ytmp, ALU.add, ['f_yacc', 'f_ytmp'], ['f_yacc'])
                        else:
                            self.tt('dve', self.f_yT[:, m, :], self.f_yacc, self.f_ytmp, ALU.add, ['f_yacc', 'f_ytmp'], [('yT', m)])
            for tt_ in range(4):
                t = g * 4 + tt_
                xsl = self.xt_n % 2
                self.xt_n += 1
                xt = self.xt[xsl]
                self.dma(xt[:], xin[t * 128:(t + 1) * 128, :], [('x1', t)], [('xt', xsl)])
                osl = xsl
                orow = xt
                for nch in range(2):
                    b = 6 + nch
                    for k in range(8):
                        self.mm(self.pb[b][:, 0:512], self.f_yT[:, k, tt_ * 128:(tt_ + 1) * 128], Wout[:, k, nch * 512:(nch + 1) * 512],
                                k == 0, k == 7, [('yT', k)] + ko, [('pb', b)])
                    self.tt('dve', orow[:, nch * 512:(nch + 1) * 512], self.pb[b][:, 0:512], xt[:, nch * 512:(nch + 1) * 512], ALU.add,
                            [('pb', b), ('xt', xsl)], [('xt', xsl)])
                o = self.dma(xout[t * 128:(t + 1) * 128, :], orow[:], [('xt', xsl)], [('y', t) if last else ('x1', t)])
                if last:
                    self.final_ops.append(o)


def host_layout(sh):
    sh = dict(sh)
    sh['norm_g'] = np.ascontiguousarray(sh['norm_g'].reshape(2, 8, 128).transpose(0, 2, 1))
    sh['cmp_pos'] = np.ascontiguousarray(sh['cmp_pos'].transpose(0, 2, 1))
    for nm in ('cmp_k_w1', 'cmp_v_w1'):
        sh[nm] = np.ascontiguousarray(sh[nm].reshape(2, 32, 64, 128).transpose(0, 2, 1, 3))
    return sh


_CACHE = {}


def kernel(**inputs):
    n = 8
    if 'nc' not in _CACHE:
        _CACHE['nc'] = Builder(2).build()
    nc = _CACHE['nc']
    x = np.ascontiguousarray(inputs['x'], dtype=np.float32)
    pos = np.ascontiguousarray(inputs['positions']).astype(np.int32)
    shared = {}
    for k in ('norm_g', 'w_in', 'q_norm_a', 'k_norm_a', 'q_norm_b', 'k_norm_b', 'q_norm_c', 'k_norm_c', 'cmp_pos',
              'cmp_k_w1', 'cmp_k_w2', 'cmp_v_w1', 'cmp_v_w2', 'w_br_a', 'w_br_b', 'w_br_c', 'w_out'):
        shared[k] = np.ascontiguousarray(inputs[k], dtype=np.float32)
    shared = host_layout(shared)
    in_maps = []
    for c in range(n):
        m = dict(shared)
        m['x'] = x[c]
        m['pos'] = np.ascontiguousarray(pos[c].reshape(NT, 128).T)
        in_maps.append(m)
    res = run_bass_kernel_spmd(nc, in_maps, core_ids=list(range(n)))
    return np.stack([np.asarray(r['y'], dtype=np.float32) for r in res.results], axis=0)
```

```python
import contextlib
import math
import numpy as np
import concourse.bass as bass
import concourse.mybir as mybir
from concourse.bass_utils import run_bass_kernel_spmd

F32 = mybir.dt.float32
BF16 = mybir.dt.bfloat16
I32 = mybir.dt.int32
AF = mybir.ActivationFunctionType
ALU = mybir.AluOpType
AX = mybir.AxisListType

SAME_ENG_SYNC = {'pe': False, 'act': True, 'dve': True, 'pool': True, 'sp': True}
N_DMA_SEMS = 8

S_LEN = 4096
D = 1024
NT = 32
INW = 6540
EPS = 1e-6
NEGB = -30000.0


class _Op:
    __slots__ = ('eng', 'fn', 'deps', 'dma', 'signal', 'sem', 'val', 'idx', 'prev')


class Sched:
    def __init__(self, nc):
        self.nc = nc
        self.ops = []
        self.last_w = {}
        self.readers = {}
        self.fence_keys = []

    def add(self, eng, fn, reads=(), writes=(), dma=False):
        op = _Op()
        op.eng = eng
        op.fn = fn
        op.dma = dma
        op.signal = False
        op.sem = None
        op.val = 0
        op.idx = len(self.ops)
        deps = set()
        if self.fence_keys:
            reads = list(reads) + self.fence_keys
        for k in reads:
            w = self.last_w.get(k)
            if w is not None:
                deps.add(w)
        for k in writes:
            w = self.last_w.get(k)
            if w is not None:
                deps.add(w)
            for r in self.readers.get(k, ()):
                deps.add(r)
        op.deps = deps
        for k in reads:
            self.readers.setdefault(k, []).append(op.idx)
        for k in writes:
            self.last_w[k] = op.idx
            self.readers[k] = []
        self.ops.append(op)
        return op.idx

    def emit(self, final_waits=()):
        nc = self.nc
        ops = self.ops
        engs = ['pe', 'act', 'dve', 'pool', 'sp']
        for op in ops:
            need = set()
            best = {}
            for d in op.deps:
                Dp = ops[d]
                if Dp.eng == op.eng and not Dp.dma and not op.dma and not SAME_ENG_SYNC[op.eng]:
                    continue
                if Dp.dma:
                    need.add(d)
                    Dp.signal = True
                elif best.get(Dp.eng, -1) < d:
                    best[Dp.eng] = d
            for d in best.values():
                need.add(d)
                ops[d].signal = True
            op.deps = need
        for d in final_waits:
            ops[d].signal = True
        with contextlib.ExitStack() as st:
            esem = {e: st.enter_context(nc.semaphore('s_' + e)) for e in engs}
            dsem = {e: [st.enter_context(nc.semaphore('d_%s%d' % (e, i))) for i in range(N_DMA_SEMS)]
                    for e in ('sp', 'pool', 'act')}
            ecount = {e: 0 for e in engs}
            dcount = {e: 0 for e in engs}
            for op in ops:
                if op.dma:
                    op.signal = True
                if not op.signal:
                    continue
                if op.dma:
                    i = dcount[op.eng]
                    dcount[op.eng] += 1
                    op.sem = dsem[op.eng][i % N_DMA_SEMS]
                    op.val = 16 * (i // N_DMA_SEMS + 1)
                    op.prev = (op.sem, op.val - 16) if i >= N_DMA_SEMS else None
                else:
                    ecount[op.eng] += 1
                    op.sem = esem[op.eng]
                    op.val = ecount[op.eng]
            per = {e: [] for e in engs}
            for op in ops:
                per[op.eng].append(op)
            block = st.enter_context(nc.Block())

            def run(e, name, extra=()):
                waited = {}
                for op in per[name]:
                    ws = {}
                    for d in op.deps:
                        Dp = ops[d]
                        key = id(Dp.sem)
                        if waited.get(key, 0) >= Dp.val:
                            continue
                        if key not in ws or ws[key][1] < Dp.val:
                            ws[key] = (Dp.sem, Dp.val)
                    if op.dma and op.prev is not None:
                        key = id(op.prev[0])
                        if waited.get(key, 0) < op.prev[1] and (key not in ws or ws[key][1] < op.prev[1]):
                            ws[key] = op.prev
                    for key, (sem, val) in ws.items():
                        e.wait_ge(sem, val)
                        waited[key] = val
                    ins = op.fn(e)
                    if op.signal:
                        ins.then_inc(op.sem, 16 if op.dma else 1)
                for d in extra:
                    Dp = ops[d]
                    e.wait_ge(Dp.sem, Dp.val)

            @block.tensor
            def _(e):
                run(e, 'pe')

            @block.scalar
            def _(e):
                run(e, 'act')

            @block.vector
            def _(e):
                run(e, 'dve')

            @block.gpsimd
            def _(e):
                run(e, 'pool')

            @block.sync
            def _(e):
                run(e, 'sp', extra=final_waits)
        self.stats = {e: len(per[e]) for e in engs}
        self.stats['signals'] = dict(ecount)
        self.stats['dmasig'] = dict(dcount)


def bc(ap, axis, shape):
    return ap.unsqueeze(axis).to_broadcast(shape)


class Builder:
    def __init__(self, n_layers=2, dbg=None, stop_after=None):
        self.n_layers = n_layers
        self.dbg = dbg or ()
        self.stop_after = stop_after
        nc = bass.Bass("TRN2", target_bir_lowering=False)
        self.nc = nc
        self.S = Sched(nc)
        self.st = contextlib.ExitStack()
        self.uid = 0

    def sb(self, name, shape, dt):
        return self.st.enter_context(self.nc.sbuf_tensor(name, shape, dt))

    def ps(self, name, shape, dt=F32):
        return self.st.enter_context(self.nc.psum_tensor(name, shape, dt))

    def mm(self, out, lhsT, rhs, start, stop, r, w, skip=False):
        if skip:
            self.S.add('pe', lambda e: e.matmul(out, lhsT=lhsT, rhs=rhs, start=start, stop=stop, skip_group_check=True), reads=r, writes=w)
        else:
            self.S.add('pe', lambda e: e.matmul(out, lhsT=lhsT, rhs=rhs, start=start, stop=stop), reads=r, writes=w)

    def tr(self, out, in_, ident, r, w):
        self.S.add('pe', lambda e: e.transpose(out=out, in_=in_, identity=ident), reads=r, writes=w)

    def act(self, out, in_, func, r, w, bias=None, scale=None, accum_out=None):
        kw = {}
        if bias is not None:
            kw['bias'] = bias
        if scale is not None:
            kw['scale'] = scale
        if accum_out is not None:
            kw['accum_out'] = accum_out
        self.S.add('act', lambda e: e.activation(out=out, in_=in_, func=func, **kw), reads=r, writes=w)

    def rsqrt(self, out, in_, scale, r, w):
        self.act(out, in_, AF.Ln, r, w, bias=self.epsc[:out.shape[0], 0:1], scale=scale)
        self.act(out, out, AF.Exp, w, w, scale=-0.5)

    def tt(self, eng, out, in0, in1, op, r, w):
        self.S.add(eng, lambda e: e.tensor_tensor(out=out, in0=in0, in1=in1, op=op), reads=r, writes=w)

    def tsc(self, eng, out, in0, s1, s2, op0, op1, r, w):
        if op1 is None:
            self.S.add(eng, lambda e: e.tensor_scalar(out=out, in0=in0, scalar1=s1, scalar2=None, op0=op0), reads=r, writes=w)
        else:
            self.S.add(eng, lambda e: e.tensor_scalar(out=out, in0=in0, scalar1=s1, scalar2=s2, op0=op0, op1=op1), reads=r, writes=w)

    def cp(self, eng, out, in_, r, w):
        if eng == 'act':
            self.S.add('act', lambda e: e.copy(out=out, in_=in_), reads=r, writes=w)
        else:
            self.S.add(eng, lambda e: e.tensor_copy(out=out, in_=in_), reads=r, writes=w)

    def memset(self, eng, ap, val, w):
        self.S.add(eng, lambda e: e.memset(ap, val), writes=w)

    def asel(self, out, in_, pattern, op, fill, base, cm, r, w):
        self.S.add('pool', lambda e: e.affine_select(out=out, in_=in_, pattern=pattern, compare_op=op, fill=fill,
                                                     base=base, channel_multiplier=cm), reads=r, writes=w)

    def dma(self, out, in_, r, w, q='sp'):
        return self.S.add(q, lambda e: e.dma_start(out=out, in_=in_), reads=r, writes=w, dma=True)

    def build_E(self, nm, blk, npart, off):
        E = self.BIG[0:npart, off:off + S_LEN]
        self.memset('pool', E, 1.0, [nm])
        self.asel(E, E, [[1, S_LEN]], ALU.is_ge, 0.0, 0, -blk, [nm], [nm])
        self.asel(E, E, [[-1, S_LEN]], ALU.is_ge, 0.0, blk - 1, blk, [nm], [nm])
        return E

    def make_qz(self, i, nh):
        sl = self.qz_n % 2
        self.qz_n += 1
        Qz = self.Qz[sl]
        npair = nh // 2
        for par in range(2):
            self.cp('pool', Qz[par * 64:(par + 1) * 64, par:nh:2, :], self.QKT[par * 64:(par + 1) * 64, 0:npair, i * 128:(i + 1) * 128],
                    [('QKT', i), 'Qz0'], [('Qz', sl, par)])
        return Qz, [('Qz', sl, 0), ('Qz', sl, 1)]

    def emit_step(self, s_fn, r_fn, L=1):
        if s_fn is not None:
            s_fn()
        if not hasattr(self, '_pend'):
            self._pend = []
        self._pend.append(r_fn)
        while len(self._pend) > L:
            self._pend.pop(0)()

    def flush_steps(self):
        for p in getattr(self, '_pend', []):
            p()
        self._pend = []

    def fence(self):
        self.S.fence_keys = []
        self.fence_n = getattr(self, 'fence_n', 0) + 1
        n = self.fence_n
        fs = self.fsc
        self.mm(self.pb[7][0:1, 0:1], self.identb[0:1, 0:1], self.identb[0:1, 0:1], True, True, ['identb'], [('pb', 7), ('fence', 'pe', n)])
        self.cp('act', fs[0:1, 0:1], fs[0:1, 4:5], ['fsc'], [('fence', 'act', n), 'fscw_act'])
        self.cp('dve', fs[0:1, 1:2], fs[0:1, 5:6], ['fsc'], [('fence', 'dve', n), 'fscw_dve'])
        self.cp('pool', fs[0:1, 2:3], fs[0:1, 6:7], ['fsc'], [('fence', 'pool', n), 'fscw_pool'])
        self.S.fence_keys = [('fence', e, n) for e in ('pe', 'act', 'dve', 'pool')]

    def build(self):
        nc = self.nc
        dr = {}

        def din(name, shape, dt=F32):
            dr[name] = nc.dram_tensor(name, shape, dt, kind="ExternalInput").ap()

        din('x', [S_LEN, D])
        din('pos', [128, NT], I32)
        din('norm_g', [2, 128, 8])
        din('w_in', [2, D, INW])
        for nm in ('q_norm_a', 'k_norm_a', 'q_norm_b', 'k_norm_b', 'q_norm_c', 'k_norm_c'):
            din(nm, [2, 64])
        din('cmp_pos', [2, 64, 32])
        din('cmp_k_w1', [2, 64, 32, 128])
        din('cmp_k_w2', [2, 128, 64])
        din('cmp_v_w1', [2, 64, 32, 128])
        din('cmp_v_w2', [2, 128, 64])
        din('w_br_a', [2, 384, D])
        din('w_br_b', [2, 256, D])
        din('w_br_c', [2, 256, D])
        din('w_out', [2, D, D])
        dr['y'] = nc.dram_tensor('y', [S_LEN, D], F32, kind="ExternalOutput").ap()
        dr['x1'] = nc.dram_tensor('x1s', [S_LEN, D], F32).ap()
        dr['hnT'] = nc.dram_tensor('hnTs', [NT, 128, 1024], BF16).ap()
        dr['oT'] = nc.dram_tensor('oTs', [NT, 128, 7 * 128], BF16).ap()
        for nm, shape, dt in self.dbg:
            dr[nm] = nc.dram_tensor(nm, shape, dt, kind="ExternalOutput").ap()
        self.dr = dr
        self.final_ops = []
        with self.st:
            self.setup()
            for l in range(self.n_layers):
                xin = dr['x'] if l == 0 else dr['x1']
                xout = dr['y'] if l == self.n_layers - 1 else dr['x1']
                self.layer(l, xin, xout)
            self.S.emit(final_waits=self.final_ops)
        return nc

    def setup(self):
        nc = self.nc
        dr = self.dr
        self.alloc_common()
        self.identf = self.sb('identf', [128, 128], F32)
        self.identb = self.sb('identb', [128, 128], BF16)
        self.onesf = self.sb('onesf', [128, 128], F32)
        self.memset('pool', self.identf[:], 1.0, ['identf'])
        self.asel(self.identf[:], self.identf[:], [[-1, 128]], ALU.is_equal, 0.0, 0, 1, ['identf'], ['identf'])
        self.cp('dve', self.identb[:], self.identf[:], ['identf'], ['identb'])
        self.memset('pool', self.onesf[:], 1.0, ['onesf'])
        self.epsc = self.sb('epsc', [128, 1], F32)
        self.memset('dve', self.epsc[:], EPS, ['epsc'])
        for q_ in self.Qz:
            self.memset('pool', q_[:], 0.0, ['Qz0'])
        self.fsc = self.sb('fsc', [128, 8], F32)
        self.memset('dve', self.fsc[:], 0.0, ['fsc'])
        posi = self.sb('posi', [128, NT], I32)
        posf = self.sb('posf', [128, NT], F32)
        fr = self.sb('fr', [128, 8], F32)
        wpF = self.BIG[:, 46592:46592 + 9216].bitcast(F32)
        wpI = self.BIG[:, 46592:46592 + 9216].bitcast(I32)
        ang = wpF[:, 0:256].rearrange("p (t f) -> p t f", f=8)
        tmpa = wpF[:, 256:512].rearrange("p (t f) -> p t f", f=8)
        self.cos = self.sb('cos', [128, NT, 8], F32)
        self.sin = self.sb('sin', [128, NT, 8], F32)
        self.dma(posi[:], dr['pos'][:, :], [], ['posi'])
        self.cp('dve', posf[:], posi[:], ['posi'], ['posf'])
        for i in range(8):
            f = float(np.float32(500000.0) ** np.float32(-i / 8.0))
            self.memset('dve', fr[:, i:i + 1], f, ['fr'])
        self.tt('dve', ang[:], bc(posf[:], 2, [128, NT, 8]), bc(fr[:], 1, [128, NT, 8]), ALU.mult, ['posf', 'fr'], ['ang'])
        PI = math.pi
        HI = 6.28125
        LO = 2 * PI - 6.28125
        ni = wpI[:, 512:768].rearrange("p (t f) -> p t f", f=8)
        nf = wpF[:, 768:1024].rearrange("p (t f) -> p t f", f=8)
        rr = wpF[:, 1024:1280].rearrange("p (t f) -> p t f", f=8)
        self.tsc('dve', tmpa[:], ang[:], 1.0 / (2 * PI), None, ALU.mult, None, ['ang'], ['tmpa'])
        self.cp('dve', ni[:], tmpa[:], ['tmpa'], ['rr_ni'])
        self.cp('dve', nf[:], ni[:], ['rr_ni'], ['rr_nf'])
        self.S.add('dve', lambda e: e.scalar_tensor_tensor(out=rr[:], in0=nf[:], scalar=-HI, in1=ang[:], op0=ALU.mult, op1=ALU.add),
                   reads=['rr_nf', 'ang'], writes=['rr_r'])
        self.S.add('dve', lambda e: e.scalar_tensor_tensor(out=rr[:], in0=nf[:], scalar=-LO, in1=rr[:], op0=ALU.mult, op1=ALU.add),
                   reads=['rr_nf', 'rr_r'], writes=['rr_r'])

        def wrap(buf, key):
            self.tsc('dve', tmpa[:], buf[:], PI, -2 * PI, ALU.is_gt, ALU.mult, [key], ['tmpa'])
            self.tt('dve', buf[:], buf[:], tmpa[:], ALU.add, [key, 'tmpa'], [key])
            self.tsc('dve', tmpa[:], buf[:], -PI, 2 * PI, ALU.is_lt, ALU.mult, [key], ['tmpa'])
            self.tt('dve', buf[:], buf[:], tmpa[:], ALU.add, [key, 'tmpa'], [key])
            self.tsc('dve', buf[:], buf[:], -3.141592, 3.141592, ALU.max, ALU.min, [key], [key])
        wrap(rr, 'rr_r')
        self.act(self.sin[:], rr[:], AF.Sin, ['rr_r'], ['sin'])
        self.tsc('dve', rr[:], rr[:], PI / 2, None, ALU.add, None, ['rr_r', 'sin'], ['rr_r'])
        wrap(rr, 'rr_r')
        self.act(self.cos[:], rr[:], AF.Sin, ['rr_r'], ['cos'])
        scrF = self.BIG[:, 0:24576].bitcast(F32)
        scrI = self.BIG[:, 0:24576].bitcast(I32)

        def carve(src, k):
            return src[:, k * 2176:(k + 1) * 2176].rearrange("p (o q) -> p o q", q=128)
        dA = carve(scrF, 0)
        dAi = carve(scrI, 1)
        t1 = carve(scrF, 2)
        t2 = carve(scrF, 3)
        t3 = carve(scrF, 4)
        t4 = carve(scrI, 4)
        self.MA = self.sb('MA', [128, 17, 128], BF16)
        self.S.add('pool', lambda e: e.iota(dAi[:], pattern=[[128, 17], [1, 128]], base=0, channel_multiplier=-1), writes=['dAi'])
        self.cp('dve', dA[:], dAi[:], ['dAi'], ['dA'])
        self.tsc('dve', t1[:], dA[:], 128.0, None, ALU.is_le, None, ['dA'], ['mt1'])
        self.tsc('dve', t4[:], dAi[:], 3, None, ALU.bitwise_and, None, ['dAi'], ['mt3'])
        self.cp('dve', t2[:], t4[:], ['mt3'], ['mt2'])
        self.tsc('dve', t2[:], t2[:], 0.0, None, ALU.is_equal, None, ['mt2'], ['mt2'])
        self.tsc('dve', t3[:], dA[:], 512.0, None, ALU.is_le, None, ['dA'], ['mt3'])
        self.tt('dve', t2[:], t2[:], t3[:], ALU.mult, ['mt2', 'mt3'], ['mt2'])
        self.tt('dve', t1[:], t1[:], t2[:], ALU.add, ['mt1', 'mt2'], ['mt1'])
        self.tsc('dve', t4[:], dAi[:], 15, None, ALU.bitwise_and, None, ['dAi', 'mt2'], ['mt3'])
        self.cp('dve', t2[:], t4[:], ['mt3', 'mt1'], ['mt2'])
        self.tsc('dve', t2[:], t2[:], 0.0, None, ALU.is_equal, None, ['mt2'], ['mt2'])
        self.tsc('dve', t3[:], dA[:], 2048.0, None, ALU.is_le, None, ['dA', 'mt2'], ['mt3'])
        self.tt('dve', t2[:], t2[:], t3[:], ALU.mult, ['mt2', 'mt3'], ['mt2'])
        self.tt('dve', t1[:], t1[:], t2[:], ALU.add, ['mt1', 'mt2'], ['mt1'])
        self.tsc('dve', t2[:], dA[:], 0.0, None, ALU.is_ge, None, ['dA', 'mt1'], ['mt2'])
        self.tt('dve', self.MA[:], t1[:], t2[:], ALU.mult, ['mt1', 'mt2'], ['MA'])
        self.AM = self.sb('AM', [128, NT, 64], F32)
        vsF = self.BIG[:, 32768:32768 + NT * 432].bitcast(F32)
        vsI = self.BIG[:, 32768:32768 + NT * 432].bitcast(I32)
        am1 = vsF[:, 0:2048].rearrange("p (t j) -> p t j", j=64)
        am1i = vsI[:, 2048:4096].rearrange("p (t j) -> p t j", j=64)
        am2 = vsF[:, 4096:6144].rearrange("p (t j) -> p t j", j=64)
        for a in range(2):
            self.S.add('pool', (lambda a: lambda e: e.iota(am1i[a * 64:(a + 1) * 64], pattern=[[-2, NT], [1, 64]], base=-a,
                                                           channel_multiplier=0))(a),
                       writes=['am1i_%d' % a])
        self.cp('dve', am1[:], am1i[:], ['am1i_0', 'am1i_1'], ['am1_0', 'am1_1'])
        self.tsc('dve', am2[:], am1[:], 0.0, -1e30, ALU.is_gt, ALU.mult, ['am1_0', 'am1_1'], ['am2'])
        self.tsc('dve', am1[:], am1[:], -1.0, 1e4, ALU.is_ge, ALU.mult, ['am1_0', 'am1_1', 'am2'], ['am1', 'am1_0', 'am1_1'])
        self.tt('dve', self.AM[:], am1[:], am2[:], ALU.add, ['am1', 'am2'], ['AM'])
        self.tsc('dve', self.AM[:, :, 0:1], self.AM[:, :, 0:1], 1e4, None, ALU.add, None, ['AM'], ['AM'])
        self.cover = self.sb('cover', [128, 2, 64], F32)
        self.memset('pool', self.cover[:], 1.0, ['cover'])
        self.asel(self.cover[:], self.cover[:], [[-128, 2], [4, 64]], ALU.is_ge, 0.0, 3, -1, ['cover'], ['cover'])
        self.asel(self.cover[:], self.cover[:], [[128, 2], [-4, 64]], ALU.is_ge, 0.0, 1, 1, ['cover'], ['cover'])
        self.pb = [self.ps('pb%d' % i, [128, 512], F32) for i in range(8)]
        self.xt = [self.sb('xt%d' % i, [128, D], F32) for i in range(2)]
        self.hg = [self.sb('hg%d' % i, [128, 4, 8, 128], BF16) for i in range(2)]
        self.wstage = [self.sb('wst%d' % i, [128, 8, 128], F32) for i in range(2)]
        self.wst_n = 0
        self.hg_n = 0
        self.xt_n = 0
        if 'dbg_cs' in [d[0] for d in self.dbg]:
            o = self.dma(self.dr['dbg_cs'][:, 0:256], self.cos[:].rearrange("p t f -> p (t f)"), ['cos'], [])
            self.final_ops.append(o)
            o = self.dma(self.dr['dbg_cs'][:, 256:512], self.sin[:].rearrange("p t f -> p (t f)"), ['sin'], [])
            self.final_ops.append(o)
            mtmp = self.sb('mtmp', [128, 17 * 128], F32)
            self.cp('dve', mtmp[:], self.MA[:].rearrange("p o q -> p (o q)"), ['MA'], ['mtmp'])
            o = self.dma(self.dr['dbg_ma'][:, :], mtmp[:], ['mtmp'], [])
            self.final_ops.append(o)
            o = self.dma(self.dr['dbg_am'][:, :], self.AM[:].rearrange("p t f -> p (t f)"), ['AM'], [])
            self.final_ops.append(o)
            o = self.dma(self.dr['dbg_cov'][:, :], self.cover[:].rearrange("p t f -> p (t f)"), ['cover'], [])
            self.final_ops.append(o)

    def load_w(self, W, wkey, src, c0, n, o, nk=8, k0=0, eng_cycle=('pool', 'dve')):
        done = 0
        while done < n:
            m = min(512, n - done)
            kc = max(1, min(nk, 1024 // m))
            for kk in range(0, nk, kc):
                kn = min(kc, nk - kk)
                sl = self.wst_n % 4
                self.wst_n += 1
                if sl < 2:
                    flat = self.wstage[sl][:].rearrange("p k c -> p (k c)")
                    skey = ('wst', sl)
                else:
                    flat = self.xt[sl - 2][:]
                    skey = ('xt', sl - 2)
                stg = flat[:, 0:kn * m].rearrange("p (k c) -> p k c", c=m)
                self.dma(stg, src[kk * 128:(kk + kn) * 128, c0 + done:c0 + done + m].rearrange("(k p) c -> p k c", p=128),
                         [], [skey])
                eng = eng_cycle[self.wst_n % len(eng_cycle)]
                self.cp(eng, W[:, k0 + kk:k0 + kk + kn, o + done:o + done + m], stg, [skey], [(wkey, self.wst_n)])
                self.wkeys.setdefault(wkey, []).append((wkey, self.wst_n))
            done += m

    def layer(self, l, xin, xout):
        if l > 0:
            self.fence()
        self.stage1(l, xin)
        if self.stop_after == 'stage1':
            return
        only = getattr(self, 'only', None)
        if only is None or 'A' in only:
            self.fence()
            self.pass_A(l)
        if self.stop_after in ('A', 'Aproj'):
            return
        if only is None or 'C' in only:
            self.fence()
            self.pass_C(l)
        if self.stop_after == 'C':
            return
        if only is None or 'B' in only:
            self.fence()
            self.pass_B(l)
        if self.stop_after == 'B':
            return
        self.fence()
        self.final(l, xin, xout)

    def stage1(self, l, xin):
        dr = self.dr
        if l == 0:
            self.gT = self.sb('gT', [128, 8], F32)
            self.s1_all = self.sb('s1all', [128, 4096], BF16)
            self.s1_sq = self.s1_all[:, 0:1024]
            self.s1_ss = self.sb('s1ss', [128, 2], F32)
            self.s1_ss4 = self.sb('s1ss4', [128, 4], F32)
            self.s1_xs = self.s1_all[:, 1024:2048]
            self.s1_hT = [self.s1_all[:, 2048 + i * 1024:3072 + i * 1024].rearrange("p (k c) -> p k c", c=128) for i in range(2)]
            self.prs = [self.pr, self.s1_all[:, 0:1536].bitcast(F32)]
        self.dma(self.gT[:], dr['norm_g'][l], [], ['gT'])
        ring = [(self.xt[0][:], ('xt', 0)), (self.xt[1][:], ('xt', 1)),
                (self.wstage[0][:].rearrange("p k c -> p (k c)"), ('wst', 0)), (self.wstage[1][:].rearrange("p k c -> p (k c)"), ('wst', 1))]
        ss4 = self.s1_ss4

        def st_a(t):
            xt_ap, xkey = ring[t % 4]
            self.dma(xt_ap, xin[t * 128:(t + 1) * 128, :], [('x1', t)], [xkey])
            junk = self.BIG[:, 24576 + (t % 2) * 1024:24576 + (t % 2 + 1) * 1024]
            self.act(junk, xt_ap, AF.Square, [xkey], [('s1junk', t % 2), ('s1ss', t % 4)], accum_out=ss4[:, t % 4:t % 4 + 1])

        def st_b(t):
            ss = ss4[:, t % 4:t % 4 + 1]
            self.act(ss, ss, AF.Ln, [('s1ss', t % 4)], [('s1ss', t % 4)], bias=self.epsc[:, 0:1], scale=1.0 / D)

        def st_c(t):
            ss = ss4[:, t % 4:t % 4 + 1]
            self.act(ss, ss, AF.Exp, [('s1ss', t % 4)], [('s1ss', t % 4)], scale=-0.5)

        def st_d(t):
            sl = t % 2
            xt_ap, xkey = ring[t % 4]
            ss = ss4[:, t % 4:t % 4 + 1]
            xs = (self.s1_sq, self.s1_xs)[sl]
            self.act(xs, xt_ap, AF.Copy, [xkey, ('s1ss', t % 4)], [('s1xs', sl)], scale=ss)
            pT = self.pb[sl][:].bitcast(BF16)
            for k in range(8):
                self.tr(pT[:, k * 128:(k + 1) * 128], xs[:, k * 128:(k + 1) * 128], self.identb[:],
                        [('s1xs', sl), 'identb'], [('pb', sl)])
            hT = self.s1_hT[sl]
            self.tt('dve', hT, pT[:, 0:1024].rearrange("p (k t) -> p k t", k=8), bc(self.gT[:], 2, [128, 8, 128]), ALU.mult,
                    [('pb', sl), 'gT'], [('s1hT', sl)])
            self.dma(dr['hnT'][t], hT.rearrange("p k t -> p (k t)"), [('s1hT', sl)], [('hnT', t)])
        for u in range(-3, NT):
            if 0 <= u + 3 < NT:
                st_a(u + 3)
            if 0 <= u + 2 < NT:
                st_b(u + 2)
            if 0 <= u + 1 < NT:
                st_c(u + 1)
            if 0 <= u < NT:
                st_d(u)
        if 'dbg_hnT' in [d[0] for d in self.dbg]:
            for t in range(NT):
                o = self.dma(dr['dbg_hnT'][t], dr['hnT'][t], [('hnT', t)], [])
                self.final_ops.append(o)

    def load_hg(self, g):
        sl = self.hg_n % 2
        self.hg_n += 1
        self.dma(self.hg[sl][:].rearrange("p t k c -> p t (k c)"), self.dr['hnT'][g * 4:(g + 1) * 4].rearrange("t p c -> p t c"),
                 [('hnT', g * 4 + i) for i in range(4)], [('hg', sl)])
        return sl

    def proj_tile(self, hsl, tt, W, wreads, chunks, banks):
        for (c0, n), b in zip(chunks, banks):
            for k in range(8):
                self.mm(self.pb[b][:, 0:n], self.hg[hsl][:, tt, k, :], W[:, k, c0:c0 + n], k == 0, k == 7,
                        [('hg', hsl)] + wreads, [('pb', b)])

    def qk_Q(self, pr, nh, prk, sl):
        W_ = nh * 64
        sq = self.qk_sq[:, 0:W_]
        ss = self.qk_ss2[:, sl, 0:nh]
        self.tt('dve', sq, pr, pr, ALU.mult, [prk], ['qk_sq'])
        self.S.add('dve', lambda e: e.tensor_reduce(out=ss, in_=sq.rearrange("p (h d) -> p h d", d=64), axis=AX.X, op=ALU.add),
                   reads=['qk_sq'], writes=[('qk_ss', sl)])

    def qk_R(self, nh, sl):
        ss = self.qk_ss2[:, sl, 0:nh]
        self.rsqrt(ss, ss, 1.0 / 64, [('qk_ss', sl)], [('qk_ss', sl)])

    def qk_N(self, pr, nh, Gt, t, xb, prk, xbk, sl):
        W_ = nh * 64
        ss = self.qk_ss2[:, sl, 0:nh]
        na = (2 * nh + 2) // 3
        pr3 = pr.rearrange("p (h d) -> p h d", d=64)
        xn3 = self.qk_xn[:, 0:W_].rearrange("p (h d) -> p h d", d=64)
        xb3 = xb.rearrange("p (h d) -> p h d", d=64)
        G3 = Gt.rearrange("p (h d) -> p h d", d=64)
        for eng, h0, h1 in (('dve', 0, na), ('pool', na, nh)):
            n_ = h1 - h0
            if n_ <= 0:
                continue
            kx = 'qk_xn_' + eng
            xn_ = xn3[:, h0:h1, :]
            self.tt(eng, xn_, pr3[:, h0:h1, :], bc(ss[:, h0:h1], 2, [128, n_, 64]), ALU.mult, [prk, ('qk_ss', sl)], [kx])
            self.tt(eng, xn_, xn_, G3[:, h0:h1, :], ALU.mult, [kx, 'Gt'], [kx])
            cosb = bc(self.cos[:, t, :], 1, [128, n_, 8])
            sinb = bc(self.sin[:, t, :], 1, [128, n_, 8])
            r = [self.qk_r[i][:, h0:h1, :] for i in range(4)]
            rk = ['qk_r%d_%s' % (i, eng) for i in range(4)]
            self.tt(eng, r[0], xn_[:, :, 0:8], cosb, ALU.mult, [kx, 'cos'], [rk[0]])
            self.tt(eng, r[1], xn_[:, :, 8:16], sinb, ALU.mult, [kx, 'sin'], [rk[1]])
            self.tt(eng, r[2], xn_[:, :, 8:16], cosb, ALU.mult, [kx, 'cos'], [rk[2]])
            self.tt(eng, r[3], xn_[:, :, 0:8], sinb, ALU.mult, [kx, 'sin'], [rk[3]])
            xk = xbk + '_' + eng
            self.tt(eng, xb3[:, h0:h1, 0:8], r[0], r[1], ALU.subtract, [rk[0], rk[1]], [xk])
            self.tt(eng, xb3[:, h0:h1, 8:16], r[2], r[3], ALU.add, [rk[2], rk[3]], [xk])
            self.cp(eng, xb3[:, h0:h1, 16:64], xn_[:, :, 16:64], [kx], [xk])

    def proj_pipeline(self, P, E, Q, R, N, T):
        P(0)
        E(0)
        Q(0)
        R(0)
        if NT > 1:
            P(1)
        for t in range(NT):
            if t + 1 < NT:
                E(t + 1)
                Q(t + 1)
                R(t + 1)
            N(t)
            if t + 2 < NT:
                P(t + 2)
            T(t)

    def alloc_common(self):
        if hasattr(self, 'qk_sq'):
            return
        self.qk_sq = self.sb('qk_sq', [128, 768], F32)
        self.qk_ss = self.sb('qk_ss', [128, 12], F32)
        self.qk_ss2 = self.sb('qk_ss2', [128, 2, 12], F32)
        self.qk_xn = self.sb('qk_xn', [128, 768], F32)
        self.qk_r = [self.sb('qk_r%d' % i, [128, 12, 8], F32) for i in range(4)]
        self.pr = self.sb('pr', [128, 768], F32)
        self.xb = self.sb('xb', [128, 768], BF16)
        self.Gt = self.sb('Gt', [128, 768], F32)
        self.g64 = self.sb('g64', [128, 2, 64], F32)
        self.PT = [self.sb('PT%d' % i, [128, 768], BF16) for i in range(3)]
        self.pt_n = 0
        self.ob = self.sb('ob', [128, 384], BF16)
        self.rec = self.sb('rec', [128, 12], F32)
        self.oTt = [self.sb('oTt%d' % i, [128, 384], BF16) for i in range(2)]
        self.selT = [self.sb('selT%d' % i, [64, 512], BF16) for i in range(2)]
        self.Qz = [self.sb('Qz%d' % i, [128, 6, 128], BF16) for i in range(2)]
        self.qz_n = 0
        self.ot_n = 0
        self.BIG = self.sb('BIG', [128, 57344], BF16)
        self.QKT = self.BIG[:, 0:32768].rearrange("p (a c) -> p a c", c=S_LEN)
        self.VS = self.BIG[:, 32768:32768 + NT * 432].rearrange("p (t c) -> p t c", c=432)
        self.Wp = self.BIG[:, 46592:46592 + 9216].rearrange("p (k c) -> p k c", c=1152)

    def load_gains(self, l, qn, kn, nq, nk):
        dr = self.dr
        self.dma(self.g64[:, 0, :], dr[qn][l].partition_broadcast(128), [], ['g64q'])
        self.dma(self.g64[:, 1, :], dr[kn][l].partition_broadcast(128), [], ['g64k'])
        G3 = self.Gt[:, 0:(nq + nk) * 64].rearrange("p (h d) -> p h d", d=64)
        self.cp('dve', G3[:, 0:nq, :], bc(self.g64[:, 0, :], 1, [128, nq, 64]), ['g64q'], ['Gt'])
        self.cp('dve', G3[:, nq:nq + nk, :], bc(self.g64[:, 1, :], 1, [128, nk, 64]), ['g64k'], ['Gt'])

    def out_tile(self, i, acc_key, ob_ap, ncol, c0):
        npair = ncol // 128
        pT = self.pb[7][:].bitcast(BF16)
        for p in range(npair):
            self.tr(pT[:, p * 128:(p + 1) * 128], ob_ap[:, p * 128:(p + 1) * 128], self.identb[:], [acc_key, 'identb'], [('pb', 7)])
        sl = self.ot_n % 2
        self.ot_n += 1
        self.cp('act', self.oTt[sl][:, 0:ncol], pT[:, 0:ncol], [('pb', 7)], [('oTt', sl)])
        self.dma(self.dr['oT'][i][:, c0:c0 + ncol], self.oTt[sl][:, 0:ncol], [('oTt', sl)], [('oT', i, c0)])

    def pass_A(self, l):
        dr = self.dr
        self.alloc_common()
        self.wkeys = {}
        W = self.Wp
        self.load_w(W, 'Wp', dr['w_in'][l], 0, 1152, 0)
        wreads = list(self.wkeys['Wp'])
        self.load_gains(l, 'q_norm_a', 'k_norm_a', 6, 6)
        VS4 = self.VS.rearrange("p t (h e) -> p t h e", e=72)
        self.S.add('pool', lambda e: e.memset(VS4[:, :, :, 64:65], 1.0), reads=['MA', 'AM'], writes=['VS_ones'])
        QKT = self.QKT
        hs_ = {}

        def do_proj(t):
            if t % 4 == 0:
                hs_['sl'] = self.load_hg(t // 4)
            self.proj_tile(hs_['sl'], t % 4, W, wreads, [(0, 384), (384, 384), (768, 384)], [0, 1, 2])

        def E_(t):
            pr = self.prs[t % 2]
            self.cp('act', pr[:, 0:384], self.pb[0][:, 0:384], [('pb', 0)], [('pr', t % 2)])
            self.cp('act', pr[:, 384:768], self.pb[1][:, 0:384], [('pb', 1)], [('pr', t % 2)])
            self.cp('act', VS4[:, t, :, 0:64], self.pb[2][:, 0:384].rearrange("p (h d) -> p h d", d=64), [('pb', 2), 'VS_ones', 'MA', 'AM'], [('VS', t)])

        def T_(t):
            pT = self.pb[3][:].bitcast(BF16)
            for p in range(6):
                self.tr(pT[:, p * 128:(p + 1) * 128], self.xb[:, p * 128:(p + 1) * 128], self.identb[:], ['xb_dve', 'xb_pool', 'identb'], [('pb', 3)])
            self.cp('act', QKT[:, 0:6, t * 128:(t + 1) * 128], pT[:, 0:768].rearrange("p (a c) -> p a c", c=128), [('pb', 3), 'MA', 'AM'], [('QKT', t)])
        self.proj_pipeline(do_proj, E_,
                           lambda t: self.qk_Q(self.prs[t % 2][:, 0:768], 12, ('pr', t % 2), t % 2),
                           lambda t: self.qk_R(12, t % 2),
                           lambda t: self.qk_N(self.prs[t % 2][:, 0:768], 12, self.Gt[:, 0:768], t, self.xb[:, 0:768], ('pr', t % 2), 'xb', t % 2),
                           T_)
        if self.stop_after == 'Aproj':
            return
        stA = {'sb': 0}

        def a_pair(i, j, j0, Qz, qzk):
            o = i - j
            d_ = {}

            def s_():
                bS = [(2, 3), (4, 5), (0, 1)][stA['sb'] % 3]
                stA['sb'] += 1
                d_['bS'] = bS
                for p in range(3):
                    bb = bS[0] if p < 2 else bS[1]
                    c0 = (p % 2) * 256
                    self.mm(self.pb[bb][:, c0:c0 + 256], QKT[:, 3 + p, j * 128:(j + 1) * 128],
                            Qz[:, 2 * p:2 * p + 2, :], True, True, [('QKT', j)] + qzk, [('pb', bb)])

            def r_():
                bS = d_['bS']
                ps_ = self.pt_n % 3
                self.pt_n += 1
                PT = self.PT[ps_]
                self.act(PT[:, 0:512], self.pb[bS[0]][:, 0:512], AF.Exp, [('pb', bS[0])], [('PT', ps_)], scale=0.125)
                self.act(PT[:, 512:768], self.pb[bS[1]][:, 0:256], AF.Exp, [('pb', bS[1])], [('PT', ps_)], scale=0.125)
                self.tt('dve', PT[:, 0:768].rearrange("p (h q) -> p h q", q=128), PT[:, 0:768].rearrange("p (h q) -> p h q", q=128),
                        bc(self.MA[:, o, :], 1, [128, 6, 128]), ALU.mult, [('PT', ps_), 'MA'], [('PT', ps_)])
                for h in range(6):
                    self.mm(self.pb[6][:, h * 72:h * 72 + 65], PT[:, h * 128:(h + 1) * 128], VS4[:, j, h, 0:65],
                            (j == j0 and h == 0), j == i, [('PT', ps_), ('VS', j), 'VS_ones'], [('pb', 6)], skip=True)
            return s_, r_

        def a_fin(i):
            def r_():
                acc = self.pb[6][:, 0:432].rearrange("p (h e) -> p h e", e=72)
                self.S.add('dve', lambda e: e.reciprocal(out=self.rec[:, 0:6], in_=acc[:, :, 64]), reads=[('pb', 6)], writes=['rec'])
                self.tt('dve', self.ob[:, 0:384].rearrange("p (h d) -> p h d", d=64), acc[:, :, 0:64], bc(self.rec[:, 0:6], 2, [128, 6, 64]),
                        ALU.mult, [('pb', 6), 'rec'], ['ob'])
                self.out_tile(i, 'ob', self.ob, 384, 0)
            return r_
        for i in range(NT):
            j0 = max(0, i - 16)
            Qz, qzk = self.make_qz(i, 6)
            for j in range(j0, i + 1):
                s_, r_ = a_pair(i, j, j0, Qz, qzk)
                self.emit_step(s_, r_, L=2)
            self.emit_step(None, a_fin(i), L=2)
        self.flush_steps()
        self.dbg_oT()

    def dbg_oT(self):
        if 'dbg_oT' in [d[0] for d in self.dbg] and self.stop_after is not None:
            rng = [r for k, r in (('A', (0, 384)), ('B', (384, 640)), ('C', (640, 896))) if getattr(self, 'only', None) is None or k in self.only]
            for t in range(NT):
                for (a, b) in rng:
                    o = self.dma(self.dr['dbg_oT'][t][:, a:b], self.dr['oT'][t][:, a:b], [('oT', t, 0), ('oT', t, 384), ('oT', t, 640)], [])
                    self.final_ops.append(o)

    def pass_B(self, l):
        dr = self.dr
        if not hasattr(self, 'b_GS'):
            self.b_GS = self.sb('b_GS', [128, NT, 12], F32)
            self.b_posT = self.sb('b_posT', [64, 32], BF16)
            self.b_posTf = self.sb('b_posTf', [64, 32], F32)
            self.b_W2 = self.sb('b_W2', [128, 2, 64], BF16)
            self.b_W2f = self.sb('b_W2f', [128, 2, 64], F32)
            self.b_W2vf = self.b_W2f
            self.b_h1 = self.sb('b_h1', [128, 2, 256], BF16)
            self.b_hb = self.sb('b_hb', [128, 2], F32)
            self.b_kcT = self.sb('b_kcT', [64, 256], BF16)
            self.b_vc = self.sb('b_vc', [128, 2, 72], F32)
            self.b_rdc = self.sb('b_rdc', [128, 2, 4], F32)
            self.b_Pc = self.sb('b_Pc', [128, 2, 512], F32)
            self.b_rden = self.qk_sq[:, 0:512]
            self.b_sc = self.sb('b_sc', [128, 64], F32)
            self.b_sc2 = self.sb('b_sc2', [128, 64], F32)
            self.b_m1 = self.sb('b_m1', [128, 8], F32)
            self.b_m2 = self.sb('b_m2', [128, 8], F32)
            self.b_selb = self.sb('b_selb', [128, 64], F32)
            self.b_selbT2 = [t_[0:64, :].rearrange('p (h q) -> p h q', q=128) for t_ in self.selT]
            self.b_f = self.sb('b_f', [128, 12], F32)
            self.b_ocmp = self.pr[:, 0:512].rearrange('p (a c) -> p a c', c=256)
            self.b_obf = self.qk_xn[:, 0:256].rearrange('p (h d) -> p h d', d=64)
            self.b_tmp = self.qk_xn[:, 256:512].rearrange('p (h d) -> p h d', d=64)
        GS = self.b_GS
        self.wkeys = {}
        W = self.BIG[:, 37376:37376 + 8 * 652].rearrange("p (k c) -> p k c", c=652)
        self.load_w(W, 'Wp', dr['w_in'][l], 1536, 652, 0)
        wreads = list(self.wkeys['Wp'])
        self.load_gains(l, 'q_norm_b', 'k_norm_b', 4, 6)
        Esel = self.BIG[64:128, 5 * S_LEN:6 * S_LEN]
        self.memset('pool', Esel, 1.0, ['Esel'])
        self.asel(Esel, Esel, [[1, S_LEN]], ALU.is_ge, 0.0, 0, -64, ['Esel'], ['Esel'])
        self.asel(Esel, Esel, [[-1, S_LEN]], ALU.is_ge, 0.0, 63, 64, ['Esel'], ['Esel'])
        QS = self.BIG[:, 0:32768].rearrange("p (a c) -> p a c", c=S_LEN)
        VSB = self.BIG[:, 32768:32768 + NT * 144].rearrange("p (t h e) -> p t h e", h=2, e=72)
        self.memset('pool', VSB[:, :, :, 64:65], 1.0, ['VS_ones'])
        QTB = self.BIG[0:64, 0:32768].rearrange("p (a c) -> p a c", c=S_LEN)
        W1 = [self.BIG[0:64, 42592 + i * 4096:42592 + (i + 1) * 4096].rearrange("p (q h) -> p q h", h=128) for i in range(2)]
        for wi, nm in enumerate(('cmp_k_w1', 'cmp_v_w1')):
            for qtr in range(4):
                sl = self.wst_n % 2
                self.wst_n += 1
                stg = self.wstage[sl][0:64]
                self.dma(stg, dr[nm][l][:, qtr * 8:(qtr + 1) * 8, :], [], [('wst', sl)])
                self.cp('pool', W1[wi][:, qtr * 8:(qtr + 1) * 8, :], stg, [('wst', sl)], [('W1', wi, qtr)])
        w1keys = [[('W1', wi, q) for q in range(4)] for wi in range(2)]
        self.dma(self.b_posTf[:], dr['cmp_pos'][l], [], ['b_posTf'])
        self.cp('dve', self.b_posT[:], self.b_posTf[:], ['b_posTf'], ['b_posT'])
        self.dma(self.b_W2f[:, 0, :], dr['cmp_k_w2'][l], [], ['b_W2f0'])
        self.dma(self.b_W2f[:, 1, :], dr['cmp_v_w2'][l], [], ['b_W2f1'])
        self.cp('dve', self.b_W2[:], self.b_W2f[:], ['b_W2f0', 'b_W2f1'], ['b_W2'])
        srcs = [0, 64, 128, 192, 256, 384, 512, 320]
        hs_ = {}

        def do_proj(t):
            if t % 4 == 0:
                hs_['sl'] = self.load_hg(t // 4)
            self.proj_tile(hs_['sl'], t % 4, W, wreads, [(0, 512), (512, 140)], [0, 1])

        def E_(t):
            pr = self.prs[t % 2]
            self.cp('act', pr[:, 0:512], self.pb[0][:, 0:512], [('pb', 0)], [('pr', t % 2)])
            self.cp('act', pr[:, 512:652], self.pb[1][:, 0:140], [('pb', 1)], [('pr', t % 2)])

        def N_(t):
            pr = self.prs[t % 2]
            pk = ('pr', t % 2)
            self.qk_N(pr[:, 0:640], 10, self.Gt[:, 0:640], t, self.xb[:, 0:640], pk, 'xb', t % 2)
            self.cp('dve', self.xb[:, 320:384], pr[:, 320:384], [pk, 'xb_dve', 'xb_pool'], ['xb_dve', 'xb_pool'])
            self.cp('dve', VSB[:, t, 0, 0:64], pr[:, 448:512], [pk, 'VS_ones'], [('VS', t)])
            self.cp('dve', VSB[:, t, 1, 0:64], pr[:, 576:640], [pk, 'VS_ones'], [('VS', t)])
            self.cp('dve', GS[:, t, :], pr[:, 640:652], [pk], [('GS', t)])

        def T_(t):
            pT = self.pb[3][:].bitcast(BF16)
            for si, c0 in enumerate(srcs):
                self.tr(pT[0:64, si * 128:(si + 1) * 128], self.xb[:, c0:c0 + 64], self.identb[:], ['xb_dve', 'xb_pool', 'identb'], [('pb', 3)])
            self.cp('act', QTB[:, 0:8, t * 128:(t + 1) * 128], pT[0:64, 0:1024].rearrange("p (a c) -> p a c", c=128), [('pb', 3)], [('QKT', t)])
        self.proj_pipeline(do_proj, E_,
                           lambda t: self.qk_Q(self.prs[t % 2][:, 0:640], 10, ('pr', t % 2), t % 2),
                           lambda t: self.qk_R(10, t % 2),
                           N_, T_)
        allq = [('QKT', t) for t in range(NT)]
        self.act(GS[:].rearrange("p t g -> p (t g)"), GS[:].rearrange("p t g -> p (t g)"), AF.Sigmoid, [('GS', t) for t in range(NT)], ['GSs'])
        self.memset('dve', self.b_h1[:, :, 255:256], 0.0, ['b_h1z'])
        for wi, slot in ((0, 4), (1, 7)):
            for p in range(32):
                self.mm(self.pb[5][:, 0:255], W1[wi][:, p, :], QTB[:, slot, p:p + 16 * 254 + 1:16], p == 0, p == 31,
                        allq + w1keys[wi], [('pb', 5)])
            for p in range(32):
                self.mm(self.pb[4][:, 0:1], W1[wi][:, p, :], self.b_posT[:, p:p + 1], p == 0, p == 31, w1keys[wi] + ['b_posT'], [('pb', 4)])
            self.cp('dve', self.b_hb[:, wi:wi + 1], self.pb[4][:, 0:1], [('pb', 4)], [('b_hb', wi)])
            self.act(self.b_h1[:, wi, 0:255], self.pb[5][:, 0:255], AF.Silu, [('pb', 5), ('b_hb', wi), 'b_h1z'], [('b_h1', wi)],
                     bias=self.b_hb[:, wi:wi + 1])
        self.mm(self.pb[5][0:64, 0:256], self.b_W2[:, 0, :], self.b_h1[:, 0, :], True, True, ['b_W2', ('b_h1', 0), 'b_h1z'], [('pb', 5)])
        self.cp('dve', self.b_kcT[:, :], self.pb[5][0:64, 0:256], [('pb', 5)], ['b_kcT'])
        for ct in range(2):
            self.mm(self.pb[4][:, ct * 64:(ct + 1) * 64], self.b_h1[:, 1, ct * 128:(ct + 1) * 128], self.b_W2[:, 1, :], True, True,
                    ['b_W2', ('b_h1', 1), 'b_h1z'], [('pb', 4)])
        self.cp('dve', self.b_vc[:, :, 0:64], self.pb[4][:, 0:128].rearrange("p (c d) -> p c d", d=64), [('pb', 4)], ['b_vc'])
        self.memset('dve', self.b_vc[:, :, 64:65], 1.0, ['b_vc'])
        Pc = self.b_Pc
        st = {'sb': 0}

        def qap(i):
            return QTB[:, 0:4, i * 128:(i + 1) * 128]

        def nbank():
            b = (2, 3, 1)[st['sb'] % 3]
            st['sb'] += 1
            return b

        def sel_steps(i):
            qr = [('QKT', i)]
            nct = 2 if i >= 16 else 1
            ob = 0
            selbT = self.b_selbT2[i % 2]
            sk = ('b_selbT', i)

            def s_a():
                for ct in range(nct):
                    b = 7
                    self.mm(self.pb[b][:, 0:512], self.b_kcT[:, ct * 128:(ct + 1) * 128], qap(i), True, True, qr + ['b_kcT'], [('pb', b)])
                    self.act(Pc[:, ct, :], self.pb[b][:, 0:512], AF.Exp, [('pb', b)], [('Pc', ct)], scale=0.125)
                    if ct == 1 or i < 17:
                        self.asel(Pc[:, ct, :].rearrange("p (h q) -> p h q", q=128), Pc[:, ct, :].rearrange("p (h q) -> p h q", q=128),
                                  [[0, 4], [1, 128]], ALU.is_ge, 0.0, 128 * i - 2048 * ct - 31, -16, [('Pc', ct)], [('Pc', ct)])

            def s_d():
                first = True
                for h in range(4):
                    for ct in range(nct):
                        self.mm(self.pb[4][:, h * 64:(h + 1) * 64], Pc[:, ct, h * 128:(h + 1) * 128], self.cover[:, ct, :], first,
                                (h == 3 and ct == nct - 1), [('Pc', ct), 'cover'], [('pb', 4)], skip=True)
                        first = False
                first = True
                for h in range(4):
                    for ct in range(nct):
                        self.mm(self.pb[ob][:, h * 65:h * 65 + 65], Pc[:, ct, h * 128:(h + 1) * 128], self.b_vc[:, ct, 0:65], first,
                                (h == 3 and ct == nct - 1), [('Pc', ct), 'b_vc'], [('pb', ob)], skip=True)
                        first = False

            def s_e():
                rd = self.b_rdc[:, i % 2, :]
                oc = self.pb[ob][:, 0:260].rearrange("p (h e) -> p h e", e=65)
                self.tsc('dve', rd, oc[:, :, 64], 1e-30, None, ALU.add, None, [('pb', ob)], [('b_rdc', i % 2)])
                self.S.add('dve', lambda e: e.reciprocal(out=rd, in_=rd), reads=[('b_rdc', i % 2)], writes=[('b_rdc', i % 2)])
                self.cp('dve', self.b_ocmp[:, i % 2, :].rearrange("p (h d) -> p h d", d=64), oc[:, :, 0:64], [('pb', ob)], [('b_ocmp', i % 2)])
                for h in range(4):
                    in1 = self.AM[:, i, :] if h == 0 else self.b_sc[:]
                    self.S.add('dve', (lambda h, in1: lambda e: e.scalar_tensor_tensor(
                        out=self.b_sc[:], in0=self.pb[4][:, h * 64:(h + 1) * 64], scalar=rd[:, h:h + 1], in1=in1,
                        op0=ALU.mult, op1=ALU.add))(h, in1), reads=[('pb', 4), ('b_rdc', i % 2), 'AM', 'b_sc'], writes=['b_sc'])
                self.S.add('dve', lambda e: e.max(out=self.b_m1[:], in_=self.b_sc[:]), reads=['b_sc'], writes=['b_m1'])
                self.S.add('dve', lambda e: e.match_replace(out=self.b_sc2[:], in_to_replace=self.b_m1[:], in_values=self.b_sc[:],
                                                            imm_value=-3e38), reads=['b_sc', 'b_m1'], writes=['b_sc2'])
                self.S.add('dve', lambda e: e.max(out=self.b_m2[:], in_=self.b_sc2[:]), reads=['b_sc2'], writes=['b_m2'])
                self.tsc('dve', self.b_selb[:], self.b_sc[:], self.b_m2[:, 7:8], None, ALU.is_ge, None, ['b_sc', 'b_m2'], ['b_selb'])
                self.tsc('dve', self.b_selb[:], self.b_selb[:], 1.0, -NEGB, ALU.subtract, ALU.mult, ['b_selb'], ['b_selb'])

            def s_f():
                self.tr(self.pb[4][0:64, 0:128], self.b_selb[:, :], self.identf[:], ['b_selb', 'identf'], [('pb', 4)])
                self.cp('act', QS[64:128, 0:4, i * 128:(i + 1) * 128], bc(self.pb[4][0:64, 0:128], 1, [64, 4, 128]), [('pb', 4)], [sk])
            return [s_a, s_d, s_e, s_f]

        def attn_steps(i):
            qr = [('QKT', i)]
            sk = ('b_selbT', i)
            steps = []

            def sel_pair(j):
                d_ = {}

                def s_():
                    b = nbank()
                    d_['b'] = b
                    bank = self.pb[b]
                    self.mm(bank[:, 0:512], QS[:, 5, j * 128:(j + 1) * 128], QS[:, 0:4, i * 128:(i + 1) * 128], True, True,
                            qr + [('QKT', j), 'Esel', sk], [('pb', b)])

                def f():
                    b = d_['b']
                    bank = self.pb[b]
                    ps_ = self.pt_n % 3
                    self.pt_n += 1
                    PT = self.PT[ps_]
                    self.act(PT[:, 0:512], bank[:, 0:512], AF.Exp, [('pb', b)], [('PT', ps_)], scale=0.125)
                    if j == i:
                        self.asel(PT[:, 0:512].rearrange("p (h q) -> p h q", q=128), PT[:, 0:512].rearrange("p (h q) -> p h q", q=128),
                                  [[0, 4], [1, 128]], ALU.is_ge, 0.0, 0, -1, [('PT', ps_)], [('PT', ps_)])
                    for h in range(4):
                        self.mm(self.pb[6][:, h * 72:h * 72 + 65], PT[:, h * 128:(h + 1) * 128], VSB[:, j, 0, 0:65],
                                (j == 0 and h == 0), j == i, [('PT', ps_), ('VS', j), 'VS_ones'], [('pb', 6)], skip=True)
                return (s_, f)
            j0 = max(0, i - 4)

            def win_pair(j):
                d_ = {}

                def s_():
                    b = nbank()
                    d_['b'] = b
                    bank = self.pb[b]
                    self.mm(bank[:, 0:512], QTB[:, 6, j * 128:(j + 1) * 128], qap(i), True, True, qr + [('QKT', j)], [('pb', b)])

                def f():
                    b = d_['b']
                    bank = self.pb[b]
                    ps_ = self.pt_n % 3
                    self.pt_n += 1
                    PT = self.PT[ps_]
                    self.act(PT[:, 0:512], bank[:, 0:512], AF.Exp, [('pb', b)], [('PT', ps_)], scale=0.125)
                    PT3 = PT[:, 0:512].rearrange("p (h q) -> p h q", q=128)
                    if j == i:
                        self.asel(PT3, PT3, [[0, 4], [1, 128]], ALU.is_ge, 0.0, 0, -1, [('PT', ps_)], [('PT', ps_)])
                    if j == i - 4:
                        self.asel(PT3, PT3, [[0, 4], [-1, 128]], ALU.is_ge, 0.0, -1, 1, [('PT', ps_)], [('PT', ps_)])
                    for h in range(4):
                        self.mm(self.pb[5][:, h * 72:h * 72 + 65], PT[:, h * 128:(h + 1) * 128], VSB[:, j, 1, 0:65],
                                (j == j0 and h == 0), j == i, [('PT', ps_), ('VS', j), 'VS_ones'], [('pb', 5)], skip=True)
                return (s_, f)
            for j in range(i + 1):
                steps.append(sel_pair(j))
            for j in range(j0, i + 1):
                steps.append(win_pair(j))

            def combine():
                accs = self.pb[6][:, 0:288].rearrange("p (h e) -> p h e", e=72)
                accw = self.pb[5][:, 0:288].rearrange("p (h e) -> p h e", e=72)
                ocmp = self.b_ocmp[:, i % 2, :].rearrange("p (h d) -> p h d", d=64)
                f = self.b_f
                self.S.add('dve', lambda e: e.reciprocal(out=f[:, 4:8], in_=accs[:, :, 64]), reads=[('pb', 6)], writes=['b_f'])
                self.S.add('dve', lambda e: e.reciprocal(out=f[:, 8:12], in_=accw[:, :, 64]), reads=[('pb', 5)], writes=['b_f'])
                self.tt('dve', f[:, 4:12], f[:, 4:12], GS[:, i, 4:12], ALU.mult, ['b_f', 'GSs'], ['b_f'])
                self.tt('dve', f[:, 0:4], GS[:, i, 0:4], self.b_rdc[:, i % 2, :], ALU.mult, ['GSs', ('b_rdc', i % 2)], ['b_f'])
                self.tt('dve', self.b_obf, ocmp, bc(f[:, 0:4], 2, [128, 4, 64]), ALU.mult, [('b_ocmp', i % 2), 'b_f'], ['b_obf'])
                self.tt('dve', self.b_tmp, accs[:, :, 0:64], bc(f[:, 4:8], 2, [128, 4, 64]), ALU.mult, [('pb', 6), 'b_f'], ['b_tmp'])
                self.tt('dve', self.b_obf, self.b_obf, self.b_tmp, ALU.add, ['b_obf', 'b_tmp'], ['b_obf'])
                self.tt('dve', self.b_tmp, accw[:, :, 0:64], bc(f[:, 8:12], 2, [128, 4, 64]), ALU.mult, [('pb', 5), 'b_f'], ['b_tmp'])
                self.tt('dve', self.ob[:, 0:256].rearrange("p (h d) -> p h d", d=64), self.b_obf, self.b_tmp, ALU.add,
                        ['b_obf', 'b_tmp'], ['ob'])
                self.out_tile(i, 'ob', self.ob, 256, 384)
            steps.append((None, combine))
            return steps

        for f_ in sel_steps(0):
            f_()
        for i in range(NT):
            pend = sel_steps(i + 1) if i + 1 < NT else []
            asteps = attn_steps(i)
            gap = max(1, (len(asteps) - 1) // (len(pend) + 1)) if pend else 1
            for n_, a in enumerate(asteps):
                self.emit_step(a[0], a[1], L=2)
                if pend and (n_ % gap == gap - 1) and n_ < len(asteps) - 1:
                    pend.pop(0)()
            while pend:
                pend.pop(0)()
        self.flush_steps()
        self.dbg_oT()

    def pass_C(self, l):
        dr = self.dr
        if not hasattr(self, 'c_ksum'):
            self.c_ksum = self.sb('c_ksum', [128, 2, 16], F32)
            self.c_khi = self.sb('c_khi', [128, 2, 16], BF16)
            self.c_klo = self.sb('c_klo', [128, 2, 16], BF16)
            self.c_tmp = self.sb('c_tmp', [128, 2, 16], F32)
            self.c_sc = self.sb('c_sc', [128, 4, 16], F32)
            self.c_mx = self.sb('c_mx', [128, 4, 8], F32)
            self.c_selb = self.sb('c_selb', [128, 4, 16], F32)
            self.c_selbT2 = [t_[0:16, :] for t_ in self.selT]
        self.wkeys = {}
        W = self.Wp
        self.load_w(W, 'Wp', dr['w_in'][l], 2444, 768, 0)
        wreads = list(self.wkeys['Wp'])
        self.load_gains(l, 'q_norm_c', 'k_norm_c', 4, 4)
        self.Ec = self.build_E('Ec', 256, 16, 16384)
        VS4 = self.VS.rearrange("p t (h e) -> p t h e", e=72)
        self.memset('pool', VS4[:, :, 0:4, 64:65], 1.0, ['VS_ones'])
        QKT = self.QKT
        hs_ = {}

        def do_proj(t):
            if t % 4 == 0:
                hs_['sl'] = self.load_hg(t // 4)
            self.proj_tile(hs_['sl'], t % 4, W, wreads, [(0, 512), (512, 256)], [0, 1])

        def E_(t):
            pr = self.prs[t % 2]
            self.cp('act', pr[:, 0:512], self.pb[0][:, 0:512], [('pb', 0)], [('pr', t % 2)])
            self.cp('act', VS4[:, t, 0:4, 0:64], self.pb[1][:, 0:256].rearrange("p (h d) -> p h d", d=64), [('pb', 1), 'VS_ones'], [('VS', t)])

        def T_(t):
            pT = self.pb[3][:].bitcast(BF16)
            for p in range(4):
                self.tr(pT[:, p * 128:(p + 1) * 128], self.xb[:, p * 128:(p + 1) * 128], self.identb[:], ['xb_dve', 'xb_pool', 'identb'], [('pb', 3)])
            self.cp('act', QKT[:, 0:4, t * 128:(t + 1) * 128], pT[:, 0:512].rearrange("p (a c) -> p a c", c=128), [('pb', 3)], [('QKT', t)])
        self.proj_pipeline(do_proj, E_,
                           lambda t: self.qk_Q(self.prs[t % 2][:, 0:512], 8, ('pr', t % 2), t % 2),
                           lambda t: self.qk_R(8, t % 2),
                           lambda t: self.qk_N(self.prs[t % 2][:, 0:512], 8, self.Gt[:, 0:512], t, self.xb[:, 0:512], ('pr', t % 2), 'xb', t % 2),
                           T_)
        allq = [('QKT', t) for t in range(NT)]
        ksum, khi, klo, ktmp = self.c_ksum, self.c_khi, self.c_klo, self.c_tmp
        self.S.add('dve', lambda e: e.tensor_reduce(out=ksum[:], in_=QKT[:, 2:4, :].rearrange("p a (n k) -> p a n k", k=256),
                                                    axis=AX.X, op=ALU.add), reads=allq, writes=['c_ksum'])
        self.cp('dve', khi[:], ksum[:], ['c_ksum'], ['c_khi'])
        self.cp('dve', ktmp[:], khi[:], ['c_khi'], ['c_tmp'])
        self.tt('dve', ktmp[:], ksum[:], ktmp[:], ALU.subtract, ['c_ksum', 'c_tmp'], ['c_tmp'])
        self.cp('dve', klo[:], ktmp[:], ['c_tmp'], ['c_klo'])
        st = {'sb': 0}

        def sel_steps(i):
            nb = i // 2
            if nb == 0:
                return []
            sl = i % 2
            Qz, qzk = self.c_qz[i]
            sc = self.c_sc
            selbT = self.c_selbT2[sl]
            sk = ('c_selbT', sl)

            def s_a():
                for h in range(4):
                    self.mm(self.pb[5][:, h * 16:(h + 1) * 16], Qz[:, h, :], khi[:, h // 2, :], True, False, qzk + ['c_khi'], [('pb', 5)])
                    self.mm(self.pb[5][:, h * 16:(h + 1) * 16], Qz[:, h, :], klo[:, h // 2, :], False, True, qzk + ['c_klo'], [('pb', 5)])

            def s_b():
                self.cp('dve', sc[:].rearrange("p h n -> p (h n)"), self.pb[5][:, 0:64], [('pb', 5)], ['c_sc'])
                if nb < 16:
                    self.memset('dve', sc[:, :, nb:16], -1e30, ['c_sc'])
                for h in range(4):
                    self.S.add('dve', (lambda h: lambda e: e.max(out=self.c_mx[:, h, :], in_=sc[:, h, :]))(h), reads=['c_sc'], writes=['c_mx'])
                self.tt('dve', self.c_selb[:], sc[:], bc(self.c_mx[:, :, 2], 2, [128, 4, 16]), ALU.is_ge, ['c_sc', 'c_mx'], ['c_selb'])
                self.tsc('dve', self.c_selb[:], self.c_selb[:], 1.0, -NEGB, ALU.subtract, ALU.mult, ['c_selb'], ['c_selb'])
                if nb < 16:
                    self.memset('dve', self.c_selb[:, :, nb:16], NEGB, ['c_selb'])

            def s_c():
                for h in range(4):
                    self.tr(self.pb[4][0:16, h * 128:(h + 1) * 128], self.c_selb[:, h, :], self.identf[:], ['c_selb', 'identf'], [('pb', 4)])
                self.cp('act', selbT[:, :], self.pb[4][0:16, 0:512], [('pb', 4)], [sk])
            return [s_a, s_b, s_c]

        def attn_steps(i):
            nb = i // 2
            Qz, qzk = self.c_qz[i]
            selbT = self.c_selbT2[i % 2]
            sk = ('c_selbT', i % 2)
            steps = []

            def pair(j):
                d_ = {}
                past = j < 2 * nb

                def s_():
                    b = (2, 3, 0, 1)[st['sb'] % 4]
                    st['sb'] += 1
                    d_['b'] = b
                    bank = self.pb[b]
                    if past:
                        self.mm(bank[:, 0:512], self.Ec[:, j * 128:(j + 1) * 128], selbT[0:16, 0:512], True, False,
                                ['Ec', sk], [('pb', b)], skip=True)
                    for p in range(2):
                        self.mm(bank[:, p * 256:(p + 1) * 256], QKT[:, 2 + p, j * 128:(j + 1) * 128], Qz[:, 2 * p:2 * p + 2, :],
                                (not past) and p == 0, p == 1, [('QKT', j)] + qzk, [('pb', b)], skip=True)

                def r_():
                    b = d_['b']
                    bank = self.pb[b]
                    ps_ = self.pt_n % 3
                    self.pt_n += 1
                    PT = self.PT[ps_]
                    self.act(PT[:, 0:512], bank[:, 0:512], AF.Exp, [('pb', b)], [('PT', ps_)], scale=0.125)
                    if j == i:
                        self.asel(PT[:, 0:512].rearrange("p (h q) -> p h q", q=128), PT[:, 0:512].rearrange("p (h q) -> p h q", q=128),
                                  [[0, 4], [1, 128]], ALU.is_ge, 0.0, 0, -1, [('PT', ps_)], [('PT', ps_)])
                    for h in range(4):
                        self.mm(self.pb[6][:, h * 72:h * 72 + 65], PT[:, h * 128:(h + 1) * 128], VS4[:, j, h, 0:65],
                                (j == 0 and h == 0), j == i, [('PT', ps_), ('VS', j), 'VS_ones'], [('pb', 6)], skip=True)
                return (s_, r_)
            for j in range(i + 1):
                steps.append(pair(j))

            def fin():
                acc = self.pb[6][:, 0:288].rearrange("p (h e) -> p h e", e=72)
                self.S.add('dve', lambda e: e.reciprocal(out=self.rec[:, 0:4], in_=acc[:, :, 64]), reads=[('pb', 6)], writes=['rec'])
                self.tt('dve', self.ob[:, 0:256].rearrange("p (h d) -> p h d", d=64), acc[:, :, 0:64], bc(self.rec[:, 0:4], 2, [128, 4, 64]),
                        ALU.mult, [('pb', 6), 'rec'], ['ob'])
                self.out_tile(i, 'ob', self.ob, 256, 640)
            steps.append((None, fin))
            return steps

        self.c_qz = {}
        self.c_qz[0] = self.make_qz(0, 4)
        for i in range(NT):
            if i + 1 < NT:
                self.c_qz[i + 1] = self.make_qz(i + 1, 4)
            pend = sel_steps(i + 1) if i + 1 < NT else []
            asteps = attn_steps(i)
            gap = max(1, (len(asteps) - 1) // (len(pend) + 1)) if pend else 1
            for n_, a in enumerate(asteps):
                self.emit_step(a[0], a[1], L=3)
                if pend and (n_ % gap == gap - 1) and n_ < len(asteps) - 1:
                    pend.pop(0)()
            while pend:
                pend.pop(0)()
        self.flush_steps()
        self.dbg_oT()

    def final(self, l, xin, xout):
        dr = self.dr
        if not hasattr(self, 'f_yacc'):
            self.f_yacc = self.b_Pc[:, 0, :]
            self.f_ytmp = self.b_Pc[:, 1, :]
            self.f_yT = self.s1_all[:, 0:4096].rearrange("p (k c) -> p k c", c=512)
        Wzg = self.BIG[:, 0:31744].rearrange("p (k c) -> p k c", c=3968)
        Wbr = self.BIG[:, 31744:38912].rearrange("p (k c) -> p k c", c=1024)
        Wout = self.BIG[:, 38912:47104].rearrange("p (k c) -> p k c", c=1024)
        oz = self.BIG[:, 47104:47104 + 3584].rearrange("p (t k c) -> p t k c", k=7, c=128)
        og = self.BIG[:, 50688:50688 + 3584].rearrange("p (t c) -> p t c", c=896)
        sz = [self.qk_sq[:, 0:512], self.qk_xn[:, 0:512]]
        sg = [self.pr[:, 0:512], self.Gt[:, 0:512]]
        self.wkeys = {}
        for (c0, n, o) in ((1152, 384, 0), (2188, 256, 384), (3212, 256, 640), (3468, 3072, 896)):
            self.load_w(Wzg, 'Wzg', dr['w_in'][l], c0, n, o)
        self.load_w(Wbr, 'Wbr', dr['w_br_a'][l], 0, 1024, 0, nk=3, k0=0)
        self.load_w(Wbr, 'Wbr', dr['w_br_b'][l], 0, 1024, 0, nk=2, k0=3)
        self.load_w(Wbr, 'Wbr', dr['w_br_c'][l], 0, 1024, 0, nk=2, k0=5)
        self.load_w(Wout, 'Wout', dr['w_out'][l], 0, 1024, 0)
        kz, kb, ko = list(self.wkeys['Wzg']), list(self.wkeys['Wbr']), list(self.wkeys['Wout'])
        last = (xout is dr['y'])
        n = 0
        for g in range(NT // 4):
            hsl = self.load_hg(g)
            hgt = self.hg[hsl]
            self.dma(og, dr['oT'][g * 4:(g + 1) * 4].rearrange("t p c -> p t c"),
                     [('oT', g * 4 + i, c) for i in range(4) for c in (0, 384, 640)], ['og'])
            for zc in range(7):
                b = zc % 2
                for k in range(8):
                    self.mm(self.pb[b][:, 0:512], Wzg[:, k, zc * 128:(zc + 1) * 128], hgt[:, :, k, :], k == 0, k == 7,
                            [('hg', hsl)] + kz, [('pb', b)])
                self.act(sz[b], self.pb[b][:, 0:512], AF.Silu, [('pb', b)], [('sz', b)])
                self.tt('dve', oz[:, :, zc, :], og[:, :, zc * 128:(zc + 1) * 128], sz[b].rearrange("p (t c) -> p t c", c=128), ALU.mult,
                        [('sz', b), 'og'], [('oz', zc)])
            for m in range(8):
                for br in range(3):
                    gb = 2 + n % 2
                    ub = 4 + n % 2
                    sgb = sg[n % 2]
                    sgk = ('sg', n % 2)
                    n += 1
                    c0 = 896 + br * 1024 + m * 128
                    for k in range(8):
                        self.mm(self.pb[gb][:, 0:512], Wzg[:, k, c0:c0 + 128], hgt[:, :, k, :], k == 0, k == 7,
                                [('hg', hsl)] + kz, [('pb', gb)])
                    self.act(sgb, self.pb[gb][:, 0:512], AF.Sigmoid, [('pb', gb)], [sgk])
                    kcs = ([0, 1, 2], [3, 4], [5, 6])[br]
                    for ii, kc in enumerate(kcs):
                        self.mm(self.pb[ub][:, 0:512], Wbr[:, kc, m * 128:(m + 1) * 128], oz[:, :, kc, :], ii == 0, ii == len(kcs) - 1,
                                [('oz', kc)] + kb, [('pb', ub)])
                    if br == 0:
                        self.tt('dve', self.f_yacc, self.pb[ub][:, 0:512], sgb, ALU.mult, [('pb', ub), sgk], ['f_yacc'])
                    else:
                        self.tt('dve', self.f_ytmp, self.pb[ub][:, 0:512], sgb, ALU.mult, [('pb', ub), sgk], ['f_ytmp'])
                        if br == 1:
                            self.tt('dve', self.f_yacc, self.f_yacc, self.f_ytmp, ALU.add, ['f_yacc', 'f_ytmp'], ['f_yacc'])
                        else:
                            self.tt('dve', self.f_yT[:, m, :], self.f_yacc, self.f_ytmp, ALU.add, ['f_yacc', 'f_ytmp'], [('yT', m)])
            for tt_ in range(4):
                t = g * 4 + tt_
                xsl = self.xt_n % 2
                self.xt_n += 1
                xt = self.xt[xsl]
                self.dma(xt[:], xin[t * 128:(t + 1) * 128, :], [('x1', t)], [('xt', xsl)])
                osl = xsl
                orow = xt
                for nch in range(2):
                    b = 6 + nch
                    for k in range(8):
                        self.mm(self.pb[b][:, 0:512], self.f_yT[:, k, tt_ * 128:(tt_ + 1) * 128], Wout[:, k, nch * 512:(nch + 1) * 512],
                                k == 0, k == 7, [('yT', k)] + ko, [('pb', b)])
                    self.tt('dve', orow[:, nch * 512:(nch + 1) * 512], self.pb[b][:, 0:512], xt[:, nch * 512:(nch + 1) * 512], ALU.add,
                            [('pb', b), ('xt', xsl)], [('xt', xsl)])
                o = self.dma(xout[t * 128:(t + 1) * 128, :], orow[:], [('xt', xsl)], [('y', t) if last else ('x1', t)])
                if last:
                    self.final_ops.append(o)


def host_layout(sh):
    sh = dict(sh)
    sh['norm_g'] = np.ascontiguousarray(sh['norm_g'].reshape(2, 8, 128).transpose(0, 2, 1))
    sh['cmp_pos'] = np.ascontiguousarray(sh['cmp_pos'].transpose(0, 2, 1))
    for nm in ('cmp_k_w1', 'cmp_v_w1'):
        sh[nm] = np.ascontiguousarray(sh[nm].reshape(2, 32, 64, 128).transpose(0, 2, 1, 3))
    return sh


_CACHE = {}


def kernel(**inputs):
    n = 8
    if 'nc' not in _CACHE:
        _CACHE['nc'] = Builder(2).build()
    nc = _CACHE['nc']
    x = np.ascontiguousarray(inputs['x'], dtype=np.float32)
    pos = np.ascontiguousarray(inputs['positions']).astype(np.int32)
    shared = {}
    for k in ('norm_g', 'w_in', 'q_norm_a', 'k_norm_a', 'q_norm_b', 'k_norm_b', 'q_norm_c', 'k_norm_c', 'cmp_pos',
              'cmp_k_w1', 'cmp_k_w2', 'cmp_v_w1', 'cmp_v_w2', 'w_br_a', 'w_br_b', 'w_br_c', 'w_out'):
        shared[k] = np.ascontiguousarray(inputs[k], dtype=np.float32)
    shared = host_layout(shared)
    in_maps = []
    for c in range(n):
        m = dict(shared)
        m['x'] = x[c]
        m['pos'] = np.ascontiguousarray(pos[c].reshape(NT, 128).T)
        in_maps.append(m)
    res = run_bass_kernel_spmd(nc, in_maps, core_ids=list(range(n)))
    return np.stack([np.asarray(r['y'], dtype=np.float32) for r in res.results], axis=0)
```

```python
import contextlib
import math
import numpy as np
import concourse.bass as bass
import concourse.mybir as mybir
from concourse.bass_utils import run_bass_kernel_spmd

F32 = mybir.dt.float32
BF16 = mybir.dt.bfloat16
I32 = mybir.dt.int32
AF = mybir.ActivationFunctionType
ALU = mybir.AluOpType
AX = mybir.AxisListType

SAME_ENG_SYNC = {'pe': False, 'act': True, 'dve': True, 'pool': True, 'sp': True}
N_DMA_SEMS = 8

S_LEN = 4096
D = 1024
NT = 32
INW = 6540
EPS = 1e-6
NEGB = -30000.0


class _Op:
    __slots__ = ('eng', 'fn', 'deps', 'dma', 'signal', 'sem', 'val', 'idx', 'prev')


class Sched:
    def __init__(self, nc):
        self.nc = nc
        self.ops = []
        self.last_w = {}
        self.readers = {}
        self.fence_keys = []

    def add(self, eng, fn, reads=(), writes=(), dma=False):
        op = _Op()
        op.eng = eng
        op.fn = fn
        op.dma = dma
        op.signal = False
        op.sem = None
        op.val = 0
        op.idx = len(self.ops)
        deps = set()
        if self.fence_keys:
            reads = list(reads) + self.fence_keys
        for k in reads:
            w = self.last_w.get(k)
            if w is not None:
                deps.add(w)
        for k in writes:
            w = self.last_w.get(k)
            if w is not None:
                deps.add(w)
            for r in self.readers.get(k, ()):
                deps.add(r)
        op.deps = deps
        for k in reads:
            self.readers.setdefault(k, []).append(op.idx)
        for k in writes:
            self.last_w[k] = op.idx
            self.readers[k] = []
        self.ops.append(op)
        return op.idx

    def emit(self, final_waits=()):
        nc = self.nc
        ops = self.ops
        engs = ['pe', 'act', 'dve', 'pool', 'sp']
        for op in ops:
            need = set()
            best = {}
            for d in op.deps:
                Dp = ops[d]
                if Dp.eng == op.eng and not Dp.dma and not op.dma and not SAME_ENG_SYNC[op.eng]:
                    continue
                if Dp.dma:
                    need.add(d)
                    Dp.signal = True
                elif best.get(Dp.eng, -1) < d:
                    best[Dp.eng] = d
            for d in best.values():
                need.add(d)
                ops[d].signal = True
            op.deps = need
        for d in final_waits:
            ops[d].signal = True
        with contextlib.ExitStack() as st:
            esem = {e: st.enter_context(nc.semaphore('s_' + e)) for e in engs}
            dsem = {e: [st.enter_context(nc.semaphore('d_%s%d' % (e, i))) for i in range(N_DMA_SEMS)]
                    for e in ('sp', 'pool', 'act')}
            ecount = {e: 0 for e in engs}
            dcount = {e: 0 for e in engs}
            for op in ops:
                if op.dma:
                    op.signal = True
                if not op.signal:
                    continue
                if op.dma:
                    i = dcount[op.eng]
                    dcount[op.eng] += 1
                    op.sem = dsem[op.eng][i % N_DMA_SEMS]
                    op.val = 16 * (i // N_DMA_SEMS + 1)
                    op.prev = (op.sem, op.val - 16) if i >= N_DMA_SEMS else None
                else:
                    ecount[op.eng] += 1
                    op.sem = esem[op.eng]
                    op.val = ecount[op.eng]
            per = {e: [] for e in engs}
            for op in ops:
                per[op.eng].append(op)
            block = st.enter_context(nc.Block())

            def run(e, name, extra=()):
                waited = {}
                for op in per[name]:
                    ws = {}
                    for d in op.deps:
                        Dp = ops[d]
                        key = id(Dp.sem)
                        if waited.get(key, 0) >= Dp.val:
                            continue
                        if key not in ws or ws[key][1] < Dp.val:
                            ws[key] = (Dp.sem, Dp.val)
                    if op.dma and op.prev is not None:
                        key = id(op.prev[0])
                        if waited.get(key, 0) < op.prev[1] and (key not in ws or ws[key][1] < op.prev[1]):
                            ws[key] = op.prev
                    for key, (sem, val) in ws.items():
                        e.wait_ge(sem, val)
                        waited[key] = val
                    ins = op.fn(e)
                    if op.signal:
                        ins.then_inc(op.sem, 16 if op.dma else 1)
                for d in extra:
                    Dp = ops[d]
                    e.wait_ge(Dp.sem, Dp.val)

            @block.tensor
            def _(e):
                run(e, 'pe')

            @block.scalar
            def _(e):
                run(e, 'act')

            @block.vector
            def _(e):
                run(e, 'dve')

            @block.gpsimd
            def _(e):
                run(e, 'pool')

            @block.sync
            def _(e):
                run(e, 'sp', extra=final_waits)
        self.stats = {e: len(per[e]) for e in engs}
        self.stats['signals'] = dict(ecount)
        self.stats['dmasig'] = dict(dcount)


def bc(ap, axis, shape):
    return ap.unsqueeze(axis).to_broadcast(shape)


class Builder:
    def __init__(self, n_layers=2, dbg=None, stop_after=None):
        self.n_layers = n_layers
        self.dbg = dbg or ()
        self.stop_after = stop_after
        nc = bass.Bass("TRN2", target_bir_lowering=False)
        self.nc = nc
        self.S = Sched(nc)
        self.st = contextlib.ExitStack()
        self.uid = 0

    def sb(self, name, shape, dt):
        return self.st.enter_context(self.nc.sbuf_tensor(name, shape, dt))

    def ps(self, name, shape, dt=F32):
        return self.st.enter_context(self.nc.psum_tensor(name, shape, dt))

    def mm(self, out, lhsT, rhs, start, stop, r, w, skip=False):
        if skip:
            self.S.add('pe', lambda e: e.matmul(out, lhsT=lhsT, rhs=rhs, start=start, stop=stop, skip_group_check=True), reads=r, writes=w)
        else:
            self.S.add('pe', lambda e: e.matmul(out, lhsT=lhsT, rhs=rhs, start=start, stop=stop), reads=r, writes=w)

    def tr(self, out, in_, ident, r, w):
        self.S.add('pe', lambda e: e.transpose(out=out, in_=in_, identity=ident), reads=r, writes=w)

    def act(self, out, in_, func, r, w, bias=None, scale=None, accum_out=None):
        kw = {}
        if bias is not None:
            kw['bias'] = bias
        if scale is not None:
            kw['scale'] = scale
        if accum_out is not None:
            kw['accum_out'] = accum_out
        self.S.add('act', lambda e: e.activation(out=out, in_=in_, func=func, **kw), reads=r, writes=w)

    def rsqrt(self, out, in_, scale, r, w):
        self.act(out, in_, AF.Ln, r, w, bias=self.epsc[:out.shape[0], 0:1], scale=scale)
        self.act(out, out, AF.Exp, w, w, scale=-0.5)

    def tt(self, eng, out, in0, in1, op, r, w):
        self.S.add(eng, lambda e: e.tensor_tensor(out=out, in0=in0, in1=in1, op=op), reads=r, writes=w)

    def tsc(self, eng, out, in0, s1, s2, op0, op1, r, w):
        if op1 is None:
            self.S.add(eng, lambda e: e.tensor_scalar(out=out, in0=in0, scalar1=s1, scalar2=None, op0=op0), reads=r, writes=w)
        else:
            self.S.add(eng, lambda e: e.tensor_scalar(out=out, in0=in0, scalar1=s1, scalar2=s2, op0=op0, op1=op1), reads=r, writes=w)

    def cp(self, eng, out, in_, r, w):
        if eng == 'act':
            self.S.add('act', lambda e: e.copy(out=out, in_=in_), reads=r, writes=w)
        else:
            self.S.add(eng, lambda e: e.tensor_copy(out=out, in_=in_), reads=r, writes=w)

    def memset(self, eng, ap, val, w):
        self.S.add(eng, lambda e: e.memset(ap, val), writes=w)

    def asel(self, out, in_, pattern, op, fill, base, cm, r, w):
        self.S.add('pool', lambda e: e.affine_select(out=out, in_=in_, pattern=pattern, compare_op=op, fill=fill,
                                                     base=base, channel_multiplier=cm), reads=r, writes=w)

    def dma(self, out, in_, r, w, q='sp'):
        return self.S.add(q, lambda e: e.dma_start(out=out, in_=in_), reads=r, writes=w, dma=True)

    def build_E(self, nm, blk, npart, off):
        E = self.BIG[0:npart, off:off + S_LEN]
        self.memset('pool', E, 1.0, [nm])
        self.asel(E, E, [[1, S_LEN]], ALU.is_ge, 0.0, 0, -blk, [nm], [nm])
        self.asel(E, E, [[-1, S_LEN]], ALU.is_ge, 0.0, blk - 1, blk, [nm], [nm])
        return E

    def make_qz(self, i, nh):
        sl = self.qz_n % 2
        self.qz_n += 1
        Qz = self.Qz[sl]
        npair = nh // 2
        for par in range(2):
            self.cp('pool', Qz[par * 64:(par + 1) * 64, par:nh:2, :], self.QKT[par * 64:(par + 1) * 64, 0:npair, i * 128:(i + 1) * 128],
                    [('QKT', i), 'Qz0'], [('Qz', sl, par)])
        return Qz, [('Qz', sl, 0), ('Qz', sl, 1)]

    def emit_step(self, s_fn, r_fn, L=1):
        if s_fn is not None:
            s_fn()
        if not hasattr(self, '_pend'):
            self._pend = []
        self._pend.append(r_fn)
        while len(self._pend) > L:
            self._pend.pop(0)()

    def flush_steps(self):
        for p in getattr(self, '_pend', []):
            p()
        self._pend = []

    def fence(self):
        self.S.fence_keys = []
        self.fence_n = getattr(self, 'fence_n', 0) + 1
        n = self.fence_n
        fs = self.fsc
        self.mm(self.pb[7][0:1, 0:1], self.identb[0:1, 0:1], self.identb[0:1, 0:1], True, True, ['identb'], [('pb', 7), ('fence', 'pe', n)])
        self.cp('act', fs[0:1, 0:1], fs[0:1, 4:5], ['fsc'], [('fence', 'act', n), 'fscw_act'])
        self.cp('dve', fs[0:1, 1:2], fs[0:1, 5:6], ['fsc'], [('fence', 'dve', n), 'fscw_dve'])
        self.cp('pool', fs[0:1, 2:3], fs[0:1, 6:7], ['fsc'], [('fence', 'pool', n), 'fscw_pool'])
        self.S.fence_keys = [('fence', e, n) for e in ('pe', 'act', 'dve', 'pool')]

    def build(self):
        nc = self.nc
        dr = {}

        def din(name, shape, dt=F32):
            dr[name] = nc.dram_tensor(name, shape, dt, kind="ExternalInput").ap()

        din('x', [S_LEN, D])
        din('pos', [128, NT], I32)
        din('norm_g', [2, 128, 8])
        din('w_in', [2, D, INW])
        for nm in ('q_norm_a', 'k_norm_a', 'q_norm_b', 'k_norm_b', 'q_norm_c', 'k_norm_c'):
            din(nm, [2, 64])
        din('cmp_pos', [2, 64, 32])
        din('cmp_k_w1', [2, 64, 32, 128])
        din('cmp_k_w2', [2, 128, 64])
        din('cmp_v_w1', [2, 64, 32, 128])
        din('cmp_v_w2', [2, 128, 64])
        din('w_br_a', [2, 384, D])
        din('w_br_b', [2, 256, D])
        din('w_br_c', [2, 256, D])
        din('w_out', [2, D, D])
        dr['y'] = nc.dram_tensor('y', [S_LEN, D], F32, kind="ExternalOutput").ap()
        dr['x1'] = nc.dram_tensor('x1s', [S_LEN, D], F32).ap()
        dr['hnT'] = nc.dram_tensor('hnTs', [NT, 128, 1024], BF16).ap()
        dr['oT'] = nc.dram_tensor('oTs', [NT, 128, 7 * 128], BF16).ap()
        for nm, shape, dt in self.dbg:
            dr[nm] = nc.dram_tensor(nm, shape, dt, kind="ExternalOutput").ap()
        self.dr = dr
        self.final_ops = []
        with self.st:
            self.setup()
            for l in range(self.n_layers):
                xin = dr['x'] if l == 0 else dr['x1']
                xout = dr['y'] if l == self.n_layers - 1 else dr['x1']
                self.layer(l, xin, xout)
            self.S.emit(final_waits=self.final_ops)
        return nc

    def setup(self):
        nc = self.nc
        dr = self.dr
        self.alloc_common()
        self.identf = self.sb('identf', [128, 128], F32)
        self.identb = self.sb('identb', [128, 128], BF16)
        self.onesf = self.sb('onesf', [128, 128], F32)
        self.memset('pool', self.identf[:], 1.0, ['identf'])
        self.asel(self.identf[:], self.identf[:], [[-1, 128]], ALU.is_equal, 0.0, 0, 1, ['identf'], ['identf'])
        self.cp('dve', self.identb[:], self.identf[:], ['identf'], ['identb'])
        self.memset('pool', self.onesf[:], 1.0, ['onesf'])
        self.epsc = self.sb('epsc', [128, 1], F32)
        self.memset('dve', self.epsc[:], EPS, ['epsc'])
        for q_ in self.Qz:
            self.memset('pool', q_[:], 0.0, ['Qz0'])
        self.fsc = self.sb('fsc', [128, 8], F32)
        self.memset('dve', self.fsc[:], 0.0, ['fsc'])
        posi = self.sb('posi', [128, NT], I32)
        posf = self.sb('posf', [128, NT], F32)
        fr = self.sb('fr', [128, 8], F32)
        wpF = self.BIG[:, 46592:46592 + 9216].bitcast(F32)
        wpI = self.BIG[:, 46592:46592 + 9216].bitcast(I32)
        ang = wpF[:, 0:256].rearrange("p (t f) -> p t f", f=8)
        tmpa = wpF[:, 256:512].rearrange("p (t f) -> p t f", f=8)
        self.cos = self.sb('cos', [128, NT, 8], F32)
        self.sin = self.sb('sin', [128, NT, 8], F32)
        self.dma(posi[:], dr['pos'][:, :], [], ['posi'])
        self.cp('dve', posf[:], posi[:], ['posi'], ['posf'])
        for i in range(8):
            f = float(np.float32(500000.0) ** np.float32(-i / 8.0))
            self.memset('dve', fr[:, i:i + 1], f, ['fr'])
        self.tt('dve', ang[:], bc(posf[:], 2, [128, NT, 8]), bc(fr[:], 1, [128, NT, 8]), ALU.mult, ['posf', 'fr'], ['ang'])
        PI = math.pi
        HI = 6.28125
        LO = 2 * PI - 6.28125
        ni = wpI[:, 512:768].rearrange("p (t f) -> p t f", f=8)
        nf = wpF[:, 768:1024].rearrange("p (t f) -> p t f", f=8)
        rr = wpF[:, 1024:1280].rearrange("p (t f) -> p t f", f=8)
        self.tsc('dve', tmpa[:], ang[:], 1.0 / (2 * PI), None, ALU.mult, None, ['ang'], ['tmpa'])
        self.cp('dve', ni[:], tmpa[:], ['tmpa'], ['rr_ni'])
        self.cp('dve', nf[:], ni[:], ['rr_ni'], ['rr_nf'])
        self.S.add('dve', lambda e: e.scalar_tensor_tensor(out=rr[:], in0=nf[:], scalar=-HI, in1=ang[:], op0=ALU.mult, op1=ALU.add),
                   reads=['rr_nf', 'ang'], writes=['rr_r'])
        self.S.add('dve', lambda e: e.scalar_tensor_tensor(out=rr[:], in0=nf[:], scalar=-LO, in1=rr[:], op0=ALU.mult, op1=ALU.add),
                   reads=['rr_nf', 'rr_r'], writes=['rr_r'])

        def wrap(buf, key):
            self.tsc('dve', tmpa[:], buf[:], PI, -2 * PI, ALU.is_gt, ALU.mult, [key], ['tmpa'])
            self.tt('dve', buf[:], buf[:], tmpa[:], ALU.add, [key, 'tmpa'], [key])
            self.tsc('dve', tmpa[:], buf[:], -PI, 2 * PI, ALU.is_lt, ALU.mult, [key], ['tmpa'])
            self.tt('dve', buf[:], buf[:], tmpa[:], ALU.add, [key, 'tmpa'], [key])
            self.tsc('dve', buf[:], buf[:], -3.141592, 3.141592, ALU.max, ALU.min, [key], [key])
        wrap(rr, 'rr_r')
        self.act(self.sin[:], rr[:], AF.Sin, ['rr_r'], ['sin'])
        self.tsc('dve', rr[:], rr[:], PI / 2, None, ALU.add, None, ['rr_r', 'sin'], ['rr_r'])
        wrap(rr, 'rr_r')
        self.act(self.cos[:], rr[:], AF.Sin, ['rr_r'], ['cos'])
        scrF = self.BIG[:, 0:24576].bitcast(F32)
        scrI = self.BIG[:, 0:24576].bitcast(I32)

        def carve(src, k):
            return src[:, k * 2176:(k + 1) * 2176].rearrange("p (o q) -> p o q", q=128)
        dA = carve(scrF, 0)
        dAi = carve(scrI, 1)
        t1 = carve(scrF, 2)
        t2 = carve(scrF, 3)
        t3 = carve(scrF, 4)
        t4 = carve(scrI, 4)
        self.MA = self.sb('MA', [128, 17, 128], BF16)
        self.S.add('pool', lambda e: e.iota(dAi[:], pattern=[[128, 17], [1, 128]], base=0, channel_multiplier=-1), writes=['dAi'])
        self.cp('dve', dA[:], dAi[:], ['dAi'], ['dA'])
        self.tsc('dve', t1[:], dA[:], 128.0, None, ALU.is_le, None, ['dA'], ['mt1'])
        self.tsc('dve', t4[:], dAi[:], 3, None, ALU.bitwise_and, None, ['dAi'], ['mt3'])
        self.cp('dve', t2[:], t4[:], ['mt3'], ['mt2'])
        self.tsc('dve', t2[:], t2[:], 0.0, None, ALU.is_equal, None, ['mt2'], ['mt2'])
        self.tsc('dve', t3[:], dA[:], 512.0, None, ALU.is_le, None, ['dA'], ['mt3'])
        self.tt('dve', t2[:], t2[:], t3[:], ALU.mult, ['mt2', 'mt3'], ['mt2'])
        self.tt('dve', t1[:], t1[:], t2[:], ALU.add, ['mt1', 'mt2'], ['mt1'])
        self.tsc('dve', t4[:], dAi[:], 15, None, ALU.bitwise_and, None, ['dAi', 'mt2'], ['mt3'])
        self.cp('dve', t2[:], t4[:], ['mt3', 'mt1'], ['mt2'])
        self.tsc('dve', t2[:], t2[:], 0.0, None, ALU.is_equal, None, ['mt2'], ['mt2'])
        self.tsc('dve', t3[:], dA[:], 2048.0, None, ALU.is_le, None, ['dA', 'mt2'], ['mt3'])
        self.tt('dve', t2[:], t2[:], t3[:], ALU.mult, ['mt2', 'mt3'], ['mt2'])
        self.tt('dve', t1[:], t1[:], t2[:], ALU.add, ['mt1', 'mt2'], ['mt1'])
        self.tsc('dve', t2[:], dA[:], 0.0, None, ALU.is_ge, None, ['dA', 'mt1'], ['mt2'])
        self.tt('dve', self.MA[:], t1[:], t2[:], ALU.mult, ['mt1', 'mt2'], ['MA'])
        self.AM = self.sb('AM', [128, NT, 64], F32)
        vsF = self.BIG[:, 32768:32768 + NT * 432].bitcast(F32)
        vsI = self.BIG[:, 32768:32768 + NT * 432].bitcast(I32)
        am1 = vsF[:, 0:2048].rearrange("p (t j) -> p t j", j=64)
        am1i = vsI[:, 2048:4096].rearrange("p (t j) -> p t j", j=64)
        am2 = vsF[:, 4096:6144].rearrange("p (t j) -> p t j", j=64)
        for a in range(2):
            self.S.add('pool', (lambda a: lambda e: e.iota(am1i[a * 64:(a + 1) * 64], pattern=[[-2, NT], [1, 64]], base=-a,
                                                           channel_multiplier=0))(a),
                       writes=['am1i_%d' % a])
        self.cp('dve', am1[:], am1i[:], ['am1i_0', 'am1i_1'], ['am1_0', 'am1_1'])
        self.tsc('dve', am2[:], am1[:], 0.0, -1e30, ALU.is_gt, ALU.mult, ['am1_0', 'am1_1'], ['am2'])
        self.tsc('dve', am1[:], am1[:], -1.0, 1e4, ALU.is_ge, ALU.mult, ['am1_0', 'am1_1', 'am2'], ['am1', 'am1_0', 'am1_1'])
        self.tt('dve', self.AM[:], am1[:], am2[:], ALU.add, ['am1', 'am2'], ['AM'])
        self.tsc('dve', self.AM[:, :, 0:1], self.AM[:, :, 0:1], 1e4, None, ALU.add, None, ['AM'], ['AM'])
        self.cover = self.sb('cover', [128, 2, 64], F32)
        self.memset('pool', self.cover[:], 1.0, ['cover'])
        self.asel(self.cover[:], self.cover[:], [[-128, 2], [4, 64]], ALU.is_ge, 0.0, 3, -1, ['cover'], ['cover'])
        self.asel(self.cover[:], self.cover[:], [[128, 2], [-4, 64]], ALU.is_ge, 0.0, 1, 1, ['cover'], ['cover'])
        self.pb = [self.ps('pb%d' % i, [128, 512], F32) for i in range(8)]
        self.xt = [self.sb('xt%d' % i, [128, D], F32) for i in range(2)]
        self.hg = [self.sb('hg%d' % i, [128, 4, 8, 128], BF16) for i in range(2)]
        self.wstage = [self.sb('wst%d' % i, [128, 8, 128], F32) for i in range(2)]
        self.wst_n = 0
        self.hg_n = 0
        self.xt_n = 0
        if 'dbg_cs' in [d[0] for d in self.dbg]:
            o = self.dma(self.dr['dbg_cs'][:, 0:256], self.cos[:].rearrange("p t f -> p (t f)"), ['cos'], [])
            self.final_ops.append(o)
            o = self.dma(self.dr['dbg_cs'][:, 256:512], self.sin[:].rearrange("p t f -> p (t f)"), ['sin'], [])
            self.final_ops.append(o)
            mtmp = self.sb('mtmp', [128, 17 * 128], F32)
            self.cp('dve', mtmp[:], self.MA[:].rearrange("p o q -> p (o q)"), ['MA'], ['mtmp'])
            o = self.dma(self.dr['dbg_ma'][:, :], mtmp[:], ['mtmp'], [])
            self.final_ops.append(o)
            o = self.dma(self.dr['dbg_am'][:, :], self.AM[:].rearrange("p t f -> p (t f)"), ['AM'], [])
            self.final_ops.append(o)
            o = self.dma(self.dr['dbg_cov'][:, :], self.cover[:].rearrange("p t f -> p (t f)"), ['cover'], [])
            self.final_ops.append(o)

    def load_w(self, W, wkey, src, c0, n, o, nk=8, k0=0, eng_cycle=('pool', 'dve')):
        done = 0
        while done < n:
            m = min(512, n - done)
            kc = max(1, min(nk, 1024 // m))
            for kk in range(0, nk, kc):
                kn = min(kc, nk - kk)
                sl = self.wst_n % 4
                self.wst_n += 1
                if sl < 2:
                    flat = self.wstage[sl][:].rearrange("p k c -> p (k c)")
                    skey = ('wst', sl)
                else:
                    flat = self.xt[sl - 2][:]
                    skey = ('xt', sl - 2)
                stg = flat[:, 0:kn * m].rearrange("p (k c) -> p k c", c=m)
                self.dma(stg, src[kk * 128:(kk + kn) * 128, c0 + done:c0 + done + m].rearrange("(k p) c -> p k c", p=128),
                         [], [skey])
                eng = eng_cycle[self.wst_n % len(eng_cycle)]
                self.cp(eng, W[:, k0 + kk:k0 + kk + kn, o + done:o + done + m], stg, [skey], [(wkey, self.wst_n)])
                self.wkeys.setdefault(wkey, []).append((wkey, self.wst_n))
            done += m

    def layer(self, l, xin, xout):
        if l > 0:
            self.fence()
        self.stage1(l, xin)
        if self.stop_after == 'stage1':
            return
        only = getattr(self, 'only', None)
        if only is None or 'A' in only:
            self.fence()
            self.pass_A(l)
        if self.stop_after in ('A', 'Aproj'):
            return
        if only is None or 'C' in only:
            self.fence()
            self.pass_C(l)
        if self.stop_after == 'C':
            return
        if only is None or 'B' in only:
            self.fence()
            self.pass_B(l)
        if self.stop_after == 'B':
            return
        self.fence()
        self.final(l, xin, xout)

    def stage1(self, l, xin):
        dr = self.dr
        if l == 0:
            self.gT = self.sb('gT', [128, 8], F32)
            self.s1_all = self.sb('s1all', [128, 4096], BF16)
            self.s1_sq = self.s1_all[:, 0:1024]
            self.s1_ss = self.sb('s1ss', [128, 2], F32)
            self.s1_ss4 = self.sb('s1ss4', [128, 4], F32)
            self.s1_xs = self.s1_all[:, 1024:2048]
            self.s1_hT = [self.s1_all[:, 2048 + i * 1024:3072 + i * 1024].rearrange("p (k c) -> p k c", c=128) for i in range(2)]
            self.prs = [self.pr, self.s1_all[:, 0:1536].bitcast(F32)]
        self.dma(self.gT[:], dr['norm_g'][l], [], ['gT'])
        ring = [(self.xt[0][:], ('xt', 0)), (self.xt[1][:], ('xt', 1)),
                (self.wstage[0][:].rearrange("p k c -> p (k c)"), ('wst', 0)), (self.wstage[1][:].rearrange("p k c -> p (k c)"), ('wst', 1))]
        ss4 = self.s1_ss4

        def st_a(t):
            xt_ap, xkey = ring[t % 4]
            self.dma(xt_ap, xin[t * 128:(t + 1) * 128, :], [('x1', t)], [xkey])
            junk = self.BIG[:, 24576 + (t % 2) * 1024:24576 + (t % 2 + 1) * 1024]
            self.act(junk, xt_ap, AF.Square, [xkey], [('s1junk', t % 2), ('s1ss', t % 4)], accum_out=ss4[:, t % 4:t % 4 + 1])

        def st_b(t):
            ss = ss4[:, t % 4:t % 4 + 1]
            self.act(ss, ss, AF.Ln, [('s1ss', t % 4)], [('s1ss', t % 4)], bias=self.epsc[:, 0:1], scale=1.0 / D)

        def st_c(t):
            ss = ss4[:, t % 4:t % 4 + 1]
            self.act(ss, ss, AF.Exp, [('s1ss', t % 4)], [('s1ss', t % 4)], scale=-0.5)

        def st_d(t):
            sl = t % 2
            xt_ap, xkey = ring[t % 4]
            ss = ss4[:, t % 4:t % 4 + 1]
            xs = (self.s1_sq, self.s1_xs)[sl]
            self.act(xs, xt_ap, AF.Copy, [xkey, ('s1ss', t % 4)], [('s1xs', sl)], scale=ss)
            pT = self.pb[sl][:].bitcast(BF16)
            for k in range(8):
                self.tr(pT[:, k * 128:(k + 1) * 128], xs[:, k * 128:(k + 1) * 128], self.identb[:],
                        [('s1xs', sl), 'identb'], [('pb', sl)])
            hT = self.s1_hT[sl]
            self.tt('dve', hT, pT[:, 0:1024].rearrange("p (k t) -> p k t", k=8), bc(self.gT[:], 2, [128, 8, 128]), ALU.mult,
                    [('pb', sl), 'gT'], [('s1hT', sl)])
            self.dma(dr['hnT'][t], hT.rearrange("p k t -> p (k t)"), [('s1hT', sl)], [('hnT', t)], q='pool')
        for u in range(-3, NT):
            if 0 <= u + 3 < NT:
                st_a(u + 3)
            if 0 <= u + 2 < NT:
                st_b(u + 2)
            if 0 <= u + 1 < NT:
                st_c(u + 1)
            if 0 <= u < NT:
                st_d(u)
        if 'dbg_hnT' in [d[0] for d in self.dbg]:
            for t in range(NT):
                o = self.dma(dr['dbg_hnT'][t], dr['hnT'][t], [('hnT', t)], [])
                self.final_ops.append(o)

    def load_hg(self, g):
        sl = self.hg_n % 2
        self.hg_n += 1
        self.dma(self.hg[sl][:].rearrange("p t k c -> p t (k c)"), self.dr['hnT'][g * 4:(g + 1) * 4].rearrange("t p c -> p t c"),
                 [('hnT', g * 4 + i) for i in range(4)], [('hg', sl)])
        return sl

    def proj_tile(self, hsl, tt, W, wreads, chunks, banks):
        for (c0, n), b in zip(chunks, banks):
            for k in range(8):
                self.mm(self.pb[b][:, 0:n], self.hg[hsl][:, tt, k, :], W[:, k, c0:c0 + n], k == 0, k == 7,
                        [('hg', hsl)] + wreads, [('pb', b)])

    def qk_Q(self, pr, nh, prk, sl):
        W_ = nh * 64
        sq = self.qk_sq[:, 0:W_]
        ss = self.qk_ss2[:, sl, 0:nh]
        self.tt('dve', sq, pr, pr, ALU.mult, [prk], ['qk_sq'])
        self.S.add('dve', lambda e: e.tensor_reduce(out=ss, in_=sq.rearrange("p (h d) -> p h d", d=64), axis=AX.X, op=ALU.add),
                   reads=['qk_sq'], writes=[('qk_ss', sl)])

    def qk_R(self, nh, sl):
        ss = self.qk_ss2[:, sl, 0:nh]
        self.rsqrt(ss, ss, 1.0 / 64, [('qk_ss', sl)], [('qk_ss', sl)])

    def qk_N(self, pr, nh, Gt, t, xb, prk, xbk, sl):
        W_ = nh * 64
        ss = self.qk_ss2[:, sl, 0:nh]
        na = (2 * nh + 2) // 3
        pr3 = pr.rearrange("p (h d) -> p h d", d=64)
        xn3 = self.qk_xn[:, 0:W_].rearrange("p (h d) -> p h d", d=64)
        xb3 = xb.rearrange("p (h d) -> p h d", d=64)
        G3 = Gt.rearrange("p (h d) -> p h d", d=64)
        for eng, h0, h1 in (('dve', 0, na), ('pool', na, nh)):
            n_ = h1 - h0
            if n_ <= 0:
                continue
            kx = 'qk_xn_' + eng
            xn_ = xn3[:, h0:h1, :]
            self.tt(eng, xn_, pr3[:, h0:h1, :], bc(ss[:, h0:h1], 2, [128, n_, 64]), ALU.mult, [prk, ('qk_ss', sl)], [kx])
            self.tt(eng, xn_, xn_, G3[:, h0:h1, :], ALU.mult, [kx, 'Gt'], [kx])
            cosb = bc(self.cos[:, t, :], 1, [128, n_, 8])
            sinb = bc(self.sin[:, t, :], 1, [128, n_, 8])
            r = [self.qk_r[i][:, h0:h1, :] for i in range(4)]
            rk = ['qk_r%d_%s' % (i, eng) for i in range(4)]
            self.tt(eng, r[0], xn_[:, :, 0:8], cosb, ALU.mult, [kx, 'cos'], [rk[0]])
            self.tt(eng, r[1], xn_[:, :, 8:16], sinb, ALU.mult, [kx, 'sin'], [rk[1]])
            self.tt(eng, r[2], xn_[:, :, 8:16], cosb, ALU.mult, [kx, 'cos'], [rk[2]])
            self.tt(eng, r[3], xn_[:, :, 0:8], sinb, ALU.mult, [kx, 'sin'], [rk[3]])
            xk = xbk + '_' + eng
            self.tt(eng, xb3[:, h0:h1, 0:8], r[0], r[1], ALU.subtract, [rk[0], rk[1]], [xk])
            self.tt(eng, xb3[:, h0:h1, 8:16], r[2], r[3], ALU.add, [rk[2], rk[3]], [xk])
            self.cp(eng, xb3[:, h0:h1, 16:64], xn_[:, :, 16:64], [kx], [xk])

    def proj_pipeline(self, P, E, Q, R, N, T):
        P(0)
        E(0)
        Q(0)
        R(0)
        if NT > 1:
            P(1)
        for t in range(NT):
            if t + 1 < NT:
                E(t + 1)
                Q(t + 1)
                R(t + 1)
            N(t)
            if t + 2 < NT:
                P(t + 2)
            T(t)

    def alloc_common(self):
        if hasattr(self, 'qk_sq'):
            return
        self.qk_sq = self.sb('qk_sq', [128, 768], F32)
        self.qk_ss = self.sb('qk_ss', [128, 12], F32)
        self.qk_ss2 = self.sb('qk_ss2', [128, 2, 12], F32)
        self.qk_xn = self.sb('qk_xn', [128, 768], F32)
        self.qk_r = [self.sb('qk_r%d' % i, [128, 12, 8], F32) for i in range(4)]
        self.pr = self.sb('pr', [128, 768], F32)
        self.xb = self.sb('xb', [128, 768], BF16)
        self.Gt = self.sb('Gt', [128, 768], F32)
        self.g64 = self.sb('g64', [128, 2, 64], F32)
        self.PT = [self.sb('PT%d' % i, [128, 768], BF16) for i in range(3)]
        self.pt_n = 0
        self.ob = self.sb('ob', [128, 384], BF16)
        self.rec = self.sb('rec', [128, 12], F32)
        self.oTt = [self.sb('oTt%d' % i, [128, 384], BF16) for i in range(2)]
        self.selT = [self.sb('selT%d' % i, [64, 512], BF16) for i in range(2)]
        self.Qz = [self.sb('Qz%d' % i, [128, 6, 128], BF16) for i in range(2)]
        self.qz_n = 0
        self.ot_n = 0
        self.BIG = self.sb('BIG', [128, 57344], BF16)
        self.QKT = self.BIG[:, 0:32768].rearrange("p (a c) -> p a c", c=S_LEN)
        self.VS = self.BIG[:, 32768:32768 + NT * 432].rearrange("p (t c) -> p t c", c=432)
        self.Wp = self.BIG[:, 46592:46592 + 9216].rearrange("p (k c) -> p k c", c=1152)

    def load_gains(self, l, qn, kn, nq, nk):
        dr = self.dr
        self.dma(self.g64[:, 0, :], dr[qn][l].partition_broadcast(128), [], ['g64q'])
        self.dma(self.g64[:, 1, :], dr[kn][l].partition_broadcast(128), [], ['g64k'])
        G3 = self.Gt[:, 0:(nq + nk) * 64].rearrange("p (h d) -> p h d", d=64)
        self.cp('dve', G3[:, 0:nq, :], bc(self.g64[:, 0, :], 1, [128, nq, 64]), ['g64q'], ['Gt'])
        self.cp('dve', G3[:, nq:nq + nk, :], bc(self.g64[:, 1, :], 1, [128, nk, 64]), ['g64k'], ['Gt'])

    def out_tile(self, i, acc_key, ob_ap, ncol, c0):
        npair = ncol // 128
        pT = self.pb[7][:].bitcast(BF16)
        for p in range(npair):
            self.tr(pT[:, p * 128:(p + 1) * 128], ob_ap[:, p * 128:(p + 1) * 128], self.identb[:], [acc_key, 'identb'], [('pb', 7)])
        sl = self.ot_n % 2
        self.ot_n += 1
        self.cp('act', self.oTt[sl][:, 0:ncol], pT[:, 0:ncol], [('pb', 7)], [('oTt', sl)])
        self.dma(self.dr['oT'][i][:, c0:c0 + ncol], self.oTt[sl][:, 0:ncol], [('oTt', sl)], [('oT', i, c0)])

    def pass_A(self, l):
        dr = self.dr
        self.alloc_common()
        self.wkeys = {}
        W = self.Wp
        self.load_w(W, 'Wp', dr['w_in'][l], 0, 1152, 0)
        wreads = list(self.wkeys['Wp'])
        self.load_gains(l, 'q_norm_a', 'k_norm_a', 6, 6)
        VS4 = self.VS.rearrange("p t (h e) -> p t h e", e=72)
        self.S.add('pool', lambda e: e.memset(VS4[:, :, :, 64:65], 1.0), reads=['MA', 'AM'], writes=['VS_ones'])
        QKT = self.QKT
        hs_ = {}

        def do_proj(t):
            if t % 4 == 0:
                hs_['sl'] = self.load_hg(t // 4)
            self.proj_tile(hs_['sl'], t % 4, W, wreads, [(0, 384), (384, 384), (768, 384)], [0, 1, 2])

        def E_(t):
            pr = self.prs[t % 2]
            self.cp('act', pr[:, 0:384], self.pb[0][:, 0:384], [('pb', 0)], [('pr', t % 2)])
            self.cp('act', pr[:, 384:768], self.pb[1][:, 0:384], [('pb', 1)], [('pr', t % 2)])
            self.cp('act', VS4[:, t, :, 0:64], self.pb[2][:, 0:384].rearrange("p (h d) -> p h d", d=64), [('pb', 2), 'VS_ones', 'MA', 'AM'], [('VS', t)])

        def T_(t):
            pT = self.pb[3][:].bitcast(BF16)
            for p in range(6):
                self.tr(pT[:, p * 128:(p + 1) * 128], self.xb[:, p * 128:(p + 1) * 128], self.identb[:], ['xb_dve', 'xb_pool', 'identb'], [('pb', 3)])
            self.cp('act', QKT[:, 0:6, t * 128:(t + 1) * 128], pT[:, 0:768].rearrange("p (a c) -> p a c", c=128), [('pb', 3), 'MA', 'AM'], [('QKT', t)])
        self.proj_pipeline(do_proj, E_,
                           lambda t: self.qk_Q(self.prs[t % 2][:, 0:768], 12, ('pr', t % 2), t % 2),
                           lambda t: self.qk_R(12, t % 2),
                           lambda t: self.qk_N(self.prs[t % 2][:, 0:768], 12, self.Gt[:, 0:768], t, self.xb[:, 0:768], ('pr', t % 2), 'xb', t % 2),
                           T_)
        if self.stop_after == 'Aproj':
            return
        stA = {'sb': 0}

        def a_pair(i, j, j0, Qz, qzk):
            o = i - j
            d_ = {}

            def s_():
                bS = [(2, 3), (4, 5), (0, 1)][stA['sb'] % 3]
                stA['sb'] += 1
                d_['bS'] = bS
                for p in range(3):
                    bb = bS[0] if p < 2 else bS[1]
                    c0 = (p % 2) * 256
                    self.mm(self.pb[bb][:, c0:c0 + 256], QKT[:, 3 + p, j * 128:(j + 1) * 128],
                            Qz[:, 2 * p:2 * p + 2, :], True, True, [('QKT', j)] + qzk, [('pb', bb)])

            def r_():
                bS = d_['bS']
                ps_ = self.pt_n % 3
                self.pt_n += 1
                PT = self.PT[ps_]
                self.act(PT[:, 0:512], self.pb[bS[0]][:, 0:512], AF.Exp, [('pb', bS[0])], [('PT', ps_)], scale=0.125)
                self.act(PT[:, 512:768], self.pb[bS[1]][:, 0:256], AF.Exp, [('pb', bS[1])], [('PT', ps_)], scale=0.125)
                self.tt('dve', PT[:, 0:768].rearrange("p (h q) -> p h q", q=128), PT[:, 0:768].rearrange("p (h q) -> p h q", q=128),
                        bc(self.MA[:, o, :], 1, [128, 6, 128]), ALU.mult, [('PT', ps_), 'MA'], [('PT', ps_)])
                for h in range(6):
                    self.mm(self.pb[6][:, h * 72:h * 72 + 65], PT[:, h * 128:(h + 1) * 128], VS4[:, j, h, 0:65],
                            (j == j0 and h == 0), j == i, [('PT', ps_), ('VS', j), 'VS_ones'], [('pb', 6)], skip=True)
            return s_, r_

        def a_fin(i):
            def r_():
                acc = self.pb[6][:, 0:432].rearrange("p (h e) -> p h e", e=72)
                self.S.add('dve', lambda e: e.reciprocal(out=self.rec[:, 0:6], in_=acc[:, :, 64]), reads=[('pb', 6)], writes=['rec'])
                self.tt('dve', self.ob[:, 0:384].rearrange("p (h d) -> p h d", d=64), acc[:, :, 0:64], bc(self.rec[:, 0:6], 2, [128, 6, 64]),
                        ALU.mult, [('pb', 6), 'rec'], ['ob'])
                self.out_tile(i, 'ob', self.ob, 384, 0)
            return r_
        for i in range(NT):
            j0 = max(0, i - 16)
            Qz, qzk = self.make_qz(i, 6)
            for j in range(j0, i + 1):
                s_, r_ = a_pair(i, j, j0, Qz, qzk)
                self.emit_step(s_, r_, L=2)
            self.emit_step(None, a_fin(i), L=2)
        self.flush_steps()
        self.dbg_oT()

    def dbg_oT(self):
        if 'dbg_oT' in [d[0] for d in self.dbg] and self.stop_after is not None:
            rng = [r for k, r in (('A', (0, 384)), ('B', (384, 640)), ('C', (640, 896))) if getattr(self, 'only', None) is None or k in self.only]
            for t in range(NT):
                for (a, b) in rng:
                    o = self.dma(self.dr['dbg_oT'][t][:, a:b], self.dr['oT'][t][:, a:b], [('oT', t, 0), ('oT', t, 384), ('oT', t, 640)], [])
                    self.final_ops.append(o)

    def pass_B(self, l):
        dr = self.dr
        if not hasattr(self, 'b_GS'):
            self.b_GS = self.sb('b_GS', [128, NT, 12], F32)
            self.b_posT = self.sb('b_posT', [64, 32], BF16)
            self.b_posTf = self.sb('b_posTf', [64, 32], F32)
            self.b_W2 = self.sb('b_W2', [128, 2, 64], BF16)
            self.b_W2f = self.sb('b_W2f', [128, 2, 64], F32)
            self.b_W2vf = self.b_W2f
            self.b_h1 = self.sb('b_h1', [128, 2, 256], BF16)
            self.b_hb = self.sb('b_hb', [128, 2], F32)
            self.b_kcT = self.sb('b_kcT', [64, 256], BF16)
            self.b_vc = self.sb('b_vc', [128, 2, 72], F32)
            self.b_rdc = self.sb('b_rdc', [128, 2, 4], F32)
            self.b_Pc = self.sb('b_Pc', [128, 2, 512], F32)
            self.b_rden = self.qk_sq[:, 0:512]
            self.b_sc = self.sb('b_sc', [128, 64], F32)
            self.b_sc2 = self.sb('b_sc2', [128, 64], F32)
            self.b_m1 = self.sb('b_m1', [128, 8], F32)
            self.b_m2 = self.sb('b_m2', [128, 8], F32)
            self.b_selb = self.sb('b_selb', [128, 64], F32)
            self.b_selbT2 = [t_[0:64, :].rearrange('p (h q) -> p h q', q=128) for t_ in self.selT]
            self.b_f = self.sb('b_f', [128, 12], F32)
            self.b_ocmp = self.pr[:, 0:512].rearrange('p (a c) -> p a c', c=256)
            self.b_obf = self.qk_xn[:, 0:256].rearrange('p (h d) -> p h d', d=64)
            self.b_tmp = self.qk_xn[:, 256:512].rearrange('p (h d) -> p h d', d=64)
        GS = self.b_GS
        self.wkeys = {}
        W = self.BIG[:, 37376:37376 + 8 * 652].rearrange("p (k c) -> p k c", c=652)
        self.load_w(W, 'Wp', dr['w_in'][l], 1536, 652, 0)
        wreads = list(self.wkeys['Wp'])
        self.load_gains(l, 'q_norm_b', 'k_norm_b', 4, 6)
        Esel = self.BIG[64:128, 5 * S_LEN:6 * S_LEN]
        self.memset('pool', Esel, 1.0, ['Esel'])
        self.asel(Esel, Esel, [[1, S_LEN]], ALU.is_ge, 0.0, 0, -64, ['Esel'], ['Esel'])
        self.asel(Esel, Esel, [[-1, S_LEN]], ALU.is_ge, 0.0, 63, 64, ['Esel'], ['Esel'])
        QS = self.BIG[:, 0:32768].rearrange("p (a c) -> p a c", c=S_LEN)
        VSB = self.BIG[:, 32768:32768 + NT * 144].rearrange("p (t h e) -> p t h e", h=2, e=72)
        self.memset('pool', VSB[:, :, :, 64:65], 1.0, ['VS_ones'])
        QTB = self.BIG[0:64, 0:32768].rearrange("p (a c) -> p a c", c=S_LEN)
        W1 = [self.BIG[0:64, 42592 + i * 4096:42592 + (i + 1) * 4096].rearrange("p (q h) -> p q h", h=128) for i in range(2)]
        for wi, nm in enumerate(('cmp_k_w1', 'cmp_v_w1')):
            for qtr in range(4):
                sl = self.wst_n % 2
                self.wst_n += 1
                stg = self.wstage[sl][0:64]
                self.dma(stg, dr[nm][l][:, qtr * 8:(qtr + 1) * 8, :], [], [('wst', sl)])
                self.cp('pool', W1[wi][:, qtr * 8:(qtr + 1) * 8, :], stg, [('wst', sl)], [('W1', wi, qtr)])
        w1keys = [[('W1', wi, q) for q in range(4)] for wi in range(2)]
        self.dma(self.b_posTf[:], dr['cmp_pos'][l], [], ['b_posTf'])
        self.cp('dve', self.b_posT[:], self.b_posTf[:], ['b_posTf'], ['b_posT'])
        self.dma(self.b_W2f[:, 0, :], dr['cmp_k_w2'][l], [], ['b_W2f0'])
        self.dma(self.b_W2f[:, 1, :], dr['cmp_v_w2'][l], [], ['b_W2f1'])
        self.cp('dve', self.b_W2[:], self.b_W2f[:], ['b_W2f0', 'b_W2f1'], ['b_W2'])
        srcs = [0, 64, 128, 192, 256, 384, 512, 320]
        hs_ = {}

        def do_proj(t):
            if t % 4 == 0:
                hs_['sl'] = self.load_hg(t // 4)
            self.proj_tile(hs_['sl'], t % 4, W, wreads, [(0, 512), (512, 140)], [0, 1])

        def E_(t):
            pr = self.prs[t % 2]
            self.cp('act', pr[:, 0:512], self.pb[0][:, 0:512], [('pb', 0)], [('pr', t % 2)])
            self.cp('act', pr[:, 512:652], self.pb[1][:, 0:140], [('pb', 1)], [('pr', t % 2)])

        def N_(t):
            pr = self.prs[t % 2]
            pk = ('pr', t % 2)
            self.qk_N(pr[:, 0:640], 10, self.Gt[:, 0:640], t, self.xb[:, 0:640], pk, 'xb', t % 2)
            self.cp('dve', self.xb[:, 320:384], pr[:, 320:384], [pk, 'xb_dve', 'xb_pool'], ['xb_dve', 'xb_pool'])
            self.cp('dve', VSB[:, t, 0, 0:64], pr[:, 448:512], [pk, 'VS_ones'], [('VS', t)])
            self.cp('dve', VSB[:, t, 1, 0:64], pr[:, 576:640], [pk, 'VS_ones'], [('VS', t)])
            self.cp('dve', GS[:, t, :], pr[:, 640:652], [pk], [('GS', t)])

        def T_(t):
            pT = self.pb[3][:].bitcast(BF16)
            for si, c0 in enumerate(srcs):
                self.tr(pT[0:64, si * 128:(si + 1) * 128], self.xb[:, c0:c0 + 64], self.identb[:], ['xb_dve', 'xb_pool', 'identb'], [('pb', 3)])
            self.cp('act', QTB[:, 0:8, t * 128:(t + 1) * 128], pT[0:64, 0:1024].rearrange("p (a c) -> p a c", c=128), [('pb', 3)], [('QKT', t)])
        self.proj_pipeline(do_proj, E_,
                           lambda t: self.qk_Q(self.prs[t % 2][:, 0:640], 10, ('pr', t % 2), t % 2),
                           lambda t: self.qk_R(10, t % 2),
                           N_, T_)
        allq = [('QKT', t) for t in range(NT)]
        self.act(GS[:].rearrange("p t g -> p (t g)"), GS[:].rearrange("p t g -> p (t g)"), AF.Sigmoid, [('GS', t) for t in range(NT)], ['GSs'])
        self.memset('dve', self.b_h1[:, :, 255:256], 0.0, ['b_h1z'])
        for wi, slot in ((0, 4), (1, 7)):
            for p in range(32):
                self.mm(self.pb[5][:, 0:255], W1[wi][:, p, :], QTB[:, slot, p:p + 16 * 254 + 1:16], p == 0, p == 31,
                        allq + w1keys[wi], [('pb', 5)])
            for p in range(32):
                self.mm(self.pb[4][:, 0:1], W1[wi][:, p, :], self.b_posT[:, p:p + 1], p == 0, p == 31, w1keys[wi] + ['b_posT'], [('pb', 4)])
            self.cp('dve', self.b_hb[:, wi:wi + 1], self.pb[4][:, 0:1], [('pb', 4)], [('b_hb', wi)])
            self.act(self.b_h1[:, wi, 0:255], self.pb[5][:, 0:255], AF.Silu, [('pb', 5), ('b_hb', wi), 'b_h1z'], [('b_h1', wi)],
                     bias=self.b_hb[:, wi:wi + 1])
        self.mm(self.pb[5][0:64, 0:256], self.b_W2[:, 0, :], self.b_h1[:, 0, :], True, True, ['b_W2', ('b_h1', 0), 'b_h1z'], [('pb', 5)])
        self.cp('dve', self.b_kcT[:, :], self.pb[5][0:64, 0:256], [('pb', 5)], ['b_kcT'])
        for ct in range(2):
            self.mm(self.pb[4][:, ct * 64:(ct + 1) * 64], self.b_h1[:, 1, ct * 128:(ct + 1) * 128], self.b_W2[:, 1, :], True, True,
                    ['b_W2', ('b_h1', 1), 'b_h1z'], [('pb', 4)])
        self.cp('dve', self.b_vc[:, :, 0:64], self.pb[4][:, 0:128].rearrange("p (c d) -> p c d", d=64), [('pb', 4)], ['b_vc'])
        self.memset('dve', self.b_vc[:, :, 64:65], 1.0, ['b_vc'])
        Pc = self.b_Pc
        st = {'sb': 0}

        def qap(i):
            return QTB[:, 0:4, i * 128:(i + 1) * 128]

        def nbank():
            b = (2, 3, 1)[st['sb'] % 3]
            st['sb'] += 1
            return b

        def sel_steps(i):
            qr = [('QKT', i)]
            nct = 2 if i >= 16 else 1
            ob = 0
            selbT = self.b_selbT2[i % 2]
            sk = ('b_selbT', i)

            def s_a():
                for ct in range(nct):
                    b = 7
                    self.mm(self.pb[b][:, 0:512], self.b_kcT[:, ct * 128:(ct + 1) * 128], qap(i), True, True, qr + ['b_kcT'], [('pb', b)])
                    self.act(Pc[:, ct, :], self.pb[b][:, 0:512], AF.Exp, [('pb', b)], [('Pc', ct)], scale=0.125)
                    if ct == 1 or i < 17:
                        self.asel(Pc[:, ct, :].rearrange("p (h q) -> p h q", q=128), Pc[:, ct, :].rearrange("p (h q) -> p h q", q=128),
                                  [[0, 4], [1, 128]], ALU.is_ge, 0.0, 128 * i - 2048 * ct - 31, -16, [('Pc', ct)], [('Pc', ct)])

            def s_d():
                first = True
                for h in range(4):
                    for ct in range(nct):
                        self.mm(self.pb[4][:, h * 64:(h + 1) * 64], Pc[:, ct, h * 128:(h + 1) * 128], self.cover[:, ct, :], first,
                                (h == 3 and ct == nct - 1), [('Pc', ct), 'cover'], [('pb', 4)], skip=True)
                        first = False
                first = True
                for h in range(4):
                    for ct in range(nct):
                        self.mm(self.pb[ob][:, h * 65:h * 65 + 65], Pc[:, ct, h * 128:(h + 1) * 128], self.b_vc[:, ct, 0:65], first,
                                (h == 3 and ct == nct - 1), [('Pc', ct), 'b_vc'], [('pb', ob)], skip=True)
                        first = False

            def s_e():
                rd = self.b_rdc[:, i % 2, :]
                oc = self.pb[ob][:, 0:260].rearrange("p (h e) -> p h e", e=65)
                self.tsc('dve', rd, oc[:, :, 64], 1e-30, None, ALU.add, None, [('pb', ob)], [('b_rdc', i % 2)])
                self.S.add('dve', lambda e: e.reciprocal(out=rd, in_=rd), reads=[('b_rdc', i % 2)], writes=[('b_rdc', i % 2)])
                self.cp('dve', self.b_ocmp[:, i % 2, :].rearrange("p (h d) -> p h d", d=64), oc[:, :, 0:64], [('pb', ob)], [('b_ocmp', i % 2)])
                for h in range(4):
                    in1 = self.AM[:, i, :] if h == 0 else self.b_sc[:]
                    self.S.add('dve', (lambda h, in1: lambda e: e.scalar_tensor_tensor(
                        out=self.b_sc[:], in0=self.pb[4][:, h * 64:(h + 1) * 64], scalar=rd[:, h:h + 1], in1=in1,
                        op0=ALU.mult, op1=ALU.add))(h, in1), reads=[('pb', 4), ('b_rdc', i % 2), 'AM', 'b_sc'], writes=['b_sc'])
                self.S.add('dve', lambda e: e.max(out=self.b_m1[:], in_=self.b_sc[:]), reads=['b_sc'], writes=['b_m1'])
                self.S.add('dve', lambda e: e.match_replace(out=self.b_sc2[:], in_to_replace=self.b_m1[:], in_values=self.b_sc[:],
                                                            imm_value=-3e38), reads=['b_sc', 'b_m1'], writes=['b_sc2'])
                self.S.add('dve', lambda e: e.max(out=self.b_m2[:], in_=self.b_sc2[:]), reads=['b_sc2'], writes=['b_m2'])
                self.tsc('dve', self.b_selb[:], self.b_sc[:], self.b_m2[:, 7:8], None, ALU.is_ge, None, ['b_sc', 'b_m2'], ['b_selb'])
                self.tsc('dve', self.b_selb[:], self.b_selb[:], 1.0, -NEGB, ALU.subtract, ALU.mult, ['b_selb'], ['b_selb'])

            def s_f():
                self.tr(self.pb[4][0:64, 0:128], self.b_selb[:, :], self.identf[:], ['b_selb', 'identf'], [('pb', 4)])
                self.cp('act', QS[64:128, 0:4, i * 128:(i + 1) * 128], bc(self.pb[4][0:64, 0:128], 1, [64, 4, 128]), [('pb', 4)], [sk])
            return [s_a, s_d, s_e, s_f]

        def attn_steps(i):
            qr = [('QKT', i)]
            sk = ('b_selbT', i)
            steps = []

            def sel_pair(j):
                d_ = {}

                def s_():
                    b = nbank()
                    d_['b'] = b
                    bank = self.pb[b]
                    self.mm(bank[:, 0:512], QS[:, 5, j * 128:(j + 1) * 128], QS[:, 0:4, i * 128:(i + 1) * 128], True, True,
                            qr + [('QKT', j), 'Esel', sk], [('pb', b)])

                def f():
                    b = d_['b']
                    bank = self.pb[b]
                    ps_ = self.pt_n % 3
                    self.pt_n += 1
                    PT = self.PT[ps_]
                    self.act(PT[:, 0:512], bank[:, 0:512], AF.Exp, [('pb', b)], [('PT', ps_)], scale=0.125)
                    if j == i:
                        self.asel(PT[:, 0:512].rearrange("p (h q) -> p h q", q=128), PT[:, 0:512].rearrange("p (h q) -> p h q", q=128),
                                  [[0, 4], [1, 128]], ALU.is_ge, 0.0, 0, -1, [('PT', ps_)], [('PT', ps_)])
                    for h in range(4):
                        self.mm(self.pb[6][:, h * 72:h * 72 + 65], PT[:, h * 128:(h + 1) * 128], VSB[:, j, 0, 0:65],
                                (j == 0 and h == 0), j == i, [('PT', ps_), ('VS', j), 'VS_ones'], [('pb', 6)], skip=True)
                return (s_, f)
            j0 = max(0, i - 4)

            def win_pair(j):
                d_ = {}

                def s_():
                    b = nbank()
                    d_['b'] = b
                    bank = self.pb[b]
                    self.mm(bank[:, 0:512], QTB[:, 6, j * 128:(j + 1) * 128], qap(i), True, True, qr + [('QKT', j)], [('pb', b)])

                def f():
                    b = d_['b']
                    bank = self.pb[b]
                    ps_ = self.pt_n % 3
                    self.pt_n += 1
                    PT = self.PT[ps_]
                    self.act(PT[:, 0:512], bank[:, 0:512], AF.Exp, [('pb', b)], [('PT', ps_)], scale=0.125)
                    PT3 = PT[:, 0:512].rearrange("p (h q) -> p h q", q=128)
                    if j == i:
                        self.asel(PT3, PT3, [[0, 4], [1, 128]], ALU.is_ge, 0.0, 0, -1, [('PT', ps_)], [('PT', ps_)])
                    if j == i - 4:
                        self.asel(PT3, PT3, [[0, 4], [-1, 128]], ALU.is_ge, 0.0, -1, 1, [('PT', ps_)], [('PT', ps_)])
                    for h in range(4):
                        self.mm(self.pb[5][:, h * 72:h * 72 + 65], PT[:, h * 128:(h + 1) * 128], VSB[:, j, 1, 0:65],
                                (j == j0 and h == 0), j == i, [('PT', ps_), ('VS', j), 'VS_ones'], [('pb', 5)], skip=True)
                return (s_, f)
            for j in range(i + 1):
                steps.append(sel_pair(j))
            for j in range(j0, i + 1):
                steps.append(win_pair(j))

            def combine():
                accs = self.pb[6][:, 0:288].rearrange("p (h e) -> p h e", e=72)
                accw = self.pb[5][:, 0:288].rearrange("p (h e) -> p h e", e=72)
                ocmp = self.b_ocmp[:, i % 2, :].rearrange("p (h d) -> p h d", d=64)
                f = self.b_f
                self.S.add('dve', lambda e: e.reciprocal(out=f[:, 4:8], in_=accs[:, :, 64]), reads=[('pb', 6)], writes=['b_f'])
                self.S.add('dve', lambda e: e.reciprocal(out=f[:, 8:12], in_=accw[:, :, 64]), reads=[('pb', 5)], writes=['b_f'])
                self.tt('dve', f[:, 4:12], f[:, 4:12], GS[:, i, 4:12], ALU.mult, ['b_f', 'GSs'], ['b_f'])
                self.tt('dve', f[:, 0:4], GS[:, i, 0:4], self.b_rdc[:, i % 2, :], ALU.mult, ['GSs', ('b_rdc', i % 2)], ['b_f'])
                self.tt('dve', self.b_obf, ocmp, bc(f[:, 0:4], 2, [128, 4, 64]), ALU.mult, [('b_ocmp', i % 2), 'b_f'], ['b_obf'])
                self.tt('dve', self.b_tmp, accs[:, :, 0:64], bc(f[:, 4:8], 2, [128, 4, 64]), ALU.mult, [('pb', 6), 'b_f'], ['b_tmp'])
                self.tt('dve', self.b_obf, self.b_obf, self.b_tmp, ALU.add, ['b_obf', 'b_tmp'], ['b_obf'])
                self.tt('dve', self.b_tmp, accw[:, :, 0:64], bc(f[:, 8:12], 2, [128, 4, 64]), ALU.mult, [('pb', 5), 'b_f'], ['b_tmp'])
                self.tt('dve', self.ob[:, 0:256].rearrange("p (h d) -> p h d", d=64), self.b_obf, self.b_tmp, ALU.add,
                        ['b_obf', 'b_tmp'], ['ob'])
                self.out_tile(i, 'ob', self.ob, 256, 384)
            steps.append((None, combine))
            return steps

        for f_ in sel_steps(0):
            f_()
        for i in range(NT):
            pend = sel_steps(i + 1) if i + 1 < NT else []
            asteps = attn_steps(i)
            gap = max(1, (len(asteps) - 1) // (len(pend) + 1)) if pend else 1
            for n_, a in enumerate(asteps):
                self.emit_step(a[0], a[1], L=2)
                if pend and (n_ % gap == gap - 1) and n_ < len(asteps) - 1:
                    pend.pop(0)()
            while pend:
                pend.pop(0)()
        self.flush_steps()
        self.dbg_oT()

    def pass_C(self, l):
        dr = self.dr
        if not hasattr(self, 'c_ksum'):
            self.c_ksum = self.sb('c_ksum', [128, 2, 16], F32)
            self.c_khi = self.sb('c_khi', [128, 2, 16], BF16)
            self.c_klo = self.sb('c_klo', [128, 2, 16], BF16)
            self.c_tmp = self.sb('c_tmp', [128, 2, 16], F32)
            self.c_sc = self.sb('c_sc', [128, 4, 16], F32)
            self.c_mx = self.sb('c_mx', [128, 4, 8], F32)
            self.c_selb = self.sb('c_selb', [128, 4, 16], F32)
            self.c_selbT2 = [t_[0:16, :] for t_ in self.selT]
        self.wkeys = {}
        W = self.Wp
        self.load_w(W, 'Wp', dr['w_in'][l], 2444, 768, 0)
        wreads = list(self.wkeys['Wp'])
        self.load_gains(l, 'q_norm_c', 'k_norm_c', 4, 4)
        self.Ec = self.build_E('Ec', 256, 16, 16384)
        VS4 = self.VS.rearrange("p t (h e) -> p t h e", e=72)
        self.memset('pool', VS4[:, :, 0:4, 64:65], 1.0, ['VS_ones'])
        QKT = self.QKT
        hs_ = {}

        def do_proj(t):
            if t % 4 == 0:
                hs_['sl'] = self.load_hg(t // 4)
            self.proj_tile(hs_['sl'], t % 4, W, wreads, [(0, 512), (512, 256)], [0, 1])

        def E_(t):
            pr = self.prs[t % 2]
            self.cp('act', pr[:, 0:512], self.pb[0][:, 0:512], [('pb', 0)], [('pr', t % 2)])
            self.cp('act', VS4[:, t, 0:4, 0:64], self.pb[1][:, 0:256].rearrange("p (h d) -> p h d", d=64), [('pb', 1), 'VS_ones'], [('VS', t)])

        def T_(t):
            pT = self.pb[3][:].bitcast(BF16)
            for p in range(4):
                self.tr(pT[:, p * 128:(p + 1) * 128], self.xb[:, p * 128:(p + 1) * 128], self.identb[:], ['xb_dve', 'xb_pool', 'identb'], [('pb', 3)])
            self.cp('act', QKT[:, 0:4, t * 128:(t + 1) * 128], pT[:, 0:512].rearrange("p (a c) -> p a c", c=128), [('pb', 3)], [('QKT', t)])
        self.proj_pipeline(do_proj, E_,
                           lambda t: self.qk_Q(self.prs[t % 2][:, 0:512], 8, ('pr', t % 2), t % 2),
                           lambda t: self.qk_R(8, t % 2),
                           lambda t: self.qk_N(self.prs[t % 2][:, 0:512], 8, self.Gt[:, 0:512], t, self.xb[:, 0:512], ('pr', t % 2), 'xb', t % 2),
                           T_)
        allq = [('QKT', t) for t in range(NT)]
        ksum, khi, klo, ktmp = self.c_ksum, self.c_khi, self.c_klo, self.c_tmp
        self.S.add('dve', lambda e: e.tensor_reduce(out=ksum[:], in_=QKT[:, 2:4, :].rearrange("p a (n k) -> p a n k", k=256),
                                                    axis=AX.X, op=ALU.add), reads=allq, writes=['c_ksum'])
        self.cp('dve', khi[:], ksum[:], ['c_ksum'], ['c_khi'])
        self.cp('dve', ktmp[:], khi[:], ['c_khi'], ['c_tmp'])
        self.tt('dve', ktmp[:], ksum[:], ktmp[:], ALU.subtract, ['c_ksum', 'c_tmp'], ['c_tmp'])
        self.cp('dve', klo[:], ktmp[:], ['c_tmp'], ['c_klo'])
        st = {'sb': 0}

        def sel_steps(i):
            nb = i // 2
            if nb == 0:
                return []
            sl = i % 2
            Qz, qzk = self.c_qz[i]
            sc = self.c_sc
            selbT = self.c_selbT2[sl]
            sk = ('c_selbT', sl)

            def s_a():
                for h in range(4):
                    self.mm(self.pb[5][:, h * 16:(h + 1) * 16], Qz[:, h, :], khi[:, h // 2, :], True, False, qzk + ['c_khi'], [('pb', 5)])
                    self.mm(self.pb[5][:, h * 16:(h + 1) * 16], Qz[:, h, :], klo[:, h // 2, :], False, True, qzk + ['c_klo'], [('pb', 5)])

            def s_b():
                self.cp('dve', sc[:].rearrange("p h n -> p (h n)"), self.pb[5][:, 0:64], [('pb', 5)], ['c_sc'])
                if nb < 16:
                    self.memset('dve', sc[:, :, nb:16], -1e30, ['c_sc'])
                for h in range(4):
                    self.S.add('dve', (lambda h: lambda e: e.max(out=self.c_mx[:, h, :], in_=sc[:, h, :]))(h), reads=['c_sc'], writes=['c_mx'])
                self.tt('dve', self.c_selb[:], sc[:], bc(self.c_mx[:, :, 2], 2, [128, 4, 16]), ALU.is_ge, ['c_sc', 'c_mx'], ['c_selb'])
                self.tsc('dve', self.c_selb[:], self.c_selb[:], 1.0, -NEGB, ALU.subtract, ALU.mult, ['c_selb'], ['c_selb'])
                if nb < 16:
                    self.memset('dve', self.c_selb[:, :, nb:16], NEGB, ['c_selb'])

            def s_c():
                for h in range(4):
                    self.tr(self.pb[4][0:16, h * 128:(h + 1) * 128], self.c_selb[:, h, :], self.identf[:], ['c_selb', 'identf'], [('pb', 4)])
                self.cp('act', selbT[:, :], self.pb[4][0:16, 0:512], [('pb', 4)], [sk])
            return [s_a, s_b, s_c]

        def attn_steps(i):
            nb = i // 2
            Qz, qzk = self.c_qz[i]
            selbT = self.c_selbT2[i % 2]
            sk = ('c_selbT', i % 2)
            steps = []

            def pair(j):
                d_ = {}
                past = j < 2 * nb

                def s_():
                    b = (2, 3, 0, 1)[st['sb'] % 4]
                    st['sb'] += 1
                    d_['b'] = b
                    bank = self.pb[b]
                    if past:
                        self.mm(bank[:, 0:512], self.Ec[:, j * 128:(j + 1) * 128], selbT[0:16, 0:512], True, False,
                                ['Ec', sk], [('pb', b)], skip=True)
                    for p in range(2):
                        self.mm(bank[:, p * 256:(p + 1) * 256], QKT[:, 2 + p, j * 128:(j + 1) * 128], Qz[:, 2 * p:2 * p + 2, :],
                                (not past) and p == 0, p == 1, [('QKT', j)] + qzk, [('pb', b)], skip=True)

                def r_():
                    b = d_['b']
                    bank = self.pb[b]
                    ps_ = self.pt_n % 3
                    self.pt_n += 1
                    PT = self.PT[ps_]
                    self.act(PT[:, 0:512], bank[:, 0:512], AF.Exp, [('pb', b)], [('PT', ps_)], scale=0.125)
                    if j == i:
                        self.asel(PT[:, 0:512].rearrange("p (h q) -> p h q", q=128), PT[:, 0:512].rearrange("p (h q) -> p h q", q=128),
                                  [[0, 4], [1, 128]], ALU.is_ge, 0.0, 0, -1, [('PT', ps_)], [('PT', ps_)])
                    for h in range(4):
                        self.mm(self.pb[6][:, h * 72:h * 72 + 65], PT[:, h * 128:(h + 1) * 128], VS4[:, j, h, 0:65],
                                (j == 0 and h == 0), j == i, [('PT', ps_), ('VS', j), 'VS_ones'], [('pb', 6)], skip=True)
                return (s_, r_)
            for j in range(i + 1):
                steps.append(pair(j))

            def fin():
                acc = self.pb[6][:, 0:288].rearrange("p (h e) -> p h e", e=72)
                self.S.add('dve', lambda e: e.reciprocal(out=self.rec[:, 0:4], in_=acc[:, :, 64]), reads=[('pb', 6)], writes=['rec'])
                self.tt('dve', self.ob[:, 0:256].rearrange("p (h d) -> p h d", d=64), acc[:, :, 0:64], bc(self.rec[:, 0:4], 2, [128, 4, 64]),
                        ALU.mult, [('pb', 6), 'rec'], ['ob'])
                self.out_tile(i, 'ob', self.ob, 256, 640)
            steps.append((None, fin))
            return steps

        self.c_qz = {}
        self.c_qz[0] = self.make_qz(0, 4)
        for i in range(NT):
            if i + 1 < NT:
                self.c_qz[i + 1] = self.make_qz(i + 1, 4)
            pend = sel_steps(i + 1) if i + 1 < NT else []
            asteps = attn_steps(i)
            gap = max(1, (len(asteps) - 1) // (len(pend) + 1)) if pend else 1
            for n_, a in enumerate(asteps):
                self.emit_step(a[0], a[1], L=3)
                if pend and (n_ % gap == gap - 1) and n_ < len(asteps) - 1:
                    pend.pop(0)()
            while pend:
                pend.pop(0)()
        self.flush_steps()
        self.dbg_oT()

    def final(self, l, xin, xout):
        dr = self.dr
        if not hasattr(self, 'f_yacc'):
            self.f_yacc = self.b_Pc[:, 0, :]
            self.f_ytmp = self.b_Pc[:, 1, :]
            self.f_yT = self.s1_all[:, 0:4096].rearrange("p (k c) -> p k c", c=512)
        Wzg = self.BIG[:, 0:31744].rearrange("p (k c) -> p k c", c=3968)
        Wbr = self.BIG[:, 31744:38912].rearrange("p (k c) -> p k c", c=1024)
        Wout = self.BIG[:, 38912:47104].rearrange("p (k c) -> p k c", c=1024)
        oz = self.BIG[:, 47104:47104 + 3584].rearrange("p (t k c) -> p t k c", k=7, c=128)
        og = self.BIG[:, 50688:50688 + 3584].rearrange("p (t c) -> p t c", c=896)
        sz = [self.qk_sq[:, 0:512], self.qk_xn[:, 0:512]]
        sg = [self.pr[:, 0:512], self.Gt[:, 0:512]]
        self.wkeys = {}
        for (c0, n, o) in ((1152, 384, 0), (2188, 256, 384), (3212, 256, 640), (3468, 3072, 896)):
            self.load_w(Wzg, 'Wzg', dr['w_in'][l], c0, n, o)
        self.load_w(Wbr, 'Wbr', dr['w_br_a'][l], 0, 1024, 0, nk=3, k0=0)
        self.load_w(Wbr, 'Wbr', dr['w_br_b'][l], 0, 1024, 0, nk=2, k0=3)
        self.load_w(Wbr, 'Wbr', dr['w_br_c'][l], 0, 1024, 0, nk=2, k0=5)
        self.load_w(Wout, 'Wout', dr['w_out'][l], 0, 1024, 0)
        kz, kb, ko = list(self.wkeys['Wzg']), list(self.wkeys['Wbr']), list(self.wkeys['Wout'])
        last = (xout is dr['y'])
        n = 0
        for g in range(NT // 4):
            hsl = self.load_hg(g)
            hgt = self.hg[hsl]
            self.dma(og, dr['oT'][g * 4:(g + 1) * 4].rearrange("t p c -> p t c"),
                     [('oT', g * 4 + i, c) for i in range(4) for c in (0, 384, 640)], ['og'])
            for zc in range(7):
                b = zc % 2
                for k in range(8):
                    self.mm(self.pb[b][:, 0:512], Wzg[:, k, zc * 128:(zc + 1) * 128], hgt[:, :, k, :], k == 0, k == 7,
                            [('hg', hsl)] + kz, [('pb', b)])
                self.act(sz[b], self.pb[b][:, 0:512], AF.Silu, [('pb', b)], [('sz', b)])
                self.tt('dve', oz[:, :, zc, :], og[:, :, zc * 128:(zc + 1) * 128], sz[b].rearrange("p (t c) -> p t c", c=128), ALU.mult,
                        [('sz', b), 'og'], [('oz', zc)])
            for m in range(8):
                for br in range(3):
                    gb = 2 + n % 2
                    ub = 4 + n % 2
                    sgb = sg[n % 2]
                    sgk = ('sg', n % 2)
                    n += 1
                    c0 = 896 + br * 1024 + m * 128
                    for k in range(8):
                        self.mm(self.pb[gb][:, 0:512], Wzg[:, k, c0:c0 + 128], hgt[:, :, k, :], k == 0, k == 7,
                                [('hg', hsl)] + kz, [('pb', gb)])
                    self.act(sgb, self.pb[gb][:, 0:512], AF.Sigmoid, [('pb', gb)], [sgk])
                    kcs = ([0, 1, 2], [3, 4], [5, 6])[br]
                    for ii, kc in enumerate(kcs):
                        self.mm(self.pb[ub][:, 0:512], Wbr[:, kc, m * 128:(m + 1) * 128], oz[:, :, kc, :], ii == 0, ii == len(kcs) - 1,
                                [('oz', kc)] + kb, [('pb', ub)])
                    if br == 0:
                        self.tt('dve', self.f_yacc, self.pb[ub][:, 0:512], sgb, ALU.mult, [('pb', ub), sgk], ['f_yacc'])
                    else:
                        self.tt('dve', self.f_ytmp, self.pb[ub][:, 0:512], sgb, ALU.mult, [('pb', ub), sgk], ['f_ytmp'])
                        if br == 1:
                            self.tt('dve', self.f_yacc, self.f_yacc, self.f_ytmp, ALU.add, ['f_yacc', 'f_ytmp'], ['f_yacc'])
                        else:
                            self.tt('dve', self.f_yT[:, m, :], self.f_yacc, self.f_ytmp, ALU.add, ['f_yacc', 'f_ytmp'], [('yT', m)])
            for tt_ in range(4):
                t = g * 4 + tt_
                xsl = self.xt_n % 2
                self.xt_n += 1
                xt = self.xt[xsl]
                self.dma(xt[:], xin[t * 128:(t + 1) * 128, :], [('x1', t)], [('xt', xsl)])
                osl = xsl
                orow = xt
                for nch in range(2):
                    b = 6 + nch
                    for k in range(8):
                        self.mm(self.pb[b][:, 0:512], self.f_yT[:, k, tt_ * 128:(tt_ + 1) * 128], Wout[:, k, nch * 512:(nch + 1) * 512],
                                k == 0, k == 7, [('yT', k)] + ko, [('pb', b)])
                    self.tt('dve', orow[:, nch * 512:(nch + 1) * 512], self.pb[b][:, 0:512], xt[:, nch * 512:(nch + 1) * 512], ALU.add,
                            [('pb', b), ('xt', xsl)], [('xt', xsl)])
                o = self.dma(xout[t * 128:(t + 1) * 128, :], orow[:], [('xt', xsl)], [('y', t) if last else ('x1', t)], q='pool')
                if last:
                    self.final_ops.append(o)


def host_layout(sh):
    sh = dict(sh)
    sh['norm_g'] = np.ascontiguousarray(sh['norm_g'].reshape(2, 8, 128).transpose(0, 2, 1))
    sh['cmp_pos'] = np.ascontiguousarray(sh['cmp_pos'].transpose(0, 2, 1))
    for nm in ('cmp_k_w1', 'cmp_v_w1'):
        sh[nm] = np.ascontiguousarray(sh[nm].reshape(2, 32, 64, 128).transpose(0, 2, 1, 3))
    return sh


_CACHE = {}


def kernel(**inputs):
    n = 8
    if 'nc' not in _CACHE:
        _CACHE['nc'] = Builder(2).build()
    nc = _CACHE['nc']
    x = np.ascontiguousarray(inputs['x'], dtype=np.float32)
    pos = np.ascontiguousarray(inputs['positions']).astype(np.int32)
    shared = {}
    for k in ('norm_g', 'w_in', 'q_norm_a', 'k_norm_a', 'q_norm_b', 'k_norm_b', 'q_norm_c', 'k_norm_c', 'cmp_pos',
              'cmp_k_w1', 'cmp_k_w2', 'cmp_v_w1', 'cmp_v_w2', 'w_br_a', 'w_br_b', 'w_br_c', 'w_out'):
        shared[k] = np.ascontiguousarray(inputs[k], dtype=np.float32)
    shared = host_layout(shared)
    in_maps = []
    for c in range(n):
        m = dict(shared)
        m['x'] = x[c]
        m['pos'] = np.ascontiguousarray(pos[c].reshape(NT, 128).T)
        in_maps.append(m)
    res = run_bass_kernel_spmd(nc, in_maps, core_ids=list(range(n)))
    return np.stack([np.asarray(r['y'], dtype=np.float32) for r in res.results], axis=0)
```
